# Optimizing a Trainium2 kernel written in Bass

```python
import math
import jax
import jax.numpy as jnp
from jax import lax
import numpy as np

D_MODEL = 2048
BATCH = 32
SEQ = 256
DEPTH = 4
DEC_BATCH = 8
DEC_SEQ = 2048
PAST_LEN = 512

N_MIXERS = 3
N_WIN = (DEPTH + 2) // 3
N_HY = (DEPTH + 1) // 3
N_AX = DEPTH // 3
N_HEADS = 16
N_KV_HEADS = 4
HEAD_DIM = 128
QKV_DIM = (N_HEADS + 2 * N_KV_HEADS) * HEAD_DIM
WINDOW = 128
BLOCK = 128
GRID_W = 64
ROPE_THETA = 10000.0
D_FF = -(-8 * D_MODEL // (3 * 256)) * 256
HY_ORDER = 2
HY_EMB = 33
HY_FILTER_HIDDEN = 64
HY_TARGET = 1e-2
HY_FAST_DECAY = 0.3
HY_SLOW_DECAY = 1.5
HY_MIN_DECAY = math.log(HY_TARGET) / HY_SLOW_DECAY
HY_MAX_DECAY = math.log(HY_TARGET) / HY_FAST_DECAY
EPS = 1e-6
NEG = -1e30

kernel_name = 'hybrid_diffusion_step'


def rmsnorm(x, g):
    xf = x.astype(jnp.float32)
    y = xf * lax.rsqrt(jnp.mean(xf * xf, axis=-1, keepdims=True) + EPS)
    return (y * g.astype(jnp.float32)).astype(x.dtype)


def adaln(cvec, w, b):
    m = jax.nn.silu(cvec) @ w + b
    return jnp.split(m[:, None, :], 6, axis=-1)


def modulate(h, shift, scale):
    return h * (1 + scale) + shift


def swiglu(h, w_gu, w_down):
    g, u = jnp.split(h @ w_gu, 2, axis=-1)
    return (jax.nn.silu(g) * u) @ w_down


def project_qkv(h, wqkv, q_g, k_g):
    B, L, _ = h.shape
    y = h @ wqkv
    nq, nk = N_HEADS * HEAD_DIM, N_KV_HEADS * HEAD_DIM
    q = y[..., :nq].reshape(B, L, N_HEADS, HEAD_DIM)
    k = y[..., nq:nq + nk].reshape(B, L, N_KV_HEADS, HEAD_DIM)
    v = y[..., nq + nk:].reshape(B, L, N_KV_HEADS, HEAD_DIM)
    if q_g is not None:
        q = rmsnorm(q, q_g)
        k = rmsnorm(k, k_g)
    return q, k, v


def axial_rope_tables(L):
    n_rows = L // GRID_W
    row = jnp.repeat(jnp.arange(n_rows, dtype=jnp.float32), GRID_W)
    col = jnp.tile(jnp.arange(GRID_W, dtype=jnp.float32), n_rows)
    half = HEAD_DIM // 2
    inv = ROPE_THETA ** (-jnp.arange(0, half, 2, dtype=jnp.float32) / half)
    ang = jnp.concatenate([row[:, None] * inv, col[:, None] * inv], axis=-1)
    return jnp.cos(ang), jnp.sin(ang)


def apply_axial_rope(x, cos, sin):
    half, quarter = HEAD_DIM // 2, HEAD_DIM // 4
    xf = x.astype(jnp.float32)
    parts = []
    for a in range(2):
        xa = xf[..., a * half:(a + 1) * half]
        x1, x2 = xa[..., :quarter], xa[..., quarter:]
        cs = cos[None, :, None, a * quarter:(a + 1) * quarter]
        sn = sin[None, :, None, a * quarter:(a + 1) * quarter]
        parts += [x1 * cs - x2 * sn, x2 * cs + x1 * sn]
    return jnp.concatenate(parts, axis=-1).astype(x.dtype)


def blocked_attention(q, k_ctx, v_ctx, sink=None, k_lat=None, v_lat=None, window=None):
    B, Lq, H, hd = q.shape
    kvh = k_ctx.shape[2]
    grp = H // kvh
    nb = Lq // BLOCK
    scale = hd ** -0.5
    qb = jnp.moveaxis(q.reshape(B, nb, BLOCK, kvh, grp, hd), 1, 0)
    if k_lat is None:
        k_all, v_all = k_ctx, v_ctx
    elif window is None:
        k_all = jnp.concatenate([k_lat, k_ctx], axis=1)
        v_all = jnp.concatenate([v_lat, v_ctx], axis=1)
    else:
        pad = ((0, 0), (BLOCK, BLOCK), (0, 0), (0, 0))
        k_src, v_src = jnp.pad(k_lat, pad), jnp.pad(v_lat, pad)
        ctx_mask = jnp.ones((BLOCK, k_ctx.shape[1]), dtype=bool)

    def one_block(args):
        b, qblk = args
        if k_lat is not None and window is not None:
            kw = lax.dynamic_slice_in_dim(k_src, b * BLOCK, 3 * BLOCK, axis=1)
            vw = lax.dynamic_slice_in_dim(v_src, b * BLOCK, 3 * BLOCK, axis=1)
            qpos = b * BLOCK + jnp.arange(BLOCK)
            kpos = (b - 1) * BLOCK + jnp.arange(3 * BLOCK)
            m_lat = ((jnp.abs(qpos[:, None] - kpos[None, :]) <= window)
                     & (kpos >= 0)[None, :] & (kpos < Lq)[None, :])
            keys = jnp.concatenate([kw, k_ctx], axis=1)
            vals = jnp.concatenate([vw, v_ctx], axis=1)
            mask = jnp.concatenate([m_lat, ctx_mask], axis=1)
        else:
            keys, vals, mask = k_all, v_all, None
        s = jnp.einsum('bqkgd,bskd->bkgqs', qblk, keys).astype(jnp.float32) * scale
        if mask is not None:
            s = jnp.where(mask, s, NEG)
        if sink is not None:
            sk = sink.astype(jnp.float32).reshape(kvh, grp)[None, :, :, None, None]
            s = jnp.concatenate([s, jnp.broadcast_to(sk, s.shape[:-1] + (1,))], axis=-1)
        p = jax.nn.softmax(s, axis=-1)
        if sink is not None:
            p = p[..., :-1]
        return jnp.einsum('bkgqs,bskd->bqkgd', p.astype(vals.dtype), vals)

    out = lax.map(one_block, (jnp.arange(nb), qb))
    return jnp.moveaxis(out, 0, 1).reshape(B, Lq, H * hd)


def attn_context(h, wqkv, wo, sink, q_g, k_g):
    q, k, v = project_qkv(h, wqkv, q_g, k_g)
    o = blocked_attention(q, k, v, sink=sink)
    return o @ wo, k, v


def attn_latent(h, wqkv, wo, sink, q_g, k_g, k_ctx, v_ctx, window):
    q, k, v = project_qkv(h, wqkv, q_g, k_g)
    cos, sin = axial_rope_tables(h.shape[1])
    q = apply_axial_rope(q, cos, sin)
    k = apply_axial_rope(k, cos, sin)
    o = blocked_attention(q, k_ctx, v_ctx, sink=sink, k_lat=k, v_lat=v, window=window)
    return o @ wo


def short_conv(u, w, b):
    up = jnp.pad(u, ((0, 0), (1, 1), (0, 0)))
    return up[:, :-2] * w[0] + up[:, 1:-1] * w[1] + up[:, 2:] * w[2] + b


def hyena_filters(L, f_w1, f_b1, f_w2, f_b2, f_w3, freq):
    D = f_w3.shape[-1] // (2 * HY_ORDER)
    f32 = jnp.float32
    t = jnp.arange(L, dtype=f32)
    tn = t / max(L - 1, 1)
    bands = (HY_EMB - 1) // 2
    fb = jnp.linspace(1e-4, bands - 1, bands, dtype=f32)
    w = 2.0 * math.pi * t / L
    feats = jnp.concatenate([tn[:, None], jnp.cos(w[:, None] * fb), -jnp.sin(w[:, None] * fb)], axis=-1)
    a = jnp.sin(freq[0].astype(f32) * (feats @ f_w1.astype(f32) + f_b1.astype(f32)))
    a = jnp.sin(freq[1].astype(f32) * (a @ f_w2.astype(f32) + f_b2.astype(f32)))
    hf = (a @ f_w3.astype(f32)).reshape(L, 2, HY_ORDER, D)
    deltas = jnp.abs(jnp.linspace(HY_MIN_DECAY, HY_MAX_DECAY, D, dtype=f32))
    hf = hf * jnp.exp(-tn[:, None] * deltas[None, :])[:, None, None, :]
    fwd, bwd = hf[:, 0], hf[:, 1]
    two = jnp.concatenate([fwd, jnp.zeros((1, HY_ORDER, D), f32), bwd[:0:-1]], axis=0)
    two = two / (jnp.sum(jnp.abs(two), axis=0, keepdims=True) + EPS)
    return jnp.fft.rfft(two, n=2 * L, axis=0)


def long_conv(u, hf, bias):
    L = u.shape[1]
    uf32 = u.astype(jnp.float32)
    uf = jnp.fft.rfft(uf32, n=2 * L, axis=1)
    y = jnp.fft.irfft(uf * hf[None], n=2 * L, axis=1)[:, :L]
    return (y + uf32 * bias.astype(jnp.float32)).astype(u.dtype)


def hyena(h, w_in, conv_w, conv_b, f_w1, f_b1, f_w2, f_b2, f_w3, freq, skip, w_out):
    L = h.shape[1]
    u = short_conv(h @ w_in, conv_w, conv_b)
    v, x1, x2 = jnp.split(u, 3, axis=-1)
    filt = hyena_filters(L, f_w1, f_b1, f_w2, f_b2, f_w3, freq)
    z = x1 * long_conv(v, filt[:, 0], skip[0])
    z = x2 * long_conv(z, filt[:, 1], skip[1])
    return z @ w_out


def setup_inputs(seed: int = 0) -> dict:
    key = jax.random.key(seed)
    ks = iter(jax.random.split(key, 40))
    D = D_MODEL

    def nrm(shape, scale):
        return jax.random.normal(next(ks), shape, jnp.float32) * scale

    kv_win = (DEC_BATCH, N_WIN, PAST_LEN, N_KV_HEADS, HEAD_DIM)
    kv_ax = (DEC_BATCH, N_AX, PAST_LEN, N_KV_HEADS, HEAD_DIM)
    return {
        'x_prompt': nrm((BATCH, SEQ, D), 1.0),
        'x_sample': nrm((DEC_BATCH, DEC_SEQ, D), 1.0),
        'cache_win_k': nrm(kv_win, 1.0),
        'cache_win_v': nrm(kv_win, 1.0),
        'cache_ax_k': nrm(kv_ax, 1.0),
        'cache_ax_v': nrm(kv_ax, 1.0),
        'c': nrm((DEC_BATCH, D), 1.0),
        'c_ctx': nrm((D,), 1.0),
        'norm_mix_g': 1.0 + nrm((DEPTH, D), 0.02),
        'norm_ffn_g': 1.0 + nrm((DEPTH, D), 0.02),
        'mod_w': nrm((DEPTH, D, 6 * D), D ** -0.5),
        'mod_b': nrm((DEPTH, 6 * D), 0.02),
        'win_wqkv': nrm((N_WIN, D, QKV_DIM), D ** -0.5),
        'win_wo': nrm((N_WIN, N_HEADS * HEAD_DIM, D), (N_HEADS * HEAD_DIM) ** -0.5),
        'win_sink': nrm((N_WIN, N_HEADS), 0.5),
        'hy_w_in': nrm((N_HY, D, 3 * D), D ** -0.5),
        'hy_conv_w': nrm((N_HY, 3, 3 * D), 3 ** -0.5),
        'hy_conv_b': nrm((N_HY, 3 * D), 0.02),
        'hy_f_w1': nrm((N_HY, HY_EMB, HY_FILTER_HIDDEN), 1.0),
        'hy_f_b1': nrm((N_HY, HY_FILTER_HIDDEN), 0.1),
        'hy_f_w2': nrm((N_HY, HY_FILTER_HIDDEN, HY_FILTER_HIDDEN), HY_FILTER_HIDDEN ** -0.5),
        'hy_f_b2': nrm((N_HY, HY_FILTER_HIDDEN), 0.1),
        'hy_f_w3': nrm((N_HY, HY_FILTER_HIDDEN, 2 * HY_ORDER * D), HY_FILTER_HIDDEN ** -0.5),
        'hy_freq': 1.0 + nrm((N_HY, 2, HY_FILTER_HIDDEN), 0.02),
        'hy_skip': nrm((N_HY, HY_ORDER, D), 0.1),
        'hy_wo': nrm((N_HY, D, D), D ** -0.5),
        'ax_wqkv': nrm((N_AX, D, QKV_DIM), D ** -0.5),
        'ax_q_g': 1.0 + nrm((N_AX, HEAD_DIM), 0.02),
        'ax_k_g': 1.0 + nrm((N_AX, HEAD_DIM), 0.02),
        'ax_wo': nrm((N_AX, N_HEADS * HEAD_DIM, D), (N_HEADS * HEAD_DIM) ** -0.5),
        'ffn_w_gu': nrm((DEPTH, D, 2 * D_FF), D ** -0.5),
        'ffn_w_down': nrm((DEPTH, D_FF, D), D_FF ** -0.5),
        'final_g': 1.0 + nrm((D,), 0.02),
    }


def reference(x_prompt, x_sample, cache_win_k, cache_win_v, cache_ax_k, cache_ax_v, c, c_ctx,
              norm_mix_g, norm_ffn_g, mod_w, mod_b, win_wqkv, win_wo, win_sink,
              hy_w_in, hy_conv_w, hy_conv_b, hy_f_w1, hy_f_b1, hy_f_w2, hy_f_b2, hy_f_w3, hy_freq, hy_skip, hy_wo,
              ax_wqkv, ax_q_g, ax_k_g, ax_wo, ffn_w_gu, ffn_w_down, final_g):
    ctx, lat = x_prompt, x_sample
    win_k, win_v, ax_k, ax_v = [], [], [], []
    for i in range(DEPTH):
        sh1c, sc1c, g1c, sh2c, sc2c, g2c = adaln(c_ctx[None, :], mod_w[i], mod_b[i])
        sh1l, sc1l, g1l, sh2l, sc2l, g2l = adaln(c, mod_w[i], mod_b[i])
        h_c = modulate(rmsnorm(ctx, norm_mix_g[i]), sh1c, sc1c)
        h_l = modulate(rmsnorm(lat, norm_mix_g[i]), sh1l, sc1l)
        kind, j = i % N_MIXERS, i // N_MIXERS
        if kind == 0:
            o_c, k_c, v_c = attn_context(h_c, win_wqkv[j], win_wo[j], win_sink[j], None, None)
            o_l = attn_latent(h_l, win_wqkv[j], win_wo[j], win_sink[j], None, None,
                              cache_win_k[:, j], cache_win_v[:, j], WINDOW)
            win_k.append(k_c)
            win_v.append(v_c)
        elif kind == 1:
            hp = (hy_w_in[j], hy_conv_w[j], hy_conv_b[j], hy_f_w1[j], hy_f_b1[j], hy_f_w2[j], hy_f_b2[j],
                  hy_f_w3[j], hy_freq[j], hy_skip[j], hy_wo[j])
            o_c = hyena(h_c, *hp)
            o_l = hyena(h_l, *hp)
        else:
            o_c, k_c, v_c = attn_context(h_c, ax_wqkv[j], ax_wo[j], None, ax_q_g[j], ax_k_g[j])
            o_l = attn_latent(h_l, ax_wqkv[j], ax_wo[j], None, ax_q_g[j], ax_k_g[j],
                              cache_ax_k[:, j], cache_ax_v[:, j], None)
            ax_k.append(k_c)
            ax_v.append(v_c)
        ctx = ctx + g1c * o_c
        lat = lat + g1l * o_l
        h_c = modulate(rmsnorm(ctx, norm_ffn_g[i]), sh2c, sc2c)
        h_l = modulate(rmsnorm(lat, norm_ffn_g[i]), sh2l, sc2l)
        ctx = ctx + g2c * swiglu(h_c, ffn_w_gu[i], ffn_w_down[i])
        lat = lat + g2l * swiglu(h_l, ffn_w_gu[i], ffn_w_down[i])
    y_prompt = rmsnorm(ctx, final_g)
    y_sample = rmsnorm(lat, final_g)
    state_win_k = jnp.stack(win_k, axis=1)
    state_win_v = jnp.stack(win_v, axis=1)
    state_ax_k = jnp.stack(ax_k, axis=1)
    state_ax_v = jnp.stack(ax_v, axis=1)
    return (y_prompt, y_sample, state_win_k, state_win_v, state_ax_k, state_ax_v)
```

```python
import math
import os
from contextlib import ExitStack

import numpy as np
import ml_dtypes

import concourse.bass as bass
import concourse.mybir as mybir
from concourse.bass_utils import run_bass_kernel_spmd

F32 = mybir.dt.float32
BF16 = mybir.dt.bfloat16
AF = mybir.ActivationFunctionType
ALU = mybir.AluOpType
NPBF = ml_dtypes.bfloat16

D = 2048
DC = 16
T = 3072
TL = 2048
NTT = 6
TW = 512
DFF = 5632
FC = 44
NH = 16
NKV = 4
HD = 128
PAST = 512
EPS = 1e-6
DEPTH = 4
ENG_NAMES = ("pe", "act", "dve", "pool", "sp")


class Buf:
    __slots__ = ("name", "w", "r")

    def __init__(self, name=""):
        self.name = name
        self.w = None
        self.r = []


class Op:
    __slots__ = ("eng", "fn", "deps", "flag", "val", "sem", "dma")

    def __init__(self, eng, fn, dma):
        self.eng = eng
        self.fn = fn
        self.deps = []
        self.flag = False
        self.val = 0
        self.sem = None
        self.dma = dma


class Sched:
    def __init__(self, nc, stack, n_dma_sems=48):
        self.nc = nc
        self.sem = {e: stack.enter_context(nc.semaphore("s_" + e)) for e in ENG_NAMES}
        self.cnt = {e: 0 for e in ENG_NAMES}
        self.dma_sems = [stack.enter_context(nc.semaphore("s_dma%d" % i)) for i in range(n_dma_sems)]
        self.dma_val = [0] * n_dma_sems
        self.dma_last = [None] * n_dma_sems
        self.dma_rr = 0
        self.ops = {e: [] for e in ENG_NAMES}
        self.waited = {e: {} for e in ENG_NAMES}
        self.bufs = []
        self.n_ops = 0
        self.stopped = False

    def buf(self, name=""):
        b = Buf(name)
        self.bufs.append(b)
        return b

    def bufs_n(self, n, name=""):
        return [self.buf("%s%d" % (name, i)) for i in range(n)]

    def op(self, eng, fn, reads=(), writes=(), dma=0):
        o = Op(eng, fn, dma)
        deps = {}

        def add(d, war):
            if d is None or d is o:
                return
            if d.dma == 0 and d.eng == eng and dma == 0:
                if eng == "pe" or war:
                    return
            deps[id(d)] = d

        for b in reads:
            add(b.w, False)
        for b in writes:
            add(b.w, False)
            for r in b.r:
                add(r, True)
        if dma:
            k = self.dma_rr
            self.dma_rr = (k + 1) % len(self.dma_sems)
            prev = self.dma_last[k]
            if prev is not None:
                deps[id(prev)] = prev
            self.dma_val[k] += 16 * dma
            o.val = self.dma_val[k]
            o.sem = self.dma_sems[k]
            self.dma_last[k] = o
        for d in deps.values():
            d.flag = True
        o.deps = list(deps.values())
        for b in reads:
            if dma == 0:
                b.r = [r for r in b.r if not (r.dma == 0 and r.eng == eng)]
            b.r.append(o)
        for b in writes:
            b.w = o
            b.r = []
        self.ops[eng].append(o)
        self.n_ops += 1
        return o

    def flush(self):
        nc = self.nc
        lasts = {}
        for e in ENG_NAMES:
            for o in reversed(self.ops[e]):
                if o.dma == 0:
                    lasts[e] = o
                    o.flag = True
                    break
        for e in ENG_NAMES:
            for o in self.ops[e]:
                if o.dma == 0 and o.flag:
                    self.cnt[e] += 1
                    o.val = self.cnt[e]
                    o.sem = self.sem[e]
        dma_final = [(self.dma_sems[k], self.dma_val[k]) for k in range(len(self.dma_sems))
                     if self.dma_val[k] > 0]
        with nc.Block() as block:
            for e in ENG_NAMES:
                def body(eng, e=e):
                    waited = self.waited[e]

                    def wait(sem, val):
                        key = id(sem)
                        if waited.get(key, 0) < val:
                            eng.wait_ge(sem, val)
                            waited[key] = val

                    for o in self.ops[e]:
                        for d in o.deps:
                            wait(d.sem, d.val)
                        r = o.fn(eng)
                        if o.dma:
                            if not isinstance(r, (list, tuple)):
                                r = [r]
                            assert len(r) == o.dma, (len(r), o.dma)
                            for ins in r:
                                ins.then_inc(o.sem, 16)
                        elif o.flag:
                            r.then_inc(o.sem, 1)
                    for e2 in ENG_NAMES:
                        if e2 in lasts:
                            wait(lasts[e2].sem, lasts[e2].val)
                    for sem, val in dma_final:
                        wait(sem, val)
                name = {"pe": "tensor", "act": "scalar", "dve": "vector",
                        "pool": "gpsimd", "sp": "sync"}[e]
                getattr(block, name)(body)
        self.ops = {e: [] for e in ENG_NAMES}
        for b in self.bufs:
            b.w = None
            b.r = []
        self.bufs = []
        self.dma_last = [None] * len(self.dma_sems)
        self.nflush = getattr(self, "nflush", 0) + 1
        if self.nflush == getattr(self, "stop", -1):
            self.stopped = True


class StopBuild(Exception):
    pass


class Rot:
    def __init__(self, tiles, bufs):
        self.t = tiles
        self.b = bufs
        self.i = 0

    def next(self):
        k = self.i % len(self.t)
        self.i += 1
        return self.t[k], self.b[k]


class KB:
    def __init__(self, n_layers=DEPTH, dbg=False, stop=-1):
        self.nc = bass.Bass("TRN2", target_bir_lowering=False)
        self.stop = stop
        self.n_layers = n_layers
        self.dbg = dbg
        self.uid = 0
        self.inputs = {}

    def nm(self, s):
        self.uid += 1
        return "%s_%d" % (s, self.uid)

    def din(self, name, shape, dt=F32):
        ap = self.nc.dram_tensor(name, list(shape), dt, kind="ExternalInput").ap()
        self.inputs[name] = (tuple(shape), dt)
        return ap

    def dout(self, name, shape, dt=F32):
        return self.nc.dram_tensor(name, list(shape), dt, kind="ExternalOutput").ap()

    def dscr(self, name, shape, dt):
        return self.nc.dram_tensor(name, list(shape), dt).ap()

    IN_SPECS = {
        "xT_in": ([D, T], F32), "cvec": ([128, DC, 2], F32), "normg_in": ([128, 2, DEPTH, DC], F32),
        "modb_in": ([128, DEPTH, 96], F32), "finalg_in": ([128, DC], F32), "mod_w": ([DEPTH, D, 6 * D], F32),
        "win_wqkv": ([2, D, 3072], F32), "win_wo": ([2, D, D], F32), "esink_in": ([2, 1, NH * 128], F32),
        "ax_wqkv": ([1, D, 3072], F32), "ax_wo": ([1, D, D], F32), "axg_in": ([128, 2], F32),
        "axkg_rep": ([128, 128], F32), "ffn_w_gu": ([DEPTH, D, 2 * DFF], F32), "ffn_w_down": ([DEPTH, DFF, D], F32),
        "cwkT": ([2, NKV, 128, PAST], F32), "cwv": ([2, PAST, 512], F32), "cakT": ([1, NKV, 128, PAST], F32),
        "cav": ([1, PAST, 512], F32), "rope_in": ([128, 2, TL], F32), "ropeP_in": ([128, 128], BF16),
        "wmask_in": ([128, 2, 512], BF16),
        "hy_w_in": ([1, D, 3 * D], F32), "hy_wo": ([1, D, D], F32),
        "hyconv_in": ([128, 48, 4], F32), "ident_in": ([128, 128], BF16), "hyskip_in": ([128, 2, DC], F32),
        "hyw1_in": ([33, 64], F32), "hyw2_in": ([64, 64], F32), "hyw3_in": ([64, 4 * D], F32), "hyfb_in": ([64, 4], F32),
        "drep_in": ([128, D], F32),
        "feats2048_in": ([33, 2048], F32), "feats256_in": ([33, 256], F32),
        "tnn2048_in": ([128, 16], F32), "tnn256_in": ([128, 2], F32),
        "tabF2048_in": ([16, 128, 2, 16, 128], BF16), "tabF256_in": ([2, 128, 2, 2, 128], BF16),
        "tabI2048_in": ([2, 128, 16, 2048], BF16), "tabI256_in": ([2, 128, 2, 256], BF16),
    }

    LAYERED = ("mod_w", "win_wqkv", "win_wo", "ax_wqkv", "ax_wo", "ffn_w_gu", "ffn_w_down", "hy_w_in", "hy_wo")

    def W(self, base, idx):
        name = "%s__%d" % (base, idx)
        if name not in self.inputs:
            shape, dt = type(self).IN_SPECS[base]
            self._lay = getattr(self, "_lay", {})
            self._lay[name] = self.din(name, shape[1:], dt)
        return self._lay[name]

    def __getattr__(self, name):
        specs = type(self).IN_SPECS
        if name in specs and name not in type(self).LAYERED:
            shape, dt = specs[name]
            ap = self.din(name, shape, dt)
            setattr(self, name, ap)
            return ap
        raise AttributeError(name)

    def declare(self):
        nc = self.nc
        self.yT = self.dout("yT", [D, T])
        self.swk = self.dout("swk", [4, 2, 256, 512])
        self.swv = self.dout("swv", [4, 2, 256, 512])
        self.sak = self.dout("sak", [4, 1, 256, 512])
        self.sav = self.dout("sav", [4, 1, 256, 512])
        if self.dbg:
            self.xT = self.dout("xT_dbg", [D, T])
        else:
            self.xT = self.dscr("xT", [D, T], F32)
        self.hT = self.dscr("hT", [D, T], BF16)
        self.qT = self.dscr("qT", [D, T], BF16)
        self.kT = self.dscr("kT", [512, T], BF16)
        self.vtm = self.dscr("vtm", [T, 512], BF16)
        self.oT = self.dscr("oT", [D, T], BF16)
        self.actT = self.dscr("actT", [DFF, T], BF16)

    def persistent(self, stk):
        nc = self.nc
        self.mod = stk.enter_context(nc.sbuf_tensor("mod", [128, DEPTH, 2, 96], F32))
        self.modb = stk.enter_context(nc.sbuf_tensor("modb_sb", [128, DEPTH, 96], F32))
        self.normg = stk.enter_context(nc.sbuf_tensor("normg_sb", [128, 2, DEPTH, DC], F32))
        self.amod = stk.enter_context(nc.sbuf_tensor("amod", [128, DEPTH, 2, 2, DC], F32))
        self.finalg = stk.enter_context(nc.sbuf_tensor("finalg_sb", [128, DC], F32))
        self.ones_bf = stk.enter_context(nc.sbuf_tensor("ones_bf", [128, 128], BF16))
        self.ones_d = stk.enter_context(nc.sbuf_tensor("ones_d", [128, 128], BF16))
        self.ones_h = stk.enter_context(nc.sbuf_tensor("ones_h", [128, 128], BF16))
        self.zero_c = stk.enter_context(nc.sbuf_tensor("zero_c", [128, 1], F32))
        self.eps_c = stk.enter_context(nc.sbuf_tensor("eps_c", [128, 1], F32))

    def stage_mod(self):
        nc, S = self.nc, self.S
        if S.stopped:
            return
        with ExitStack() as stk:
            def sb(n, s, d):
                return stk.enter_context(nc.sbuf_tensor(self.nm(n), s, d))
            cv = sb("cv", [128, DC, 2], F32)
            sg = sb("sg", [128, DC, 2], F32)
            sT = sb("sT", [128, DC, 2], BF16)
            NW = 4
            wbuf = [sb("mw", [128, DC, 512], BF16) for _ in range(NW)]
            Bw = S.bufs_n(NW)
            pst = [stk.enter_context(nc.psum_tensor(self.nm("mps"), [128, 4, 2], F32)) for _ in range(4)]
            Bp = S.bufs_n(4)
            Bcv, BsT, Bmod, Bmb, Bng, Bc = S.buf(), S.buf(), S.buf(), S.buf(), S.buf(), S.buf()
            S.op("sp", lambda e: e.dma_start(out=cv[:], in_=self.cvec), writes=[Bcv], dma=1)
            S.op("sp", lambda e: e.dma_start(out=self.modb[:], in_=self.modb_in), writes=[Bmb], dma=1)
            S.op("sp", lambda e: e.dma_start(out=self.normg[:], in_=self.normg_in), writes=[Bng], dma=1)
            S.op("sp", lambda e: e.dma_start(out=self.finalg[:], in_=self.finalg_in), writes=[Bng], dma=1)
            S.op("dve", lambda e: e.memset(self.ones_bf[:], 1.0), writes=[Bc])
            S.op("dve", lambda e: e.memset(self.ones_d[:], 1.0 / D), writes=[Bc])
            S.op("dve", lambda e: e.memset(self.ones_h[:], 1.0 / HD), writes=[Bc])
            S.op("dve", lambda e: e.memset(self.zero_c[:], 0.0), writes=[Bc])
            S.op("dve", lambda e: e.memset(self.eps_c[:], EPS), writes=[Bc])
            S.op("act", lambda e: e.activation(out=sg[:], in_=cv[:], func=AF.Sigmoid), reads=[Bcv], writes=[BsT])
            S.op("dve", lambda e: e.tensor_tensor(out=sT[:], in0=sg[:], in1=cv[:], op=ALU.mult), reads=[Bcv, BsT], writes=[BsT])
            i = 0
            for l in range(self.n_layers):
                wv = self.W("mod_w", l).rearrange("(c p) n -> p c n", p=128)
                for nb in range(24):
                    s = i % NW
                    ps, bp = pst[i % 4], Bp[i % 4]
                    S.op("pool", lambda e, s=s, nb=nb, wv=wv: e.dma_start(out=wbuf[s][:], in_=wv[:, :, nb * 512:(nb + 1) * 512]),
                         writes=[Bw[s]], dma=1)
                    for oc in range(4):
                        for kc in range(DC):
                            S.op("pe", lambda e, s=s, oc=oc, kc=kc, ps=ps: e.matmul(
                                ps[:, oc, :], lhsT=wbuf[s][:, kc, oc * 128:(oc + 1) * 128], rhs=sT[:, kc, :],
                                start=(kc == 0), stop=(kc == DC - 1)), reads=[Bw[s], BsT], writes=[bp])
                    for g in range(2):
                        S.op("dve", lambda e, l=l, g=g, nb=nb, ps=ps: e.tensor_tensor(
                            out=self.mod[:, l, g, nb * 4:(nb + 1) * 4], in0=ps[:, :, g],
                            in1=self.modb[:, l, nb * 4:(nb + 1) * 4], op=ALU.add), reads=[bp, Bmb], writes=[Bmod])
                    i += 1
            for l in range(self.n_layers):
                for g in range(2):
                    for w in range(2):
                        sc = 16 + 48 * w
                        S.op("dve", lambda e, l=l, g=g, w=w, sc=sc: e.scalar_tensor_tensor(
                            out=self.amod[:, l, g, w, :], in0=self.mod[:, l, g, sc:sc + 16], scalar=1.0,
                            in1=self.normg[:, w, l, :], op0=ALU.add, op1=ALU.mult), reads=[Bmod, Bng], writes=[Bc])
            S.flush()

    def modcol(self, l, g, which, c):
        return self.mod[:, l, g, which * 16 + c: which * 16 + c + 1]

    def stage_copy_in(self):
        nc, S = self.nc, self.S
        if S.stopped:
            return
        with ExitStack() as stk:
            xs = [stk.enter_context(nc.sbuf_tensor(self.nm("cpx"), [128, DC, TW], F32)) for _ in range(3)]
            Bx = S.bufs_n(3)
            xi = self.xT_in.rearrange("(c p) t -> p c t", p=128)
            xo = self.xT.rearrange("(c p) t -> p c t", p=128)
            for tt in range(NTT):
                k = tt % 3
                S.op("sp", lambda e, k=k, tt=tt: e.dma_start(out=xs[k][:], in_=xi[:, :, tt * TW:(tt + 1) * TW]), writes=[Bx[k]], dma=1)
                S.op("sp", lambda e, k=k, tt=tt: e.dma_start(out=xo[:, :, tt * TW:(tt + 1) * TW], in_=xs[k][:]), reads=[Bx[k]], dma=1)
            S.flush()

    def stage_norm(self, l, w, final=False):
        nc, S = self.nc, self.S
        if S.stopped:
            return
        with ExitStack() as stk:
            def sb(n, s, d):
                return stk.enter_context(nc.sbuf_tensor(self.nm(n), s, d))
            NX = 2
            xin = [sb("xin", [128, DC, TW], F32) for _ in range(NX)]
            Bx = [S.bufs_n(4) for _ in range(NX)]
            sq = [sb("sq", [128, DC, TW], BF16) for _ in range(2)]
            Bsq = [S.bufs_n(4) for _ in range(2)]
            pss = [stk.enter_context(nc.psum_tensor(self.nm("nps"), [128, TW], F32)) for _ in range(2)]
            Bps = S.bufs_n(2)
            rstd = [sb("rstd", [128, TW], F32) for _ in range(2)]
            Br = S.bufs_n(2)
            tmp = Rot([sb("ntmp", [128, TW], F32) for _ in range(4)], S.bufs_n(4))
            odt = F32 if final else BF16
            hs = [sb("hs", [128, DC, TW], odt) for _ in range(2)]
            Bh = [S.bufs_n(4) for _ in range(2)]
            xv = self.xT.rearrange("(c p) t -> p c t", p=128)
            ov = (self.yT if final else self.hT).rearrange("(c p) t -> p c t", p=128)

            def load(tt):
                k = tt % NX
                for q in range(4):
                    S.op("sp", lambda e, k=k, q=q, tt=tt: e.dma_start(
                        out=xin[k][:, 4 * q:4 * q + 4, :], in_=xv[:, 4 * q:4 * q + 4, tt * TW:(tt + 1) * TW]),
                        writes=[Bx[k][q]], dma=1)
            load(0)
            for tt in range(NTT):
                if tt + 1 < NTT:
                    load(tt + 1)
                g = 0 if tt < 4 else 1
                k = tt % NX
                k2 = tt % 2
                for q in range(4):
                    S.op("act", lambda e, k=k, k2=k2, q=q: e.activation(
                        out=sq[k2][:, 4 * q:4 * q + 4, :], in_=xin[k][:, 4 * q:4 * q + 4, :], func=AF.Square),
                        reads=[Bx[k][q]], writes=[Bsq[k2][q]])
                for c in range(DC):
                    S.op("pe", lambda e, k2=k2, c=c: e.matmul(pss[k2][:], lhsT=self.ones_d[:], rhs=sq[k2][:, c, :],
                                                             start=(c == 0), stop=(c == DC - 1)),
                         reads=[Bsq[k2][c // 4]], writes=[Bps[k2]])
                S.op("act", lambda e, k2=k2: e.activation(out=rstd[k2][:], in_=pss[k2][:], func=AF.Sqrt, bias=self.eps_c[:, 0:1]),
                     reads=[Bps[k2]], writes=[Br[k2]])
                S.op("dve", lambda e, k2=k2: e.reciprocal(out=rstd[k2][:], in_=rstd[k2][:]), reads=[Br[k2]], writes=[Br[k2]])
                for c in range(DC):
                    tm, Btm = tmp.next()
                    S.op("dve", lambda e, k=k, k2=k2, c=c, tm=tm: e.tensor_tensor(
                        out=tm[:], in0=xin[k][:, c, :], in1=rstd[k2][:], op=ALU.mult),
                        reads=[Bx[k][c // 4], Br[k2]], writes=[Btm])
                    if final:
                        sc_ap, bi_ap = self.finalg[:, c:c + 1], self.zero_c[:, 0:1]
                    else:
                        sc_ap, bi_ap = self.amod[:, l, g, w, c:c + 1], self.modcol(l, g, 3 * w, c)
                    S.op("act", lambda e, k2=k2, c=c, tm=tm, sc_ap=sc_ap, bi_ap=bi_ap: e.activation(
                        out=hs[k2][:, c, :], in_=tm[:], func=AF.Identity, bias=bi_ap, scale=sc_ap),
                        reads=[Btm], writes=[Bh[k2][c // 4]])
                for q in range(4):
                    S.op("sp", lambda e, k2=k2, q=q, tt=tt: e.dma_start(
                        out=ov[:, 4 * q:4 * q + 4, tt * TW:(tt + 1) * TW], in_=hs[k2][:, 4 * q:4 * q + 4, :]),
                        reads=[Bh[k2][q]], dma=1)
            S.flush()

    def gemm(self, A_dram, KC, blocks, epilogue, stk, supertiles=None, nw=3, extra=None):
        nc, S = self.nc, self.S
        if supertiles is None:
            supertiles = [list(range(NTT))]
        maxt = max(len(s) for s in supertiles)
        bw = max(sum(n for _, _, n in segs) for segs, _ in blocks)
        A_sb = stk.enter_context(nc.sbuf_tensor(self.nm("A"), [128, KC, maxt * TW], BF16))
        BA = S.bufs_n(maxt)
        wb = [stk.enter_context(nc.sbuf_tensor(self.nm("W"), [128, KC, bw], BF16)) for _ in range(nw)]
        Bw = S.bufs_n(nw)
        npb = 8 if extra is None else 8 - extra
        banks = Rot([stk.enter_context(nc.psum_tensor(self.nm("gps"), [128, TW], F32)) for _ in range(npb)], S.bufs_n(npb))
        Av = A_dram.rearrange("(c p) t -> p c t", p=128)
        wi = 0
        for st in supertiles:
            for j, tt in enumerate(st):
                npc = 4 if KC > 16 else 2
                h = KC // npc
                S.op("sp", lambda e, j=j, tt=tt, h=h, npc=npc: [
                    e.dma_start(out=A_sb[:, i * h:(i + 1) * h, j * TW:(j + 1) * TW], in_=Av[:, i * h:(i + 1) * h, tt * TW:(tt + 1) * TW])
                    for i in range(npc)], writes=[BA[j]], dma=npc)
            for bi, (segs, groups) in enumerate(blocks):
                s = wi % nw
                wi += 1

                nsp = 4 if KC > 16 else 1
                hk = KC // nsp

                def wload(e, s=s, segs=segs, nsp=nsp, hk=hk):
                    r = []
                    off = 0
                    for W_ap, c0, n in segs:
                        Wv = W_ap.rearrange("(c p) n -> p c n", p=128)
                        for i in range(nsp):
                            r.append(e.dma_start(out=wb[s][:, i * hk:(i + 1) * hk, off:off + n], in_=Wv[:, i * hk:(i + 1) * hk, c0:c0 + n]))
                        off += n
                    return r
                S.op("pool", wload, writes=[Bw[s]], dma=len(segs) * nsp)
                for j, tt in enumerate(st):
                    for gi, grp in enumerate(groups):
                        pl = []
                        for coff in grp:
                            ps, Bp = banks.next()
                            for kc in range(KC):
                                S.op("pe", lambda e, ps=ps, s=s, kc=kc, coff=coff, j=j: e.matmul(
                                    ps[:], lhsT=wb[s][:, kc, coff:coff + 128], rhs=A_sb[:, kc, j * TW:(j + 1) * TW],
                                    start=(kc == 0), stop=(kc == KC - 1)), reads=[Bw[s], BA[j]], writes=[Bp])
                            pl.append((ps, Bp))
                        epilogue(bi, gi, tt, pl)
        return A_sb, BA, wb, Bw

    def make_resid_epilogue(self, stk, l, which, chunk_of):
        nc, S = self.nc, self.S
        xt = Rot([stk.enter_context(nc.sbuf_tensor(self.nm("xr"), [128, TW], F32)) for _ in range(6)], S.bufs_n(6))
        xv = self.xT.rearrange("(c p) t -> p c t", p=128)

        def epi(bi, gi, tt, pl):
            oc = chunk_of(bi, gi)
            g = 0 if tt < 4 else 1
            (ps, Bp), = pl
            x, Bx = xt.next()
            S.op("sp", lambda e: e.dma_start(out=x[:], in_=xv[:, oc, tt * TW:(tt + 1) * TW]), writes=[Bx], dma=1)
            S.op("dve", lambda e: e.scalar_tensor_tensor(out=x[:], in0=ps[:], scalar=self.modcol(l, g, which, oc), in1=x[:],
                                                         op0=ALU.mult, op1=ALU.add), reads=[Bp, Bx], writes=[Bx])
            S.op("sp", lambda e: e.dma_start(out=xv[:, oc, tt * TW:(tt + 1) * TW], in_=x[:]), reads=[Bx], dma=1)
        return epi

    def stage_ffn(self, l):
        nc, S = self.nc, self.S
        self.stage_norm(l, 1)
        Wgu = self.W("ffn_w_gu", l)
        if S.stopped:
            return
        with ExitStack() as stk:
            sg = Rot([stk.enter_context(nc.sbuf_tensor(self.nm("sg"), [128, TW], F32)) for _ in range(3)], S.bufs_n(3))
            ao = Rot([stk.enter_context(nc.sbuf_tensor(self.nm("ao"), [128, TW], BF16)) for _ in range(4)], S.bufs_n(4))
            av = self.actT.rearrange("(c p) t -> p c t", p=128)
            blocks = []
            for jb in range(FC // 2):
                blocks.append(([(Wgu, jb * 256, 256), (Wgu, DFF + jb * 256, 256)], [(0, 256), (128, 384)]))

            def epi(bi, gi, tt, pl):
                j = bi * 2 + gi
                (pg, Bg), (pu, Bu) = pl
                s, Bs = sg.next()
                a, Ba = ao.next()
                S.op("act", lambda e: e.activation(out=s[:], in_=pg[:], func=AF.Sigmoid), reads=[Bg], writes=[Bs])
                S.op("dve", lambda e: e.tensor_tensor(out=s[:], in0=s[:], in1=pg[:], op=ALU.mult), reads=[Bs, Bg], writes=[Bs])
                S.op("dve", lambda e: e.tensor_tensor(out=a[:], in0=s[:], in1=pu[:], op=ALU.mult), reads=[Bs, Bu], writes=[Ba])
                S.op("sp", lambda e: e.dma_start(out=av[:, j, tt * TW:(tt + 1) * TW], in_=a[:]), reads=[Ba], dma=1)
            self.gemm(self.hT, DC, blocks, epi, stk)
            S.flush()
        Wd = self.W("ffn_w_down", l)
        if S.stopped:
            return
        with ExitStack() as stk:
            blocks = [([(Wd, b * 256, 256)], [(0,), (128,)]) for b in range(8)]
            epi = self.make_resid_epilogue(stk, l, 5, lambda bi, gi: bi * 2 + gi)
            self.gemm(self.actT, FC, blocks, epi, stk, supertiles=[[0, 1], [2, 3], [4, 5]], nw=3)
            S.flush()

    def stage_attn(self, l, kind, j):
        nc, S = self.nc, self.S
        self.stage_norm(l, 0)
        win = (kind == 0)
        Wqkv = self.W("win_wqkv" if win else "ax_wqkv", j)
        Wo = self.W("win_wo" if win else "ax_wo", j)
        so_k = (self.swk if win else self.sak)
        so_v = (self.swv if win else self.sav)
        if S.stopped:
            return
        with ExitStack() as stk:
            def sb(n, s, d):
                return stk.enter_context(nc.sbuf_tensor(self.nm(n), s, d))
            rope = sb("rope", [128, 2, TL], F32)
            ropeP = sb("ropeP", [128, 128], BF16)
            axg = sb("axg", [128, 2], F32)
            kgrep = sb("kgrep", [128, 128], F32)
            Bc = S.buf()
            S.op("sp", lambda e: e.dma_start(out=rope[:], in_=self.rope_in), writes=[Bc], dma=1)
            S.op("sp", lambda e: e.dma_start(out=ropeP[:], in_=self.ropeP_in), writes=[Bc], dma=1)
            S.op("sp", lambda e: e.dma_start(out=axg[:], in_=self.axg_in), writes=[Bc], dma=1)
            S.op("sp", lambda e: e.dma_start(out=kgrep[:], in_=self.axkg_rep), writes=[Bc], dma=1)
            qn = Rot([sb("qn", [128, TW], BF16) for _ in range(3)], S.bufs_n(3))
            sqh = Rot([sb("sqh", [128, TW], BF16) for _ in range(2)], S.bufs_n(2))
            rs = Rot([sb("rs", [128, TW], F32) for _ in range(2)], S.bufs_n(2))
            t1 = Rot([sb("t1", [128, TW], F32) for _ in range(2)], S.bufs_n(2))
            t2 = Rot([sb("t2", [128, TW], F32) for _ in range(2)], S.bufs_n(2))
            qo = Rot([sb("qo", [128, TW], BF16) for _ in range(3)], S.bufs_n(3))
            xps = Rot([stk.enter_context(nc.psum_tensor(self.nm("xps"), [128, TW], F32)) for _ in range(3)], S.bufs_n(3))
            qv = self.qT.rearrange("(c p) t -> p c t", p=128)
            kv = self.kT.rearrange("(c p) t -> p c t", p=128)
            blocks = [([(Wqkv, b * 512, 512)], [(0,), (128,), (256,), (384,)]) for b in range(5)]

            def epi(bi, gi, tt, pl):
                oc = bi * 4 + gi
                (ps, Bp), = pl
                isk = oc >= NH
                dst = (kv[:, oc - NH, tt * TW:(tt + 1) * TW] if isk else qv[:, oc, tt * TW:(tt + 1) * TW])
                lat = tt < 4
                q_, Bq = qn.next()
                if win:
                    S.op("act", lambda e: e.activation(out=q_[:], in_=ps[:], func=AF.Copy), reads=[Bp], writes=[Bq])
                else:
                    s_, Bs = sqh.next()
                    r_, Br = rs.next()
                    m_, Bm = xps.next()
                    S.op("act", lambda e: e.activation(out=s_[:], in_=ps[:], func=AF.Square), reads=[Bp], writes=[Bs])
                    S.op("pe", lambda e: e.matmul(m_[:], lhsT=self.ones_h[:], rhs=s_[:], start=True, stop=True), reads=[Bs], writes=[Bm])
                    S.op("act", lambda e: e.activation(out=r_[:], in_=m_[:], func=AF.Sqrt, bias=self.eps_c[:, 0:1]),
                         reads=[Bm], writes=[Br])
                    S.op("dve", lambda e: e.reciprocal(out=r_[:], in_=r_[:]), reads=[Br], writes=[Br])
                    gcol = axg[:, 1:2] if isk else axg[:, 0:1]
                    S.op("dve", lambda e: e.scalar_tensor_tensor(out=q_[:], in0=ps[:], scalar=gcol, in1=r_[:], op0=ALU.mult, op1=ALU.mult),
                         reads=[Bp, Br, Bc], writes=[Bq])
                if not lat or os.environ.get('KDBG_NOROPE'):
                    S.op("sp", lambda e: e.dma_start(out=dst, in_=q_[:]), reads=[Bq], dma=1)
                    return
                m_, Bm = xps.next()
                a_, Ba = t1.next()
                b_, Bb = t2.next()
                o_, Bo = qo.next()
                S.op("pe", lambda e: e.matmul(m_[:], lhsT=ropeP[:], rhs=q_[:], start=True, stop=True), reads=[Bq, Bc], writes=[Bm])
                S.op("dve", lambda e: e.tensor_tensor(out=a_[:], in0=q_[:], in1=rope[:, 0, tt * TW:(tt + 1) * TW], op=ALU.mult),
                     reads=[Bq, Bc], writes=[Ba])
                S.op("dve", lambda e: e.tensor_tensor(out=b_[:], in0=m_[:], in1=rope[:, 1, tt * TW:(tt + 1) * TW], op=ALU.mult),
                     reads=[Bm, Bc], writes=[Bb])
                S.op("pool", lambda e: e.tensor_tensor(out=o_[:], in0=a_[:], in1=b_[:], op=ALU.add), reads=[Ba, Bb], writes=[Bo])
                S.op("sp", lambda e: e.dma_start(out=dst, in_=o_[:]), reads=[Bo], dma=1)
            A_sb, BA, wkv, Bwkv = self.gemm(self.hT, DC, blocks, epi, stk, nw=2, extra=3)
            for i_, c0 in enumerate((2048, 2560)):
                S.op("pool", lambda e, i_=i_, c0=c0: e.dma_start(
                    out=wkv[i_][:], in_=Wqkv.rearrange("(c p) n -> p c n", p=128)[:, :, c0:c0 + 512]), writes=[Bwkv[i_]], dma=1)
            vb = Rot([sb("vb", [128, 512], BF16) for _ in range(3)], S.bufs_n(3))
            vf = Rot([sb("vf", [128, 512], F32) for _ in range(3)], S.bufs_n(3))
            ssq = Rot([sb("ssq", [128, 4], F32) for _ in range(2)], S.bufs_n(2))
            junk = Rot([sb("junk", [128, 128], F32) for _ in range(2)], S.bufs_n(2))
            for tc in range(0 if not os.environ.get('KDBG_NOVK') else 999, int(os.environ.get('KDBG_VKMAX', T // 128))):
                ctx = tc >= TL // 128
                for which in ((1, 0) if ctx else (1,)):
                    ps, Bp = xps.next()
                    for kc in range(DC):
                        S.op("pe", lambda e, ps=ps, kc=kc, tc=tc, which=which: e.matmul(
                            ps[:], lhsT=A_sb[:, kc, tc * 128:(tc + 1) * 128], rhs=wkv[which][:, kc, :],
                            start=(kc == 0), stop=(kc == DC - 1)), reads=[BA[tc // 4], Bwkv[which]], writes=[Bp])
                    if which == 1:
                        v_, Bv = vb.next()
                        S.op("act", lambda e, v_=v_, ps=ps: e.activation(out=v_[:], in_=ps[:], func=AF.Copy), reads=[Bp], writes=[Bv])
                        S.op("sp", lambda e, v_=v_, tc=tc: e.dma_start(out=self.vtm[tc * 128:(tc + 1) * 128, :], in_=v_[:]), reads=[Bv], dma=1)
                    if ctx:
                        sq_, half = divmod(tc - TL // 128, 2)
                        f_, Bf = vf.next()
                        if which == 1 or win:
                            S.op("act", lambda e, f_=f_, ps=ps: e.activation(out=f_[:], in_=ps[:], func=AF.Copy), reads=[Bp], writes=[Bf])
                        else:
                            s_, Bs = ssq.next()
                            jk, Bj = junk.next()
                            for h in range(NKV):
                                S.op("act", lambda e, h=h, jk=jk, ps=ps, s_=s_: e.activation(
                                    out=jk[:], in_=ps[:, h * 128:(h + 1) * 128], func=AF.Square, accum_out=s_[:, h:h + 1]),
                                    reads=[Bp], writes=[Bj, Bs])
                            S.op("dve", lambda e, s_=s_: e.tensor_scalar(out=s_[:], in0=s_[:], scalar1=1.0 / HD, scalar2=EPS,
                                                                        op0=ALU.mult, op1=ALU.add), reads=[Bs], writes=[Bs])
                            S.op("act", lambda e, s_=s_: e.activation(out=s_[:], in_=s_[:], func=AF.Sqrt), reads=[Bs], writes=[Bs])
                            S.op("dve", lambda e, s_=s_: e.reciprocal(out=s_[:], in_=s_[:]), reads=[Bs], writes=[Bs])
                            for h in range(NKV):
                                S.op("dve", lambda e, h=h, f_=f_, ps=ps, s_=s_: e.scalar_tensor_tensor(
                                    out=f_[:, h * 128:(h + 1) * 128], in0=ps[:, h * 128:(h + 1) * 128], scalar=s_[:, h:h + 1],
                                    in1=kgrep[:], op0=ALU.mult, op1=ALU.mult), reads=[Bp, Bs, Bc], writes=[Bf])
                        dst = (so_v if which == 1 else so_k)[sq_, j, half * 128:(half + 1) * 128, :]
                        if not os.environ.get('KDBG_NOSTATE'):
                            S.op("sp", lambda e, f_=f_, dst=dst: e.dma_start(out=dst, in_=f_[:]), reads=[Bf], dma=1)
            S.flush()
        self.attn_core(l, kind, j)
        if S.stopped:
            return
        with ExitStack() as stk:
            blocks = [([(Wo, b * 512, 512)], [(0,), (128,), (256,), (384,)]) for b in range(4)]
            epi = self.make_resid_epilogue(stk, l, 2, lambda bi, gi: bi * 4 + gi)
            self.gemm(self.oT, DC, blocks, epi, stk)
            S.flush()

    def attn_core(self, l, kind, j):
        nc, S = self.nc, self.S
        win = (kind == 0)
        ckT = (self.cwkT if win else self.cakT)[j]
        cv_ = (self.cwv if win else self.cav)[j]
        scale = HD ** -0.5
        if S.stopped:
            return
        with ExitStack() as stk:
            def sb(n, s, d):
                return stk.enter_context(nc.sbuf_tensor(self.nm(n), s, d))
            NKC = T // 128
            k_sb = sb("k_sb", [128, NKV, T + PAST], BF16)
            v_sb = sb("v_sb", [128, NKC + 4, 512], BF16)
            Bk, Bv, Bc = S.buf(), S.buf(), S.buf()
            S.op("sp", lambda e: e.dma_start(out=k_sb[:, :, 0:T], in_=self.kT.rearrange("(g p) t -> p g t", p=128)), writes=[Bk], dma=1)
            S.op("pool", lambda e: e.dma_start(out=k_sb[:, :, T:T + PAST], in_=ckT.rearrange("g p t -> p g t")), writes=[Bk], dma=1)
            S.op("sp", lambda e: e.dma_start(out=v_sb[:, 0:NKC, :], in_=self.vtm.rearrange("(c p) n -> p c n", p=128)), writes=[Bv], dma=1)
            S.op("pool", lambda e: e.dma_start(out=v_sb[:, NKC:NKC + 4, :], in_=cv_.rearrange("(c p) n -> p c n", p=128)), writes=[Bv], dma=1)
            wmask = sb("wmask", [128, 2, 512], BF16)
            S.op("sp", lambda e: e.dma_start(out=wmask[:], in_=self.wmask_in), writes=[Bc], dma=1)
            esk = None
            if win:
                esr = sb("esr", [1, NH * 128], F32)
                esk = sb("esk", [1, NH * 128], BF16)
                ones1 = sb("ones1", [1, 128], BF16)
                S.op("sp", lambda e: e.dma_start(out=esr[:], in_=self.esink_in[j]), writes=[Bc], dma=1)
                S.op("act", lambda e: e.activation(out=esr[:], in_=esr[:], func=AF.Exp), reads=[Bc], writes=[Bc])
                esl = sb("esl", [1, NH * 128], BF16)
                esf = sb("esf", [1, NH * 128], F32)
                S.op("act", lambda e: e.activation(out=esk[:], in_=esr[:], func=AF.Copy), reads=[Bc], writes=[Bc])
                S.op("act", lambda e: e.activation(out=esf[:], in_=esk[:], func=AF.Copy), reads=[Bc], writes=[Bc])
                S.op("dve", lambda e: e.tensor_tensor(out=esf[:], in0=esr[:], in1=esf[:], op=ALU.subtract), reads=[Bc], writes=[Bc])
                S.op("act", lambda e: e.activation(out=esl[:], in_=esf[:], func=AF.Copy), reads=[Bc], writes=[Bc])
                S.op("dve", lambda e: e.memset(ones1[:], 1.0), writes=[Bc])
            q_sb = [sb("q_sb", [128, 4, T], BF16) for _ in range(2)]
            Bq = S.bufs_n(2)
            o_sb = [sb("o_sb", [128, 4, T], BF16) for _ in range(2)]
            Bo = S.bufs_n(2)
            pT = Rot([sb("pT", [128, 512], BF16) for _ in range(4)], S.bufs_n(4))
            rc = Rot([sb("rc", [128, 512], F32) for _ in range(2)], S.bufs_n(2))
            sps = Rot([stk.enter_context(nc.psum_tensor(self.nm("sps"), [128, 512], F32)) for _ in range(3)], S.bufs_n(3))
            dps = Rot([stk.enter_context(nc.psum_tensor(self.nm("dps"), [128, 512], F32)) for _ in range(2)], S.bufs_n(2))
            ops_ = Rot([stk.enter_context(nc.psum_tensor(self.nm("ops"), [128, 512], F32)) for _ in range(2)], S.bufs_n(2))
            seqs = [(0, TL, True)] + [(TL + 256 * s, 256, False) for s in range(4)]
            qv = self.qT.rearrange("(h p) t -> p h t", p=128)
            ov = self.oT.rearrange("(h p) t -> p h t", p=128)
            for g in range(NKV):
                qs, Bqs = q_sb[g % 2], Bq[g % 2]
                os_, Bos = o_sb[g % 2], Bo[g % 2]
                S.op("sp", lambda e, g=g, qs=qs: e.dma_start(out=qs[:], in_=qv[:, 4 * g:4 * g + 4, :]), writes=[Bqs], dma=1)
                for (t0, ln, has_cache) in seqs:
                    nqb = ln // 128
                    for qb in range(nqb):
                        chunks = []
                        if win and has_cache:
                            for d_, mk in ((-1, 0), (0, None), (1, 1)):
                                kb = qb + d_
                                if 0 <= kb < nqb:
                                    chunks.append((t0 + kb * 128, (t0 + kb * 128) // 128, mk))
                        else:
                            for kb in range(nqb):
                                chunks.append((t0 + kb * 128, (t0 + kb * 128) // 128, None))
                        if has_cache:
                            for c in range(4):
                                chunks.append((T + c * 128, NKC + c, None))
                        dp, Bd = dps.next()
                        op_, Bop = ops_.next()
                        q0 = t0 + qb * 128
                        n = len(chunks)
                        for ci, (kc0, vc, mk) in enumerate(chunks):
                            sp_, Bs = sps.next()
                            p_, Bp = pT.next()
                            S.op("pe", lambda e, sp_=sp_, kc0=kc0, g=g, qs=qs, q0=q0: e.matmul(
                                sp_[:].rearrange("p (h q) -> p h q", h=4), lhsT=k_sb[:, g, kc0:kc0 + 128], rhs=qs[:, :, q0:q0 + 128],
                                start=True, stop=True), reads=[Bk, Bqs], writes=[Bs])
                            S.op("act", lambda e, sp_=sp_, p_=p_: e.activation(out=p_[:], in_=sp_[:], func=AF.Exp, scale=scale),
                                 reads=[Bs], writes=[Bp])
                            if mk is not None:
                                S.op("pool", lambda e, p_=p_, mk=mk: e.tensor_tensor(out=p_[:], in0=p_[:], in1=wmask[:, mk, :], op=ALU.mult),
                                     reads=[Bp, Bc], writes=[Bp])
                            last = (ci == n - 1) and not win
                            S.op("pe", lambda e, dp=dp, p_=p_, ci=ci, last=last: e.matmul(
                                dp[:], lhsT=self.ones_bf[:], rhs=p_[:], start=(ci == 0), stop=last), reads=[Bp], writes=[Bd])
                            S.op("pe", lambda e, op_=op_, p_=p_, ci=ci, vc=vc, g=g, n=n: e.matmul(
                                op_[:], lhsT=v_sb[:, vc, g * 128:(g + 1) * 128], rhs=p_[:], start=(ci == 0), stop=(ci == n - 1)),
                                reads=[Bp, Bv], writes=[Bop])
                        if win:
                            S.op("pe", lambda e, dp=dp, g=g: e.matmul(dp[:], lhsT=ones1[:], rhs=esk[:, g * 512:(g + 1) * 512],
                                                                      start=False, stop=False), reads=[Bc], writes=[Bd])
                            S.op("pe", lambda e, dp=dp, g=g: e.matmul(dp[:], lhsT=ones1[:], rhs=esl[:, g * 512:(g + 1) * 512],
                                                                      start=False, stop=True), reads=[Bc], writes=[Bd])
                        r_, Br = rc.next()
                        S.op("dve", lambda e, r_=r_, dp=dp: e.reciprocal(out=r_[:], in_=dp[:]), reads=[Bd], writes=[Br])
                        S.op("dve", lambda e, r_=r_, op_=op_, os_=os_, q0=q0: e.tensor_tensor(
                            out=os_[:, :, q0:q0 + 128], in0=op_[:].rearrange("p (h q) -> p h q", h=4),
                            in1=r_[:].rearrange("p (h q) -> p h q", h=4), op=ALU.mult), reads=[Bop, Br], writes=[Bos])
                S.op("sp", lambda e, g=g, os_=os_: e.dma_start(out=ov[:, 4 * g:4 * g + 4, :], in_=os_[:]), reads=[Bos], dma=1)
            S.flush()

    def build(self):
        nc = self.nc
        self.declare()
        with ExitStack() as stk:
            self.S = Sched(nc, stk)
            self.S.stop = self.stop
            self.persistent(stk)
            try:
                self.stage_copy_in()
                self.stage_mod()
                for l in range(self.n_layers):
                    kind, j = l % 3, l // 3
                    if kind == 1:
                        self.stage_hyena(l, j)
                    else:
                        self.stage_attn(l, kind, j)
                    self.stage_ffn(l)
                self.stage_norm(0, 0, final=True)
            except StopBuild:
                pass
        return nc


    def hy_decl(self):
        if hasattr(self, "yin"):
            return
        self.yin = self.dscr("yin", [3 * D, T], F32)
        self.vfm = self.dscr("vfm", [D, T], F32)
        self.x1fm = self.dscr("x1fm", [D, T], F32)
        self.x2fm = self.dscr("x2fm", [D, T], F32)
        self.z1fm = self.dscr("z1fm", [D, T], F32)
        self.vtmh = self.dscr("vtmh", [T, D], BF16)
        self.ztmh = self.dscr("ztmh", [T, D], BF16)
        self.z2T = self.dscr("z2T", [D, T], BF16)
        self.Hs = {2048: self.dscr("Hs2048", [2, 2, 2048, D], F32), 256: self.dscr("Hs256", [2, 2, 256, D], F32)}

    def stage_hyena(self, l, j):
        nc, S = self.nc, self.S
        self.hy_decl()
        self.stage_norm(l, 0)
        if S.stopped:
            return
        Win = self.W("hy_w_in", j)
        with ExitStack() as stk:
            yo = Rot([stk.enter_context(nc.sbuf_tensor(self.nm("yo"), [128, TW], F32)) for _ in range(4)], S.bufs_n(4))
            yv = self.yin.rearrange("(c p) t -> p c t", p=128)
            blocks = [([(Win, b * 512, 512)], [(0,), (128,), (256,), (384,)]) for b in range(12)]
            cnt = [0]

            def epi(bi, gi, tt, pl):
                oc = bi * 4 + gi
                (ps, Bp), = pl
                y_, By = yo.next()
                cnt[0] += 1
                S.op("act", lambda e: e.activation(out=y_[:], in_=ps[:], func=AF.Copy), reads=[Bp], writes=[By])
                S.op("sp", lambda e: e.dma_start(out=yv[:, oc, tt * TW:(tt + 1) * TW], in_=y_[:]), reads=[By], dma=1)
            self.gemm(self.hT, DC, blocks, epi, stk)
            S.flush()
        self.hy_shortconv(j)
        self.hy_filter(j, 2048)
        self.hy_filter(j, 256)
        self.hy_conv(j, 0)
        self.hy_conv(j, 1)
        if S.stopped:
            return
        Wo = self.W("hy_wo", j)
        with ExitStack() as stk:
            blocks = [([(Wo, b * 512, 512)], [(0,), (128,), (256,), (384,)]) for b in range(4)]
            epi = self.make_resid_epilogue(stk, l, 2, lambda bi, gi: bi * 4 + gi)
            self.gemm(self.z2T, DC, blocks, epi, stk)
            S.flush()

    SEQS = [(0, TL)] + [(TL + 256 * s_, 256) for s_ in range(4)]

    def hy_shortconv(self, j):
        nc, S = self.nc, self.S
        if S.stopped:
            return
        with ExitStack() as stk:
            def sb(n, s, d):
                return stk.enter_context(nc.sbuf_tensor(self.nm(n), s, d))
            cw = sb("cw", [128, 48, 4], F32)
            ident = sb("ident", [128, 128], BF16)
            Bc = S.buf()
            S.op("sp", lambda e: e.dma_start(out=cw[:], in_=self.hyconv_in), writes=[Bc], dma=1)
            S.op("sp", lambda e: e.dma_start(out=ident[:], in_=self.ident_in), writes=[Bc], dma=1)
            yi = Rot([sb("yi", [128, T], F32) for _ in range(2)], S.bufs_n(2))
            uo = Rot([sb("uo", [128, T], F32) for _ in range(2)], S.bufs_n(2))
            ub = Rot([sb("ub", [128, T], BF16) for _ in range(2)], S.bufs_n(2))
            vt = Rot([sb("vt", [128, 4, 128], BF16) for _ in range(3)], S.bufs_n(3))
            tps = Rot([stk.enter_context(nc.psum_tensor(self.nm("tps"), [128, 4, 128], BF16)) for _ in range(3)], S.bufs_n(3))
            yv = self.yin.rearrange("(c p) t -> p c t", p=128)
            dsts = [self.vfm.rearrange("(c p) t -> p c t", p=128), self.x1fm.rearrange("(c p) t -> p c t", p=128),
                    self.x2fm.rearrange("(c p) t -> p c t", p=128)]
            for oc in range(48):
                y_, By = yi.next()
                u_, Bu = uo.next()
                S.op("sp", lambda e, y_=y_, oc=oc: e.dma_start(out=y_[:], in_=yv[:, oc, :]), writes=[By], dma=1)
                S.op("act", lambda e, y_=y_, u_=u_, oc=oc: e.activation(out=u_[:], in_=y_[:], func=AF.Identity,
                                                                       bias=cw[:, oc, 3:4], scale=cw[:, oc, 1:2]), reads=[By, Bc], writes=[Bu])
                for (a, ln) in self.SEQS:
                    b = a + ln
                    S.op("dve", lambda e, y_=y_, u_=u_, oc=oc, a=a, b=b: e.scalar_tensor_tensor(
                        out=u_[:, a + 1:b], in0=y_[:, a:b - 1], scalar=cw[:, oc, 0:1], in1=u_[:, a + 1:b], op0=ALU.mult, op1=ALU.add),
                        reads=[By, Bu, Bc], writes=[Bu])
                    S.op("dve", lambda e, y_=y_, u_=u_, oc=oc, a=a, b=b: e.scalar_tensor_tensor(
                        out=u_[:, a:b - 1], in0=y_[:, a + 1:b], scalar=cw[:, oc, 2:3], in1=u_[:, a:b - 1], op0=ALU.mult, op1=ALU.add),
                        reads=[By, Bu, Bc], writes=[Bu])
                S.op("sp", lambda e, u_=u_, oc=oc: e.dma_start(out=dsts[oc // 16][:, oc % 16, :], in_=u_[:]), reads=[Bu], dma=1)
                if oc < 16:
                    self.fm_to_tm(u_, Bu, oc, self.vtmh, ub, vt, tps, ident, Bc)
            S.flush()

    def fm_to_tm(self, u_, Bu, oc, dst_tm, ub, vt, tps, ident, Bc, c0=0, ncols=T):
        S = self.S
        b_, Bb = ub.next()
        S.op("act", lambda e: e.activation(out=b_[:, 0:ncols], in_=u_[:, 0:ncols], func=AF.Copy), reads=[Bu], writes=[Bb])
        dv = dst_tm.rearrange("(c p) d -> p c d", p=128)
        for q in range(ncols // 512):
            tp, Bt = tps.next()
            v_, Bv = vt.next()
            for i in range(4):
                S.op("pe", lambda e, tp=tp, i=i, q=q: e.transpose(tp[:, i, :], b_[:, q * 512 + i * 128: q * 512 + (i + 1) * 128], ident[:]),
                     reads=[Bb, Bc], writes=[Bt])
            S.op("act", lambda e, tp=tp, v_=v_: e.activation(out=v_[:], in_=tp[:], func=AF.Copy), reads=[Bt], writes=[Bv])
            tc0 = (c0 + q * 512) // 128
            S.op("sp", lambda e, v_=v_, tc0=tc0: e.dma_start(out=dv[:, tc0:tc0 + 4, oc * 128:(oc + 1) * 128], in_=v_[:]), reads=[Bv], dma=1)

    def hy_filter(self, j, L):
        nc, S = self.nc, self.S
        if S.stopped:
            return
        nT = L // 128
        NW = min(512, L)
        TWO_PI = 2.0 * math.pi
        Hs = self.Hs[L]
        feats_in = self.feats2048_in if L == 2048 else self.feats256_in
        tnn_in = self.tnn2048_in if L == 2048 else self.tnn256_in
        tabF = self.tabF2048_in if L == 2048 else self.tabF256_in
        with ExitStack() as stk:
            def sb(n, s, d):
                return stk.enter_context(nc.sbuf_tensor(self.nm(n), s, d))
            feats = sb("feats", [33, L], F32)
            w1 = sb("fw1", [33, 64], F32)
            w2 = sb("fw2", [64, 64], F32)
            w3 = sb("fw3", [64, 4 * D], F32)
            fb = sb("ffb", [64, 4], F32)
            fraw = sb("fraw", [64, 4], F32)
            tnn = sb("tnn", [128, nT], F32)
            drep = sb("drep", [128, D], F32)
            a1 = sb("a1", [64, L], F32)
            a2 = sb("a2", [64, L], F32)
            Bc, Ba1, Ba2 = S.buf(), S.buf(), S.buf()
            S.op("sp", lambda e: e.dma_start(out=feats[:], in_=feats_in), writes=[Bc], dma=1)
            S.op("sp", lambda e: e.dma_start(out=w1[:], in_=self.hyw1_in), writes=[Bc], dma=1)
            S.op("sp", lambda e: e.dma_start(out=w2[:], in_=self.hyw2_in), writes=[Bc], dma=1)
            S.op("sp", lambda e: e.dma_start(out=w3[:], in_=self.hyw3_in), writes=[Bc], dma=1)
            S.op("sp", lambda e: e.dma_start(out=fraw[:], in_=self.hyfb_in), writes=[Bc], dma=1)
            S.op("sp", lambda e: e.dma_start(out=tnn[:], in_=tnn_in), writes=[Bc], dma=1)
            S.op("sp", lambda e: e.dma_start(out=drep[:], in_=self.drep_in), writes=[Bc], dma=1)
            S.op("act", lambda e: e.activation(out=fb[:, 0:1], in_=fraw[:, 0:1], func=AF.Copy), reads=[Bc], writes=[Bc])
            S.op("act", lambda e: e.activation(out=fb[:, 2:3], in_=fraw[:, 1:2], func=AF.Copy), reads=[Bc], writes=[Bc])
            S.op("dve", lambda e: e.tensor_tensor(out=fb[:, 1:2], in0=fraw[:, 2:3], in1=fraw[:, 0:1], op=ALU.mult), reads=[Bc], writes=[Bc])
            S.op("dve", lambda e: e.tensor_tensor(out=fb[:, 3:4], in0=fraw[:, 3:4], in1=fraw[:, 1:2], op=ALU.mult), reads=[Bc], writes=[Bc])
            fps = Rot([stk.enter_context(nc.psum_tensor(self.nm("fps"), [128, 512], F32)) for _ in range(4)], S.bufs_n(4))
            nps = Rot([stk.enter_context(nc.psum_tensor(self.nm("fnps"), [128, 512], F32)) for _ in range(1)], S.bufs_n(1))
            hps = Rot([stk.enter_context(nc.psum_tensor(self.nm("hps"), [128, 512], F32)) for _ in range(3)], S.bufs_n(3))
            arg = Rot([sb("arg", [64, NW], F32) for _ in range(2)], S.bufs_n(2))
            kf = Rot([sb("kf", [64, NW], F32) for _ in range(2)], S.bufs_n(2))
            ki = Rot([sb("ki", [64, NW], mybir.dt.int32) for _ in range(2)], S.bufs_n(2))

            def sin_layer(src, Bsrc, wt, kdim, col, dst, Bdst):
                for c in range(L // NW):
                    ps, Bp = fps.next()
                    a_, Ba = arg.next()
                    k_, Bk = kf.next()
                    i_, Bi = ki.next()
                    S.op("pe", lambda e, ps=ps, c=c: e.matmul(ps[0:64, 0:NW], lhsT=wt[0:kdim, :], rhs=src[0:kdim, c * NW:(c + 1) * NW],
                                                             start=True, stop=True), reads=[Bsrc, Bc], writes=[Bp])
                    S.op("act", lambda e, ps=ps, a_=a_: e.activation(out=a_[:], in_=ps[0:64, 0:NW], func=AF.Identity,
                                                                    bias=fb[:, col + 1:col + 2], scale=fb[:, col:col + 1]), reads=[Bp, Bc], writes=[Ba])
                    S.op("dve", lambda e, a_=a_, k_=k_: e.tensor_scalar(out=k_[:], in0=a_[:], scalar1=1.0 / TWO_PI, scalar2=12582912.0, op0=ALU.mult, op1=ALU.add),
                         reads=[Ba], writes=[Bk])
                    S.op("dve", lambda e, k_=k_: e.tensor_scalar(out=k_[:], in0=k_[:], scalar1=12582912.0, scalar2=None, op0=ALU.subtract),
                         reads=[Bk], writes=[Bk])
                    S.op("dve", lambda e, a_=a_, k_=k_: e.scalar_tensor_tensor(out=a_[:], in0=k_[:], scalar=-TWO_PI, in1=a_[:], op0=ALU.mult, op1=ALU.add),
                         reads=[Bk, Ba], writes=[Ba])
                    S.op("dve", lambda e, a_=a_, k_=k_: e.tensor_scalar(out=k_[:], in0=a_[:], scalar1=math.pi, scalar2=TWO_PI, op0=ALU.is_gt, op1=ALU.mult),
                         reads=[Ba], writes=[Bk])
                    S.op("dve", lambda e, a_=a_, k_=k_: e.tensor_tensor(out=a_[:], in0=a_[:], in1=k_[:], op=ALU.subtract), reads=[Ba, Bk], writes=[Ba])
                    S.op("dve", lambda e, a_=a_, k_=k_: e.tensor_scalar(out=k_[:], in0=a_[:], scalar1=-math.pi, scalar2=TWO_PI, op0=ALU.is_lt, op1=ALU.mult),
                         reads=[Ba], writes=[Bk])
                    S.op("dve", lambda e, a_=a_, k_=k_: e.tensor_tensor(out=a_[:], in0=a_[:], in1=k_[:], op=ALU.add), reads=[Ba, Bk], writes=[Ba])
                    S.op("act", lambda e, a_=a_, c=c: e.activation(out=dst[:, c * NW:(c + 1) * NW], in_=a_[:], func=AF.Sin), reads=[Ba], writes=[Bdst])
            sin_layer(feats, Bc, w1, 33, 0, a1, Ba1)
            sin_layer(a1, Ba1, w2, 64, 2, a2, Ba2)
            gp = [sb("gp", [128, nT, 512], BF16) for _ in range(2)]
            gm = [sb("gm", [128, nT, 512], BF16) for _ in range(2)]
            Bg = S.bufs_n(2)
            dec = Rot([sb("dec", [128, 512], F32) for _ in range(2)], S.bufs_n(2))
            fw = Rot([sb("fw", [128, 512], F32) for _ in range(2)], S.bufs_n(2))
            bw = Rot([sb("bw", [128, 512], F32) for _ in range(2)], S.bufs_n(2))
            ab = Rot([sb("ab", [128, 2, 512], BF16) for _ in range(2)], S.bufs_n(2))
            rn = Rot([sb("rn", [128, 512], F32) for _ in range(2)], S.bufs_n(2))
            tb = Rot([sb("ftb", [128, 2, nT, 128], BF16) for _ in range(3)], S.bufs_n(3))
            ho = Rot([sb("ho", [128, 2, 512], F32) for _ in range(2)], S.bufs_n(2))
            it = 0
            for o in range(2):
                for db in range(4):
                    k = it % 2
                    it += 1
                    npz, Bn = nps.next()
                    for tc in range(nT):
                        pf, Bpf = fps.next()
                        pb, Bpb = fps.next()
                        d_, Bd = dec.next()
                        f_, Bf = fw.next()
                        b_, Bb = bw.next()
                        a_, Bab = ab.next()
                        cf = o * D + db * 512
                        S.op("pe", lambda e, pf=pf, tc=tc, cf=cf: e.matmul(pf[:], lhsT=a2[:, tc * 128:(tc + 1) * 128], rhs=w3[:, cf:cf + 512],
                                                                          start=True, stop=True), reads=[Ba2, Bc], writes=[Bpf])
                        S.op("pe", lambda e, pb=pb, tc=tc, cf=cf: e.matmul(pb[:], lhsT=a2[:, tc * 128:(tc + 1) * 128], rhs=w3[:, 2 * D + cf:2 * D + cf + 512],
                                                                          start=True, stop=True), reads=[Ba2, Bc], writes=[Bpb])
                        S.op("act", lambda e, d_=d_, tc=tc, db=db: e.activation(out=d_[:], in_=drep[:, db * 512:(db + 1) * 512], func=AF.Exp,
                                                                               scale=tnn[:, tc:tc + 1]), reads=[Bc], writes=[Bd])
                        S.op("dve", lambda e, f_=f_, pf=pf, d_=d_: e.tensor_tensor(out=f_[:], in0=pf[:], in1=d_[:], op=ALU.mult), reads=[Bpf, Bd], writes=[Bf])
                        S.op("dve", lambda e, b_=b_, pb=pb, d_=d_: e.tensor_tensor(out=b_[:], in0=pb[:], in1=d_[:], op=ALU.mult), reads=[Bpb, Bd], writes=[Bb])
                        if tc == 0:
                            S.op("dve", lambda e, b_=b_: e.memset(b_[0:1, :], 0.0), reads=[Bb], writes=[Bb])
                        S.op("pool", lambda e, f_=f_, b_=b_, k=k, tc=tc: e.tensor_tensor(out=gp[k][:, tc, :], in0=f_[:], in1=b_[:], op=ALU.add),
                             reads=[Bf, Bb], writes=[Bg[k]])
                        S.op("pool", lambda e, f_=f_, b_=b_, k=k, tc=tc: e.tensor_tensor(out=gm[k][:, tc, :], in0=f_[:], in1=b_[:], op=ALU.subtract),
                             reads=[Bf, Bb], writes=[Bg[k]])
                        S.op("act", lambda e, f_=f_, a_=a_: e.activation(out=a_[:, 0, :], in_=f_[:], func=AF.Abs), reads=[Bf], writes=[Bab])
                        S.op("act", lambda e, b_=b_, a_=a_: e.activation(out=a_[:, 1, :], in_=b_[:], func=AF.Abs), reads=[Bb], writes=[Bab])
                        for h in range(2):
                            S.op("pe", lambda e, npz=npz, a_=a_, h=h, tc=tc: e.matmul(npz[:], lhsT=self.ones_bf[:], rhs=a_[:, h, :],
                                                                                     start=(tc == 0 and h == 0), stop=(tc == nT - 1 and h == 1)),
                                 reads=[Bab], writes=[Bn])
                    r_, Br = rn.next()
                    S.op("dve", lambda e, r_=r_, npz=npz: e.tensor_scalar(out=r_[:], in0=npz[:], scalar1=EPS, scalar2=None, op0=ALU.add),
                         reads=[Bn], writes=[Br])
                    S.op("dve", lambda e, r_=r_: e.reciprocal(out=r_[:], in_=r_[:]), reads=[Br], writes=[Br])
                    for fc in range(nT):
                        t_, Bt = tb.next()
                        S.op("sp", lambda e, t_=t_, fc=fc: e.dma_start(out=t_[:], in_=tabF[fc]), writes=[Bt], dma=1)
                        h_, Bh = ho.next()
                        for ri, gsrc in enumerate((gp, gm)):
                            hp, Bhp = hps.next()
                            for tc in range(nT):
                                S.op("pe", lambda e, hp=hp, t_=t_, ri=ri, tc=tc, gsrc=gsrc, k=k: e.matmul(
                                    hp[:], lhsT=t_[:, ri, tc, :], rhs=gsrc[k][:, tc, :], start=(tc == 0), stop=(tc == nT - 1)),
                                    reads=[Bt, Bg[k]], writes=[Bhp])
                            S.op("dve", lambda e, hp=hp, h_=h_, ri=ri, r_=r_: e.tensor_tensor(out=h_[:, ri, :], in0=hp[:], in1=r_[:], op=ALU.mult),
                                 reads=[Bhp, Br], writes=[Bh])
                        S.op("sp", lambda e, h_=h_, o=o, fc=fc, db=db: e.dma_start(
                            out=Hs[o, :, fc * 128:(fc + 1) * 128, db * 512:(db + 1) * 512].rearrange("r p d -> p r d"), in_=h_[:]),
                            reads=[Bh], dma=1)
            S.flush()

    def hy_conv(self, j, o):
        nc, S = self.nc, self.S
        if S.stopped:
            return
        src_tm = self.vtmh if o == 0 else self.ztmh
        src_fm = self.vfm if o == 0 else self.z1fm
        gate_fm = self.x1fm if o == 0 else self.x2fm
        with ExitStack() as stk:
            def sb(n, s, d):
                return stk.enter_context(nc.sbuf_tensor(self.nm(n), s, d))
            skip = sb("skip", [128, 2, DC], F32)
            ident = sb("ident2", [128, 128], BF16)
            Bc = S.buf()
            S.op("sp", lambda e: e.dma_start(out=skip[:], in_=self.hyskip_in), writes=[Bc], dma=1)
            S.op("sp", lambda e: e.dma_start(out=ident[:], in_=self.ident_in), writes=[Bc], dma=1)
            vb = Rot([sb("cvb", [128, 16, 512], BF16) for _ in range(2)], S.bufs_n(2))
            yre = Rot([sb("yre", [128, 16, 512], BF16) for _ in range(1)], S.bufs_n(1))
            yim = Rot([sb("yim", [128, 16, 512], BF16) for _ in range(1)], S.bufs_n(1))
            tb = Rot([sb("ctb", [128, 2, 16, 128], BF16) for _ in range(3)], S.bufs_n(3))
            hh = Rot([sb("hh", [128, 2, 512], F32) for _ in range(2)], S.bufs_n(2))
            tt_ = [Rot([sb("cp%d" % i, [128, 512], F32) for _ in range(2)], S.bufs_n(2)) for i in range(4)]
            ti = Rot([sb("cti", [128, 2, 16, 512], BF16) for _ in range(2)], S.bufs_n(2))
            ui = Rot([sb("cui", [128, 512], F32) for _ in range(2)], S.bufs_n(2))
            gi_ = Rot([sb("cgi", [128, 512], F32) for _ in range(2)], S.bufs_n(2))
            zo = Rot([sb("czo", [128, 512], F32) for _ in range(2)], S.bufs_n(2))
            zb = Rot([sb("czb", [128, 512], BF16) for _ in range(2)], S.bufs_n(2))
            vt = Rot([sb("cvt", [128, 4, 128], BF16) for _ in range(2)], S.bufs_n(2))
            ups = Rot([stk.enter_context(nc.psum_tensor(self.nm("ups"), [128, 512], F32)) for _ in range(4)], S.bufs_n(4))
            yps = Rot([stk.enter_context(nc.psum_tensor(self.nm("yps"), [128, 512], F32)) for _ in range(2)], S.bufs_n(2))
            tps = Rot([stk.enter_context(nc.psum_tensor(self.nm("ctps"), [128, 4, 128], BF16)) for _ in range(2)], S.bufs_n(2))
            sfv = src_fm.rearrange("(c p) t -> p c t", p=128)
            gfv = gate_fm.rearrange("(c p) t -> p c t", p=128)
            z1v = self.z1fm.rearrange("(c p) t -> p c t", p=128)
            z2v = self.z2T.rearrange("(c p) t -> p c t", p=128)
            for (t0, L) in self.SEQS:
                nT = L // 128
                NW = min(512, L)
                Hs = self.Hs[L]
                tabF = self.tabF2048_in if L == 2048 else self.tabF256_in
                tabI = self.tabI2048_in if L == 2048 else self.tabI256_in
                for db in range(4):
                    v_, Bv = vb.next()
                    yr, Byr = yre.next()
                    yi, Byi = yim.next()
                    S.op("sp", lambda e, v_=v_, t0=t0, L=L, nT=nT, db=db: e.dma_start(
                        out=v_[:, 0:nT, :], in_=src_tm[t0:t0 + L, db * 512:(db + 1) * 512].rearrange("(c p) d -> p c d", p=128)),
                        writes=[Bv], dma=1)
                    for fc in range(nT):
                        t_, Bt = tb.next()
                        h_, Bh = hh.next()
                        S.op("sp", lambda e, t_=t_, fc=fc, nT=nT, tabF=tabF: e.dma_start(out=t_[:, :, 0:nT, :], in_=tabF[fc]), writes=[Bt], dma=1)
                        S.op("sp", lambda e, h_=h_, fc=fc, db=db, Hs=Hs: e.dma_start(
                            out=h_[:], in_=Hs[o, :, fc * 128:(fc + 1) * 128, db * 512:(db + 1) * 512].rearrange("r p d -> p r d")),
                            writes=[Bh], dma=1)
                        pu = []
                        for ri in range(2):
                            p_, Bp = ups.next()
                            for tc in range(nT):
                                S.op("pe", lambda e, p_=p_, t_=t_, ri=ri, tc=tc, v_=v_, nT=nT: e.matmul(
                                    p_[:], lhsT=t_[:, ri, tc, :], rhs=v_[:, tc, :], start=(tc == 0), stop=(tc == nT - 1)),
                                    reads=[Bt, Bv], writes=[Bp])
                            pu.append((p_, Bp))
                        (ur, Bur), (um, Bum) = pu
                        (a1_, Ba1), (a2_, Ba2), (a3_, Ba3), (a4_, Ba4) = [r.next() for r in tt_]
                        S.op("dve", lambda e, a1_=a1_, ur=ur, h_=h_: e.tensor_tensor(out=a1_[:], in0=ur[:], in1=h_[:, 0, :], op=ALU.mult), reads=[Bur, Bh], writes=[Ba1])
                        S.op("dve", lambda e, a2_=a2_, um=um, h_=h_: e.tensor_tensor(out=a2_[:], in0=um[:], in1=h_[:, 1, :], op=ALU.mult), reads=[Bum, Bh], writes=[Ba2])
                        S.op("dve", lambda e, a3_=a3_, um=um, h_=h_: e.tensor_tensor(out=a3_[:], in0=um[:], in1=h_[:, 0, :], op=ALU.mult), reads=[Bum, Bh], writes=[Ba3])
                        S.op("dve", lambda e, a4_=a4_, ur=ur, h_=h_: e.tensor_tensor(out=a4_[:], in0=ur[:], in1=h_[:, 1, :], op=ALU.mult), reads=[Bur, Bh], writes=[Ba4])
                        S.op("pool", lambda e, yr=yr, fc=fc, a1_=a1_, a2_=a2_: e.tensor_tensor(out=yr[:, fc, :], in0=a1_[:], in1=a2_[:], op=ALU.subtract),
                             reads=[Ba1, Ba2], writes=[Byr])
                        S.op("pool", lambda e, yi=yi, fc=fc, a3_=a3_, a4_=a4_: e.tensor_tensor(out=yi[:, fc, :], in0=a3_[:], in1=a4_[:], op=ALU.add),
                             reads=[Ba3, Ba4], writes=[Byi])
                    for tt in range(L // NW):
                        c_, Bci = ti.next()
                        S.op("sp", lambda e, c_=c_, tt=tt, nT=nT, NW=NW, tabI=tabI: [e.dma_start(
                            out=c_[:, r, 0:nT, 0:NW], in_=tabI[r, :, :, tt * NW:(tt + 1) * NW]) for r in range(2)],
                            writes=[Bci], dma=2)
                        for dc in range(4):
                            oc = db * 4 + dc
                            col0 = t0 + tt * NW
                            y_, By = yps.next()
                            u_, Bu = ui.next()
                            g_, Bgt = gi_.next()
                            S.op("sp", lambda e, u_=u_, oc=oc, col0=col0, NW=NW: e.dma_start(out=u_[:, 0:NW], in_=sfv[:, oc, col0:col0 + NW]), writes=[Bu], dma=1)
                            S.op("sp", lambda e, g_=g_, oc=oc, col0=col0, NW=NW: e.dma_start(out=g_[:, 0:NW], in_=gfv[:, oc, col0:col0 + NW]), writes=[Bgt], dma=1)
                            for fc in range(nT):
                                S.op("pe", lambda e, y_=y_, yr=yr, fc=fc, dc=dc, c_=c_, NW=NW: e.matmul(
                                    y_[:, 0:NW], lhsT=yr[:, fc, dc * 128:(dc + 1) * 128], rhs=c_[:, 0, fc, 0:NW], start=(fc == 0), stop=False),
                                    reads=[Byr, Bci], writes=[By])
                                S.op("pe", lambda e, y_=y_, yi=yi, fc=fc, dc=dc, c_=c_, NW=NW, nT=nT: e.matmul(
                                    y_[:, 0:NW], lhsT=yi[:, fc, dc * 128:(dc + 1) * 128], rhs=c_[:, 1, fc, 0:NW], start=False, stop=(fc == nT - 1)),
                                    reads=[Byi, Bci], writes=[By])
                            S.op("dve", lambda e, u_=u_, y_=y_, oc=oc, NW=NW: e.scalar_tensor_tensor(
                                out=u_[:, 0:NW], in0=u_[:, 0:NW], scalar=skip[:, o, oc:oc + 1], in1=y_[:, 0:NW], op0=ALU.mult, op1=ALU.add),
                                reads=[Bu, By, Bc], writes=[Bu])
                            if o == 0:
                                z_, Bz = zo.next()
                                S.op("pool", lambda e, z_=z_, u_=u_, g_=g_, NW=NW: e.tensor_tensor(out=z_[:, 0:NW], in0=u_[:, 0:NW], in1=g_[:, 0:NW], op=ALU.mult),
                                     reads=[Bu, Bgt], writes=[Bz])
                                S.op("sp", lambda e, z_=z_, oc=oc, col0=col0, NW=NW: e.dma_start(out=z1v[:, oc, col0:col0 + NW], in_=z_[:, 0:NW]), reads=[Bz], dma=1)
                                self.fm_to_tm(z_, Bz, oc, self.ztmh, zb, vt, tps, ident, Bc, c0=col0, ncols=NW) if NW == 512 else \
                                    self.fm_to_tm_small(z_, Bz, oc, zb, vt, tps, ident, Bc, col0)
                            else:
                                zb_, Bzb = zb.next()
                                S.op("pool", lambda e, zb_=zb_, u_=u_, g_=g_, NW=NW: e.tensor_tensor(out=zb_[:, 0:NW], in0=u_[:, 0:NW], in1=g_[:, 0:NW], op=ALU.mult),
                                     reads=[Bu, Bgt], writes=[Bzb])
                                S.op("sp", lambda e, zb_=zb_, oc=oc, col0=col0, NW=NW: e.dma_start(out=z2v[:, oc, col0:col0 + NW], in_=zb_[:, 0:NW]), reads=[Bzb], dma=1)
            S.flush()

    def fm_to_tm_small(self, u_, Bu, oc, ub, vt, tps, ident, Bc, c0):
        S = self.S
        b_, Bb = ub.next()
        S.op("act", lambda e: e.activation(out=b_[:, 0:256], in_=u_[:, 0:256], func=AF.Copy), reads=[Bu], writes=[Bb])
        dv = self.ztmh.rearrange("(c p) d -> p c d", p=128)
        tp, Bt = tps.next()
        v_, Bv = vt.next()
        for i in range(2):
            S.op("pe", lambda e, i=i: e.transpose(tp[:, i, :], b_[:, i * 128:(i + 1) * 128], ident[:]), reads=[Bb, Bc], writes=[Bt])
        S.op("act", lambda e: e.activation(out=v_[:, 0:2, :], in_=tp[:, 0:2, :], func=AF.Copy), reads=[Bt], writes=[Bv])
        tc0 = c0 // 128
        S.op("sp", lambda e: e.dma_start(out=dv[:, tc0:tc0 + 2, oc * 128:(oc + 1) * 128], in_=v_[:, 0:2, :]), reads=[Bv], dma=1)


def pp(v):
    v = np.asarray(v)
    n = v.shape[-1] // 128
    return np.ascontiguousarray(np.moveaxis(v.reshape(v.shape[:-1] + (n, 128)), -1, 0))


def rope_tables():
    half = HD // 2
    t = np.arange(TL)
    row = (t // 64).astype(np.float32)
    col = (t % 64).astype(np.float32)
    inv = (10000.0 ** (-np.arange(0, half, 2, dtype=np.float32) / half)).astype(np.float32)
    ang = np.zeros((128, TL), np.float32)
    ang[0:32] = inv[:, None] * row[None, :]
    ang[32:64] = inv[:, None] * row[None, :]
    ang[64:96] = inv[:, None] * col[None, :]
    ang[96:128] = inv[:, None] * col[None, :]
    tab = np.stack([np.cos(ang), np.sin(ang)], axis=1).astype(np.float32)
    P = np.zeros((128, 128), np.float32)
    for base in (0, 64):
        for m in range(32):
            P[base + m + 32, base + m] = -1.0
            P[base + m, base + m + 32] = 1.0
    return tab, P.astype(NPBF)


def window_masks():
    kj = np.arange(128)[:, None]
    qi = np.arange(128)[None, :]
    prev = (kj >= qi).astype(np.float32)
    nxt = (kj <= qi).astype(np.float32)
    m = np.stack([np.tile(prev, (1, 4)), np.tile(nxt, (1, 4))], axis=1)
    return m.astype(NPBF)


def make_in_maps(inp, kb):
    f = lambda a: np.ascontiguousarray(np.asarray(a, dtype=np.float32))
    rope, P = rope_tables()
    wm = window_masks()
    shared = {
        "normg_in": pp(np.stack([f(inp["norm_mix_g"]), f(inp["norm_ffn_g"])], 0)),
        "modb_in": pp(f(inp["mod_b"])),
        "finalg_in": pp(f(inp["final_g"])),
        "esink_in": np.ascontiguousarray(np.repeat(f(inp["win_sink"]), 128, axis=1)[:, None, :]),
        "axg_in": np.ascontiguousarray(np.stack([f(inp["ax_q_g"])[0], f(inp["ax_k_g"])[0]], axis=1)),
        "axkg_rep": np.ascontiguousarray(np.tile(f(inp["ax_k_g"])[0][None, :], (128, 1))),
        "rope_in": rope, "ropeP_in": P, "wmask_in": wm,
    }
    shared.update(hyena_shared(inp))
    for name in kb.inputs:
        if "__" in name:
            base, idx = name.split("__")
            shared[name] = np.ascontiguousarray(f(inp[base])[int(idx)])
    maps = []
    xs, xp = f(inp["x_sample"]), f(inp["x_prompt"])
    for i in range(8):
        m = dict(shared)
        xt = np.concatenate([xs[i], xp[4 * i:4 * i + 4].reshape(1024, D)], axis=0)
        m["xT_in"] = np.ascontiguousarray(xt.T)
        m["cvec"] = np.ascontiguousarray(np.stack([pp(f(inp["c"])[i]), pp(f(inp["c_ctx"]))], axis=-1))
        m["cwkT"] = np.ascontiguousarray(f(inp["cache_win_k"])[i].transpose(0, 2, 3, 1))
        m["cwv"] = np.ascontiguousarray(f(inp["cache_win_v"])[i].reshape(2, PAST, 512))
        m["cakT"] = np.ascontiguousarray(f(inp["cache_ax_k"])[i].transpose(0, 2, 3, 1))
        m["cav"] = np.ascontiguousarray(f(inp["cache_ax_v"])[i].reshape(1, PAST, 512))
        maps.append({k: v for k, v in m.items() if k in kb.inputs})
    return maps


def hyena_shared(inp):
    f = lambda a: np.ascontiguousarray(np.asarray(a, dtype=np.float32))
    out = {}
    cw = f(inp["hy_conv_w"])[0]
    cb = f(inp["hy_conv_b"])[0]
    out["hyconv_in"] = pp(np.stack([cw[0], cw[1], cw[2], cb], axis=0)).transpose(0, 2, 1).copy()
    out["ident_in"] = np.eye(128, dtype=np.float32).astype(NPBF)
    out["hyskip_in"] = pp(f(inp["hy_skip"])[0])
    out["hyw1_in"] = f(inp["hy_f_w1"])[0]
    out["hyw2_in"] = f(inp["hy_f_w2"])[0]
    out["hyw3_in"] = f(inp["hy_f_w3"])[0]
    fr = f(inp["hy_freq"])[0]
    out["hyfb_in"] = np.ascontiguousarray(np.stack([fr[0], fr[1], f(inp["hy_f_b1"])[0], f(inp["hy_f_b2"])[0]], axis=1))
    HY_MIN = math.log(1e-2) / 1.5
    HY_MAX = math.log(1e-2) / 0.3
    deltas = np.abs(np.linspace(HY_MIN, HY_MAX, D, dtype=np.float32))
    out["drep_in"] = np.ascontiguousarray(np.tile(deltas[None, :], (128, 1)).astype(np.float32))
    for L in (2048, 256):
        t = np.arange(L, dtype=np.float32)
        tn = (t / max(L - 1, 1)).astype(np.float32)
        bands = 16
        fbv = np.linspace(1e-4, bands - 1, bands, dtype=np.float32)
        w = (np.float32(2.0 * math.pi) * t / np.float32(L)).astype(np.float32)
        feats = np.concatenate([tn[:, None], np.cos(w[:, None] * fbv), -np.sin(w[:, None] * fbv)], axis=-1).astype(np.float32)
        out["feats%d_in" % L] = np.ascontiguousarray(feats.T)
        out["tnn%d_in" % L] = pp(-tn)
        nT = L // 128
        tt = np.arange(L, dtype=np.float64)
        ff = np.arange(L, dtype=np.float64)
        ang = np.pi * np.outer(tt, 2 * ff + 1) / (2 * L)
        C = np.cos(ang)
        S_ = np.sin(ang)
        CF = np.stack([C, S_], 0).reshape(2, nT, 128, nT, 128)
        out["tabF%d_in" % L] = np.ascontiguousarray(CF.transpose(3, 2, 0, 1, 4)).astype(NPBF)
        CI = np.stack([C.T, S_.T], 0) / L
        out["tabI%d_in" % L] = np.ascontiguousarray(CI.reshape(2, nT, 128, L).transpose(0, 2, 1, 3)).astype(NPBF)
    return out


_CACHE = {}


def get_kb():
    if "kb" not in _CACHE:
        kb = KB()
        kb.build()
        _CACHE["kb"] = kb
    return _CACHE["kb"]


def kernel(**inp):
    kb = get_kb()
    maps = make_in_maps(inp, kb)
    res = run_bass_kernel_spmd(kb.nc, maps, core_ids=list(range(8)))
    R = res.results
    y_s = np.stack([R[i]["yT"][:, :TL].T for i in range(8)], 0)
    y_p = np.concatenate([R[i]["yT"][:, TL:].T.reshape(4, 256, D) for i in range(8)], 0)
    outs = [np.ascontiguousarray(y_p), np.ascontiguousarray(y_s)]
    for nm_ in ("swk", "swv", "sak", "sav"):
        a = np.concatenate([R[i][nm_] for i in range(8)], 0)
        outs.append(np.ascontiguousarray(a.reshape(a.shape[0], a.shape[1], 256, NKV, HD)))
    return tuple(outs)
```

```python
import math
import os
from contextlib import ExitStack

import numpy as np
import ml_dtypes

import concourse.bass as bass
import concourse.mybir as mybir
from concourse.bass_utils import run_bass_kernel_spmd

F32 = mybir.dt.float32
BF16 = mybir.dt.bfloat16
AF = mybir.ActivationFunctionType
ALU = mybir.AluOpType
NPBF = ml_dtypes.bfloat16

D = 2048
DC = 16
T = 3072
TL = 2048
NTT = 6
TW = 512
DFF = 5632
FC = 44
NH = 16
NKV = 4
HD = 128
PAST = 512
EPS = 1e-6
DEPTH = 4
ENG_NAMES = ("pe", "act", "dve", "pool", "sp")


class Buf:
    __slots__ = ("name", "w", "r")

    def __init__(self, name=""):
        self.name = name
        self.w = None
        self.r = []


class Op:
    __slots__ = ("eng", "fn", "deps", "flag", "val", "sem", "dma")

    def __init__(self, eng, fn, dma):
        self.eng = eng
        self.fn = fn
        self.deps = []
        self.flag = False
        self.val = 0
        self.sem = None
        self.dma = dma


class Sched:
    def __init__(self, nc, stack, n_dma_sems=48):
        self.nc = nc
        self.sem = {e: stack.enter_context(nc.semaphore("s_" + e)) for e in ENG_NAMES}
        self.cnt = {e: 0 for e in ENG_NAMES}
        self.dma_sems = [stack.enter_context(nc.semaphore("s_dma%d" % i)) for i in range(n_dma_sems)]
        self.dma_val = [0] * n_dma_sems
        self.dma_last = [None] * n_dma_sems
        self.dma_rr = 0
        self.ops = {e: [] for e in ENG_NAMES}
        self.waited = {e: {} for e in ENG_NAMES}
        self.bufs = []
        self.n_ops = 0
        self.stopped = False
        self.sp_def = []
        self.DEFER = int(os.environ.get("K_DEFER", "3"))
        self.DEFER0 = self.DEFER

    def buf(self, name=""):
        b = Buf(name)
        self.bufs.append(b)
        return b

    def bufs_n(self, n, name=""):
        return [self.buf("%s%d" % (name, i)) for i in range(n)]

    def op(self, eng, fn, reads=(), writes=(), dma=0):
        o = Op(eng, fn, dma)
        deps = {}

        def add(d, war):
            if d is None or d is o:
                return
            if d.dma == 0 and d.eng == eng and dma == 0:
                if eng == "pe" or war:
                    return
            deps[id(d)] = d

        for b in reads:
            add(b.w, False)
        for b in writes:
            add(b.w, False)
            for r in b.r:
                add(r, True)
        if dma:
            k = self.dma_rr
            self.dma_rr = (k + 1) % len(self.dma_sems)
            prev = self.dma_last[k]
            if prev is not None:
                deps[id(prev)] = prev
            self.dma_val[k] += 16 * dma
            o.val = self.dma_val[k]
            o.sem = self.dma_sems[k]
            self.dma_last[k] = o
        for d in deps.values():
            d.flag = True
        o.deps = list(deps.values())
        for b in reads:
            if dma == 0:
                b.r = [r for r in b.r if not (r.dma == 0 and r.eng == eng)]
            b.r.append(o)
        for b in writes:
            b.w = o
            b.r = []
        self.n_ops += 1
        if eng == "sp":
            if reads and self.DEFER > 0:
                self.sp_def.append([o, 0])
                return o
            if self.sp_def:
                defd = set(id(x[0]) for x in self.sp_def)
                if any(id(d) in defd for d in o.deps):
                    for x in self.sp_def:
                        self.ops["sp"].append(x[0])
                    self.sp_def = []
            self.ops["sp"].append(o)
            keep = []
            for x in self.sp_def:
                x[1] += 1
                if x[1] >= self.DEFER:
                    self.ops["sp"].append(x[0])
                else:
                    keep.append(x)
            self.sp_def = keep
            return o
        self.ops[eng].append(o)
        return o

    def flush(self):
        nc = self.nc
        for x in self.sp_def:
            self.ops["sp"].append(x[0])
        self.sp_def = []
        lasts = {}
        for e in ENG_NAMES:
            for o in reversed(self.ops[e]):
                if o.dma == 0:
                    lasts[e] = o
                    o.flag = True
                    break
        for e in ENG_NAMES:
            for o in self.ops[e]:
                if o.dma == 0 and o.flag:
                    self.cnt[e] += 1
                    o.val = self.cnt[e]
                    o.sem = self.sem[e]
        dma_final = [(self.dma_sems[k], self.dma_val[k]) for k in range(len(self.dma_sems))
                     if self.dma_val[k] > 0]
        with nc.Block() as block:
            for e in ENG_NAMES:
                def body(eng, e=e):
                    waited = self.waited[e]

                    def wait(sem, val):
                        key = id(sem)
                        if waited.get(key, 0) < val:
                            eng.wait_ge(sem, val)
                            waited[key] = val

                    for o in self.ops[e]:
                        for d in o.deps:
                            wait(d.sem, d.val)
                        r = o.fn(eng)
                        if o.dma:
                            if not isinstance(r, (list, tuple)):
                                r = [r]
                            assert len(r) == o.dma, (len(r), o.dma)
                            for ins in r:
                                ins.then_inc(o.sem, 16)
                        elif o.flag:
                            r.then_inc(o.sem, 1)
                    for e2 in ENG_NAMES:
                        if e2 in lasts:
                            wait(lasts[e2].sem, lasts[e2].val)
                    for sem, val in dma_final:
                        wait(sem, val)
                name = {"pe": "tensor", "act": "scalar", "dve": "vector",
                        "pool": "gpsimd", "sp": "sync"}[e]
                getattr(block, name)(body)
        self.ops = {e: [] for e in ENG_NAMES}
        for b in self.bufs:
            b.w = None
            b.r = []
        self.bufs = []
        self.dma_last = [None] * len(self.dma_sems)
        self.nflush = getattr(self, "nflush", 0) + 1
        if self.nflush == getattr(self, "stop", -1):
            self.stopped = True


class StopBuild(Exception):
    pass


class Rot:
    def __init__(self, tiles, bufs):
        self.t = tiles
        self.b = bufs
        self.i = 0

    def next(self):
        k = self.i % len(self.t)
        self.i += 1
        return self.t[k], self.b[k]


class KB:
    def __init__(self, n_layers=DEPTH, dbg=False, stop=-1):
        self.nc = bass.Bass("TRN2", target_bir_lowering=False)
        self.stop = stop
        self.n_layers = n_layers
        self.dbg = dbg
        self.uid = 0
        self.inputs = {}

    def nm(self, s):
        self.uid += 1
        return "%s_%d" % (s, self.uid)

    def din(self, name, shape, dt=F32):
        ap = self.nc.dram_tensor(name, list(shape), dt, kind="ExternalInput").ap()
        self.inputs[name] = (tuple(shape), dt)
        return ap

    def dout(self, name, shape, dt=F32):
        return self.nc.dram_tensor(name, list(shape), dt, kind="ExternalOutput").ap()

    def dscr(self, name, shape, dt):
        return self.nc.dram_tensor(name, list(shape), dt).ap()

    IN_SPECS = {
        "xT_in": ([D, T], F32), "cvec": ([128, DC, 2], F32), "normg_in": ([128, 2, DEPTH, DC], F32),
        "modb_in": ([128, DEPTH, 96], F32), "finalg_in": ([128, DC], F32), "mod_w": ([DEPTH, D, 6 * D], F32),
        "win_wqkv": ([2, D, 3072], F32), "win_wo": ([2, D, D], F32), "esink_in": ([2, 1, NH * 128], F32),
        "ax_wqkv": ([1, D, 3072], F32), "ax_wo": ([1, D, D], F32), "axg_in": ([128, 2], F32),
        "axkg_rep": ([128, 128], F32), "ffn_w_gu": ([DEPTH, D, 2 * DFF], F32), "ffn_w_down": ([DEPTH, DFF, D], F32),
        "cwkT": ([2, NKV, 128, PAST], F32), "cwv": ([2, PAST, 512], F32), "cakT": ([1, NKV, 128, PAST], F32),
        "cav": ([1, PAST, 512], F32), "rope_in": ([128, 2, TL], F32), "ropeP_in": ([128, 128], BF16),
        "wmask_in": ([128, 2, 512], BF16),
        "hy_w_in": ([1, D, 3 * D], F32), "hy_wo": ([1, D, D], F32),
        "hyconv_in": ([128, 48, 4], F32), "ident_in": ([128, 128], BF16), "hyskip_in": ([128, 2, DC], F32),
        "hyw1_in": ([33, 64], F32), "hyw2_in": ([64, 64], F32), "hyw3_in": ([64, 4 * D], F32), "hyfb_in": ([64, 4], F32),
        "drep_in": ([128, D], F32),
        "feats2048_in": ([33, 2048], F32), "feats256_in": ([33, 256], F32),
        "tnn2048_in": ([128, 16], F32), "tnn256_in": ([128, 2], F32),
        "tabF2048_in": ([16, 128, 2, 16, 128], BF16), "tabF256_in": ([2, 128, 2, 2, 128], BF16),
        "tabI2048_in": ([2, 128, 16, 2048], BF16), "tabI256_in": ([2, 128, 2, 256], BF16),
    }

    LAYERED = ("mod_w", "win_wqkv", "win_wo", "ax_wqkv", "ax_wo", "ffn_w_gu", "ffn_w_down", "hy_w_in", "hy_wo")

    def W(self, base, idx):
        name = "%s__%d" % (base, idx)
        if name not in self.inputs:
            shape, dt = type(self).IN_SPECS[base]
            self._lay = getattr(self, "_lay", {})
            self._lay[name] = self.din(name, shape[1:], dt)
        return self._lay[name]

    def __getattr__(self, name):
        specs = type(self).IN_SPECS
        if name in specs and name not in type(self).LAYERED:
            shape, dt = specs[name]
            ap = self.din(name, shape, dt)
            setattr(self, name, ap)
            return ap
        raise AttributeError(name)

    def declare(self):
        nc = self.nc
        self.yT = self.dout("yT", [D, T])
        self.swk = self.dout("swk", [4, 2, 256, 512])
        self.swv = self.dout("swv", [4, 2, 256, 512])
        self.sak = self.dout("sak", [4, 1, 256, 512])
        self.sav = self.dout("sav", [4, 1, 256, 512])
        if self.dbg:
            self.xT = self.dout("xT_dbg", [D, T])
        else:
            self.xT = self.dscr("xT", [D, T], F32)
        self.hT = self.dscr("hT", [D, T], BF16)
        self.qT = self.dscr("qT", [D, T], BF16)
        self.kT = self.dscr("kT", [512, T], BF16)
        self.vtm = self.dscr("vtm", [T, 512], BF16)
        self.oT = self.dscr("oT", [D, T], BF16)
        self.actT = self.dscr("actT", [DFF, T], BF16)

    def persistent(self, stk):
        nc = self.nc
        self.mod = stk.enter_context(nc.sbuf_tensor("mod", [128, DEPTH, 2, 96], F32))
        self.modb = stk.enter_context(nc.sbuf_tensor("modb_sb", [128, DEPTH, 96], F32))
        self.normg = stk.enter_context(nc.sbuf_tensor("normg_sb", [128, 2, DEPTH, DC], F32))
        self.amod = stk.enter_context(nc.sbuf_tensor("amod", [128, DEPTH, 2, 2, DC], F32))
        self.finalg = stk.enter_context(nc.sbuf_tensor("finalg_sb", [128, DC], F32))
        self.ones_bf = stk.enter_context(nc.sbuf_tensor("ones_bf", [128, 128], BF16))
        self.ones_d = stk.enter_context(nc.sbuf_tensor("ones_d", [128, 128], BF16))
        self.ones_h = stk.enter_context(nc.sbuf_tensor("ones_h", [128, 128], BF16))
        self.zero_c = stk.enter_context(nc.sbuf_tensor("zero_c", [128, 1], F32))
        self.eps_c = stk.enter_context(nc.sbuf_tensor("eps_c", [128, 1], F32))

    def stage_mod(self):
        nc, S = self.nc, self.S
        if S.stopped:
            return
        with ExitStack() as stk:
            def sb(n, s, d):
                return stk.enter_context(nc.sbuf_tensor(self.nm(n), s, d))
            cv = sb("cv", [128, DC, 2], F32)
            sg = sb("sg", [128, DC, 2], F32)
            sT = sb("sT", [128, DC, 2], BF16)
            NW = 4
            wbuf = [sb("mw", [128, DC, 512], BF16) for _ in range(NW)]
            Bw = S.bufs_n(NW)
            pst = [stk.enter_context(nc.psum_tensor(self.nm("mps"), [128, 4, 2], F32)) for _ in range(4)]
            Bp = S.bufs_n(4)
            Bcv, BsT, Bmod, Bmb, Bng, Bc = S.buf(), S.buf(), S.buf(), S.buf(), S.buf(), S.buf()
            S.op("sp", lambda e: e.dma_start(out=cv[:], in_=self.cvec), writes=[Bcv], dma=1)
            S.op("sp", lambda e: e.dma_start(out=self.modb[:], in_=self.modb_in), writes=[Bmb], dma=1)
            S.op("sp", lambda e: e.dma_start(out=self.normg[:], in_=self.normg_in), writes=[Bng], dma=1)
            S.op("sp", lambda e: e.dma_start(out=self.finalg[:], in_=self.finalg_in), writes=[Bng], dma=1)
            S.op("dve", lambda e: e.memset(self.ones_bf[:], 1.0), writes=[Bc])
            S.op("dve", lambda e: e.memset(self.ones_d[:], 1.0 / D), writes=[Bc])
            S.op("dve", lambda e: e.memset(self.ones_h[:], 1.0 / HD), writes=[Bc])
            S.op("dve", lambda e: e.memset(self.zero_c[:], 0.0), writes=[Bc])
            S.op("dve", lambda e: e.memset(self.eps_c[:], EPS), writes=[Bc])
            S.op("act", lambda e: e.activation(out=sg[:], in_=cv[:], func=AF.Sigmoid), reads=[Bcv], writes=[BsT])
            S.op("dve", lambda e: e.tensor_tensor(out=sT[:], in0=sg[:], in1=cv[:], op=ALU.mult), reads=[Bcv, BsT], writes=[BsT])
            i = 0
            for l in range(self.n_layers):
                wv = self.W("mod_w", l).rearrange("(c p) n -> p c n", p=128)
                for nb in range(24):
                    s = i % NW
                    ps, bp = pst[i % 4], Bp[i % 4]
                    S.op("pool", lambda e, s=s, nb=nb, wv=wv: e.dma_start(out=wbuf[s][:], in_=wv[:, :, nb * 512:(nb + 1) * 512]),
                         writes=[Bw[s]], dma=1)
                    for oc in range(4):
                        for kc in range(DC):
                            S.op("pe", lambda e, s=s, oc=oc, kc=kc, ps=ps: e.matmul(
                                ps[:, oc, :], lhsT=wbuf[s][:, kc, oc * 128:(oc + 1) * 128], rhs=sT[:, kc, :],
                                start=(kc == 0), stop=(kc == DC - 1)), reads=[Bw[s], BsT], writes=[bp])
                    for g in range(2):
                        S.op("dve", lambda e, l=l, g=g, nb=nb, ps=ps: e.tensor_tensor(
                            out=self.mod[:, l, g, nb * 4:(nb + 1) * 4], in0=ps[:, :, g],
                            in1=self.modb[:, l, nb * 4:(nb + 1) * 4], op=ALU.add), reads=[bp, Bmb], writes=[Bmod])
                    i += 1
            for l in range(self.n_layers):
                for g in range(2):
                    for w in range(2):
                        sc = 16 + 48 * w
                        S.op("dve", lambda e, l=l, g=g, w=w, sc=sc: e.scalar_tensor_tensor(
                            out=self.amod[:, l, g, w, :], in0=self.mod[:, l, g, sc:sc + 16], scalar=1.0,
                            in1=self.normg[:, w, l, :], op0=ALU.add, op1=ALU.mult), reads=[Bmod, Bng], writes=[Bc])
            S.flush()

    def modcol(self, l, g, which, c):
        return self.mod[:, l, g, which * 16 + c: which * 16 + c + 1]

    def stage_copy_in(self):
        nc, S = self.nc, self.S
        if S.stopped:
            return
        with ExitStack() as stk:
            xs = [stk.enter_context(nc.sbuf_tensor(self.nm("cpx"), [128, DC, TW], F32)) for _ in range(3)]
            Bx = S.bufs_n(3)
            xi = self.xT_in.rearrange("(c p) t -> p c t", p=128)
            xo = self.xT.rearrange("(c p) t -> p c t", p=128)
            for tt in range(NTT):
                k = tt % 3
                S.op("sp", lambda e, k=k, tt=tt: e.dma_start(out=xs[k][:], in_=xi[:, :, tt * TW:(tt + 1) * TW]), writes=[Bx[k]], dma=1)
                S.op("sp", lambda e, k=k, tt=tt: e.dma_start(out=xo[:, :, tt * TW:(tt + 1) * TW], in_=xs[k][:]), reads=[Bx[k]], dma=1)
            S.flush()

    def stage_norm(self, l, w, final=False):
        nc, S = self.nc, self.S
        if S.stopped:
            return
        S.DEFER = S.DEFER0
        with ExitStack() as stk:
            def sb(n, s, d):
                return stk.enter_context(nc.sbuf_tensor(self.nm(n), s, d))
            NX = 2
            xin = [sb("xin", [128, DC, TW], F32) for _ in range(NX)]
            Bx = [S.bufs_n(4) for _ in range(NX)]
            sq = [sb("sq", [128, DC, TW], BF16) for _ in range(2)]
            Bsq = [S.bufs_n(4) for _ in range(2)]
            pss = [stk.enter_context(nc.psum_tensor(self.nm("nps"), [128, TW], F32)) for _ in range(2)]
            Bps = S.bufs_n(2)
            rstd = [sb("rstd", [128, TW], F32) for _ in range(2)]
            Br = S.bufs_n(2)
            tmp = Rot([sb("ntmp", [128, TW], F32) for _ in range(4)], S.bufs_n(4))
            odt = F32 if final else BF16
            hs = [sb("hs", [128, DC, TW], odt) for _ in range(2)]
            Bh = [S.bufs_n(4) for _ in range(2)]
            xv = self.xT.rearrange("(c p) t -> p c t", p=128)
            ov = (self.yT if final else self.hT).rearrange("(c p) t -> p c t", p=128)

            def load(tt):
                k = tt % NX
                for q in range(4):
                    S.op("sp", lambda e, k=k, q=q, tt=tt: e.dma_start(
                        out=xin[k][:, 4 * q:4 * q + 4, :], in_=xv[:, 4 * q:4 * q + 4, tt * TW:(tt + 1) * TW]),
                        writes=[Bx[k][q]], dma=1)
            load(0)
            for tt in range(NTT):
                if tt + 1 < NTT:
                    load(tt + 1)
                g = 0 if tt < 4 else 1
                k = tt % NX
                k2 = tt % 2
                for q in range(4):
                    S.op("act", lambda e, k=k, k2=k2, q=q: e.activation(
                        out=sq[k2][:, 4 * q:4 * q + 4, :], in_=xin[k][:, 4 * q:4 * q + 4, :], func=AF.Square),
                        reads=[Bx[k][q]], writes=[Bsq[k2][q]])
                for c in range(DC):
                    S.op("pe", lambda e, k2=k2, c=c: e.matmul(pss[k2][:], lhsT=self.ones_d[:], rhs=sq[k2][:, c, :],
                                                             start=(c == 0), stop=(c == DC - 1)),
                         reads=[Bsq[k2][c // 4]], writes=[Bps[k2]])
                S.op("act", lambda e, k2=k2: e.activation(out=rstd[k2][:], in_=pss[k2][:], func=AF.Sqrt, bias=self.eps_c[:, 0:1]),
                     reads=[Bps[k2]], writes=[Br[k2]])
                S.op("dve", lambda e, k2=k2: e.reciprocal(out=rstd[k2][:], in_=rstd[k2][:]), reads=[Br[k2]], writes=[Br[k2]])
                for c in range(DC):
                    tm, Btm = tmp.next()
                    S.op("dve", lambda e, k=k, k2=k2, c=c, tm=tm: e.tensor_tensor(
                        out=tm[:], in0=xin[k][:, c, :], in1=rstd[k2][:], op=ALU.mult),
                        reads=[Bx[k][c // 4], Br[k2]], writes=[Btm])
                    if final:
                        sc_ap, bi_ap = self.finalg[:, c:c + 1], self.zero_c[:, 0:1]
                    else:
                        sc_ap, bi_ap = self.amod[:, l, g, w, c:c + 1], self.modcol(l, g, 3 * w, c)
                    S.op("act", lambda e, k2=k2, c=c, tm=tm, sc_ap=sc_ap, bi_ap=bi_ap: e.activation(
                        out=hs[k2][:, c, :], in_=tm[:], func=AF.Identity, bias=bi_ap, scale=sc_ap),
                        reads=[Btm], writes=[Bh[k2][c // 4]])
                for q in range(4):
                    S.op("sp", lambda e, k2=k2, q=q, tt=tt: e.dma_start(
                        out=ov[:, 4 * q:4 * q + 4, tt * TW:(tt + 1) * TW], in_=hs[k2][:, 4 * q:4 * q + 4, :]),
                        reads=[Bh[k2][q]], dma=1)
            S.flush()

    def gemm(self, A_dram, KC, blocks, epilogue, stk, supertiles=None, nw=3, extra=None):
        self.S.DEFER = self.S.DEFER0
        nc, S = self.nc, self.S
        if supertiles is None:
            supertiles = [list(range(NTT))]
        maxt = max(len(s) for s in supertiles)
        bw = max(sum(n for _, _, n in segs) for segs, _ in blocks)
        A_sb = stk.enter_context(nc.sbuf_tensor(self.nm("A"), [128, KC, maxt * TW], BF16))
        BA = S.bufs_n(maxt)
        wb = [stk.enter_context(nc.sbuf_tensor(self.nm("W"), [128, KC, bw], BF16)) for _ in range(nw)]
        Bw = S.bufs_n(nw)
        npb = 8 if extra is None else 8 - extra
        banks = Rot([stk.enter_context(nc.psum_tensor(self.nm("gps"), [128, TW], F32)) for _ in range(npb)], S.bufs_n(npb))
        Av = A_dram.rearrange("(c p) t -> p c t", p=128)
        wi = 0
        for st in supertiles:
            for j, tt in enumerate(st):
                npc = 4 if KC > 16 else 2
                h = KC // npc
                S.op("sp", lambda e, j=j, tt=tt, h=h, npc=npc: [
                    e.dma_start(out=A_sb[:, i * h:(i + 1) * h, j * TW:(j + 1) * TW], in_=Av[:, i * h:(i + 1) * h, tt * TW:(tt + 1) * TW])
                    for i in range(npc)], writes=[BA[j]], dma=npc)
            for bi, (segs, groups) in enumerate(blocks):
                s = wi % nw
                wi += 1

                nsp = 4 if KC > 16 else 1
                hk = KC // nsp

                def wload(e, s=s, segs=segs, nsp=nsp, hk=hk):
                    r = []
                    off = 0
                    for W_ap, c0, n in segs:
                        Wv = W_ap.rearrange("(c p) n -> p c n", p=128)
                        for i in range(nsp):
                            r.append(e.dma_start(out=wb[s][:, i * hk:(i + 1) * hk, off:off + n], in_=Wv[:, i * hk:(i + 1) * hk, c0:c0 + n]))
                        off += n
                    return r
                S.op("pool", wload, writes=[Bw[s]], dma=len(segs) * nsp)
                for j, tt in enumerate(st):
                    for gi, grp in enumerate(groups):
                        pl = []
                        for coff in grp:
                            ps, Bp = banks.next()
                            for kc in range(KC):
                                S.op("pe", lambda e, ps=ps, s=s, kc=kc, coff=coff, j=j: e.matmul(
                                    ps[:], lhsT=wb[s][:, kc, coff:coff + 128], rhs=A_sb[:, kc, j * TW:(j + 1) * TW],
                                    start=(kc == 0), stop=(kc == KC - 1)), reads=[Bw[s], BA[j]], writes=[Bp])
                            pl.append((ps, Bp))
                        epilogue(bi, gi, tt, pl)
        return A_sb, BA, wb, Bw

    def make_resid_epilogue(self, stk, l, which, chunk_of):
        nc, S = self.nc, self.S
        xt = Rot([stk.enter_context(nc.sbuf_tensor(self.nm("xr"), [128, TW], F32)) for _ in range(6)], S.bufs_n(6))
        xv = self.xT.rearrange("(c p) t -> p c t", p=128)

        def epi(bi, gi, tt, pl):
            oc = chunk_of(bi, gi)
            g = 0 if tt < 4 else 1
            (ps, Bp), = pl
            x, Bx = xt.next()
            S.op("sp", lambda e: e.dma_start(out=x[:], in_=xv[:, oc, tt * TW:(tt + 1) * TW]), writes=[Bx], dma=1)
            S.op("dve", lambda e: e.scalar_tensor_tensor(out=x[:], in0=ps[:], scalar=self.modcol(l, g, which, oc), in1=x[:],
                                                         op0=ALU.mult, op1=ALU.add), reads=[Bp, Bx], writes=[Bx])
            S.op("sp", lambda e: e.dma_start(out=xv[:, oc, tt * TW:(tt + 1) * TW], in_=x[:]), reads=[Bx], dma=1)
        return epi

    def stage_ffn(self, l):
        nc, S = self.nc, self.S
        self.stage_norm(l, 1)
        Wgu = self.W("ffn_w_gu", l)
        if S.stopped:
            return
        with ExitStack() as stk:
            sg = Rot([stk.enter_context(nc.sbuf_tensor(self.nm("sg"), [128, TW], F32)) for _ in range(3)], S.bufs_n(3))
            ao = Rot([stk.enter_context(nc.sbuf_tensor(self.nm("ao"), [128, TW], BF16)) for _ in range(4)], S.bufs_n(4))
            av = self.actT.rearrange("(c p) t -> p c t", p=128)
            blocks = []
            for jb in range(FC // 2):
                blocks.append(([(Wgu, jb * 256, 256), (Wgu, DFF + jb * 256, 256)], [(0, 256), (128, 384)]))

            def epi(bi, gi, tt, pl):
                j = bi * 2 + gi
                (pg, Bg), (pu, Bu) = pl
                s, Bs = sg.next()
                a, Ba = ao.next()
                S.op("act", lambda e: e.activation(out=s[:], in_=pg[:], func=AF.Sigmoid), reads=[Bg], writes=[Bs])
                S.op("dve", lambda e: e.tensor_tensor(out=s[:], in0=s[:], in1=pg[:], op=ALU.mult), reads=[Bs, Bg], writes=[Bs])
                S.op("dve", lambda e: e.tensor_tensor(out=a[:], in0=s[:], in1=pu[:], op=ALU.mult), reads=[Bs, Bu], writes=[Ba])
                S.op("sp", lambda e: e.dma_start(out=av[:, j, tt * TW:(tt + 1) * TW], in_=a[:]), reads=[Ba], dma=1)
            self.gemm(self.hT, DC, blocks, epi, stk)
            S.flush()
        Wd = self.W("ffn_w_down", l)
        if S.stopped:
            return
        with ExitStack() as stk:
            blocks = [([(Wd, b * 256, 256)], [(0,), (128,)]) for b in range(8)]
            epi = self.make_resid_epilogue(stk, l, 5, lambda bi, gi: bi * 2 + gi)
            self.gemm(self.actT, FC, blocks, epi, stk, supertiles=[[0, 1], [2, 3], [4, 5]], nw=3)
            S.flush()

    def stage_attn(self, l, kind, j):
        nc, S = self.nc, self.S
        self.stage_norm(l, 0)
        win = (kind == 0)
        Wqkv = self.W("win_wqkv" if win else "ax_wqkv", j)
        Wo = self.W("win_wo" if win else "ax_wo", j)
        so_k = (self.swk if win else self.sak)
        so_v = (self.swv if win else self.sav)
        if S.stopped:
            return
        with ExitStack() as stk:
            def sb(n, s, d):
                return stk.enter_context(nc.sbuf_tensor(self.nm(n), s, d))
            rope = sb("rope", [128, 2, TL], F32)
            ropeP = sb("ropeP", [128, 128], BF16)
            axg = sb("axg", [128, 2], F32)
            kgrep = sb("kgrep", [128, 128], F32)
            Bc = S.buf()
            S.op("sp", lambda e: e.dma_start(out=rope[:], in_=self.rope_in), writes=[Bc], dma=1)
            S.op("sp", lambda e: e.dma_start(out=ropeP[:], in_=self.ropeP_in), writes=[Bc], dma=1)
            S.op("sp", lambda e: e.dma_start(out=axg[:], in_=self.axg_in), writes=[Bc], dma=1)
            S.op("sp", lambda e: e.dma_start(out=kgrep[:], in_=self.axkg_rep), writes=[Bc], dma=1)
            qn = Rot([sb("qn", [128, TW], BF16) for _ in range(3)], S.bufs_n(3))
            sqh = Rot([sb("sqh", [128, TW], BF16) for _ in range(2)], S.bufs_n(2))
            rs = Rot([sb("rs", [128, TW], F32) for _ in range(2)], S.bufs_n(2))
            t1 = Rot([sb("t1", [128, TW], F32) for _ in range(2)], S.bufs_n(2))
            t2 = Rot([sb("t2", [128, TW], F32) for _ in range(2)], S.bufs_n(2))
            qo = Rot([sb("qo", [128, TW], BF16) for _ in range(3)], S.bufs_n(3))
            xps = Rot([stk.enter_context(nc.psum_tensor(self.nm("xps"), [128, TW], F32)) for _ in range(3)], S.bufs_n(3))
            qv = self.qT.rearrange("(c p) t -> p c t", p=128)
            kv = self.kT.rearrange("(c p) t -> p c t", p=128)
            blocks = [([(Wqkv, b * 512, 512)], [(0,), (128,), (256,), (384,)]) for b in range(5)]

            def epi(bi, gi, tt, pl):
                oc = bi * 4 + gi
                (ps, Bp), = pl
                isk = oc >= NH
                dst = (kv[:, oc - NH, tt * TW:(tt + 1) * TW] if isk else qv[:, oc, tt * TW:(tt + 1) * TW])
                lat = tt < 4
                q_, Bq = qn.next()
                if win:
                    S.op("act", lambda e: e.activation(out=q_[:], in_=ps[:], func=AF.Copy), reads=[Bp], writes=[Bq])
                else:
                    s_, Bs = sqh.next()
                    r_, Br = rs.next()
                    m_, Bm = xps.next()
                    S.op("act", lambda e: e.activation(out=s_[:], in_=ps[:], func=AF.Square), reads=[Bp], writes=[Bs])
                    S.op("pe", lambda e: e.matmul(m_[:], lhsT=self.ones_h[:], rhs=s_[:], start=True, stop=True), reads=[Bs], writes=[Bm])
                    S.op("act", lambda e: e.activation(out=r_[:], in_=m_[:], func=AF.Sqrt, bias=self.eps_c[:, 0:1]),
                         reads=[Bm], writes=[Br])
                    S.op("dve", lambda e: e.reciprocal(out=r_[:], in_=r_[:]), reads=[Br], writes=[Br])
                    gcol = axg[:, 1:2] if isk else axg[:, 0:1]
                    S.op("dve", lambda e: e.scalar_tensor_tensor(out=q_[:], in0=ps[:], scalar=gcol, in1=r_[:], op0=ALU.mult, op1=ALU.mult),
                         reads=[Bp, Br, Bc], writes=[Bq])
                if not lat or os.environ.get('KDBG_NOROPE'):
                    S.op("sp", lambda e: e.dma_start(out=dst, in_=q_[:]), reads=[Bq], dma=1)
                    return
                m_, Bm = xps.next()
                a_, Ba = t1.next()
                b_, Bb = t2.next()
                o_, Bo = qo.next()
                S.op("pe", lambda e: e.matmul(m_[:], lhsT=ropeP[:], rhs=q_[:], start=True, stop=True), reads=[Bq, Bc], writes=[Bm])
                S.op("dve", lambda e: e.tensor_tensor(out=a_[:], in0=q_[:], in1=rope[:, 0, tt * TW:(tt + 1) * TW], op=ALU.mult),
                     reads=[Bq, Bc], writes=[Ba])
                S.op("dve", lambda e: e.tensor_tensor(out=b_[:], in0=m_[:], in1=rope[:, 1, tt * TW:(tt + 1) * TW], op=ALU.mult),
                     reads=[Bm, Bc], writes=[Bb])
                S.op("pool", lambda e: e.tensor_tensor(out=o_[:], in0=a_[:], in1=b_[:], op=ALU.add), reads=[Ba, Bb], writes=[Bo])
                S.op("sp", lambda e: e.dma_start(out=dst, in_=o_[:]), reads=[Bo], dma=1)
            A_sb, BA, wkv, Bwkv = self.gemm(self.hT, DC, blocks, epi, stk, nw=2, extra=3)
            for i_, c0 in enumerate((2048, 2560)):
                S.op("pool", lambda e, i_=i_, c0=c0: e.dma_start(
                    out=wkv[i_][:], in_=Wqkv.rearrange("(c p) n -> p c n", p=128)[:, :, c0:c0 + 512]), writes=[Bwkv[i_]], dma=1)
            vb = Rot([sb("vb", [128, 512], BF16) for _ in range(3)], S.bufs_n(3))
            vf = Rot([sb("vf", [128, 512], F32) for _ in range(3)], S.bufs_n(3))
            ssq = Rot([sb("ssq", [128, 4], F32) for _ in range(2)], S.bufs_n(2))
            junk = Rot([sb("junk", [128, 128], F32) for _ in range(2)], S.bufs_n(2))
            for tc in range(0 if not os.environ.get('KDBG_NOVK') else 999, int(os.environ.get('KDBG_VKMAX', T // 128))):
                ctx = tc >= TL // 128
                for which in ((1, 0) if ctx else (1,)):
                    ps, Bp = xps.next()
                    for kc in range(DC):
                        S.op("pe", lambda e, ps=ps, kc=kc, tc=tc, which=which: e.matmul(
                            ps[:], lhsT=A_sb[:, kc, tc * 128:(tc + 1) * 128], rhs=wkv[which][:, kc, :],
                            start=(kc == 0), stop=(kc == DC - 1)), reads=[BA[tc // 4], Bwkv[which]], writes=[Bp])
                    if which == 1:
                        v_, Bv = vb.next()
                        S.op("act", lambda e, v_=v_, ps=ps: e.activation(out=v_[:], in_=ps[:], func=AF.Copy), reads=[Bp], writes=[Bv])
                        S.op("sp", lambda e, v_=v_, tc=tc: e.dma_start(out=self.vtm[tc * 128:(tc + 1) * 128, :], in_=v_[:]), reads=[Bv], dma=1)
                    if ctx:
                        sq_, half = divmod(tc - TL // 128, 2)
                        f_, Bf = vf.next()
                        if which == 1 or win:
                            S.op("act", lambda e, f_=f_, ps=ps: e.activation(out=f_[:], in_=ps[:], func=AF.Copy), reads=[Bp], writes=[Bf])
                        else:
                            s_, Bs = ssq.next()
                            jk, Bj = junk.next()
                            for h in range(NKV):
                                S.op("act", lambda e, h=h, jk=jk, ps=ps, s_=s_: e.activation(
                                    out=jk[:], in_=ps[:, h * 128:(h + 1) * 128], func=AF.Square, accum_out=s_[:, h:h + 1]),
                                    reads=[Bp], writes=[Bj, Bs])
                            S.op("dve", lambda e, s_=s_: e.tensor_scalar(out=s_[:], in0=s_[:], scalar1=1.0 / HD, scalar2=EPS,
                                                                        op0=ALU.mult, op1=ALU.add), reads=[Bs], writes=[Bs])
                            S.op("act", lambda e, s_=s_: e.activation(out=s_[:], in_=s_[:], func=AF.Sqrt), reads=[Bs], writes=[Bs])
                            S.op("dve", lambda e, s_=s_: e.reciprocal(out=s_[:], in_=s_[:]), reads=[Bs], writes=[Bs])
                            for h in range(NKV):
                                S.op("dve", lambda e, h=h, f_=f_, ps=ps, s_=s_: e.scalar_tensor_tensor(
                                    out=f_[:, h * 128:(h + 1) * 128], in0=ps[:, h * 128:(h + 1) * 128], scalar=s_[:, h:h + 1],
                                    in1=kgrep[:], op0=ALU.mult, op1=ALU.mult), reads=[Bp, Bs, Bc], writes=[Bf])
                        dst = (so_v if which == 1 else so_k)[sq_, j, half * 128:(half + 1) * 128, :]
                        if not os.environ.get('KDBG_NOSTATE'):
                            S.op("sp", lambda e, f_=f_, dst=dst: e.dma_start(out=dst, in_=f_[:]), reads=[Bf], dma=1)
            S.flush()
        self.attn_core(l, kind, j)
        if S.stopped:
            return
        with ExitStack() as stk:
            blocks = [([(Wo, b * 512, 512)], [(0,), (128,), (256,), (384,)]) for b in range(4)]
            epi = self.make_resid_epilogue(stk, l, 2, lambda bi, gi: bi * 4 + gi)
            self.gemm(self.oT, DC, blocks, epi, stk)
            S.flush()

    def attn_core(self, l, kind, j):
        nc, S = self.nc, self.S
        win = (kind == 0)
        ckT = (self.cwkT if win else self.cakT)[j]
        cv_ = (self.cwv if win else self.cav)[j]
        scale = HD ** -0.5
        if S.stopped:
            return
        S.DEFER = S.DEFER0
        with ExitStack() as stk:
            def sb(n, s, d):
                return stk.enter_context(nc.sbuf_tensor(self.nm(n), s, d))
            NKC = T // 128
            k_sb = sb("k_sb", [128, NKV, T + PAST], BF16)
            v_sb = sb("v_sb", [128, NKC + 4, 512], BF16)
            Bk, Bv, Bc = S.buf(), S.buf(), S.buf()
            S.op("sp", lambda e: e.dma_start(out=k_sb[:, :, 0:T], in_=self.kT.rearrange("(g p) t -> p g t", p=128)), writes=[Bk], dma=1)
            S.op("pool", lambda e: e.dma_start(out=k_sb[:, :, T:T + PAST], in_=ckT.rearrange("g p t -> p g t")), writes=[Bk], dma=1)
            S.op("sp", lambda e: e.dma_start(out=v_sb[:, 0:NKC, :], in_=self.vtm.rearrange("(c p) n -> p c n", p=128)), writes=[Bv], dma=1)
            S.op("pool", lambda e: e.dma_start(out=v_sb[:, NKC:NKC + 4, :], in_=cv_.rearrange("(c p) n -> p c n", p=128)), writes=[Bv], dma=1)
            wmask = sb("wmask", [128, 2, 512], BF16)
            S.op("sp", lambda e: e.dma_start(out=wmask[:], in_=self.wmask_in), writes=[Bc], dma=1)
            esk = None
            if win:
                esr = sb("esr", [1, NH * 128], F32)
                esk = sb("esk", [1, NH * 128], BF16)
                ones1 = sb("ones1", [1, 128], BF16)
                S.op("sp", lambda e: e.dma_start(out=esr[:], in_=self.esink_in[j]), writes=[Bc], dma=1)
                S.op("act", lambda e: e.activation(out=esr[:], in_=esr[:], func=AF.Exp), reads=[Bc], writes=[Bc])
                esl = sb("esl", [1, NH * 128], BF16)
                esf = sb("esf", [1, NH * 128], F32)
                S.op("act", lambda e: e.activation(out=esk[:], in_=esr[:], func=AF.Copy), reads=[Bc], writes=[Bc])
                S.op("act", lambda e: e.activation(out=esf[:], in_=esk[:], func=AF.Copy), reads=[Bc], writes=[Bc])
                S.op("dve", lambda e: e.tensor_tensor(out=esf[:], in0=esr[:], in1=esf[:], op=ALU.subtract), reads=[Bc], writes=[Bc])
                S.op("act", lambda e: e.activation(out=esl[:], in_=esf[:], func=AF.Copy), reads=[Bc], writes=[Bc])
                S.op("dve", lambda e: e.memset(ones1[:], 1.0), writes=[Bc])
            q_sb = [sb("q_sb", [128, 4, T], BF16) for _ in range(2)]
            Bq = S.bufs_n(2)
            o_sb = [sb("o_sb", [128, 4, T], BF16) for _ in range(2)]
            Bo = S.bufs_n(2)
            pT = Rot([sb("pT", [128, 512], BF16) for _ in range(4)], S.bufs_n(4))
            rc = Rot([sb("rc", [128, 512], F32) for _ in range(2)], S.bufs_n(2))
            sps = Rot([stk.enter_context(nc.psum_tensor(self.nm("sps"), [128, 512], F32)) for _ in range(3)], S.bufs_n(3))
            dps = Rot([stk.enter_context(nc.psum_tensor(self.nm("dps"), [128, 512], F32)) for _ in range(2)], S.bufs_n(2))
            ops_ = Rot([stk.enter_context(nc.psum_tensor(self.nm("ops"), [128, 512], F32)) for _ in range(2)], S.bufs_n(2))
            seqs = [(0, TL, True)] + [(TL + 256 * s, 256, False) for s in range(4)]
            qv = self.qT.rearrange("(h p) t -> p h t", p=128)
            ov = self.oT.rearrange("(h p) t -> p h t", p=128)
            tasks = []
            for g in range(NKV):
                for (t0, ln, has_cache) in seqs:
                    nqb = ln // 128
                    for qb in range(nqb):
                        chunks = []
                        if win and has_cache:
                            for d_, mk in ((-1, 0), (0, None), (1, 1)):
                                kb = qb + d_
                                if 0 <= kb < nqb:
                                    chunks.append((t0 + kb * 128, (t0 + kb * 128) // 128, mk))
                        else:
                            for kb in range(nqb):
                                chunks.append((t0 + kb * 128, (t0 + kb * 128) // 128, None))
                        if has_cache:
                            for c in range(4):
                                chunks.append((T + c * 128, NKC + c, None))
                        tasks.append((g, t0 + qb * 128, chunks))
            flat = [(ti, ci) for ti, tk in enumerate(tasks) for ci in range(len(tk[2]))]
            LOOK = 2
            loaded = set()
            sbank = {}
            tstate = {}

            def load_q(g):
                if g < NKV and g not in loaded:
                    loaded.add(g)
                    S.op("sp", lambda e, g=g: e.dma_start(out=q_sb[g % 2][:], in_=qv[:, 4 * g:4 * g + 4, :]), writes=[Bq[g % 2]], dma=1)

            def emit_s(idx):
                ti, ci = flat[idx]
                g, q0, chunks = tasks[ti]
                load_q(g)
                kc0 = chunks[ci][0]
                sp_, Bs = sps.next()
                sbank[idx] = (sp_, Bs)
                qs, Bqs = q_sb[g % 2], Bq[g % 2]
                S.op("pe", lambda e, sp_=sp_, kc0=kc0, g=g, qs=qs, q0=q0: e.matmul(
                    sp_[:].rearrange("p (h q) -> p h q", h=4), lhsT=k_sb[:, g, kc0:kc0 + 128], rhs=qs[:, :, q0:q0 + 128],
                    start=True, stop=True), reads=[Bk, Bqs], writes=[Bs])

            for idx in range(min(LOOK, len(flat))):
                emit_s(idx)
            for idx in range(len(flat)):
                if idx + LOOK < len(flat):
                    emit_s(idx + LOOK)
                ti, ci = flat[idx]
                g, q0, chunks = tasks[ti]
                n = len(chunks)
                kc0, vc, mk = chunks[ci]
                os_, Bos = o_sb[g % 2], Bo[g % 2]
                if ci == 0:
                    tstate[ti] = (dps.next(), ops_.next())
                (dp, Bd), (op_, Bop) = tstate[ti]
                sp_, Bs = sbank.pop(idx)
                p_, Bp = pT.next()
                S.op("act", lambda e, sp_=sp_, p_=p_: e.activation(out=p_[:], in_=sp_[:], func=AF.Exp, scale=scale),
                     reads=[Bs], writes=[Bp])
                if mk is not None:
                    S.op("pool", lambda e, p_=p_, mk=mk: e.tensor_tensor(out=p_[:], in0=p_[:], in1=wmask[:, mk, :], op=ALU.mult),
                         reads=[Bp, Bc], writes=[Bp])
                last = (ci == n - 1) and not win
                S.op("pe", lambda e, dp=dp, p_=p_, ci=ci, last=last: e.matmul(
                    dp[:], lhsT=self.ones_bf[:], rhs=p_[:], start=(ci == 0), stop=last), reads=[Bp], writes=[Bd])
                S.op("pe", lambda e, op_=op_, p_=p_, ci=ci, vc=vc, g=g, n=n: e.matmul(
                    op_[:], lhsT=v_sb[:, vc, g * 128:(g + 1) * 128], rhs=p_[:], start=(ci == 0), stop=(ci == n - 1)),
                    reads=[Bp, Bv], writes=[Bop])
                if ci == n - 1:
                    if win:
                        S.op("pe", lambda e, dp=dp, g=g: e.matmul(dp[:], lhsT=ones1[:], rhs=esk[:, g * 512:(g + 1) * 512],
                                                                  start=False, stop=False), reads=[Bc], writes=[Bd])
                        S.op("pe", lambda e, dp=dp, g=g: e.matmul(dp[:], lhsT=ones1[:], rhs=esl[:, g * 512:(g + 1) * 512],
                                                                  start=False, stop=True), reads=[Bc], writes=[Bd])
                    r_, Br = rc.next()
                    S.op("dve", lambda e, r_=r_, dp=dp: e.reciprocal(out=r_[:], in_=dp[:]), reads=[Bd], writes=[Br])
                    S.op("dve", lambda e, r_=r_, op_=op_, os_=os_, q0=q0: e.tensor_tensor(
                        out=os_[:, :, q0:q0 + 128], in0=op_[:].rearrange("p (h q) -> p h q", h=4),
                        in1=r_[:].rearrange("p (h q) -> p h q", h=4), op=ALU.mult), reads=[Bop, Br], writes=[Bos])
                    del tstate[ti]
                    if ti + 1 == len(tasks) or tasks[ti + 1][0] != g:
                        S.op("sp", lambda e, g=g, os_=os_: e.dma_start(out=ov[:, 4 * g:4 * g + 4, :], in_=os_[:]), reads=[Bos], dma=1)
            S.flush()

    def build(self):
        nc = self.nc
        self.declare()
        with ExitStack() as stk:
            self.S = Sched(nc, stk)
            self.S.stop = self.stop
            self.persistent(stk)
            try:
                self.stage_copy_in()
                self.stage_mod()
                for l in range(self.n_layers):
                    kind, j = l % 3, l // 3
                    if kind == 1:
                        self.stage_hyena(l, j)
                    else:
                        self.stage_attn(l, kind, j)
                    self.stage_ffn(l)
                self.stage_norm(0, 0, final=True)
            except StopBuild:
                pass
        return nc


    def hy_decl(self):
        if hasattr(self, "yin"):
            return
        self.yin = self.dscr("yin", [3 * D, T], F32)
        self.vfm = self.dscr("vfm", [D, T], F32)
        self.x1fm = self.dscr("x1fm", [D, T], F32)
        self.x2fm = self.dscr("x2fm", [D, T], F32)
        self.z1fm = self.dscr("z1fm", [D, T], F32)
        self.vtmh = self.dscr("vtmh", [T, D], BF16)
        self.ztmh = self.dscr("ztmh", [T, D], BF16)
        self.z2T = self.dscr("z2T", [D, T], BF16)
        self.Hs = {2048: self.dscr("Hs2048", [2, 2, 2048, D], F32), 256: self.dscr("Hs256", [2, 2, 256, D], F32)}

    def stage_hyena(self, l, j):
        nc, S = self.nc, self.S
        self.hy_decl()
        self.stage_norm(l, 0)
        if S.stopped:
            return
        Win = self.W("hy_w_in", j)
        with ExitStack() as stk:
            yo = Rot([stk.enter_context(nc.sbuf_tensor(self.nm("yo"), [128, TW], F32)) for _ in range(4)], S.bufs_n(4))
            yv = self.yin.rearrange("(c p) t -> p c t", p=128)
            blocks = [([(Win, b * 512, 512)], [(0,), (128,), (256,), (384,)]) for b in range(12)]
            cnt = [0]

            def epi(bi, gi, tt, pl):
                oc = bi * 4 + gi
                (ps, Bp), = pl
                y_, By = yo.next()
                cnt[0] += 1
                S.op("act", lambda e: e.activation(out=y_[:], in_=ps[:], func=AF.Copy), reads=[Bp], writes=[By])
                S.op("sp", lambda e: e.dma_start(out=yv[:, oc, tt * TW:(tt + 1) * TW], in_=y_[:]), reads=[By], dma=1)
            self.gemm(self.hT, DC, blocks, epi, stk)
            S.flush()
        self.hy_shortconv(j)
        self.hy_filter(j, 2048)
        self.hy_filter(j, 256)
        self.hy_conv(j, 0)
        self.hy_conv(j, 1)
        if S.stopped:
            return
        Wo = self.W("hy_wo", j)
        with ExitStack() as stk:
            blocks = [([(Wo, b * 512, 512)], [(0,), (128,), (256,), (384,)]) for b in range(4)]
            epi = self.make_resid_epilogue(stk, l, 2, lambda bi, gi: bi * 4 + gi)
            self.gemm(self.z2T, DC, blocks, epi, stk)
            S.flush()

    SEQS = [(0, TL)] + [(TL + 256 * s_, 256) for s_ in range(4)]

    def hy_shortconv(self, j):
        nc, S = self.nc, self.S
        if S.stopped:
            return
        S.DEFER = 0
        with ExitStack() as stk:
            def sb(n, s, d):
                return stk.enter_context(nc.sbuf_tensor(self.nm(n), s, d))
            cw = sb("cw", [128, 48, 4], F32)
            ident = sb("ident", [128, 128], BF16)
            Bc = S.buf()
            S.op("sp", lambda e: e.dma_start(out=cw[:], in_=self.hyconv_in), writes=[Bc], dma=1)
            S.op("sp", lambda e: e.dma_start(out=ident[:], in_=self.ident_in), writes=[Bc], dma=1)
            yi = Rot([sb("yi", [128, T], F32) for _ in range(2)], S.bufs_n(2))
            uo = Rot([sb("uo", [128, T], F32) for _ in range(2)], S.bufs_n(2))
            ub = Rot([sb("ub", [128, T], BF16) for _ in range(2)], S.bufs_n(2))
            vt = Rot([sb("vt", [128, 4, 128], BF16) for _ in range(3)], S.bufs_n(3))
            tps = Rot([stk.enter_context(nc.psum_tensor(self.nm("tps"), [128, 4, 128], BF16)) for _ in range(3)], S.bufs_n(3))
            yv = self.yin.rearrange("(c p) t -> p c t", p=128)
            dsts = [self.vfm.rearrange("(c p) t -> p c t", p=128), self.x1fm.rearrange("(c p) t -> p c t", p=128),
                    self.x2fm.rearrange("(c p) t -> p c t", p=128)]
            for oc in range(48):
                y_, By = yi.next()
                u_, Bu = uo.next()
                S.op("sp", lambda e, y_=y_, oc=oc: e.dma_start(out=y_[:], in_=yv[:, oc, :]), writes=[By], dma=1)
                S.op("act", lambda e, y_=y_, u_=u_, oc=oc: e.activation(out=u_[:], in_=y_[:], func=AF.Identity,
                                                                       bias=cw[:, oc, 3:4], scale=cw[:, oc, 1:2]), reads=[By, Bc], writes=[Bu])
                for (a, ln) in self.SEQS:
                    b = a + ln
                    S.op("dve", lambda e, y_=y_, u_=u_, oc=oc, a=a, b=b: e.scalar_tensor_tensor(
                        out=u_[:, a + 1:b], in0=y_[:, a:b - 1], scalar=cw[:, oc, 0:1], in1=u_[:, a + 1:b], op0=ALU.mult, op1=ALU.add),
                        reads=[By, Bu, Bc], writes=[Bu])
                    S.op("dve", lambda e, y_=y_, u_=u_, oc=oc, a=a, b=b: e.scalar_tensor_tensor(
                        out=u_[:, a:b - 1], in0=y_[:, a + 1:b], scalar=cw[:, oc, 2:3], in1=u_[:, a:b - 1], op0=ALU.mult, op1=ALU.add),
                        reads=[By, Bu, Bc], writes=[Bu])
                S.op("sp", lambda e, u_=u_, oc=oc: e.dma_start(out=dsts[oc // 16][:, oc % 16, :], in_=u_[:]), reads=[Bu], dma=1)
                if oc < 16:
                    self.fm_to_tm(u_, Bu, oc, self.vtmh, ub, vt, tps, ident, Bc)
            S.flush()

    def fm_to_tm(self, u_, Bu, oc, dst_tm, ub, vt, tps, ident, Bc, c0=0, ncols=T):
        S = self.S
        b_, Bb = ub.next()
        S.op("act", lambda e: e.activation(out=b_[:, 0:ncols], in_=u_[:, 0:ncols], func=AF.Copy), reads=[Bu], writes=[Bb])
        dv = dst_tm.rearrange("(c p) d -> p c d", p=128)
        for q in range(ncols // 512):
            tp, Bt = tps.next()
            v_, Bv = vt.next()
            for i in range(4):
                S.op("pe", lambda e, tp=tp, i=i, q=q: e.transpose(tp[:, i, :], b_[:, q * 512 + i * 128: q * 512 + (i + 1) * 128], ident[:]),
                     reads=[Bb, Bc], writes=[Bt])
            S.op("act", lambda e, tp=tp, v_=v_: e.activation(out=v_[:], in_=tp[:], func=AF.Copy), reads=[Bt], writes=[Bv])
            tc0 = (c0 + q * 512) // 128
            S.op("sp", lambda e, v_=v_, tc0=tc0: e.dma_start(out=dv[:, tc0:tc0 + 4, oc * 128:(oc + 1) * 128], in_=v_[:]), reads=[Bv], dma=1)

    def hy_filter(self, j, L):
        nc, S = self.nc, self.S
        if S.stopped:
            return
        S.DEFER = 0
        nT = L // 128
        NW = min(512, L)
        TWO_PI = 2.0 * math.pi
        Hs = self.Hs[L]
        feats_in = self.feats2048_in if L == 2048 else self.feats256_in
        tnn_in = self.tnn2048_in if L == 2048 else self.tnn256_in
        tabF = self.tabF2048_in if L == 2048 else self.tabF256_in
        with ExitStack() as stk:
            def sb(n, s, d):
                return stk.enter_context(nc.sbuf_tensor(self.nm(n), s, d))
            feats = sb("feats", [33, L], F32)
            w1 = sb("fw1", [33, 64], F32)
            w2 = sb("fw2", [64, 64], F32)
            w3 = sb("fw3", [64, 4 * D], F32)
            fb = sb("ffb", [64, 4], F32)
            fraw = sb("fraw", [64, 4], F32)
            tnn = sb("tnn", [128, nT], F32)
            drep = sb("drep", [128, D], F32)
            a1 = sb("a1", [64, L], F32)
            a2 = sb("a2", [64, L], F32)
            Bc, Ba1, Ba2 = S.buf(), S.buf(), S.buf()
            S.op("sp", lambda e: e.dma_start(out=feats[:], in_=feats_in), writes=[Bc], dma=1)
            S.op("sp", lambda e: e.dma_start(out=w1[:], in_=self.hyw1_in), writes=[Bc], dma=1)
            S.op("sp", lambda e: e.dma_start(out=w2[:], in_=self.hyw2_in), writes=[Bc], dma=1)
            S.op("sp", lambda e: e.dma_start(out=w3[:], in_=self.hyw3_in), writes=[Bc], dma=1)
            S.op("sp", lambda e: e.dma_start(out=fraw[:], in_=self.hyfb_in), writes=[Bc], dma=1)
            S.op("sp", lambda e: e.dma_start(out=tnn[:], in_=tnn_in), writes=[Bc], dma=1)
            S.op("sp", lambda e: e.dma_start(out=drep[:], in_=self.drep_in), writes=[Bc], dma=1)
            S.op("act", lambda e: e.activation(out=fb[:, 0:1], in_=fraw[:, 0:1], func=AF.Copy), reads=[Bc], writes=[Bc])
            S.op("act", lambda e: e.activation(out=fb[:, 2:3], in_=fraw[:, 1:2], func=AF.Copy), reads=[Bc], writes=[Bc])
            S.op("dve", lambda e: e.tensor_tensor(out=fb[:, 1:2], in0=fraw[:, 2:3], in1=fraw[:, 0:1], op=ALU.mult), reads=[Bc], writes=[Bc])
            S.op("dve", lambda e: e.tensor_tensor(out=fb[:, 3:4], in0=fraw[:, 3:4], in1=fraw[:, 1:2], op=ALU.mult), reads=[Bc], writes=[Bc])
            fps = Rot([stk.enter_context(nc.psum_tensor(self.nm("fps"), [128, 512], F32)) for _ in range(4)], S.bufs_n(4))
            nps = Rot([stk.enter_context(nc.psum_tensor(self.nm("fnps"), [128, 512], F32)) for _ in range(1)], S.bufs_n(1))
            hps = Rot([stk.enter_context(nc.psum_tensor(self.nm("hps"), [128, 512], F32)) for _ in range(3)], S.bufs_n(3))
            arg = Rot([sb("arg", [64, NW], F32) for _ in range(2)], S.bufs_n(2))
            kf = Rot([sb("kf", [64, NW], F32) for _ in range(2)], S.bufs_n(2))
            ki = Rot([sb("ki", [64, NW], mybir.dt.int32) for _ in range(2)], S.bufs_n(2))

            def sin_layer(src, Bsrc, wt, kdim, col, dst, Bdst):
                for c in range(L // NW):
                    ps, Bp = fps.next()
                    a_, Ba = arg.next()
                    k_, Bk = kf.next()
                    i_, Bi = ki.next()
                    S.op("pe", lambda e, ps=ps, c=c: e.matmul(ps[0:64, 0:NW], lhsT=wt[0:kdim, :], rhs=src[0:kdim, c * NW:(c + 1) * NW],
                                                             start=True, stop=True), reads=[Bsrc, Bc], writes=[Bp])
                    S.op("act", lambda e, ps=ps, a_=a_: e.activation(out=a_[:], in_=ps[0:64, 0:NW], func=AF.Identity,
                                                                    bias=fb[:, col + 1:col + 2], scale=fb[:, col:col + 1]), reads=[Bp, Bc], writes=[Ba])
                    S.op("dve", lambda e, a_=a_, k_=k_: e.tensor_scalar(out=k_[:], in0=a_[:], scalar1=1.0 / TWO_PI, scalar2=12582912.0, op0=ALU.mult, op1=ALU.add),
                         reads=[Ba], writes=[Bk])
                    S.op("dve", lambda e, k_=k_: e.tensor_scalar(out=k_[:], in0=k_[:], scalar1=12582912.0, scalar2=None, op0=ALU.subtract),
                         reads=[Bk], writes=[Bk])
                    S.op("dve", lambda e, a_=a_, k_=k_: e.scalar_tensor_tensor(out=a_[:], in0=k_[:], scalar=-TWO_PI, in1=a_[:], op0=ALU.mult, op1=ALU.add),
                         reads=[Bk, Ba], writes=[Ba])
                    S.op("dve", lambda e, a_=a_, k_=k_: e.tensor_scalar(out=k_[:], in0=a_[:], scalar1=math.pi, scalar2=TWO_PI, op0=ALU.is_gt, op1=ALU.mult),
                         reads=[Ba], writes=[Bk])
                    S.op("dve", lambda e, a_=a_, k_=k_: e.tensor_tensor(out=a_[:], in0=a_[:], in1=k_[:], op=ALU.subtract), reads=[Ba, Bk], writes=[Ba])
                    S.op("dve", lambda e, a_=a_, k_=k_: e.tensor_scalar(out=k_[:], in0=a_[:], scalar1=-math.pi, scalar2=TWO_PI, op0=ALU.is_lt, op1=ALU.mult),
                         reads=[Ba], writes=[Bk])
                    S.op("dve", lambda e, a_=a_, k_=k_: e.tensor_tensor(out=a_[:], in0=a_[:], in1=k_[:], op=ALU.add), reads=[Ba, Bk], writes=[Ba])
                    S.op("act", lambda e, a_=a_, c=c: e.activation(out=dst[:, c * NW:(c + 1) * NW], in_=a_[:], func=AF.Sin), reads=[Ba], writes=[Bdst])
            sin_layer(feats, Bc, w1, 33, 0, a1, Ba1)
            sin_layer(a1, Ba1, w2, 64, 2, a2, Ba2)
            gp = [sb("gp", [128, nT, 512], BF16) for _ in range(2)]
            gm = [sb("gm", [128, nT, 512], BF16) for _ in range(2)]
            Bg = S.bufs_n(2)
            dec = Rot([sb("dec", [128, 512], F32) for _ in range(2)], S.bufs_n(2))
            fw = Rot([sb("fw", [128, 512], F32) for _ in range(2)], S.bufs_n(2))
            bw = Rot([sb("bw", [128, 512], F32) for _ in range(2)], S.bufs_n(2))
            ab = Rot([sb("ab", [128, 2, 512], BF16) for _ in range(2)], S.bufs_n(2))
            rn = Rot([sb("rn", [128, 512], F32) for _ in range(2)], S.bufs_n(2))
            tb = Rot([sb("ftb", [128, 2, nT, 128], BF16) for _ in range(3)], S.bufs_n(3))
            ho = Rot([sb("ho", [128, 2, 512], F32) for _ in range(2)], S.bufs_n(2))
            it = 0
            for o in range(2):
                for db in range(4):
                    k = it % 2
                    it += 1
                    npz, Bn = nps.next()
                    for tc in range(nT):
                        pf, Bpf = fps.next()
                        pb, Bpb = fps.next()
                        d_, Bd = dec.next()
                        f_, Bf = fw.next()
                        b_, Bb = bw.next()
                        a_, Bab = ab.next()
                        cf = o * D + db * 512
                        S.op("pe", lambda e, pf=pf, tc=tc, cf=cf: e.matmul(pf[:], lhsT=a2[:, tc * 128:(tc + 1) * 128], rhs=w3[:, cf:cf + 512],
                                                                          start=True, stop=True), reads=[Ba2, Bc], writes=[Bpf])
                        S.op("pe", lambda e, pb=pb, tc=tc, cf=cf: e.matmul(pb[:], lhsT=a2[:, tc * 128:(tc + 1) * 128], rhs=w3[:, 2 * D + cf:2 * D + cf + 512],
                                                                          start=True, stop=True), reads=[Ba2, Bc], writes=[Bpb])
                        S.op("act", lambda e, d_=d_, tc=tc, db=db: e.activation(out=d_[:], in_=drep[:, db * 512:(db + 1) * 512], func=AF.Exp,
                                                                               scale=tnn[:, tc:tc + 1]), reads=[Bc], writes=[Bd])
                        S.op("dve", lambda e, f_=f_, pf=pf, d_=d_: e.tensor_tensor(out=f_[:], in0=pf[:], in1=d_[:], op=ALU.mult), reads=[Bpf, Bd], writes=[Bf])
                        S.op("dve", lambda e, b_=b_, pb=pb, d_=d_: e.tensor_tensor(out=b_[:], in0=pb[:], in1=d_[:], op=ALU.mult), reads=[Bpb, Bd], writes=[Bb])
                        if tc == 0:
                            S.op("dve", lambda e, b_=b_: e.memset(b_[0:1, :], 0.0), reads=[Bb], writes=[Bb])
                        S.op("pool", lambda e, f_=f_, b_=b_, k=k, tc=tc: e.tensor_tensor(out=gp[k][:, tc, :], in0=f_[:], in1=b_[:], op=ALU.add),
                             reads=[Bf, Bb], writes=[Bg[k]])
                        S.op("pool", lambda e, f_=f_, b_=b_, k=k, tc=tc: e.tensor_tensor(out=gm[k][:, tc, :], in0=f_[:], in1=b_[:], op=ALU.subtract),
                             reads=[Bf, Bb], writes=[Bg[k]])
                        S.op("act", lambda e, f_=f_, a_=a_: e.activation(out=a_[:, 0, :], in_=f_[:], func=AF.Abs), reads=[Bf], writes=[Bab])
                        S.op("act", lambda e, b_=b_, a_=a_: e.activation(out=a_[:, 1, :], in_=b_[:], func=AF.Abs), reads=[Bb], writes=[Bab])
                        for h in range(2):
                            S.op("pe", lambda e, npz=npz, a_=a_, h=h, tc=tc: e.matmul(npz[:], lhsT=self.ones_bf[:], rhs=a_[:, h, :],
                                                                                     start=(tc == 0 and h == 0), stop=(tc == nT - 1 and h == 1)),
                                 reads=[Bab], writes=[Bn])
                    r_, Br = rn.next()
                    S.op("dve", lambda e, r_=r_, npz=npz: e.tensor_scalar(out=r_[:], in0=npz[:], scalar1=EPS, scalar2=None, op0=ALU.add),
                         reads=[Bn], writes=[Br])
                    S.op("dve", lambda e, r_=r_: e.reciprocal(out=r_[:], in_=r_[:]), reads=[Br], writes=[Br])
                    for fc in range(nT):
                        t_, Bt = tb.next()
                        S.op("sp", lambda e, t_=t_, fc=fc: e.dma_start(out=t_[:], in_=tabF[fc]), writes=[Bt], dma=1)
                        h_, Bh = ho.next()
                        for ri, gsrc in enumerate((gp, gm)):
                            hp, Bhp = hps.next()
                            for tc in range(nT):
                                S.op("pe", lambda e, hp=hp, t_=t_, ri=ri, tc=tc, gsrc=gsrc, k=k: e.matmul(
                                    hp[:], lhsT=t_[:, ri, tc, :], rhs=gsrc[k][:, tc, :], start=(tc == 0), stop=(tc == nT - 1)),
                                    reads=[Bt, Bg[k]], writes=[Bhp])
                            S.op("dve", lambda e, hp=hp, h_=h_, ri=ri, r_=r_: e.tensor_tensor(out=h_[:, ri, :], in0=hp[:], in1=r_[:], op=ALU.mult),
                                 reads=[Bhp, Br], writes=[Bh])
                        S.op("sp", lambda e, h_=h_, o=o, fc=fc, db=db: e.dma_start(
                            out=Hs[o, :, fc * 128:(fc + 1) * 128, db * 512:(db + 1) * 512].rearrange("r p d -> p r d"), in_=h_[:]),
                            reads=[Bh], dma=1)
            S.flush()

    def hy_conv(self, j, o):
        nc, S = self.nc, self.S
        if S.stopped:
            return
        S.DEFER = 0
        src_tm = self.vtmh if o == 0 else self.ztmh
        src_fm = self.vfm if o == 0 else self.z1fm
        gate_fm = self.x1fm if o == 0 else self.x2fm
        with ExitStack() as stk:
            def sb(n, s, d):
                return stk.enter_context(nc.sbuf_tensor(self.nm(n), s, d))
            skip = sb("skip", [128, 2, DC], F32)
            ident = sb("ident2", [128, 128], BF16)
            Bc = S.buf()
            S.op("sp", lambda e: e.dma_start(out=skip[:], in_=self.hyskip_in), writes=[Bc], dma=1)
            S.op("sp", lambda e: e.dma_start(out=ident[:], in_=self.ident_in), writes=[Bc], dma=1)
            vb = Rot([sb("cvb", [128, 16, 512], BF16) for _ in range(2)], S.bufs_n(2))
            yre = Rot([sb("yre", [128, 16, 512], BF16) for _ in range(1)], S.bufs_n(1))
            yim = Rot([sb("yim", [128, 16, 512], BF16) for _ in range(1)], S.bufs_n(1))
            tb = Rot([sb("ctb", [128, 2, 16, 128], BF16) for _ in range(3)], S.bufs_n(3))
            hh = Rot([sb("hh", [128, 2, 512], F32) for _ in range(2)], S.bufs_n(2))
            tt_ = [Rot([sb("cp%d" % i, [128, 512], F32) for _ in range(2)], S.bufs_n(2)) for i in range(4)]
            ti = Rot([sb("cti", [128, 2, 16, 512], BF16) for _ in range(2)], S.bufs_n(2))
            ui = Rot([sb("cui", [128, 512], F32) for _ in range(2)], S.bufs_n(2))
            gi_ = Rot([sb("cgi", [128, 512], F32) for _ in range(2)], S.bufs_n(2))
            zo = Rot([sb("czo", [128, 512], F32) for _ in range(2)], S.bufs_n(2))
            zb = Rot([sb("czb", [128, 512], BF16) for _ in range(2)], S.bufs_n(2))
            vt = Rot([sb("cvt", [128, 4, 128], BF16) for _ in range(2)], S.bufs_n(2))
            ups = Rot([stk.enter_context(nc.psum_tensor(self.nm("ups"), [128, 512], F32)) for _ in range(4)], S.bufs_n(4))
            yps = Rot([stk.enter_context(nc.psum_tensor(self.nm("yps"), [128, 512], F32)) for _ in range(2)], S.bufs_n(2))
            tps = Rot([stk.enter_context(nc.psum_tensor(self.nm("ctps"), [128, 4, 128], BF16)) for _ in range(2)], S.bufs_n(2))
            sfv = src_fm.rearrange("(c p) t -> p c t", p=128)
            gfv = gate_fm.rearrange("(c p) t -> p c t", p=128)
            z1v = self.z1fm.rearrange("(c p) t -> p c t", p=128)
            z2v = self.z2T.rearrange("(c p) t -> p c t", p=128)
            for (t0, L) in self.SEQS:
                nT = L // 128
                NW = min(512, L)
                Hs = self.Hs[L]
                tabF = self.tabF2048_in if L == 2048 else self.tabF256_in
                tabI = self.tabI2048_in if L == 2048 else self.tabI256_in
                for db in range(4):
                    v_, Bv = vb.next()
                    yr, Byr = yre.next()
                    yi, Byi = yim.next()
                    S.op("sp", lambda e, v_=v_, t0=t0, L=L, nT=nT, db=db: e.dma_start(
                        out=v_[:, 0:nT, :], in_=src_tm[t0:t0 + L, db * 512:(db + 1) * 512].rearrange("(c p) d -> p c d", p=128)),
                        writes=[Bv], dma=1)
                    for fc in range(nT):
                        t_, Bt = tb.next()
                        h_, Bh = hh.next()
                        S.op("sp", lambda e, t_=t_, fc=fc, nT=nT, tabF=tabF: e.dma_start(out=t_[:, :, 0:nT, :], in_=tabF[fc]), writes=[Bt], dma=1)
                        S.op("sp", lambda e, h_=h_, fc=fc, db=db, Hs=Hs: e.dma_start(
                            out=h_[:], in_=Hs[o, :, fc * 128:(fc + 1) * 128, db * 512:(db + 1) * 512].rearrange("r p d -> p r d")),
                            writes=[Bh], dma=1)
                        pu = []
                        for ri in range(2):
                            p_, Bp = ups.next()
                            for tc in range(nT):
                                S.op("pe", lambda e, p_=p_, t_=t_, ri=ri, tc=tc, v_=v_, nT=nT: e.matmul(
                                    p_[:], lhsT=t_[:, ri, tc, :], rhs=v_[:, tc, :], start=(tc == 0), stop=(tc == nT - 1)),
                                    reads=[Bt, Bv], writes=[Bp])
                            pu.append((p_, Bp))
                        (ur, Bur), (um, Bum) = pu
                        (a1_, Ba1), (a2_, Ba2), (a3_, Ba3), (a4_, Ba4) = [r.next() for r in tt_]
                        S.op("dve", lambda e, a1_=a1_, ur=ur, h_=h_: e.tensor_tensor(out=a1_[:], in0=ur[:], in1=h_[:, 0, :], op=ALU.mult), reads=[Bur, Bh], writes=[Ba1])
                        S.op("dve", lambda e, a2_=a2_, um=um, h_=h_: e.tensor_tensor(out=a2_[:], in0=um[:], in1=h_[:, 1, :], op=ALU.mult), reads=[Bum, Bh], writes=[Ba2])
                        S.op("dve", lambda e, a3_=a3_, um=um, h_=h_: e.tensor_tensor(out=a3_[:], in0=um[:], in1=h_[:, 0, :], op=ALU.mult), reads=[Bum, Bh], writes=[Ba3])
                        S.op("dve", lambda e, a4_=a4_, ur=ur, h_=h_: e.tensor_tensor(out=a4_[:], in0=ur[:], in1=h_[:, 1, :], op=ALU.mult), reads=[Bur, Bh], writes=[Ba4])
                        S.op("pool", lambda e, yr=yr, fc=fc, a1_=a1_, a2_=a2_: e.tensor_tensor(out=yr[:, fc, :], in0=a1_[:], in1=a2_[:], op=ALU.subtract),
                             reads=[Ba1, Ba2], writes=[Byr])
                        S.op("pool", lambda e, yi=yi, fc=fc, a3_=a3_, a4_=a4_: e.tensor_tensor(out=yi[:, fc, :], in0=a3_[:], in1=a4_[:], op=ALU.add),
                             reads=[Ba3, Ba4], writes=[Byi])
                    for tt in range(L // NW):
                        c_, Bci = ti.next()
                        S.op("sp", lambda e, c_=c_, tt=tt, nT=nT, NW=NW, tabI=tabI: [e.dma_start(
                            out=c_[:, r, 0:nT, 0:NW], in_=tabI[r, :, :, tt * NW:(tt + 1) * NW]) for r in range(2)],
                            writes=[Bci], dma=2)
                        for dc in range(4):
                            oc = db * 4 + dc
                            col0 = t0 + tt * NW
                            y_, By = yps.next()
                            u_, Bu = ui.next()
                            g_, Bgt = gi_.next()
                            S.op("sp", lambda e, u_=u_, oc=oc, col0=col0, NW=NW: e.dma_start(out=u_[:, 0:NW], in_=sfv[:, oc, col0:col0 + NW]), writes=[Bu], dma=1)
                            S.op("sp", lambda e, g_=g_, oc=oc, col0=col0, NW=NW: e.dma_start(out=g_[:, 0:NW], in_=gfv[:, oc, col0:col0 + NW]), writes=[Bgt], dma=1)
                            for fc in range(nT):
                                S.op("pe", lambda e, y_=y_, yr=yr, fc=fc, dc=dc, c_=c_, NW=NW: e.matmul(
                                    y_[:, 0:NW], lhsT=yr[:, fc, dc * 128:(dc + 1) * 128], rhs=c_[:, 0, fc, 0:NW], start=(fc == 0), stop=False),
                                    reads=[Byr, Bci], writes=[By])
                                S.op("pe", lambda e, y_=y_, yi=yi, fc=fc, dc=dc, c_=c_, NW=NW, nT=nT: e.matmul(
                                    y_[:, 0:NW], lhsT=yi[:, fc, dc * 128:(dc + 1) * 128], rhs=c_[:, 1, fc, 0:NW], start=False, stop=(fc == nT - 1)),
                                    reads=[Byi, Bci], writes=[By])
                            S.op("dve", lambda e, u_=u_, y_=y_, oc=oc, NW=NW: e.scalar_tensor_tensor(
                                out=u_[:, 0:NW], in0=u_[:, 0:NW], scalar=skip[:, o, oc:oc + 1], in1=y_[:, 0:NW], op0=ALU.mult, op1=ALU.add),
                                reads=[Bu, By, Bc], writes=[Bu])
                            if o == 0:
                                z_, Bz = zo.next()
                                S.op("pool", lambda e, z_=z_, u_=u_, g_=g_, NW=NW: e.tensor_tensor(out=z_[:, 0:NW], in0=u_[:, 0:NW], in1=g_[:, 0:NW], op=ALU.mult),
                                     reads=[Bu, Bgt], writes=[Bz])
                                S.op("sp", lambda e, z_=z_, oc=oc, col0=col0, NW=NW: e.dma_start(out=z1v[:, oc, col0:col0 + NW], in_=z_[:, 0:NW]), reads=[Bz], dma=1)
                                self.fm_to_tm(z_, Bz, oc, self.ztmh, zb, vt, tps, ident, Bc, c0=col0, ncols=NW) if NW == 512 else \
                                    self.fm_to_tm_small(z_, Bz, oc, zb, vt, tps, ident, Bc, col0)
                            else:
                                zb_, Bzb = zb.next()
                                S.op("pool", lambda e, zb_=zb_, u_=u_, g_=g_, NW=NW: e.tensor_tensor(out=zb_[:, 0:NW], in0=u_[:, 0:NW], in1=g_[:, 0:NW], op=ALU.mult),
                                     reads=[Bu, Bgt], writes=[Bzb])
                                S.op("sp", lambda e, zb_=zb_, oc=oc, col0=col0, NW=NW: e.dma_start(out=z2v[:, oc, col0:col0 + NW], in_=zb_[:, 0:NW]), reads=[Bzb], dma=1)
            S.flush()

    def fm_to_tm_small(self, u_, Bu, oc, ub, vt, tps, ident, Bc, c0):
        S = self.S
        b_, Bb = ub.next()
        S.op("act", lambda e: e.activation(out=b_[:, 0:256], in_=u_[:, 0:256], func=AF.Copy), reads=[Bu], writes=[Bb])
        dv = self.ztmh.rearrange("(c p) d -> p c d", p=128)
        tp, Bt = tps.next()
        v_, Bv = vt.next()
        for i in range(2):
            S.op("pe", lambda e, i=i: e.transpose(tp[:, i, :], b_[:, i * 128:(i + 1) * 128], ident[:]), reads=[Bb, Bc], writes=[Bt])
        S.op("act", lambda e: e.activation(out=v_[:, 0:2, :], in_=tp[:, 0:2, :], func=AF.Copy), reads=[Bt], writes=[Bv])
        tc0 = c0 // 128
        S.op("sp", lambda e: e.dma_start(out=dv[:, tc0:tc0 + 2, oc * 128:(oc + 1) * 128], in_=v_[:, 0:2, :]), reads=[Bv], dma=1)


def pp(v):
    v = np.asarray(v)
    n = v.shape[-1] // 128
    return np.ascontiguousarray(np.moveaxis(v.reshape(v.shape[:-1] + (n, 128)), -1, 0))


def rope_tables():
    half = HD // 2
    t = np.arange(TL)
    row = (t // 64).astype(np.float32)
    col = (t % 64).astype(np.float32)
    inv = (10000.0 ** (-np.arange(0, half, 2, dtype=np.float32) / half)).astype(np.float32)
    ang = np.zeros((128, TL), np.float32)
    ang[0:32] = inv[:, None] * row[None, :]
    ang[32:64] = inv[:, None] * row[None, :]
    ang[64:96] = inv[:, None] * col[None, :]
    ang[96:128] = inv[:, None] * col[None, :]
    tab = np.stack([np.cos(ang), np.sin(ang)], axis=1).astype(np.float32)
    P = np.zeros((128, 128), np.float32)
    for base in (0, 64):
        for m in range(32):
            P[base + m + 32, base + m] = -1.0
            P[base + m, base + m + 32] = 1.0
    return tab, P.astype(NPBF)


def window_masks():
    kj = np.arange(128)[:, None]
    qi = np.arange(128)[None, :]
    prev = (kj >= qi).astype(np.float32)
    nxt = (kj <= qi).astype(np.float32)
    m = np.stack([np.tile(prev, (1, 4)), np.tile(nxt, (1, 4))], axis=1)
    return m.astype(NPBF)


def make_in_maps(inp, kb):
    f = lambda a: np.ascontiguousarray(np.asarray(a, dtype=np.float32))
    rope, P = rope_tables()
    wm = window_masks()
    shared = {
        "normg_in": pp(np.stack([f(inp["norm_mix_g"]), f(inp["norm_ffn_g"])], 0)),
        "modb_in": pp(f(inp["mod_b"])),
        "finalg_in": pp(f(inp["final_g"])),
        "esink_in": np.ascontiguousarray(np.repeat(f(inp["win_sink"]), 128, axis=1)[:, None, :]),
        "axg_in": np.ascontiguousarray(np.stack([f(inp["ax_q_g"])[0], f(inp["ax_k_g"])[0]], axis=1)),
        "axkg_rep": np.ascontiguousarray(np.tile(f(inp["ax_k_g"])[0][None, :], (128, 1))),
        "rope_in": rope, "ropeP_in": P, "wmask_in": wm,
    }
    shared.update(hyena_shared(inp))
    for name in kb.inputs:
        if "__" in name:
            base, idx = name.split("__")
            shared[name] = np.ascontiguousarray(f(inp[base])[int(idx)])
    maps = []
    xs, xp = f(inp["x_sample"]), f(inp["x_prompt"])
    for i in range(8):
        m = dict(shared)
        xt = np.concatenate([xs[i], xp[4 * i:4 * i + 4].reshape(1024, D)], axis=0)
        m["xT_in"] = np.ascontiguousarray(xt.T)
        m["cvec"] = np.ascontiguousarray(np.stack([pp(f(inp["c"])[i]), pp(f(inp["c_ctx"]))], axis=-1))
        m["cwkT"] = np.ascontiguousarray(f(inp["cache_win_k"])[i].transpose(0, 2, 3, 1))
        m["cwv"] = np.ascontiguousarray(f(inp["cache_win_v"])[i].reshape(2, PAST, 512))
        m["cakT"] = np.ascontiguousarray(f(inp["cache_ax_k"])[i].transpose(0, 2, 3, 1))
        m["cav"] = np.ascontiguousarray(f(inp["cache_ax_v"])[i].reshape(1, PAST, 512))
        maps.append({k: v for k, v in m.items() if k in kb.inputs})
    return maps


def hyena_shared(inp):
    f = lambda a: np.ascontiguousarray(np.asarray(a, dtype=np.float32))
    out = {}
    cw = f(inp["hy_conv_w"])[0]
    cb = f(inp["hy_conv_b"])[0]
    out["hyconv_in"] = pp(np.stack([cw[0], cw[1], cw[2], cb], axis=0)).transpose(0, 2, 1).copy()
    out["ident_in"] = np.eye(128, dtype=np.float32).astype(NPBF)
    out["hyskip_in"] = pp(f(inp["hy_skip"])[0])
    out["hyw1_in"] = f(inp["hy_f_w1"])[0]
    out["hyw2_in"] = f(inp["hy_f_w2"])[0]
    out["hyw3_in"] = f(inp["hy_f_w3"])[0]
    fr = f(inp["hy_freq"])[0]
    out["hyfb_in"] = np.ascontiguousarray(np.stack([fr[0], fr[1], f(inp["hy_f_b1"])[0], f(inp["hy_f_b2"])[0]], axis=1))
    HY_MIN = math.log(1e-2) / 1.5
    HY_MAX = math.log(1e-2) / 0.3
    deltas = np.abs(np.linspace(HY_MIN, HY_MAX, D, dtype=np.float32))
    out["drep_in"] = np.ascontiguousarray(np.tile(deltas[None, :], (128, 1)).astype(np.float32))
    for L in (2048, 256):
        t = np.arange(L, dtype=np.float32)
        tn = (t / max(L - 1, 1)).astype(np.float32)
        bands = 16
        fbv = np.linspace(1e-4, bands - 1, bands, dtype=np.float32)
        w = (np.float32(2.0 * math.pi) * t / np.float32(L)).astype(np.float32)
        feats = np.concatenate([tn[:, None], np.cos(w[:, None] * fbv), -np.sin(w[:, None] * fbv)], axis=-1).astype(np.float32)
        out["feats%d_in" % L] = np.ascontiguousarray(feats.T)
        out["tnn%d_in" % L] = pp(-tn)
        nT = L // 128
        tt = np.arange(L, dtype=np.float64)
        ff = np.arange(L, dtype=np.float64)
        ang = np.pi * np.outer(tt, 2 * ff + 1) / (2 * L)
        C = np.cos(ang)
        S_ = np.sin(ang)
        CF = np.stack([C, S_], 0).reshape(2, nT, 128, nT, 128)
        out["tabF%d_in" % L] = np.ascontiguousarray(CF.transpose(3, 2, 0, 1, 4)).astype(NPBF)
        CI = np.stack([C.T, S_.T], 0) / L
        out["tabI%d_in" % L] = np.ascontiguousarray(CI.reshape(2, nT, 128, L).transpose(0, 2, 1, 3)).astype(NPBF)
    return out


_CACHE = {}


def get_kb():
    if "kb" not in _CACHE:
        kb = KB()
        kb.build()
        _CACHE["kb"] = kb
    return _CACHE["kb"]


def kernel(**inp):
    kb = get_kb()
    maps = make_in_maps(inp, kb)
    res = run_bass_kernel_spmd(kb.nc, maps, core_ids=list(range(8)))
    R = res.results
    y_s = np.stack([R[i]["yT"][:, :TL].T for i in range(8)], 0)
    y_p = np.concatenate([R[i]["yT"][:, TL:].T.reshape(4, 256, D) for i in range(8)], 0)
    outs = [np.ascontiguousarray(y_p), np.ascontiguousarray(y_s)]
    for nm_ in ("swk", "swv", "sak", "sav"):
        a = np.concatenate([R[i][nm_] for i in range(8)], 0)
        outs.append(np.ascontiguousarray(a.reshape(a.shape[0], a.shape[1], 256, NKV, HD)))
    return tuple(outs)
```

```python
import math
import os
from contextlib import ExitStack

import numpy as np
import ml_dtypes

import concourse.bass as bass
import concourse.mybir as mybir
from concourse.bass_utils import run_bass_kernel_spmd

F32 = mybir.dt.float32
BF16 = mybir.dt.bfloat16
AF = mybir.ActivationFunctionType
ALU = mybir.AluOpType
NPBF = ml_dtypes.bfloat16

D = 2048
DC = 16
T = 3072
TL = 2048
NTT = 6
TW = 512
DFF = 5632
FC = 44
NH = 16
NKV = 4
HD = 128
PAST = 512
EPS = 1e-6
DEPTH = 4
ENG_NAMES = ("pe", "act", "dve", "pool", "sp")


class Buf:
    __slots__ = ("name", "w", "r")

    def __init__(self, name=""):
        self.name = name
        self.w = None
        self.r = []


class Op:
    __slots__ = ("eng", "fn", "deps", "flag", "val", "sem", "dma")

    def __init__(self, eng, fn, dma):
        self.eng = eng
        self.fn = fn
        self.deps = []
        self.flag = False
        self.val = 0
        self.sem = None
        self.dma = dma


class Sched:
    def __init__(self, nc, stack, n_dma_sems=48):
        self.nc = nc
        self.sem = {e: stack.enter_context(nc.semaphore("s_" + e)) for e in ENG_NAMES}
        self.cnt = {e: 0 for e in ENG_NAMES}
        self.dma_sems = [stack.enter_context(nc.semaphore("s_dma%d" % i)) for i in range(n_dma_sems)]
        self.dma_val = [0] * n_dma_sems
        self.dma_last = [None] * n_dma_sems
        self.dma_rr = 0
        self.ops = {e: [] for e in ENG_NAMES}
        self.waited = {e: {} for e in ENG_NAMES}
        self.bufs = []
        self.n_ops = 0
        self.stopped = False
        self.sp_def = []
        self.DEFER = int(os.environ.get("K_DEFER", "3"))
        self.DEFER0 = self.DEFER

    def buf(self, name=""):
        b = Buf(name)
        self.bufs.append(b)
        return b

    def bufs_n(self, n, name=""):
        return [self.buf("%s%d" % (name, i)) for i in range(n)]

    def op(self, eng, fn, reads=(), writes=(), dma=0):
        o = Op(eng, fn, dma)
        deps = {}

        def add(d, war):
            if d is None or d is o:
                return
            if d.dma == 0 and d.eng == eng and dma == 0:
                if eng == "pe" or war:
                    return
            deps[id(d)] = d

        for b in reads:
            add(b.w, False)
        for b in writes:
            add(b.w, False)
            for r in b.r:
                add(r, True)
        if dma:
            k = self.dma_rr
            self.dma_rr = (k + 1) % len(self.dma_sems)
            prev = self.dma_last[k]
            if prev is not None:
                deps[id(prev)] = prev
            self.dma_val[k] += 16 * dma
            o.val = self.dma_val[k]
            o.sem = self.dma_sems[k]
            self.dma_last[k] = o
        for d in deps.values():
            d.flag = True
        o.deps = list(deps.values())
        for b in reads:
            if dma == 0:
                b.r = [r for r in b.r if not (r.dma == 0 and r.eng == eng)]
            b.r.append(o)
        for b in writes:
            b.w = o
            b.r = []
        self.n_ops += 1
        if eng == "sp":
            if reads and self.DEFER > 0:
                self.sp_def.append([o, 0])
                return o
            if self.sp_def:
                defd = set(id(x[0]) for x in self.sp_def)
                if any(id(d) in defd for d in o.deps):
                    for x in self.sp_def:
                        self.ops["sp"].append(x[0])
                    self.sp_def = []
            self.ops["sp"].append(o)
            keep = []
            for x in self.sp_def:
                x[1] += 1
                if x[1] >= self.DEFER:
                    self.ops["sp"].append(x[0])
                else:
                    keep.append(x)
            self.sp_def = keep
            return o
        self.ops[eng].append(o)
        return o

    def flush(self):
        nc = self.nc
        for x in self.sp_def:
            self.ops["sp"].append(x[0])
        self.sp_def = []
        lasts = {}
        for e in ENG_NAMES:
            for o in reversed(self.ops[e]):
                if o.dma == 0:
                    lasts[e] = o
                    o.flag = True
                    break
        for e in ENG_NAMES:
            for o in self.ops[e]:
                if o.dma == 0 and o.flag:
                    self.cnt[e] += 1
                    o.val = self.cnt[e]
                    o.sem = self.sem[e]
        dma_final = [(self.dma_sems[k], self.dma_val[k]) for k in range(len(self.dma_sems))
                     if self.dma_val[k] > 0]
        with nc.Block() as block:
            for e in ENG_NAMES:
                def body(eng, e=e):
                    waited = self.waited[e]

                    def wait(sem, val):
                        key = id(sem)
                        if waited.get(key, 0) < val:
                            eng.wait_ge(sem, val)
                            waited[key] = val

                    for o in self.ops[e]:
                        for d in o.deps:
                            wait(d.sem, d.val)
                        r = o.fn(eng)
                        if o.dma:
                            if not isinstance(r, (list, tuple)):
                                r = [r]
                            assert len(r) == o.dma, (len(r), o.dma)
                            for ins in r:
                                ins.then_inc(o.sem, 16)
                        elif o.flag:
                            r.then_inc(o.sem, 1)
                    for e2 in ENG_NAMES:
                        if e2 in lasts:
                            wait(lasts[e2].sem, lasts[e2].val)
                    for sem, val in dma_final:
                        wait(sem, val)
                name = {"pe": "tensor", "act": "scalar", "dve": "vector",
                        "pool": "gpsimd", "sp": "sync"}[e]
                getattr(block, name)(body)
        self.ops = {e: [] for e in ENG_NAMES}
        for b in self.bufs:
            b.w = None
            b.r = []
        self.bufs = []
        self.dma_last = [None] * len(self.dma_sems)
        self.nflush = getattr(self, "nflush", 0) + 1
        if self.nflush == getattr(self, "stop", -1):
            self.stopped = True


class StopBuild(Exception):
    pass


class Rot:
    def __init__(self, tiles, bufs):
        self.t = tiles
        self.b = bufs
        self.i = 0

    def next(self):
        k = self.i % len(self.t)
        self.i += 1
        return self.t[k], self.b[k]


class KB:
    def __init__(self, n_layers=DEPTH, dbg=False, stop=-1):
        self.nc = bass.Bass("TRN2", target_bir_lowering=False)
        self.stop = stop
        self.n_layers = n_layers
        self.dbg = dbg
        self.uid = 0
        self.inputs = {}

    def nm(self, s):
        self.uid += 1
        return "%s_%d" % (s, self.uid)

    def din(self, name, shape, dt=F32):
        ap = self.nc.dram_tensor(name, list(shape), dt, kind="ExternalInput").ap()
        self.inputs[name] = (tuple(shape), dt)
        return ap

    def dout(self, name, shape, dt=F32):
        return self.nc.dram_tensor(name, list(shape), dt, kind="ExternalOutput").ap()

    def dscr(self, name, shape, dt):
        return self.nc.dram_tensor(name, list(shape), dt).ap()

    IN_SPECS = {
        "xT_in": ([D, T], F32), "cvec": ([128, DC, 2], F32), "normg_in": ([128, 2, DEPTH, DC], F32),
        "modb_in": ([128, DEPTH, 96], F32), "finalg_in": ([128, DC], F32), "mod_w": ([DEPTH, D, 6 * D], F32),
        "win_wqkv": ([2, D, 3072], F32), "win_wo": ([2, D, D], F32), "esink_in": ([2, 1, NH * 128], F32),
        "ax_wqkv": ([1, D, 3072], F32), "ax_wo": ([1, D, D], F32), "axg_in": ([128, 2], F32),
        "axkg_rep": ([128, 128], F32), "ffn_w_gu": ([DEPTH, D, 2 * DFF], F32), "ffn_w_down": ([DEPTH, DFF, D], F32),
        "cwkT": ([2, NKV, 128, PAST], F32), "cwv": ([2, PAST, 512], F32), "cakT": ([1, NKV, 128, PAST], F32),
        "cav": ([1, PAST, 512], F32), "rope_in": ([128, 2, TL], F32), "ropeP_in": ([128, 128], BF16),
        "wmask_in": ([128, 2, 512], BF16),
        "hy_w_in": ([1, D, 3 * D], F32), "hy_wo": ([1, D, D], F32),
        "hyconv_in": ([128, 48, 4], F32), "ident_in": ([128, 128], BF16), "hyskip_in": ([128, 2, DC], F32),
        "hyw1_in": ([33, 64], F32), "hyw2_in": ([64, 64], F32), "hyw3_in": ([64, 4 * D], F32), "hyfb_in": ([64, 4], F32),
        "drep_in": ([128, D], F32),
        "feats2048_in": ([33, 2048], F32), "feats256_in": ([33, 256], F32),
        "tnn2048_in": ([128, 16], F32), "tnn256_in": ([128, 2], F32),
        "tabF2048_in": ([16, 128, 2, 16, 128], BF16), "tabF256_in": ([2, 128, 2, 2, 128], BF16),
        "tabI2048_in": ([2, 128, 16, 2048], BF16), "tabI256_in": ([2, 128, 2, 256], BF16),
    }

    LAYERED = ("mod_w", "win_wqkv", "win_wo", "ax_wqkv", "ax_wo", "ffn_w_gu", "ffn_w_down", "hy_w_in", "hy_wo")

    def W(self, base, idx):
        name = "%s__%d" % (base, idx)
        if name not in self.inputs:
            shape, dt = type(self).IN_SPECS[base]
            self._lay = getattr(self, "_lay", {})
            self._lay[name] = self.din(name, shape[1:], dt)
        return self._lay[name]

    def __getattr__(self, name):
        specs = type(self).IN_SPECS
        if name in specs and name not in type(self).LAYERED:
            shape, dt = specs[name]
            ap = self.din(name, shape, dt)
            setattr(self, name, ap)
            return ap
        raise AttributeError(name)

    def declare(self):
        nc = self.nc
        self.yT = self.dout("yT", [D, T])
        self.swk = self.dout("swk", [4, 2, 256, 512])
        self.swv = self.dout("swv", [4, 2, 256, 512])
        self.sak = self.dout("sak", [4, 1, 256, 512])
        self.sav = self.dout("sav", [4, 1, 256, 512])
        if self.dbg:
            self.xT = self.dout("xT_dbg", [D, T])
        else:
            self.xT = self.dscr("xT", [D, T], F32)
        self.hT = self.dscr("hT", [D, T], BF16)
        self.qT = self.dscr("qT", [D, T], BF16)
        self.kT = self.dscr("kT", [512, T], BF16)
        self.vtm = self.dscr("vtm", [T, 512], BF16)
        self.oT = self.dscr("oT", [D, T], BF16)
        self.actT = self.dscr("actT", [DFF, T], BF16)

    def persistent(self, stk):
        nc = self.nc
        self.mod = stk.enter_context(nc.sbuf_tensor("mod", [128, DEPTH, 2, 96], F32))
        self.modb = stk.enter_context(nc.sbuf_tensor("modb_sb", [128, DEPTH, 96], F32))
        self.normg = stk.enter_context(nc.sbuf_tensor("normg_sb", [128, 2, DEPTH, DC], F32))
        self.amod = stk.enter_context(nc.sbuf_tensor("amod", [128, DEPTH, 2, 2, DC], F32))
        self.finalg = stk.enter_context(nc.sbuf_tensor("finalg_sb", [128, DC], F32))
        self.ones_bf = stk.enter_context(nc.sbuf_tensor("ones_bf", [128, 128], BF16))
        self.ones_d = stk.enter_context(nc.sbuf_tensor("ones_d", [128, 128], BF16))
        self.ones_h = stk.enter_context(nc.sbuf_tensor("ones_h", [128, 128], BF16))
        self.zero_c = stk.enter_context(nc.sbuf_tensor("zero_c", [128, 1], F32))
        self.eps_c = stk.enter_context(nc.sbuf_tensor("eps_c", [128, 1], F32))

    def stage_mod(self):
        nc, S = self.nc, self.S
        if S.stopped:
            return
        with ExitStack() as stk:
            def sb(n, s, d):
                return stk.enter_context(nc.sbuf_tensor(self.nm(n), s, d))
            cv = sb("cv", [128, DC, 2], F32)
            sg = sb("sg", [128, DC, 2], F32)
            sT = sb("sT", [128, DC, 2], BF16)
            NW = 4
            wbuf = [sb("mw", [128, DC, 512], BF16) for _ in range(NW)]
            Bw = S.bufs_n(NW)
            pst = [stk.enter_context(nc.psum_tensor(self.nm("mps"), [128, 4, 2], F32)) for _ in range(4)]
            Bp = S.bufs_n(4)
            Bcv, BsT, Bmod, Bmb, Bng, Bc = S.buf(), S.buf(), S.buf(), S.buf(), S.buf(), S.buf()
            S.op("sp", lambda e: e.dma_start(out=cv[:], in_=self.cvec), writes=[Bcv], dma=1)
            S.op("sp", lambda e: e.dma_start(out=self.modb[:], in_=self.modb_in), writes=[Bmb], dma=1)
            S.op("sp", lambda e: e.dma_start(out=self.normg[:], in_=self.normg_in), writes=[Bng], dma=1)
            S.op("sp", lambda e: e.dma_start(out=self.finalg[:], in_=self.finalg_in), writes=[Bng], dma=1)
            S.op("dve", lambda e: e.memset(self.ones_bf[:], 1.0), writes=[Bc])
            S.op("dve", lambda e: e.memset(self.ones_d[:], 1.0 / D), writes=[Bc])
            S.op("dve", lambda e: e.memset(self.ones_h[:], 1.0 / HD), writes=[Bc])
            S.op("dve", lambda e: e.memset(self.zero_c[:], 0.0), writes=[Bc])
            S.op("dve", lambda e: e.memset(self.eps_c[:], EPS), writes=[Bc])
            S.op("act", lambda e: e.activation(out=sg[:], in_=cv[:], func=AF.Sigmoid), reads=[Bcv], writes=[BsT])
            S.op("dve", lambda e: e.tensor_tensor(out=sT[:], in0=sg[:], in1=cv[:], op=ALU.mult), reads=[Bcv, BsT], writes=[BsT])
            i = 0
            for l in range(self.n_layers):
                wv = self.W("mod_w", l).rearrange("(c p) n -> p c n", p=128)
                for nb in range(24):
                    s = i % NW
                    ps, bp = pst[i % 4], Bp[i % 4]
                    S.op("pool", lambda e, s=s, nb=nb, wv=wv: e.dma_start(out=wbuf[s][:], in_=wv[:, :, nb * 512:(nb + 1) * 512]),
                         writes=[Bw[s]], dma=1)
                    for oc in range(4):
                        for kc in range(DC):
                            S.op("pe", lambda e, s=s, oc=oc, kc=kc, ps=ps: e.matmul(
                                ps[:, oc, :], lhsT=wbuf[s][:, kc, oc * 128:(oc + 1) * 128], rhs=sT[:, kc, :],
                                start=(kc == 0), stop=(kc == DC - 1)), reads=[Bw[s], BsT], writes=[bp])
                    for g in range(2):
                        S.op("dve", lambda e, l=l, g=g, nb=nb, ps=ps: e.tensor_tensor(
                            out=self.mod[:, l, g, nb * 4:(nb + 1) * 4], in0=ps[:, :, g],
                            in1=self.modb[:, l, nb * 4:(nb + 1) * 4], op=ALU.add), reads=[bp, Bmb], writes=[Bmod])
                    i += 1
            for l in range(self.n_layers):
                for g in range(2):
                    for w in range(2):
                        sc = 16 + 48 * w
                        S.op("dve", lambda e, l=l, g=g, w=w, sc=sc: e.scalar_tensor_tensor(
                            out=self.amod[:, l, g, w, :], in0=self.mod[:, l, g, sc:sc + 16], scalar=1.0,
                            in1=self.normg[:, w, l, :], op0=ALU.add, op1=ALU.mult), reads=[Bmod, Bng], writes=[Bc])
            S.flush()

    def modcol(self, l, g, which, c):
        return self.mod[:, l, g, which * 16 + c: which * 16 + c + 1]

    def stage_copy_in(self):
        nc, S = self.nc, self.S
        if S.stopped:
            return
        with ExitStack() as stk:
            xs = [stk.enter_context(nc.sbuf_tensor(self.nm("cpx"), [128, DC, TW], F32)) for _ in range(3)]
            Bx = S.bufs_n(3)
            xi = self.xT_in.rearrange("(c p) t -> p c t", p=128)
            xo = self.xT.rearrange("(c p) t -> p c t", p=128)
            for tt in range(NTT):
                k = tt % 3
                S.op("sp", lambda e, k=k, tt=tt: e.dma_start(out=xs[k][:], in_=xi[:, :, tt * TW:(tt + 1) * TW]), writes=[Bx[k]], dma=1)
                S.op("sp", lambda e, k=k, tt=tt: e.dma_start(out=xo[:, :, tt * TW:(tt + 1) * TW], in_=xs[k][:]), reads=[Bx[k]], dma=1)
            S.flush()

    def stage_norm(self, l, w, final=False):
        nc, S = self.nc, self.S
        if S.stopped:
            return
        S.DEFER = S.DEFER0
        with ExitStack() as stk:
            def sb(n, s, d):
                return stk.enter_context(nc.sbuf_tensor(self.nm(n), s, d))
            NX = 2
            xin = [sb("xin", [128, DC, TW], F32) for _ in range(NX)]
            Bx = [S.bufs_n(4) for _ in range(NX)]
            sq = [sb("sq", [128, DC, TW], BF16) for _ in range(2)]
            Bsq = [S.bufs_n(4) for _ in range(2)]
            pss = [stk.enter_context(nc.psum_tensor(self.nm("nps"), [128, TW], F32)) for _ in range(2)]
            Bps = S.bufs_n(2)
            rstd = [sb("rstd", [128, TW], F32) for _ in range(2)]
            Br = S.bufs_n(2)
            tmp = Rot([sb("ntmp", [128, TW], F32) for _ in range(4)], S.bufs_n(4))
            odt = F32 if final else BF16
            hs = [sb("hs", [128, DC, TW], odt) for _ in range(2)]
            Bh = [S.bufs_n(4) for _ in range(2)]
            xv = self.xT.rearrange("(c p) t -> p c t", p=128)
            ov = (self.yT if final else self.hT).rearrange("(c p) t -> p c t", p=128)

            def load(tt):
                k = tt % NX
                for q in range(4):
                    S.op("sp", lambda e, k=k, q=q, tt=tt: e.dma_start(
                        out=xin[k][:, 4 * q:4 * q + 4, :], in_=xv[:, 4 * q:4 * q + 4, tt * TW:(tt + 1) * TW]),
                        writes=[Bx[k][q]], dma=1)
            load(0)
            for tt in range(NTT):
                if tt + 1 < NTT:
                    load(tt + 1)
                g = 0 if tt < 4 else 1
                k = tt % NX
                k2 = tt % 2
                for q in range(4):
                    S.op("act", lambda e, k=k, k2=k2, q=q: e.activation(
                        out=sq[k2][:, 4 * q:4 * q + 4, :], in_=xin[k][:, 4 * q:4 * q + 4, :], func=AF.Square),
                        reads=[Bx[k][q]], writes=[Bsq[k2][q]])
                for c in range(DC):
                    S.op("pe", lambda e, k2=k2, c=c: e.matmul(pss[k2][:], lhsT=self.ones_d[:], rhs=sq[k2][:, c, :],
                                                             start=(c == 0), stop=(c == DC - 1)),
                         reads=[Bsq[k2][c // 4]], writes=[Bps[k2]])
                S.op("act", lambda e, k2=k2: e.activation(out=rstd[k2][:], in_=pss[k2][:], func=AF.Sqrt, bias=self.eps_c[:, 0:1]),
                     reads=[Bps[k2]], writes=[Br[k2]])
                S.op("dve", lambda e, k2=k2: e.reciprocal(out=rstd[k2][:], in_=rstd[k2][:]), reads=[Br[k2]], writes=[Br[k2]])
                for c in range(DC):
                    tm, Btm = tmp.next()
                    S.op("dve", lambda e, k=k, k2=k2, c=c, tm=tm: e.tensor_tensor(
                        out=tm[:], in0=xin[k][:, c, :], in1=rstd[k2][:], op=ALU.mult),
                        reads=[Bx[k][c // 4], Br[k2]], writes=[Btm])
                    if final:
                        sc_ap, bi_ap = self.finalg[:, c:c + 1], self.zero_c[:, 0:1]
                    else:
                        sc_ap, bi_ap = self.amod[:, l, g, w, c:c + 1], self.modcol(l, g, 3 * w, c)
                    S.op("act", lambda e, k2=k2, c=c, tm=tm, sc_ap=sc_ap, bi_ap=bi_ap: e.activation(
                        out=hs[k2][:, c, :], in_=tm[:], func=AF.Identity, bias=bi_ap, scale=sc_ap),
                        reads=[Btm], writes=[Bh[k2][c // 4]])
                for q in range(4):
                    S.op("sp", lambda e, k2=k2, q=q, tt=tt: e.dma_start(
                        out=ov[:, 4 * q:4 * q + 4, tt * TW:(tt + 1) * TW], in_=hs[k2][:, 4 * q:4 * q + 4, :]),
                        reads=[Bh[k2][q]], dma=1)
            S.flush()

    def gemm(self, A_dram, KC, blocks, epilogue, stk, supertiles=None, nw=3, extra=None):
        self.S.DEFER = self.S.DEFER0
        nc, S = self.nc, self.S
        if supertiles is None:
            supertiles = [list(range(NTT))]
        maxt = max(len(s) for s in supertiles)
        bw = max(sum(n for _, _, n in segs) for segs, _ in blocks)
        A_sb = stk.enter_context(nc.sbuf_tensor(self.nm("A"), [128, KC, maxt * TW], BF16))
        BA = S.bufs_n(maxt)
        wb = [stk.enter_context(nc.sbuf_tensor(self.nm("W"), [128, KC, bw], BF16)) for _ in range(nw)]
        Bw = S.bufs_n(nw)
        npb = 8 if extra is None else 8 - extra
        banks = Rot([stk.enter_context(nc.psum_tensor(self.nm("gps"), [128, TW], F32)) for _ in range(npb)], S.bufs_n(npb))
        Av = A_dram.rearrange("(c p) t -> p c t", p=128)
        wi = 0
        for st in supertiles:
            for j, tt in enumerate(st):
                npc = 4 if KC > 16 else 2
                h = KC // npc
                S.op("sp", lambda e, j=j, tt=tt, h=h, npc=npc: [
                    e.dma_start(out=A_sb[:, i * h:(i + 1) * h, j * TW:(j + 1) * TW], in_=Av[:, i * h:(i + 1) * h, tt * TW:(tt + 1) * TW])
                    for i in range(npc)], writes=[BA[j]], dma=npc)
            for bi, (segs, groups) in enumerate(blocks):
                s = wi % nw
                wi += 1

                nsp = 4 if KC > 16 else 1
                hk = KC // nsp

                def wload(e, s=s, segs=segs, nsp=nsp, hk=hk):
                    r = []
                    off = 0
                    for W_ap, c0, n in segs:
                        Wv = W_ap.rearrange("(c p) n -> p c n", p=128)
                        for i in range(nsp):
                            r.append(e.dma_start(out=wb[s][:, i * hk:(i + 1) * hk, off:off + n], in_=Wv[:, i * hk:(i + 1) * hk, c0:c0 + n]))
                        off += n
                    return r
                S.op("pool", wload, writes=[Bw[s]], dma=len(segs) * nsp)
                for j, tt in enumerate(st):
                    for gi, grp in enumerate(groups):
                        pl = []
                        for coff in grp:
                            ps, Bp = banks.next()
                            for kc in range(KC):
                                S.op("pe", lambda e, ps=ps, s=s, kc=kc, coff=coff, j=j: e.matmul(
                                    ps[:], lhsT=wb[s][:, kc, coff:coff + 128], rhs=A_sb[:, kc, j * TW:(j + 1) * TW],
                                    start=(kc == 0), stop=(kc == KC - 1)), reads=[Bw[s], BA[j]], writes=[Bp])
                            pl.append((ps, Bp))
                        epilogue(bi, gi, tt, pl)
        return A_sb, BA, wb, Bw

    def make_resid_epilogue(self, stk, l, which, chunk_of):
        nc, S = self.nc, self.S
        xt = Rot([stk.enter_context(nc.sbuf_tensor(self.nm("xr"), [128, TW], F32)) for _ in range(6)], S.bufs_n(6))
        xv = self.xT.rearrange("(c p) t -> p c t", p=128)

        def epi(bi, gi, tt, pl):
            oc = chunk_of(bi, gi)
            g = 0 if tt < 4 else 1
            (ps, Bp), = pl
            x, Bx = xt.next()
            S.op("sp", lambda e: e.dma_start(out=x[:], in_=xv[:, oc, tt * TW:(tt + 1) * TW]), writes=[Bx], dma=1)
            S.op("dve", lambda e: e.scalar_tensor_tensor(out=x[:], in0=ps[:], scalar=self.modcol(l, g, which, oc), in1=x[:],
                                                         op0=ALU.mult, op1=ALU.add), reads=[Bp, Bx], writes=[Bx])
            S.op("sp", lambda e: e.dma_start(out=xv[:, oc, tt * TW:(tt + 1) * TW], in_=x[:]), reads=[Bx], dma=1)
        return epi

    def stage_ffn(self, l):
        nc, S = self.nc, self.S
        self.stage_norm(l, 1)
        Wgu = self.W("ffn_w_gu", l)
        if S.stopped:
            return
        with ExitStack() as stk:
            sg = Rot([stk.enter_context(nc.sbuf_tensor(self.nm("sg"), [128, TW], F32)) for _ in range(3)], S.bufs_n(3))
            ao = Rot([stk.enter_context(nc.sbuf_tensor(self.nm("ao"), [128, TW], BF16)) for _ in range(4)], S.bufs_n(4))
            av = self.actT.rearrange("(c p) t -> p c t", p=128)
            blocks = []
            for jb in range(FC // 2):
                blocks.append(([(Wgu, jb * 256, 256), (Wgu, DFF + jb * 256, 256)], [(0, 256), (128, 384)]))

            def epi(bi, gi, tt, pl):
                j = bi * 2 + gi
                (pg, Bg), (pu, Bu) = pl
                s, Bs = sg.next()
                a, Ba = ao.next()
                S.op("act", lambda e: e.activation(out=s[:], in_=pg[:], func=AF.Sigmoid), reads=[Bg], writes=[Bs])
                S.op("dve", lambda e: e.tensor_tensor(out=s[:], in0=s[:], in1=pg[:], op=ALU.mult), reads=[Bs, Bg], writes=[Bs])
                S.op("dve", lambda e: e.tensor_tensor(out=a[:], in0=s[:], in1=pu[:], op=ALU.mult), reads=[Bs, Bu], writes=[Ba])
                S.op("sp", lambda e: e.dma_start(out=av[:, j, tt * TW:(tt + 1) * TW], in_=a[:]), reads=[Ba], dma=1)
            self.gemm(self.hT, DC, blocks, epi, stk)
            S.flush()
        Wd = self.W("ffn_w_down", l)
        if S.stopped:
            return
        with ExitStack() as stk:
            blocks = [([(Wd, b * 256, 256)], [(0,), (128,)]) for b in range(8)]
            epi = self.make_resid_epilogue(stk, l, 5, lambda bi, gi: bi * 2 + gi)
            self.gemm(self.actT, FC, blocks, epi, stk, supertiles=[[0, 1], [2, 3], [4, 5]], nw=3)
            S.flush()

    def stage_attn(self, l, kind, j):
        nc, S = self.nc, self.S
        self.stage_norm(l, 0)
        win = (kind == 0)
        Wqkv = self.W("win_wqkv" if win else "ax_wqkv", j)
        Wo = self.W("win_wo" if win else "ax_wo", j)
        so_k = (self.swk if win else self.sak)
        so_v = (self.swv if win else self.sav)
        if S.stopped:
            return
        with ExitStack() as stk:
            def sb(n, s, d):
                return stk.enter_context(nc.sbuf_tensor(self.nm(n), s, d))
            rope = sb("rope", [128, 2, TL], F32)
            ropeP = sb("ropeP", [128, 128], BF16)
            axg = sb("axg", [128, 2], F32)
            kgrep = sb("kgrep", [128, 128], F32)
            Bc = S.buf()
            S.op("sp", lambda e: e.dma_start(out=rope[:], in_=self.rope_in), writes=[Bc], dma=1)
            S.op("sp", lambda e: e.dma_start(out=ropeP[:], in_=self.ropeP_in), writes=[Bc], dma=1)
            S.op("sp", lambda e: e.dma_start(out=axg[:], in_=self.axg_in), writes=[Bc], dma=1)
            S.op("sp", lambda e: e.dma_start(out=kgrep[:], in_=self.axkg_rep), writes=[Bc], dma=1)
            qn = Rot([sb("qn", [128, TW], BF16) for _ in range(3)], S.bufs_n(3))
            sqh = Rot([sb("sqh", [128, TW], BF16) for _ in range(2)], S.bufs_n(2))
            rs = Rot([sb("rs", [128, TW], F32) for _ in range(2)], S.bufs_n(2))
            t1 = Rot([sb("t1", [128, TW], F32) for _ in range(2)], S.bufs_n(2))
            t2 = Rot([sb("t2", [128, TW], F32) for _ in range(2)], S.bufs_n(2))
            qo = Rot([sb("qo", [128, TW], BF16) for _ in range(3)], S.bufs_n(3))
            xps = Rot([stk.enter_context(nc.psum_tensor(self.nm("xps"), [128, TW], F32)) for _ in range(3)], S.bufs_n(3))
            qv = self.qT.rearrange("(c p) t -> p c t", p=128)
            kv = self.kT.rearrange("(c p) t -> p c t", p=128)
            blocks = [([(Wqkv, b * 512, 512)], [(0,), (128,), (256,), (384,)]) for b in range(5)]

            def epi(bi, gi, tt, pl):
                oc = bi * 4 + gi
                (ps, Bp), = pl
                isk = oc >= NH
                dst = (kv[:, oc - NH, tt * TW:(tt + 1) * TW] if isk else qv[:, oc, tt * TW:(tt + 1) * TW])
                lat = tt < 4
                q_, Bq = qn.next()
                if win:
                    S.op("act", lambda e: e.activation(out=q_[:], in_=ps[:], func=AF.Copy), reads=[Bp], writes=[Bq])
                else:
                    s_, Bs = sqh.next()
                    r_, Br = rs.next()
                    m_, Bm = xps.next()
                    S.op("act", lambda e: e.activation(out=s_[:], in_=ps[:], func=AF.Square), reads=[Bp], writes=[Bs])
                    S.op("pe", lambda e: e.matmul(m_[:], lhsT=self.ones_h[:], rhs=s_[:], start=True, stop=True), reads=[Bs], writes=[Bm])
                    S.op("act", lambda e: e.activation(out=r_[:], in_=m_[:], func=AF.Sqrt, bias=self.eps_c[:, 0:1]),
                         reads=[Bm], writes=[Br])
                    S.op("dve", lambda e: e.reciprocal(out=r_[:], in_=r_[:]), reads=[Br], writes=[Br])
                    gcol = axg[:, 1:2] if isk else axg[:, 0:1]
                    S.op("dve", lambda e: e.scalar_tensor_tensor(out=q_[:], in0=ps[:], scalar=gcol, in1=r_[:], op0=ALU.mult, op1=ALU.mult),
                         reads=[Bp, Br, Bc], writes=[Bq])
                if not lat or os.environ.get('KDBG_NOROPE'):
                    S.op("sp", lambda e: e.dma_start(out=dst, in_=q_[:]), reads=[Bq], dma=1)
                    return
                m_, Bm = xps.next()
                a_, Ba = t1.next()
                b_, Bb = t2.next()
                o_, Bo = qo.next()
                S.op("pe", lambda e: e.matmul(m_[:], lhsT=ropeP[:], rhs=q_[:], start=True, stop=True), reads=[Bq, Bc], writes=[Bm])
                S.op("dve", lambda e: e.tensor_tensor(out=a_[:], in0=q_[:], in1=rope[:, 0, tt * TW:(tt + 1) * TW], op=ALU.mult),
                     reads=[Bq, Bc], writes=[Ba])
                S.op("dve", lambda e: e.tensor_tensor(out=b_[:], in0=m_[:], in1=rope[:, 1, tt * TW:(tt + 1) * TW], op=ALU.mult),
                     reads=[Bm, Bc], writes=[Bb])
                S.op("pool", lambda e: e.tensor_tensor(out=o_[:], in0=a_[:], in1=b_[:], op=ALU.add), reads=[Ba, Bb], writes=[Bo])
                S.op("sp", lambda e: e.dma_start(out=dst, in_=o_[:]), reads=[Bo], dma=1)
            A_sb, BA, wkv, Bwkv = self.gemm(self.hT, DC, blocks, epi, stk, nw=2, extra=3)
            for i_, c0 in enumerate((2048, 2560)):
                S.op("pool", lambda e, i_=i_, c0=c0: e.dma_start(
                    out=wkv[i_][:], in_=Wqkv.rearrange("(c p) n -> p c n", p=128)[:, :, c0:c0 + 512]), writes=[Bwkv[i_]], dma=1)
            vb = Rot([sb("vb", [128, 512], BF16) for _ in range(3)], S.bufs_n(3))
            vf = Rot([sb("vf", [128, 512], F32) for _ in range(3)], S.bufs_n(3))
            ssq = Rot([sb("ssq", [128, 4], F32) for _ in range(2)], S.bufs_n(2))
            junk = Rot([sb("junk", [128, 128], F32) for _ in range(2)], S.bufs_n(2))
            for tc in range(0 if not os.environ.get('KDBG_NOVK') else 999, int(os.environ.get('KDBG_VKMAX', T // 128))):
                ctx = tc >= TL // 128
                for which in ((1, 0) if ctx else (1,)):
                    ps, Bp = xps.next()
                    for kc in range(DC):
                        S.op("pe", lambda e, ps=ps, kc=kc, tc=tc, which=which: e.matmul(
                            ps[:], lhsT=A_sb[:, kc, tc * 128:(tc + 1) * 128], rhs=wkv[which][:, kc, :],
                            start=(kc == 0), stop=(kc == DC - 1)), reads=[BA[tc // 4], Bwkv[which]], writes=[Bp])
                    if which == 1:
                        v_, Bv = vb.next()
                        S.op("act", lambda e, v_=v_, ps=ps: e.activation(out=v_[:], in_=ps[:], func=AF.Copy), reads=[Bp], writes=[Bv])
                        S.op("sp", lambda e, v_=v_, tc=tc: e.dma_start(out=self.vtm[tc * 128:(tc + 1) * 128, :], in_=v_[:]), reads=[Bv], dma=1)
                    if ctx:
                        sq_, half = divmod(tc - TL // 128, 2)
                        f_, Bf = vf.next()
                        if which == 1 or win:
                            S.op("act", lambda e, f_=f_, ps=ps: e.activation(out=f_[:], in_=ps[:], func=AF.Copy), reads=[Bp], writes=[Bf])
                        else:
                            s_, Bs = ssq.next()
                            jk, Bj = junk.next()
                            for h in range(NKV):
                                S.op("act", lambda e, h=h, jk=jk, ps=ps, s_=s_: e.activation(
                                    out=jk[:], in_=ps[:, h * 128:(h + 1) * 128], func=AF.Square, accum_out=s_[:, h:h + 1]),
                                    reads=[Bp], writes=[Bj, Bs])
                            S.op("dve", lambda e, s_=s_: e.tensor_scalar(out=s_[:], in0=s_[:], scalar1=1.0 / HD, scalar2=EPS,
                                                                        op0=ALU.mult, op1=ALU.add), reads=[Bs], writes=[Bs])
                            S.op("act", lambda e, s_=s_: e.activation(out=s_[:], in_=s_[:], func=AF.Sqrt), reads=[Bs], writes=[Bs])
                            S.op("dve", lambda e, s_=s_: e.reciprocal(out=s_[:], in_=s_[:]), reads=[Bs], writes=[Bs])
                            for h in range(NKV):
                                S.op("dve", lambda e, h=h, f_=f_, ps=ps, s_=s_: e.scalar_tensor_tensor(
                                    out=f_[:, h * 128:(h + 1) * 128], in0=ps[:, h * 128:(h + 1) * 128], scalar=s_[:, h:h + 1],
                                    in1=kgrep[:], op0=ALU.mult, op1=ALU.mult), reads=[Bp, Bs, Bc], writes=[Bf])
                        dst = (so_v if which == 1 else so_k)[sq_, j, half * 128:(half + 1) * 128, :]
                        if not os.environ.get('KDBG_NOSTATE'):
                            S.op("sp", lambda e, f_=f_, dst=dst: e.dma_start(out=dst, in_=f_[:]), reads=[Bf], dma=1)
            S.flush()
        self.attn_core(l, kind, j)
        if S.stopped:
            return
        with ExitStack() as stk:
            blocks = [([(Wo, b * 512, 512)], [(0,), (128,), (256,), (384,)]) for b in range(4)]
            epi = self.make_resid_epilogue(stk, l, 2, lambda bi, gi: bi * 4 + gi)
            self.gemm(self.oT, DC, blocks, epi, stk)
            S.flush()

    def attn_core(self, l, kind, j):
        nc, S = self.nc, self.S
        win = (kind == 0)
        ckT = (self.cwkT if win else self.cakT)[j]
        cv_ = (self.cwv if win else self.cav)[j]
        scale = HD ** -0.5
        if S.stopped:
            return
        S.DEFER = S.DEFER0
        with ExitStack() as stk:
            def sb(n, s, d):
                return stk.enter_context(nc.sbuf_tensor(self.nm(n), s, d))
            NKC = T // 128
            k_sb = sb("k_sb", [128, NKV, T + PAST], BF16)
            v_sb = sb("v_sb", [128, NKC + 4, 512], BF16)
            Bk, Bv, Bc = S.buf(), S.buf(), S.buf()
            S.op("sp", lambda e: e.dma_start(out=k_sb[:, :, 0:T], in_=self.kT.rearrange("(g p) t -> p g t", p=128)), writes=[Bk], dma=1)
            S.op("pool", lambda e: e.dma_start(out=k_sb[:, :, T:T + PAST], in_=ckT.rearrange("g p t -> p g t")), writes=[Bk], dma=1)
            S.op("sp", lambda e: e.dma_start(out=v_sb[:, 0:NKC, :], in_=self.vtm.rearrange("(c p) n -> p c n", p=128)), writes=[Bv], dma=1)
            S.op("pool", lambda e: e.dma_start(out=v_sb[:, NKC:NKC + 4, :], in_=cv_.rearrange("(c p) n -> p c n", p=128)), writes=[Bv], dma=1)
            wmask = sb("wmask", [128, 2, 512], BF16)
            S.op("sp", lambda e: e.dma_start(out=wmask[:], in_=self.wmask_in), writes=[Bc], dma=1)
            esk = None
            if win:
                esr = sb("esr", [1, NH * 128], F32)
                esk = sb("esk", [1, NH * 128], BF16)
                ones1 = sb("ones1", [1, 128], BF16)
                S.op("sp", lambda e: e.dma_start(out=esr[:], in_=self.esink_in[j]), writes=[Bc], dma=1)
                S.op("act", lambda e: e.activation(out=esr[:], in_=esr[:], func=AF.Exp), reads=[Bc], writes=[Bc])
                esl = sb("esl", [1, NH * 128], BF16)
                esf = sb("esf", [1, NH * 128], F32)
                S.op("act", lambda e: e.activation(out=esk[:], in_=esr[:], func=AF.Copy), reads=[Bc], writes=[Bc])
                S.op("act", lambda e: e.activation(out=esf[:], in_=esk[:], func=AF.Copy), reads=[Bc], writes=[Bc])
                S.op("dve", lambda e: e.tensor_tensor(out=esf[:], in0=esr[:], in1=esf[:], op=ALU.subtract), reads=[Bc], writes=[Bc])
                S.op("act", lambda e: e.activation(out=esl[:], in_=esf[:], func=AF.Copy), reads=[Bc], writes=[Bc])
                S.op("dve", lambda e: e.memset(ones1[:], 1.0), writes=[Bc])
            q_sb = [sb("q_sb", [128, 4, T], BF16) for _ in range(2)]
            Bq = S.bufs_n(2)
            o_sb = [sb("o_sb", [128, 4, T], BF16) for _ in range(2)]
            Bo = S.bufs_n(2)
            pT = Rot([sb("pT", [128, 512], BF16) for _ in range(4)], S.bufs_n(4))
            rc = Rot([sb("rc", [128, 512], F32) for _ in range(2)], S.bufs_n(2))
            sps = Rot([stk.enter_context(nc.psum_tensor(self.nm("sps"), [128, 512], F32)) for _ in range(3)], S.bufs_n(3))
            dps = Rot([stk.enter_context(nc.psum_tensor(self.nm("dps"), [128, 512], F32)) for _ in range(2)], S.bufs_n(2))
            ops_ = Rot([stk.enter_context(nc.psum_tensor(self.nm("ops"), [128, 512], F32)) for _ in range(2)], S.bufs_n(2))
            seqs = [(0, TL, True)] + [(TL + 256 * s, 256, False) for s in range(4)]
            qv = self.qT.rearrange("(h p) t -> p h t", p=128)
            ov = self.oT.rearrange("(h p) t -> p h t", p=128)
            tasks = []
            for g in range(NKV):
                for (t0, ln, has_cache) in seqs:
                    nqb = ln // 128
                    for qb in range(nqb):
                        chunks = []
                        if win and has_cache:
                            for d_, mk in ((-1, 0), (0, None), (1, 1)):
                                kb = qb + d_
                                if 0 <= kb < nqb:
                                    chunks.append((t0 + kb * 128, (t0 + kb * 128) // 128, mk))
                        else:
                            for kb in range(nqb):
                                chunks.append((t0 + kb * 128, (t0 + kb * 128) // 128, None))
                        if has_cache:
                            for c in range(4):
                                chunks.append((T + c * 128, NKC + c, None))
                        tasks.append((g, t0 + qb * 128, chunks))
            flat = [(ti, ci) for ti, tk in enumerate(tasks) for ci in range(len(tk[2]))]
            LOOK = 2
            loaded = set()
            sbank = {}
            tstate = {}

            def load_q(g):
                if g < NKV and g not in loaded:
                    loaded.add(g)
                    S.op("sp", lambda e, g=g: e.dma_start(out=q_sb[g % 2][:], in_=qv[:, 4 * g:4 * g + 4, :]), writes=[Bq[g % 2]], dma=1)

            def emit_s(idx):
                ti, ci = flat[idx]
                g, q0, chunks = tasks[ti]
                load_q(g)
                kc0 = chunks[ci][0]
                sp_, Bs = sps.next()
                sbank[idx] = (sp_, Bs)
                qs, Bqs = q_sb[g % 2], Bq[g % 2]
                S.op("pe", lambda e, sp_=sp_, kc0=kc0, g=g, qs=qs, q0=q0: e.matmul(
                    sp_[:].rearrange("p (h q) -> p h q", h=4), lhsT=k_sb[:, g, kc0:kc0 + 128], rhs=qs[:, :, q0:q0 + 128],
                    start=True, stop=True), reads=[Bk, Bqs], writes=[Bs])

            for idx in range(min(LOOK, len(flat))):
                emit_s(idx)
            for idx in range(len(flat)):
                if idx + LOOK < len(flat):
                    emit_s(idx + LOOK)
                ti, ci = flat[idx]
                g, q0, chunks = tasks[ti]
                n = len(chunks)
                kc0, vc, mk = chunks[ci]
                os_, Bos = o_sb[g % 2], Bo[g % 2]
                if ci == 0:
                    tstate[ti] = (dps.next(), ops_.next())
                (dp, Bd), (op_, Bop) = tstate[ti]
                sp_, Bs = sbank.pop(idx)
                p_, Bp = pT.next()
                S.op("act", lambda e, sp_=sp_, p_=p_: e.activation(out=p_[:], in_=sp_[:], func=AF.Exp, scale=scale),
                     reads=[Bs], writes=[Bp])
                if mk is not None:
                    S.op("pool", lambda e, p_=p_, mk=mk: e.tensor_tensor(out=p_[:], in0=p_[:], in1=wmask[:, mk, :], op=ALU.mult),
                         reads=[Bp, Bc], writes=[Bp])
                last = (ci == n - 1) and not win
                S.op("pe", lambda e, dp=dp, p_=p_, ci=ci, last=last: e.matmul(
                    dp[:], lhsT=self.ones_bf[:], rhs=p_[:], start=(ci == 0), stop=last), reads=[Bp], writes=[Bd])
                S.op("pe", lambda e, op_=op_, p_=p_, ci=ci, vc=vc, g=g, n=n: e.matmul(
                    op_[:], lhsT=v_sb[:, vc, g * 128:(g + 1) * 128], rhs=p_[:], start=(ci == 0), stop=(ci == n - 1)),
                    reads=[Bp, Bv], writes=[Bop])
                if ci == n - 1:
                    if win:
                        S.op("pe", lambda e, dp=dp, g=g: e.matmul(dp[:], lhsT=ones1[:], rhs=esk[:, g * 512:(g + 1) * 512],
                                                                  start=False, stop=False), reads=[Bc], writes=[Bd])
                        S.op("pe", lambda e, dp=dp, g=g: e.matmul(dp[:], lhsT=ones1[:], rhs=esl[:, g * 512:(g + 1) * 512],
                                                                  start=False, stop=True), reads=[Bc], writes=[Bd])
                    r_, Br = rc.next()
                    S.op("dve", lambda e, r_=r_, dp=dp: e.reciprocal(out=r_[:], in_=dp[:]), reads=[Bd], writes=[Br])
                    S.op("dve", lambda e, r_=r_, op_=op_, os_=os_, q0=q0: e.tensor_tensor(
                        out=os_[:, :, q0:q0 + 128], in0=op_[:].rearrange("p (h q) -> p h q", h=4),
                        in1=r_[:].rearrange("p (h q) -> p h q", h=4), op=ALU.mult), reads=[Bop, Br], writes=[Bos])
                    del tstate[ti]
                    if ti + 1 == len(tasks) or tasks[ti + 1][0] != g:
                        S.op("sp", lambda e, g=g, os_=os_: e.dma_start(out=ov[:, 4 * g:4 * g + 4, :], in_=os_[:]), reads=[Bos], dma=1)
            S.flush()

    def build(self):
        nc = self.nc
        self.declare()
        with ExitStack() as stk:
            self.S = Sched(nc, stk)
            self.S.stop = self.stop
            self.persistent(stk)
            try:
                self.stage_copy_in()
                self.stage_mod()
                for l in range(self.n_layers):
                    kind, j = l % 3, l // 3
                    if kind == 1:
                        self.stage_hyena(l, j)
                    else:
                        self.stage_attn(l, kind, j)
                    self.stage_ffn(l)
                self.stage_norm(0, 0, final=True)
            except StopBuild:
                pass
        return nc


    def hy_decl(self):
        if hasattr(self, "yin"):
            return
        self.yin = self.dscr("yin", [3 * D, T], F32)
        self.vfm = self.dscr("vfm", [D, T], F32)
        self.x1fm = self.dscr("x1fm", [D, T], F32)
        self.x2fm = self.dscr("x2fm", [D, T], F32)
        self.z1fm = self.dscr("z1fm", [D, T], F32)
        self.vtmh = self.dscr("vtmh", [T, D], BF16)
        self.ztmh = self.dscr("ztmh", [T, D], BF16)
        self.z2T = self.dscr("z2T", [D, T], BF16)
        self.Hs = {2048: self.dscr("Hs2048", [2, 2, 2048, D], F32), 256: self.dscr("Hs256", [2, 2, 256, D], F32)}

    def stage_hyena(self, l, j):
        nc, S = self.nc, self.S
        self.hy_decl()
        self.stage_norm(l, 0)
        if S.stopped:
            return
        Win = self.W("hy_w_in", j)
        with ExitStack() as stk:
            yo = Rot([stk.enter_context(nc.sbuf_tensor(self.nm("yo"), [128, TW], F32)) for _ in range(4)], S.bufs_n(4))
            yv = self.yin.rearrange("(c p) t -> p c t", p=128)
            blocks = [([(Win, b * 512, 512)], [(0,), (128,), (256,), (384,)]) for b in range(12)]
            cnt = [0]

            def epi(bi, gi, tt, pl):
                oc = bi * 4 + gi
                (ps, Bp), = pl
                y_, By = yo.next()
                cnt[0] += 1
                S.op("act", lambda e: e.activation(out=y_[:], in_=ps[:], func=AF.Copy), reads=[Bp], writes=[By])
                S.op("sp", lambda e: e.dma_start(out=yv[:, oc, tt * TW:(tt + 1) * TW], in_=y_[:]), reads=[By], dma=1)
            self.gemm(self.hT, DC, blocks, epi, stk)
            S.flush()
        self.hy_shortconv(j)
        self.hy_filter(j, 2048)
        self.hy_filter(j, 256)
        self.hy_conv(j, 0)
        self.hy_conv(j, 1)
        if S.stopped:
            return
        Wo = self.W("hy_wo", j)
        with ExitStack() as stk:
            blocks = [([(Wo, b * 512, 512)], [(0,), (128,), (256,), (384,)]) for b in range(4)]
            epi = self.make_resid_epilogue(stk, l, 2, lambda bi, gi: bi * 4 + gi)
            self.gemm(self.z2T, DC, blocks, epi, stk)
            S.flush()

    SEQS = [(0, TL)] + [(TL + 256 * s_, 256) for s_ in range(4)]

    def hy_shortconv(self, j):
        nc, S = self.nc, self.S
        if S.stopped:
            return
        S.DEFER = 0
        with ExitStack() as stk:
            def sb(n, s, d):
                return stk.enter_context(nc.sbuf_tensor(self.nm(n), s, d))
            cw = sb("cw", [128, 48, 4], F32)
            ident = sb("ident", [128, 128], BF16)
            Bc = S.buf()
            S.op("sp", lambda e: e.dma_start(out=cw[:], in_=self.hyconv_in), writes=[Bc], dma=1)
            S.op("sp", lambda e: e.dma_start(out=ident[:], in_=self.ident_in), writes=[Bc], dma=1)
            yi = Rot([sb("yi", [128, T], F32) for _ in range(2)], S.bufs_n(2))
            uo = Rot([sb("uo", [128, T], F32) for _ in range(2)], S.bufs_n(2))
            ub = Rot([sb("ub", [128, T], BF16) for _ in range(2)], S.bufs_n(2))
            vt = Rot([sb("vt", [128, 4, 128], BF16) for _ in range(3)], S.bufs_n(3))
            tps = Rot([stk.enter_context(nc.psum_tensor(self.nm("tps"), [128, 4, 128], BF16)) for _ in range(3)], S.bufs_n(3))
            yv = self.yin.rearrange("(c p) t -> p c t", p=128)
            dsts = [self.vfm.rearrange("(c p) t -> p c t", p=128), self.x1fm.rearrange("(c p) t -> p c t", p=128),
                    self.x2fm.rearrange("(c p) t -> p c t", p=128)]
            ynext = None
            for oc in range(48):
                if ynext is None:
                    y_, By = yi.next()
                    S.op("sp", lambda e, y_=y_, oc=oc: e.dma_start(out=y_[:], in_=yv[:, oc, :]), writes=[By], dma=1)
                else:
                    y_, By = ynext
                u_, Bu = uo.next()
                S.op("act", lambda e, y_=y_, u_=u_, oc=oc: e.activation(out=u_[:], in_=y_[:], func=AF.Identity,
                                                                       bias=cw[:, oc, 3:4], scale=cw[:, oc, 1:2]), reads=[By, Bc], writes=[Bu])
                for (a, ln) in self.SEQS:
                    b = a + ln
                    S.op("dve", lambda e, y_=y_, u_=u_, oc=oc, a=a, b=b: e.scalar_tensor_tensor(
                        out=u_[:, a + 1:b], in0=y_[:, a:b - 1], scalar=cw[:, oc, 0:1], in1=u_[:, a + 1:b], op0=ALU.mult, op1=ALU.add),
                        reads=[By, Bu, Bc], writes=[Bu])
                    S.op("dve", lambda e, y_=y_, u_=u_, oc=oc, a=a, b=b: e.scalar_tensor_tensor(
                        out=u_[:, a:b - 1], in0=y_[:, a + 1:b], scalar=cw[:, oc, 2:3], in1=u_[:, a:b - 1], op0=ALU.mult, op1=ALU.add),
                        reads=[By, Bu, Bc], writes=[Bu])
                if oc + 1 < 48:
                    ynext = yi.next()
                    S.op("sp", lambda e, y2=ynext[0], oc=oc: e.dma_start(out=y2[:], in_=yv[:, oc + 1, :]), writes=[ynext[1]], dma=1)
                S.op("sp", lambda e, u_=u_, oc=oc: e.dma_start(out=dsts[oc // 16][:, oc % 16, :], in_=u_[:]), reads=[Bu], dma=1)
                if oc < 16:
                    self.fm_to_tm(u_, Bu, oc, self.vtmh, ub, vt, tps, ident, Bc)
            S.flush()

    def fm_to_tm(self, u_, Bu, oc, dst_tm, ub, vt, tps, ident, Bc, c0=0, ncols=T):
        S = self.S
        b_, Bb = ub.next()
        S.op("act", lambda e: e.activation(out=b_[:, 0:ncols], in_=u_[:, 0:ncols], func=AF.Copy), reads=[Bu], writes=[Bb])
        dv = dst_tm.rearrange("(c p) d -> p c d", p=128)
        for q in range(ncols // 512):
            tp, Bt = tps.next()
            v_, Bv = vt.next()
            for i in range(4):
                S.op("pe", lambda e, tp=tp, i=i, q=q: e.transpose(tp[:, i, :], b_[:, q * 512 + i * 128: q * 512 + (i + 1) * 128], ident[:]),
                     reads=[Bb, Bc], writes=[Bt])
            S.op("act", lambda e, tp=tp, v_=v_: e.activation(out=v_[:], in_=tp[:], func=AF.Copy), reads=[Bt], writes=[Bv])
            tc0 = (c0 + q * 512) // 128
            S.op("sp", lambda e, v_=v_, tc0=tc0: e.dma_start(out=dv[:, tc0:tc0 + 4, oc * 128:(oc + 1) * 128], in_=v_[:]), reads=[Bv], dma=1)

    def hy_filter(self, j, L):
        nc, S = self.nc, self.S
        if S.stopped:
            return
        S.DEFER = 0
        nT = L // 128
        NW = min(512, L)
        TWO_PI = 2.0 * math.pi
        Hs = self.Hs[L]
        feats_in = self.feats2048_in if L == 2048 else self.feats256_in
        tnn_in = self.tnn2048_in if L == 2048 else self.tnn256_in
        tabF = self.tabF2048_in if L == 2048 else self.tabF256_in
        with ExitStack() as stk:
            def sb(n, s, d):
                return stk.enter_context(nc.sbuf_tensor(self.nm(n), s, d))
            feats = sb("feats", [33, L], F32)
            w1 = sb("fw1", [33, 64], F32)
            w2 = sb("fw2", [64, 64], F32)
            w3 = sb("fw3", [64, 4 * D], F32)
            fb = sb("ffb", [64, 4], F32)
            fraw = sb("fraw", [64, 4], F32)
            tnn = sb("tnn", [128, nT], F32)
            drep = sb("drep", [128, D], F32)
            a1 = sb("a1", [64, L], F32)
            a2 = sb("a2", [64, L], F32)
            Bc, Ba1, Ba2 = S.buf(), S.buf(), S.buf()
            S.op("sp", lambda e: e.dma_start(out=feats[:], in_=feats_in), writes=[Bc], dma=1)
            S.op("sp", lambda e: e.dma_start(out=w1[:], in_=self.hyw1_in), writes=[Bc], dma=1)
            S.op("sp", lambda e: e.dma_start(out=w2[:], in_=self.hyw2_in), writes=[Bc], dma=1)
            S.op("sp", lambda e: e.dma_start(out=w3[:], in_=self.hyw3_in), writes=[Bc], dma=1)
            S.op("sp", lambda e: e.dma_start(out=fraw[:], in_=self.hyfb_in), writes=[Bc], dma=1)
            S.op("sp", lambda e: e.dma_start(out=tnn[:], in_=tnn_in), writes=[Bc], dma=1)
            S.op("sp", lambda e: e.dma_start(out=drep[:], in_=self.drep_in), writes=[Bc], dma=1)
            S.op("act", lambda e: e.activation(out=fb[:, 0:1], in_=fraw[:, 0:1], func=AF.Copy), reads=[Bc], writes=[Bc])
            S.op("act", lambda e: e.activation(out=fb[:, 2:3], in_=fraw[:, 1:2], func=AF.Copy), reads=[Bc], writes=[Bc])
            S.op("dve", lambda e: e.tensor_tensor(out=fb[:, 1:2], in0=fraw[:, 2:3], in1=fraw[:, 0:1], op=ALU.mult), reads=[Bc], writes=[Bc])
            S.op("dve", lambda e: e.tensor_tensor(out=fb[:, 3:4], in0=fraw[:, 3:4], in1=fraw[:, 1:2], op=ALU.mult), reads=[Bc], writes=[Bc])
            fps = Rot([stk.enter_context(nc.psum_tensor(self.nm("fps"), [128, 512], F32)) for _ in range(4)], S.bufs_n(4))
            nps = Rot([stk.enter_context(nc.psum_tensor(self.nm("fnps"), [128, 512], F32)) for _ in range(1)], S.bufs_n(1))
            hps = Rot([stk.enter_context(nc.psum_tensor(self.nm("hps"), [128, 512], F32)) for _ in range(3)], S.bufs_n(3))
            arg = Rot([sb("arg", [64, NW], F32) for _ in range(2)], S.bufs_n(2))
            kf = Rot([sb("kf", [64, NW], F32) for _ in range(2)], S.bufs_n(2))
            ki = Rot([sb("ki", [64, NW], mybir.dt.int32) for _ in range(2)], S.bufs_n(2))

            def sin_layer(src, Bsrc, wt, kdim, col, dst, Bdst):
                for c in range(L // NW):
                    ps, Bp = fps.next()
                    a_, Ba = arg.next()
                    k_, Bk = kf.next()
                    i_, Bi = ki.next()
                    S.op("pe", lambda e, ps=ps, c=c: e.matmul(ps[0:64, 0:NW], lhsT=wt[0:kdim, :], rhs=src[0:kdim, c * NW:(c + 1) * NW],
                                                             start=True, stop=True), reads=[Bsrc, Bc], writes=[Bp])
                    S.op("act", lambda e, ps=ps, a_=a_: e.activation(out=a_[:], in_=ps[0:64, 0:NW], func=AF.Identity,
                                                                    bias=fb[:, col + 1:col + 2], scale=fb[:, col:col + 1]), reads=[Bp, Bc], writes=[Ba])
                    S.op("dve", lambda e, a_=a_, k_=k_: e.tensor_scalar(out=k_[:], in0=a_[:], scalar1=1.0 / TWO_PI, scalar2=12582912.0, op0=ALU.mult, op1=ALU.add),
                         reads=[Ba], writes=[Bk])
                    S.op("dve", lambda e, k_=k_: e.tensor_scalar(out=k_[:], in0=k_[:], scalar1=12582912.0, scalar2=None, op0=ALU.subtract),
                         reads=[Bk], writes=[Bk])
                    S.op("dve", lambda e, a_=a_, k_=k_: e.scalar_tensor_tensor(out=a_[:], in0=k_[:], scalar=-TWO_PI, in1=a_[:], op0=ALU.mult, op1=ALU.add),
                         reads=[Bk, Ba], writes=[Ba])
                    S.op("dve", lambda e, a_=a_, k_=k_: e.tensor_scalar(out=k_[:], in0=a_[:], scalar1=math.pi, scalar2=TWO_PI, op0=ALU.is_gt, op1=ALU.mult),
                         reads=[Ba], writes=[Bk])
                    S.op("dve", lambda e, a_=a_, k_=k_: e.tensor_tensor(out=a_[:], in0=a_[:], in1=k_[:], op=ALU.subtract), reads=[Ba, Bk], writes=[Ba])
                    S.op("dve", lambda e, a_=a_, k_=k_: e.tensor_scalar(out=k_[:], in0=a_[:], scalar1=-math.pi, scalar2=TWO_PI, op0=ALU.is_lt, op1=ALU.mult),
                         reads=[Ba], writes=[Bk])
                    S.op("dve", lambda e, a_=a_, k_=k_: e.tensor_tensor(out=a_[:], in0=a_[:], in1=k_[:], op=ALU.add), reads=[Ba, Bk], writes=[Ba])
                    S.op("act", lambda e, a_=a_, c=c: e.activation(out=dst[:, c * NW:(c + 1) * NW], in_=a_[:], func=AF.Sin), reads=[Ba], writes=[Bdst])
            sin_layer(feats, Bc, w1, 33, 0, a1, Ba1)
            sin_layer(a1, Ba1, w2, 64, 2, a2, Ba2)
            gp = [sb("gp", [128, nT, 512], BF16) for _ in range(2)]
            gm = [sb("gm", [128, nT, 512], BF16) for _ in range(2)]
            Bg = S.bufs_n(2)
            dec = Rot([sb("dec", [128, 512], F32) for _ in range(2)], S.bufs_n(2))
            fw = Rot([sb("fw", [128, 512], F32) for _ in range(2)], S.bufs_n(2))
            bw = Rot([sb("bw", [128, 512], F32) for _ in range(2)], S.bufs_n(2))
            ab = Rot([sb("ab", [128, 2, 512], BF16) for _ in range(2)], S.bufs_n(2))
            rn = Rot([sb("rn", [128, 512], F32) for _ in range(2)], S.bufs_n(2))
            tb = Rot([sb("ftb", [128, 2, nT, 128], BF16) for _ in range(3)], S.bufs_n(3))
            ho = Rot([sb("ho", [128, 2, 512], F32) for _ in range(2)], S.bufs_n(2))
            it = 0
            for o in range(2):
                for db in range(4):
                    k = it % 2
                    it += 1
                    npz, Bn = nps.next()
                    for tc in range(nT):
                        pf, Bpf = fps.next()
                        pb, Bpb = fps.next()
                        d_, Bd = dec.next()
                        f_, Bf = fw.next()
                        b_, Bb = bw.next()
                        a_, Bab = ab.next()
                        cf = o * D + db * 512
                        S.op("pe", lambda e, pf=pf, tc=tc, cf=cf: e.matmul(pf[:], lhsT=a2[:, tc * 128:(tc + 1) * 128], rhs=w3[:, cf:cf + 512],
                                                                          start=True, stop=True), reads=[Ba2, Bc], writes=[Bpf])
                        S.op("pe", lambda e, pb=pb, tc=tc, cf=cf: e.matmul(pb[:], lhsT=a2[:, tc * 128:(tc + 1) * 128], rhs=w3[:, 2 * D + cf:2 * D + cf + 512],
                                                                          start=True, stop=True), reads=[Ba2, Bc], writes=[Bpb])
                        S.op("act", lambda e, d_=d_, tc=tc, db=db: e.activation(out=d_[:], in_=drep[:, db * 512:(db + 1) * 512], func=AF.Exp,
                                                                               scale=tnn[:, tc:tc + 1]), reads=[Bc], writes=[Bd])
                        S.op("dve", lambda e, f_=f_, pf=pf, d_=d_: e.tensor_tensor(out=f_[:], in0=pf[:], in1=d_[:], op=ALU.mult), reads=[Bpf, Bd], writes=[Bf])
                        S.op("dve", lambda e, b_=b_, pb=pb, d_=d_: e.tensor_tensor(out=b_[:], in0=pb[:], in1=d_[:], op=ALU.mult), reads=[Bpb, Bd], writes=[Bb])
                        if tc == 0:
                            S.op("dve", lambda e, b_=b_: e.memset(b_[0:1, :], 0.0), reads=[Bb], writes=[Bb])
                        S.op("pool", lambda e, f_=f_, b_=b_, k=k, tc=tc: e.tensor_tensor(out=gp[k][:, tc, :], in0=f_[:], in1=b_[:], op=ALU.add),
                             reads=[Bf, Bb], writes=[Bg[k]])
                        S.op("pool", lambda e, f_=f_, b_=b_, k=k, tc=tc: e.tensor_tensor(out=gm[k][:, tc, :], in0=f_[:], in1=b_[:], op=ALU.subtract),
                             reads=[Bf, Bb], writes=[Bg[k]])
                        S.op("act", lambda e, f_=f_, a_=a_: e.activation(out=a_[:, 0, :], in_=f_[:], func=AF.Abs), reads=[Bf], writes=[Bab])
                        S.op("act", lambda e, b_=b_, a_=a_: e.activation(out=a_[:, 1, :], in_=b_[:], func=AF.Abs), reads=[Bb], writes=[Bab])
                        for h in range(2):
                            S.op("pe", lambda e, npz=npz, a_=a_, h=h, tc=tc: e.matmul(npz[:], lhsT=self.ones_bf[:], rhs=a_[:, h, :],
                                                                                     start=(tc == 0 and h == 0), stop=(tc == nT - 1 and h == 1)),
                                 reads=[Bab], writes=[Bn])
                    r_, Br = rn.next()
                    S.op("dve", lambda e, r_=r_, npz=npz: e.tensor_scalar(out=r_[:], in0=npz[:], scalar1=EPS, scalar2=None, op0=ALU.add),
                         reads=[Bn], writes=[Br])
                    S.op("dve", lambda e, r_=r_: e.reciprocal(out=r_[:], in_=r_[:]), reads=[Br], writes=[Br])
                    tnext = None
                    for fc in range(nT):
                        if tnext is None:
                            t_, Bt = tb.next()
                            S.op("sp", lambda e, t_=t_, fc=fc: e.dma_start(out=t_[:], in_=tabF[fc]), writes=[Bt], dma=1)
                        else:
                            t_, Bt = tnext
                        if fc + 1 < nT:
                            tnext = tb.next()
                            S.op("sp", lambda e, t2=tnext[0], fc=fc: e.dma_start(out=t2[:], in_=tabF[fc + 1]), writes=[tnext[1]], dma=1)
                        h_, Bh = ho.next()
                        for ri, gsrc in enumerate((gp, gm)):
                            hp, Bhp = hps.next()
                            for tc in range(nT):
                                S.op("pe", lambda e, hp=hp, t_=t_, ri=ri, tc=tc, gsrc=gsrc, k=k: e.matmul(
                                    hp[:], lhsT=t_[:, ri, tc, :], rhs=gsrc[k][:, tc, :], start=(tc == 0), stop=(tc == nT - 1)),
                                    reads=[Bt, Bg[k]], writes=[Bhp])
                            S.op("dve", lambda e, hp=hp, h_=h_, ri=ri, r_=r_: e.tensor_tensor(out=h_[:, ri, :], in0=hp[:], in1=r_[:], op=ALU.mult),
                                 reads=[Bhp, Br], writes=[Bh])
                        S.op("sp", lambda e, h_=h_, o=o, fc=fc, db=db: e.dma_start(
                            out=Hs[o, :, fc * 128:(fc + 1) * 128, db * 512:(db + 1) * 512].rearrange("r p d -> p r d"), in_=h_[:]),
                            reads=[Bh], dma=1)
            S.flush()

    def hy_conv(self, j, o):
        nc, S = self.nc, self.S
        if S.stopped:
            return
        S.DEFER = 0
        src_tm = self.vtmh if o == 0 else self.ztmh
        src_fm = self.vfm if o == 0 else self.z1fm
        gate_fm = self.x1fm if o == 0 else self.x2fm
        with ExitStack() as stk:
            def sb(n, s, d):
                return stk.enter_context(nc.sbuf_tensor(self.nm(n), s, d))
            skip = sb("skip", [128, 2, DC], F32)
            ident = sb("ident2", [128, 128], BF16)
            Bc = S.buf()
            S.op("sp", lambda e: e.dma_start(out=skip[:], in_=self.hyskip_in), writes=[Bc], dma=1)
            S.op("sp", lambda e: e.dma_start(out=ident[:], in_=self.ident_in), writes=[Bc], dma=1)
            vb = Rot([sb("cvb", [128, 16, 512], BF16) for _ in range(2)], S.bufs_n(2))
            yre = Rot([sb("yre", [128, 16, 512], BF16) for _ in range(1)], S.bufs_n(1))
            yim = Rot([sb("yim", [128, 16, 512], BF16) for _ in range(1)], S.bufs_n(1))
            tb = Rot([sb("ctb", [128, 2, 16, 128], BF16) for _ in range(3)], S.bufs_n(3))
            hh = Rot([sb("hh", [128, 2, 512], F32) for _ in range(2)], S.bufs_n(2))
            tt_ = [Rot([sb("cp%d" % i, [128, 512], F32) for _ in range(2)], S.bufs_n(2)) for i in range(4)]
            ti = Rot([sb("cti", [128, 2, 16, 512], BF16) for _ in range(2)], S.bufs_n(2))
            ui = Rot([sb("cui", [128, 512], F32) for _ in range(2)], S.bufs_n(2))
            gi_ = Rot([sb("cgi", [128, 512], F32) for _ in range(2)], S.bufs_n(2))
            zo = Rot([sb("czo", [128, 512], F32) for _ in range(2)], S.bufs_n(2))
            zb = Rot([sb("czb", [128, 512], BF16) for _ in range(2)], S.bufs_n(2))
            vt = Rot([sb("cvt", [128, 4, 128], BF16) for _ in range(2)], S.bufs_n(2))
            ups = Rot([stk.enter_context(nc.psum_tensor(self.nm("ups"), [128, 512], F32)) for _ in range(4)], S.bufs_n(4))
            yps = Rot([stk.enter_context(nc.psum_tensor(self.nm("yps"), [128, 512], F32)) for _ in range(2)], S.bufs_n(2))
            tps = Rot([stk.enter_context(nc.psum_tensor(self.nm("ctps"), [128, 4, 128], BF16)) for _ in range(2)], S.bufs_n(2))
            sfv = src_fm.rearrange("(c p) t -> p c t", p=128)
            gfv = gate_fm.rearrange("(c p) t -> p c t", p=128)
            z1v = self.z1fm.rearrange("(c p) t -> p c t", p=128)
            z2v = self.z2T.rearrange("(c p) t -> p c t", p=128)
            for (t0, L) in self.SEQS:
                nT = L // 128
                NW = min(512, L)
                Hs = self.Hs[L]
                tabF = self.tabF2048_in if L == 2048 else self.tabF256_in
                tabI = self.tabI2048_in if L == 2048 else self.tabI256_in
                for db in range(4):
                    v_, Bv = vb.next()
                    yr, Byr = yre.next()
                    yi, Byi = yim.next()
                    S.op("sp", lambda e, v_=v_, t0=t0, L=L, nT=nT, db=db: e.dma_start(
                        out=v_[:, 0:nT, :], in_=src_tm[t0:t0 + L, db * 512:(db + 1) * 512].rearrange("(c p) d -> p c d", p=128)),
                        writes=[Bv], dma=1)
                    for fc in range(nT):
                        t_, Bt = tb.next()
                        h_, Bh = hh.next()
                        S.op("sp", lambda e, t_=t_, fc=fc, nT=nT, tabF=tabF: e.dma_start(out=t_[:, :, 0:nT, :], in_=tabF[fc]), writes=[Bt], dma=1)
                        S.op("sp", lambda e, h_=h_, fc=fc, db=db, Hs=Hs: e.dma_start(
                            out=h_[:], in_=Hs[o, :, fc * 128:(fc + 1) * 128, db * 512:(db + 1) * 512].rearrange("r p d -> p r d")),
                            writes=[Bh], dma=1)
                        pu = []
                        for ri in range(2):
                            p_, Bp = ups.next()
                            for tc in range(nT):
                                S.op("pe", lambda e, p_=p_, t_=t_, ri=ri, tc=tc, v_=v_, nT=nT: e.matmul(
                                    p_[:], lhsT=t_[:, ri, tc, :], rhs=v_[:, tc, :], start=(tc == 0), stop=(tc == nT - 1)),
                                    reads=[Bt, Bv], writes=[Bp])
                            pu.append((p_, Bp))
                        (ur, Bur), (um, Bum) = pu
                        (a1_, Ba1), (a2_, Ba2), (a3_, Ba3), (a4_, Ba4) = [r.next() for r in tt_]
                        S.op("dve", lambda e, a1_=a1_, ur=ur, h_=h_: e.tensor_tensor(out=a1_[:], in0=ur[:], in1=h_[:, 0, :], op=ALU.mult), reads=[Bur, Bh], writes=[Ba1])
                        S.op("dve", lambda e, a2_=a2_, um=um, h_=h_: e.tensor_tensor(out=a2_[:], in0=um[:], in1=h_[:, 1, :], op=ALU.mult), reads=[Bum, Bh], writes=[Ba2])
                        S.op("dve", lambda e, a3_=a3_, um=um, h_=h_: e.tensor_tensor(out=a3_[:], in0=um[:], in1=h_[:, 0, :], op=ALU.mult), reads=[Bum, Bh], writes=[Ba3])
                        S.op("dve", lambda e, a4_=a4_, ur=ur, h_=h_: e.tensor_tensor(out=a4_[:], in0=ur[:], in1=h_[:, 1, :], op=ALU.mult), reads=[Bur, Bh], writes=[Ba4])
                        S.op("pool", lambda e, yr=yr, fc=fc, a1_=a1_, a2_=a2_: e.tensor_tensor(out=yr[:, fc, :], in0=a1_[:], in1=a2_[:], op=ALU.subtract),
                             reads=[Ba1, Ba2], writes=[Byr])
                        S.op("pool", lambda e, yi=yi, fc=fc, a3_=a3_, a4_=a4_: e.tensor_tensor(out=yi[:, fc, :], in0=a3_[:], in1=a4_[:], op=ALU.add),
                             reads=[Ba3, Ba4], writes=[Byi])
                    def ld_ug(tt, dc):
                        oc_ = db * 4 + dc
                        c0_ = t0 + tt * NW
                        u2, Bu2 = ui.next()
                        g2, Bg2 = gi_.next()
                        S.op("sp", lambda e, NW=NW: e.dma_start(out=u2[:, 0:NW], in_=sfv[:, oc_, c0_:c0_ + NW]), writes=[Bu2], dma=1)
                        S.op("sp", lambda e, NW=NW: e.dma_start(out=g2[:, 0:NW], in_=gfv[:, oc_, c0_:c0_ + NW]), writes=[Bg2], dma=1)
                        return (u2, Bu2, g2, Bg2)

                    def ld_tab(tt):
                        c2, Bc2 = ti.next()
                        S.op("sp", lambda e, NW=NW, nT=nT, tabI=tabI: [e.dma_start(
                            out=c2[:, r, 0:nT, 0:NW], in_=tabI[r, :, :, tt * NW:(tt + 1) * NW]) for r in range(2)],
                            writes=[Bc2], dma=2)
                        return (c2, Bc2)
                    its = [(tt, dc) for tt in range(L // NW) for dc in range(4)]
                    pre_ug = ld_ug(*its[0])
                    pre_tab = ld_tab(0)
                    for ii, (tt, dc) in enumerate(its):
                        if dc == 0:
                            c_, Bci = pre_tab
                            if tt + 1 < L // NW:
                                pre_tab = ld_tab(tt + 1)
                        if True:
                            oc = db * 4 + dc
                            col0 = t0 + tt * NW
                            y_, By = yps.next()
                            u_, Bu, g_, Bgt = pre_ug
                            if ii + 1 < len(its):
                                pre_ug = ld_ug(*its[ii + 1])
                            for fc in range(nT):
                                S.op("pe", lambda e, y_=y_, yr=yr, fc=fc, dc=dc, c_=c_, NW=NW: e.matmul(
                                    y_[:, 0:NW], lhsT=yr[:, fc, dc * 128:(dc + 1) * 128], rhs=c_[:, 0, fc, 0:NW], start=(fc == 0), stop=False),
                                    reads=[Byr, Bci], writes=[By])
                                S.op("pe", lambda e, y_=y_, yi=yi, fc=fc, dc=dc, c_=c_, NW=NW, nT=nT: e.matmul(
                                    y_[:, 0:NW], lhsT=yi[:, fc, dc * 128:(dc + 1) * 128], rhs=c_[:, 1, fc, 0:NW], start=False, stop=(fc == nT - 1)),
                                    reads=[Byi, Bci], writes=[By])
                            S.op("dve", lambda e, u_=u_, y_=y_, oc=oc, NW=NW: e.scalar_tensor_tensor(
                                out=u_[:, 0:NW], in0=u_[:, 0:NW], scalar=skip[:, o, oc:oc + 1], in1=y_[:, 0:NW], op0=ALU.mult, op1=ALU.add),
                                reads=[Bu, By, Bc], writes=[Bu])
                            if o == 0:
                                z_, Bz = zo.next()
                                S.op("pool", lambda e, z_=z_, u_=u_, g_=g_, NW=NW: e.tensor_tensor(out=z_[:, 0:NW], in0=u_[:, 0:NW], in1=g_[:, 0:NW], op=ALU.mult),
                                     reads=[Bu, Bgt], writes=[Bz])
                                S.op("sp", lambda e, z_=z_, oc=oc, col0=col0, NW=NW: e.dma_start(out=z1v[:, oc, col0:col0 + NW], in_=z_[:, 0:NW]), reads=[Bz], dma=1)
                                self.fm_to_tm(z_, Bz, oc, self.ztmh, zb, vt, tps, ident, Bc, c0=col0, ncols=NW) if NW == 512 else \
                                    self.fm_to_tm_small(z_, Bz, oc, zb, vt, tps, ident, Bc, col0)
                            else:
                                zb_, Bzb = zb.next()
                                S.op("pool", lambda e, zb_=zb_, u_=u_, g_=g_, NW=NW: e.tensor_tensor(out=zb_[:, 0:NW], in0=u_[:, 0:NW], in1=g_[:, 0:NW], op=ALU.mult),
                                     reads=[Bu, Bgt], writes=[Bzb])
                                S.op("sp", lambda e, zb_=zb_, oc=oc, col0=col0, NW=NW: e.dma_start(out=z2v[:, oc, col0:col0 + NW], in_=zb_[:, 0:NW]), reads=[Bzb], dma=1)
            S.flush()

    def fm_to_tm_small(self, u_, Bu, oc, ub, vt, tps, ident, Bc, c0):
        S = self.S
        b_, Bb = ub.next()
        S.op("act", lambda e: e.activation(out=b_[:, 0:256], in_=u_[:, 0:256], func=AF.Copy), reads=[Bu], writes=[Bb])
        dv = self.ztmh.rearrange("(c p) d -> p c d", p=128)
        tp, Bt = tps.next()
        v_, Bv = vt.next()
        for i in range(2):
            S.op("pe", lambda e, i=i: e.transpose(tp[:, i, :], b_[:, i * 128:(i + 1) * 128], ident[:]), reads=[Bb, Bc], writes=[Bt])
        S.op("act", lambda e: e.activation(out=v_[:, 0:2, :], in_=tp[:, 0:2, :], func=AF.Copy), reads=[Bt], writes=[Bv])
        tc0 = c0 // 128
        S.op("sp", lambda e: e.dma_start(out=dv[:, tc0:tc0 + 2, oc * 128:(oc + 1) * 128], in_=v_[:, 0:2, :]), reads=[Bv], dma=1)


def pp(v):
    v = np.asarray(v)
    n = v.shape[-1] // 128
    return np.ascontiguousarray(np.moveaxis(v.reshape(v.shape[:-1] + (n, 128)), -1, 0))


def rope_tables():
    half = HD // 2
    t = np.arange(TL)
    row = (t // 64).astype(np.float32)
    col = (t % 64).astype(np.float32)
    inv = (10000.0 ** (-np.arange(0, half, 2, dtype=np.float32) / half)).astype(np.float32)
    ang = np.zeros((128, TL), np.float32)
    ang[0:32] = inv[:, None] * row[None, :]
    ang[32:64] = inv[:, None] * row[None, :]
    ang[64:96] = inv[:, None] * col[None, :]
    ang[96:128] = inv[:, None] * col[None, :]
    tab = np.stack([np.cos(ang), np.sin(ang)], axis=1).astype(np.float32)
    P = np.zeros((128, 128), np.float32)
    for base in (0, 64):
        for m in range(32):
            P[base + m + 32, base + m] = -1.0
            P[base + m, base + m + 32] = 1.0
    return tab, P.astype(NPBF)


def window_masks():
    kj = np.arange(128)[:, None]
    qi = np.arange(128)[None, :]
    prev = (kj >= qi).astype(np.float32)
    nxt = (kj <= qi).astype(np.float32)
    m = np.stack([np.tile(prev, (1, 4)), np.tile(nxt, (1, 4))], axis=1)
    return m.astype(NPBF)


def make_in_maps(inp, kb):
    f = lambda a: np.ascontiguousarray(np.asarray(a, dtype=np.float32))
    rope, P = rope_tables()
    wm = window_masks()
    shared = {
        "normg_in": pp(np.stack([f(inp["norm_mix_g"]), f(inp["norm_ffn_g"])], 0)),
        "modb_in": pp(f(inp["mod_b"])),
        "finalg_in": pp(f(inp["final_g"])),
        "esink_in": np.ascontiguousarray(np.repeat(f(inp["win_sink"]), 128, axis=1)[:, None, :]),
        "axg_in": np.ascontiguousarray(np.stack([f(inp["ax_q_g"])[0], f(inp["ax_k_g"])[0]], axis=1)),
        "axkg_rep": np.ascontiguousarray(np.tile(f(inp["ax_k_g"])[0][None, :], (128, 1))),
        "rope_in": rope, "ropeP_in": P, "wmask_in": wm,
    }
    shared.update(hyena_shared(inp))
    for name in kb.inputs:
        if "__" in name:
            base, idx = name.split("__")
            shared[name] = np.ascontiguousarray(f(inp[base])[int(idx)])
    maps = []
    xs, xp = f(inp["x_sample"]), f(inp["x_prompt"])
    for i in range(8):
        m = dict(shared)
        xt = np.concatenate([xs[i], xp[4 * i:4 * i + 4].reshape(1024, D)], axis=0)
        m["xT_in"] = np.ascontiguousarray(xt.T)
        m["cvec"] = np.ascontiguousarray(np.stack([pp(f(inp["c"])[i]), pp(f(inp["c_ctx"]))], axis=-1))
        m["cwkT"] = np.ascontiguousarray(f(inp["cache_win_k"])[i].transpose(0, 2, 3, 1))
        m["cwv"] = np.ascontiguousarray(f(inp["cache_win_v"])[i].reshape(2, PAST, 512))
        m["cakT"] = np.ascontiguousarray(f(inp["cache_ax_k"])[i].transpose(0, 2, 3, 1))
        m["cav"] = np.ascontiguousarray(f(inp["cache_ax_v"])[i].reshape(1, PAST, 512))
        maps.append({k: v for k, v in m.items() if k in kb.inputs})
    return maps


def hyena_shared(inp):
    f = lambda a: np.ascontiguousarray(np.asarray(a, dtype=np.float32))
    out = {}
    cw = f(inp["hy_conv_w"])[0]
    cb = f(inp["hy_conv_b"])[0]
    out["hyconv_in"] = pp(np.stack([cw[0], cw[1], cw[2], cb], axis=0)).transpose(0, 2, 1).copy()
    out["ident_in"] = np.eye(128, dtype=np.float32).astype(NPBF)
    out["hyskip_in"] = pp(f(inp["hy_skip"])[0])
    out["hyw1_in"] = f(inp["hy_f_w1"])[0]
    out["hyw2_in"] = f(inp["hy_f_w2"])[0]
    out["hyw3_in"] = f(inp["hy_f_w3"])[0]
    fr = f(inp["hy_freq"])[0]
    out["hyfb_in"] = np.ascontiguousarray(np.stack([fr[0], fr[1], f(inp["hy_f_b1"])[0], f(inp["hy_f_b2"])[0]], axis=1))
    HY_MIN = math.log(1e-2) / 1.5
    HY_MAX = math.log(1e-2) / 0.3
    deltas = np.abs(np.linspace(HY_MIN, HY_MAX, D, dtype=np.float32))
    out["drep_in"] = np.ascontiguousarray(np.tile(deltas[None, :], (128, 1)).astype(np.float32))
    for L in (2048, 256):
        t = np.arange(L, dtype=np.float32)
        tn = (t / max(L - 1, 1)).astype(np.float32)
        bands = 16
        fbv = np.linspace(1e-4, bands - 1, bands, dtype=np.float32)
        w = (np.float32(2.0 * math.pi) * t / np.float32(L)).astype(np.float32)
        feats = np.concatenate([tn[:, None], np.cos(w[:, None] * fbv), -np.sin(w[:, None] * fbv)], axis=-1).astype(np.float32)
        out["feats%d_in" % L] = np.ascontiguousarray(feats.T)
        out["tnn%d_in" % L] = pp(-tn)
        nT = L // 128
        tt = np.arange(L, dtype=np.float64)
        ff = np.arange(L, dtype=np.float64)
        ang = np.pi * np.outer(tt, 2 * ff + 1) / (2 * L)
        C = np.cos(ang)
        S_ = np.sin(ang)
        CF = np.stack([C, S_], 0).reshape(2, nT, 128, nT, 128)
        out["tabF%d_in" % L] = np.ascontiguousarray(CF.transpose(3, 2, 0, 1, 4)).astype(NPBF)
        CI = np.stack([C.T, S_.T], 0) / L
        out["tabI%d_in" % L] = np.ascontiguousarray(CI.reshape(2, nT, 128, L).transpose(0, 2, 1, 3)).astype(NPBF)
    return out


_CACHE = {}


def get_kb():
    if "kb" not in _CACHE:
        kb = KB()
        kb.build()
        _CACHE["kb"] = kb
    return _CACHE["kb"]


def kernel(**inp):
    kb = get_kb()
    maps = make_in_maps(inp, kb)
    res = run_bass_kernel_spmd(kb.nc, maps, core_ids=list(range(8)))
    R = res.results
    y_s = np.stack([R[i]["yT"][:, :TL].T for i in range(8)], 0)
    y_p = np.concatenate([R[i]["yT"][:, TL:].T.reshape(4, 256, D) for i in range(8)], 0)
    outs = [np.ascontiguousarray(y_p), np.ascontiguousarray(y_s)]
    for nm_ in ("swk", "swv", "sak", "sav"):
        a = np.concatenate([R[i][nm_] for i in range(8)], 0)
        outs.append(np.ascontiguousarray(a.reshape(a.shape[0], a.shape[1], 256, NKV, HD)))
    return tuple(outs)
```

```python
import math
import os
from contextlib import ExitStack

import numpy as np
import ml_dtypes

import concourse.bass as bass
import concourse.mybir as mybir
from concourse.bass_utils import run_bass_kernel_spmd

F32 = mybir.dt.float32
BF16 = mybir.dt.bfloat16
AF = mybir.ActivationFunctionType
ALU = mybir.AluOpType
NPBF = ml_dtypes.bfloat16

D = 2048
DC = 16
T = 3072
TL = 2048
NTT = 6
TW = 512
DFF = 5632
FC = 44
NH = 16
NKV = 4
HD = 128
PAST = 512
EPS = 1e-6
DEPTH = 4
ENG_NAMES = ("pe", "act", "dve", "pool", "sp")


class Buf:
    __slots__ = ("name", "w", "r")

    def __init__(self, name=""):
        self.name = name
        self.w = None
        self.r = []


class Op:
    __slots__ = ("eng", "fn", "deps", "flag", "val", "sem", "dma")

    def __init__(self, eng, fn, dma):
        self.eng = eng
        self.fn = fn
        self.deps = []
        self.flag = False
        self.val = 0
        self.sem = None
        self.dma = dma


class Sched:
    def __init__(self, nc, stack, n_dma_sems=48):
        self.nc = nc
        self.sem = {e: stack.enter_context(nc.semaphore("s_" + e)) for e in ENG_NAMES}
        self.cnt = {e: 0 for e in ENG_NAMES}
        self.dma_sems = [stack.enter_context(nc.semaphore("s_dma%d" % i)) for i in range(n_dma_sems)]
        self.dma_val = [0] * n_dma_sems
        self.dma_last = [None] * n_dma_sems
        self.dma_rr = 0
        self.ops = {e: [] for e in ENG_NAMES}
        self.waited = {e: {} for e in ENG_NAMES}
        self.bufs = []
        self.n_ops = 0
        self.stopped = False
        self.sp_def = []
        self.DEFER = int(os.environ.get("K_DEFER", "3"))
        self.DEFER0 = self.DEFER

    def buf(self, name=""):
        b = Buf(name)
        self.bufs.append(b)
        return b

    def bufs_n(self, n, name=""):
        return [self.buf("%s%d" % (name, i)) for i in range(n)]

    def op(self, eng, fn, reads=(), writes=(), dma=0):
        o = Op(eng, fn, dma)
        deps = {}

        def add(d, war):
            if d is None or d is o:
                return
            if d.dma == 0 and d.eng == eng and dma == 0:
                if eng == "pe" or war:
                    return
            deps[id(d)] = d

        for b in reads:
            add(b.w, False)
        for b in writes:
            add(b.w, False)
            for r in b.r:
                add(r, True)
        if dma:
            k = self.dma_rr
            self.dma_rr = (k + 1) % len(self.dma_sems)
            prev = self.dma_last[k]
            if prev is not None:
                deps[id(prev)] = prev
            self.dma_val[k] += 16 * dma
            o.val = self.dma_val[k]
            o.sem = self.dma_sems[k]
            self.dma_last[k] = o
        for d in deps.values():
            d.flag = True
        o.deps = list(deps.values())
        for b in reads:
            if dma == 0:
                b.r = [r for r in b.r if not (r.dma == 0 and r.eng == eng)]
            b.r.append(o)
        for b in writes:
            b.w = o
            b.r = []
        self.n_ops += 1
        if eng == "sp":
            if reads and self.DEFER > 0:
                self.sp_def.append([o, 0])
                return o
            if self.sp_def:
                defd = set(id(x[0]) for x in self.sp_def)
                if any(id(d) in defd for d in o.deps):
                    for x in self.sp_def:
                        self.ops["sp"].append(x[0])
                    self.sp_def = []
            self.ops["sp"].append(o)
            keep = []
            for x in self.sp_def:
                x[1] += 1
                if x[1] >= self.DEFER:
                    self.ops["sp"].append(x[0])
                else:
                    keep.append(x)
            self.sp_def = keep
            return o
        self.ops[eng].append(o)
        return o

    def flush(self):
        nc = self.nc
        for x in self.sp_def:
            self.ops["sp"].append(x[0])
        self.sp_def = []
        lasts = {}
        for e in ENG_NAMES:
            for o in reversed(self.ops[e]):
                if o.dma == 0:
                    lasts[e] = o
                    o.flag = True
                    break
        for e in ENG_NAMES:
            for o in self.ops[e]:
                if o.dma == 0 and o.flag:
                    self.cnt[e] += 1
                    o.val = self.cnt[e]
                    o.sem = self.sem[e]
        dma_final = [(self.dma_sems[k], self.dma_val[k]) for k in range(len(self.dma_sems))
                     if self.dma_val[k] > 0]
        with nc.Block() as block:
            for e in ENG_NAMES:
                def body(eng, e=e):
                    waited = self.waited[e]

                    def wait(sem, val):
                        key = id(sem)
                        if waited.get(key, 0) < val:
                            eng.wait_ge(sem, val)
                            waited[key] = val

                    for o in self.ops[e]:
                        for d in o.deps:
                            wait(d.sem, d.val)
                        r = o.fn(eng)
                        if o.dma:
                            if not isinstance(r, (list, tuple)):
                                r = [r]
                            assert len(r) == o.dma, (len(r), o.dma)
                            for ins in r:
                                ins.then_inc(o.sem, 16)
                        elif o.flag:
                            r.then_inc(o.sem, 1)
                    for e2 in ENG_NAMES:
                        if e2 in lasts:
                            wait(lasts[e2].sem, lasts[e2].val)
                    for sem, val in dma_final:
                        wait(sem, val)
                name = {"pe": "tensor", "act": "scalar", "dve": "vector",
                        "pool": "gpsimd", "sp": "sync"}[e]
                getattr(block, name)(body)
        self.ops = {e: [] for e in ENG_NAMES}
        for b in self.bufs:
            b.w = None
            b.r = []
        self.bufs = []
        self.dma_last = [None] * len(self.dma_sems)
        self.nflush = getattr(self, "nflush", 0) + 1
        if self.nflush == getattr(self, "stop", -1):
            self.stopped = True


class StopBuild(Exception):
    pass


class Rot:
    def __init__(self, tiles, bufs):
        self.t = tiles
        self.b = bufs
        self.i = 0

    def next(self):
        k = self.i % len(self.t)
        self.i += 1
        return self.t[k], self.b[k]


class KB:
    def __init__(self, n_layers=DEPTH, dbg=False, stop=-1):
        self.nc = bass.Bass("TRN2", target_bir_lowering=False)
        self.stop = stop
        self.n_layers = n_layers
        self.dbg = dbg
        self.uid = 0
        self.inputs = {}

    def nm(self, s):
        self.uid += 1
        return "%s_%d" % (s, self.uid)

    def din(self, name, shape, dt=F32):
        ap = self.nc.dram_tensor(name, list(shape), dt, kind="ExternalInput").ap()
        self.inputs[name] = (tuple(shape), dt)
        return ap

    def dout(self, name, shape, dt=F32):
        return self.nc.dram_tensor(name, list(shape), dt, kind="ExternalOutput").ap()

    def dscr(self, name, shape, dt):
        return self.nc.dram_tensor(name, list(shape), dt).ap()

    IN_SPECS = {
        "xT_in": ([D, T], F32), "cvec": ([128, DC, 2], F32), "normg_in": ([128, 2, DEPTH, DC], F32),
        "modb_in": ([128, DEPTH, 96], F32), "finalg_in": ([128, DC], F32), "mod_w": ([DEPTH, D, 6 * D], F32),
        "win_wqkv": ([2, D, 3072], F32), "win_wo": ([2, D, D], F32), "esink_in": ([2, 1, NH * 128], F32),
        "ax_wqkv": ([1, D, 3072], F32), "ax_wo": ([1, D, D], F32), "axg_in": ([128, 2], F32),
        "axkg_rep": ([128, 128], F32), "ffn_w_gu": ([DEPTH, D, 2 * DFF], F32), "ffn_w_down": ([DEPTH, DFF, D], F32),
        "cwkT": ([2, NKV, 128, PAST], F32), "cwv": ([2, PAST, 512], F32), "cakT": ([1, NKV, 128, PAST], F32),
        "cav": ([1, PAST, 512], F32), "rope_in": ([128, 2, TL], F32), "ropeP_in": ([128, 128], BF16),
        "wmask_in": ([128, 2, 512], BF16),
        "hy_w_in": ([1, D, 3 * D], F32), "hy_wo": ([1, D, D], F32),
        "hyconv_in": ([128, 48, 4], F32), "ident_in": ([128, 128], BF16), "hyskip_in": ([128, 2, DC], F32),
        "hyw1_in": ([33, 64], F32), "hyw2_in": ([64, 64], F32), "hyw3_in": ([64, 4 * D], F32), "hyfb_in": ([64, 4], F32),
        "drep_in": ([128, D], F32),
        "feats2048_in": ([33, 2048], F32), "feats256_in": ([33, 256], F32),
        "tnn2048_in": ([128, 16], F32), "tnn256_in": ([128, 2], F32),
        "tabF2048_in": ([16, 128, 2, 16, 128], BF16), "tabF256_in": ([2, 128, 2, 2, 128], BF16),
        "tabI2048_in": ([2, 128, 16, 2048], BF16), "tabI256_in": ([2, 128, 2, 256], BF16),
    }

    LAYERED = ("mod_w", "win_wqkv", "win_wo", "ax_wqkv", "ax_wo", "ffn_w_gu", "ffn_w_down", "hy_w_in", "hy_wo")

    def W(self, base, idx):
        name = "%s__%d" % (base, idx)
        if name not in self.inputs:
            shape, dt = type(self).IN_SPECS[base]
            self._lay = getattr(self, "_lay", {})
            self._lay[name] = self.din(name, shape[1:], dt)
        return self._lay[name]

    def __getattr__(self, name):
        specs = type(self).IN_SPECS
        if name in specs and name not in type(self).LAYERED:
            shape, dt = specs[name]
            ap = self.din(name, shape, dt)
            setattr(self, name, ap)
            return ap
        raise AttributeError(name)

    def declare(self):
        nc = self.nc
        self.yT = self.dout("yT", [D, T])
        self.swk = self.dout("swk", [4, 2, 256, 512])
        self.swv = self.dout("swv", [4, 2, 256, 512])
        self.sak = self.dout("sak", [4, 1, 256, 512])
        self.sav = self.dout("sav", [4, 1, 256, 512])
        if self.dbg:
            self.xT = self.dout("xT_dbg", [D, T])
        else:
            self.xT = self.dscr("xT", [D, T], F32)
        self.hT = self.dscr("hT", [D, T], BF16)
        self.qT = self.dscr("qT", [D, T], BF16)
        self.kT = self.dscr("kT", [512, T], BF16)
        self.vtm = self.dscr("vtm", [T, 512], BF16)
        self.oT = self.dscr("oT", [D, T], BF16)
        self.actT = self.dscr("actT", [DFF, T], BF16)

    def persistent(self, stk):
        nc = self.nc
        self.mod = stk.enter_context(nc.sbuf_tensor("mod", [128, DEPTH, 2, 96], F32))
        self.modb = stk.enter_context(nc.sbuf_tensor("modb_sb", [128, DEPTH, 96], F32))
        self.normg = stk.enter_context(nc.sbuf_tensor("normg_sb", [128, 2, DEPTH, DC], F32))
        self.amod = stk.enter_context(nc.sbuf_tensor("amod", [128, DEPTH, 2, 2, DC], F32))
        self.finalg = stk.enter_context(nc.sbuf_tensor("finalg_sb", [128, DC], F32))
        self.ones_bf = stk.enter_context(nc.sbuf_tensor("ones_bf", [128, 128], BF16))
        self.ones_d = stk.enter_context(nc.sbuf_tensor("ones_d", [128, 128], BF16))
        self.ones_h = stk.enter_context(nc.sbuf_tensor("ones_h", [128, 128], BF16))
        self.zero_c = stk.enter_context(nc.sbuf_tensor("zero_c", [128, 1], F32))
        self.eps_c = stk.enter_context(nc.sbuf_tensor("eps_c", [128, 1], F32))

    def stage_mod(self):
        nc, S = self.nc, self.S
        if S.stopped:
            return
        with ExitStack() as stk:
            def sb(n, s, d):
                return stk.enter_context(nc.sbuf_tensor(self.nm(n), s, d))
            cv = sb("cv", [128, DC, 2], F32)
            sg = sb("sg", [128, DC, 2], F32)
            sT = sb("sT", [128, DC, 2], BF16)
            NW = 4
            wbuf = [sb("mw", [128, DC, 512], BF16) for _ in range(NW)]
            Bw = S.bufs_n(NW)
            pst = [stk.enter_context(nc.psum_tensor(self.nm("mps"), [128, 4, 2], F32)) for _ in range(4)]
            Bp = S.bufs_n(4)
            Bcv, BsT, Bmod, Bmb, Bng, Bc = S.buf(), S.buf(), S.buf(), S.buf(), S.buf(), S.buf()
            S.op("sp", lambda e: e.dma_start(out=cv[:], in_=self.cvec), writes=[Bcv], dma=1)
            S.op("sp", lambda e: e.dma_start(out=self.modb[:], in_=self.modb_in), writes=[Bmb], dma=1)
            S.op("sp", lambda e: e.dma_start(out=self.normg[:], in_=self.normg_in), writes=[Bng], dma=1)
            S.op("sp", lambda e: e.dma_start(out=self.finalg[:], in_=self.finalg_in), writes=[Bng], dma=1)
            S.op("dve", lambda e: e.memset(self.ones_bf[:], 1.0), writes=[Bc])
            S.op("dve", lambda e: e.memset(self.ones_d[:], 1.0 / D), writes=[Bc])
            S.op("dve", lambda e: e.memset(self.ones_h[:], 1.0 / HD), writes=[Bc])
            S.op("dve", lambda e: e.memset(self.zero_c[:], 0.0), writes=[Bc])
            S.op("dve", lambda e: e.memset(self.eps_c[:], EPS), writes=[Bc])
            S.op("act", lambda e: e.activation(out=sg[:], in_=cv[:], func=AF.Sigmoid), reads=[Bcv], writes=[BsT])
            S.op("dve", lambda e: e.tensor_tensor(out=sT[:], in0=sg[:], in1=cv[:], op=ALU.mult), reads=[Bcv, BsT], writes=[BsT])
            i = 0
            for l in range(self.n_layers):
                wv = self.W("mod_w", l).rearrange("(c p) n -> p c n", p=128)
                for nb in range(24):
                    s = i % NW
                    ps, bp = pst[i % 4], Bp[i % 4]
                    S.op("pool", lambda e, s=s, nb=nb, wv=wv: e.dma_start(out=wbuf[s][:], in_=wv[:, :, nb * 512:(nb + 1) * 512]),
                         writes=[Bw[s]], dma=1)
                    for oc in range(4):
                        for kc in range(DC):
                            S.op("pe", lambda e, s=s, oc=oc, kc=kc, ps=ps: e.matmul(
                                ps[:, oc, :], lhsT=wbuf[s][:, kc, oc * 128:(oc + 1) * 128], rhs=sT[:, kc, :],
                                start=(kc == 0), stop=(kc == DC - 1)), reads=[Bw[s], BsT], writes=[bp])
                    for g in range(2):
                        S.op("dve", lambda e, l=l, g=g, nb=nb, ps=ps: e.tensor_tensor(
                            out=self.mod[:, l, g, nb * 4:(nb + 1) * 4], in0=ps[:, :, g],
                            in1=self.modb[:, l, nb * 4:(nb + 1) * 4], op=ALU.add), reads=[bp, Bmb], writes=[Bmod])
                    i += 1
            for l in range(self.n_layers):
                for g in range(2):
                    for w in range(2):
                        sc = 16 + 48 * w
                        S.op("dve", lambda e, l=l, g=g, w=w, sc=sc: e.scalar_tensor_tensor(
                            out=self.amod[:, l, g, w, :], in0=self.mod[:, l, g, sc:sc + 16], scalar=1.0,
                            in1=self.normg[:, w, l, :], op0=ALU.add, op1=ALU.mult), reads=[Bmod, Bng], writes=[Bc])
            S.flush()

    def modcol(self, l, g, which, c):
        return self.mod[:, l, g, which * 16 + c: which * 16 + c + 1]

    def stage_copy_in(self):
        nc, S = self.nc, self.S
        if S.stopped:
            return
        with ExitStack() as stk:
            xs = [stk.enter_context(nc.sbuf_tensor(self.nm("cpx"), [128, DC, TW], F32)) for _ in range(3)]
            Bx = S.bufs_n(3)
            xi = self.xT_in.rearrange("(c p) t -> p c t", p=128)
            xo = self.xT.rearrange("(c p) t -> p c t", p=128)
            for tt in range(NTT):
                k = tt % 3
                S.op("sp", lambda e, k=k, tt=tt: e.dma_start(out=xs[k][:], in_=xi[:, :, tt * TW:(tt + 1) * TW]), writes=[Bx[k]], dma=1)
                S.op("sp", lambda e, k=k, tt=tt: e.dma_start(out=xo[:, :, tt * TW:(tt + 1) * TW], in_=xs[k][:]), reads=[Bx[k]], dma=1)
            S.flush()

    def stage_norm(self, l, w, final=False):
        nc, S = self.nc, self.S
        if S.stopped:
            return
        S.DEFER = S.DEFER0
        with ExitStack() as stk:
            def sb(n, s, d):
                return stk.enter_context(nc.sbuf_tensor(self.nm(n), s, d))
            NX = 2
            xin = [sb("xin", [128, DC, TW], F32) for _ in range(NX)]
            Bx = [S.bufs_n(4) for _ in range(NX)]
            sq = [sb("sq", [128, DC, TW], BF16) for _ in range(2)]
            Bsq = [S.bufs_n(4) for _ in range(2)]
            pss = [stk.enter_context(nc.psum_tensor(self.nm("nps"), [128, TW], F32)) for _ in range(2)]
            Bps = S.bufs_n(2)
            rstd = [sb("rstd", [128, TW], F32) for _ in range(2)]
            Br = S.bufs_n(2)
            tmp = Rot([sb("ntmp", [128, TW], F32) for _ in range(4)], S.bufs_n(4))
            odt = F32 if final else BF16
            hs = [sb("hs", [128, DC, TW], odt) for _ in range(2)]
            Bh = [S.bufs_n(4) for _ in range(2)]
            xv = (self.xT_in if (l == 0 and w == 0 and not final) else self.xT).rearrange("(c p) t -> p c t", p=128)
            ov = (self.yT if final else self.hT).rearrange("(c p) t -> p c t", p=128)

            def load(tt):
                k = tt % NX
                for q in range(4):
                    S.op("sp", lambda e, k=k, q=q, tt=tt: e.dma_start(
                        out=xin[k][:, 4 * q:4 * q + 4, :], in_=xv[:, 4 * q:4 * q + 4, tt * TW:(tt + 1) * TW]),
                        writes=[Bx[k][q]], dma=1)
            load(0)
            for tt in range(NTT):
                if tt + 1 < NTT:
                    load(tt + 1)
                g = 0 if tt < 4 else 1
                k = tt % NX
                k2 = tt % 2
                for q in range(4):
                    S.op("act", lambda e, k=k, k2=k2, q=q: e.activation(
                        out=sq[k2][:, 4 * q:4 * q + 4, :], in_=xin[k][:, 4 * q:4 * q + 4, :], func=AF.Square),
                        reads=[Bx[k][q]], writes=[Bsq[k2][q]])
                for c in range(DC):
                    S.op("pe", lambda e, k2=k2, c=c: e.matmul(pss[k2][:], lhsT=self.ones_d[:], rhs=sq[k2][:, c, :],
                                                             start=(c == 0), stop=(c == DC - 1)),
                         reads=[Bsq[k2][c // 4]], writes=[Bps[k2]])
                S.op("act", lambda e, k2=k2: e.activation(out=rstd[k2][:], in_=pss[k2][:], func=AF.Sqrt, bias=self.eps_c[:, 0:1]),
                     reads=[Bps[k2]], writes=[Br[k2]])
                S.op("dve", lambda e, k2=k2: e.reciprocal(out=rstd[k2][:], in_=rstd[k2][:]), reads=[Br[k2]], writes=[Br[k2]])
                for c in range(DC):
                    tm, Btm = tmp.next()
                    S.op("dve", lambda e, k=k, k2=k2, c=c, tm=tm: e.tensor_tensor(
                        out=tm[:], in0=xin[k][:, c, :], in1=rstd[k2][:], op=ALU.mult),
                        reads=[Bx[k][c // 4], Br[k2]], writes=[Btm])
                    if final:
                        sc_ap, bi_ap = self.finalg[:, c:c + 1], self.zero_c[:, 0:1]
                    else:
                        sc_ap, bi_ap = self.amod[:, l, g, w, c:c + 1], self.modcol(l, g, 3 * w, c)
                    if c % 3 == 2:
                        S.op("dve", lambda e, k2=k2, c=c, tm=tm, sc_ap=sc_ap, bi_ap=bi_ap: e.tensor_scalar(
                            out=hs[k2][:, c, :], in0=tm[:], scalar1=sc_ap, scalar2=bi_ap, op0=ALU.mult, op1=ALU.add),
                            reads=[Btm], writes=[Bh[k2][c // 4]])
                    else:
                        S.op("act", lambda e, k2=k2, c=c, tm=tm, sc_ap=sc_ap, bi_ap=bi_ap: e.activation(
                            out=hs[k2][:, c, :], in_=tm[:], func=AF.Identity, bias=bi_ap, scale=sc_ap),
                            reads=[Btm], writes=[Bh[k2][c // 4]])
                for q in range(4):
                    S.op("sp", lambda e, k2=k2, q=q, tt=tt: e.dma_start(
                        out=ov[:, 4 * q:4 * q + 4, tt * TW:(tt + 1) * TW], in_=hs[k2][:, 4 * q:4 * q + 4, :]),
                        reads=[Bh[k2][q]], dma=1)
            S.flush()

    def gemm(self, A_dram, KC, blocks, epilogue, stk, supertiles=None, nw=3, extra=None):
        self.S.DEFER = self.S.DEFER0
        nc, S = self.nc, self.S
        if supertiles is None:
            supertiles = [list(range(NTT))]
        maxt = max(len(s) for s in supertiles)
        bw = max(sum(n for _, _, n in segs) for segs, _ in blocks)
        A_sb = stk.enter_context(nc.sbuf_tensor(self.nm("A"), [128, KC, maxt * TW], BF16))
        BA = S.bufs_n(maxt)
        wb = [stk.enter_context(nc.sbuf_tensor(self.nm("W"), [128, KC, bw], BF16)) for _ in range(nw)]
        Bw = S.bufs_n(nw)
        npb = 8 if extra is None else 8 - extra
        banks = Rot([stk.enter_context(nc.psum_tensor(self.nm("gps"), [128, TW], F32)) for _ in range(npb)], S.bufs_n(npb))
        Av = A_dram.rearrange("(c p) t -> p c t", p=128)
        wi = 0
        for st in supertiles:
            for j, tt in enumerate(st):
                npc = 4 if KC > 16 else 2
                h = KC // npc
                S.op("sp", lambda e, j=j, tt=tt, h=h, npc=npc: [
                    e.dma_start(out=A_sb[:, i * h:(i + 1) * h, j * TW:(j + 1) * TW], in_=Av[:, i * h:(i + 1) * h, tt * TW:(tt + 1) * TW])
                    for i in range(npc)], writes=[BA[j]], dma=npc)
            for bi, (segs, groups) in enumerate(blocks):
                s = wi % nw
                wi += 1

                nsp = 4 if KC > 16 else 1
                hk = KC // nsp

                def wload(e, s=s, segs=segs, nsp=nsp, hk=hk):
                    r = []
                    off = 0
                    for W_ap, c0, n in segs:
                        Wv = W_ap.rearrange("(c p) n -> p c n", p=128)
                        for i in range(nsp):
                            r.append(e.dma_start(out=wb[s][:, i * hk:(i + 1) * hk, off:off + n], in_=Wv[:, i * hk:(i + 1) * hk, c0:c0 + n]))
                        off += n
                    return r
                S.op("pool", wload, writes=[Bw[s]], dma=len(segs) * nsp)
                for j, tt in enumerate(st):
                    for gi, grp in enumerate(groups):
                        pl = []
                        for coff in grp:
                            ps, Bp = banks.next()
                            for kc in range(KC):
                                S.op("pe", lambda e, ps=ps, s=s, kc=kc, coff=coff, j=j: e.matmul(
                                    ps[:], lhsT=wb[s][:, kc, coff:coff + 128], rhs=A_sb[:, kc, j * TW:(j + 1) * TW],
                                    start=(kc == 0), stop=(kc == KC - 1)), reads=[Bw[s], BA[j]], writes=[Bp])
                            pl.append((ps, Bp))
                        epilogue(bi, gi, tt, pl)
        return A_sb, BA, wb, Bw

    def make_resid_epilogue(self, stk, l, which, chunk_of):
        nc, S = self.nc, self.S
        xt = Rot([stk.enter_context(nc.sbuf_tensor(self.nm("xr"), [128, TW], F32)) for _ in range(6)], S.bufs_n(6))
        xv = self.xT.rearrange("(c p) t -> p c t", p=128)
        xsrc = (self.xT_in if (l == 0 and which == 2) else self.xT).rearrange("(c p) t -> p c t", p=128)

        def epi(bi, gi, tt, pl):
            oc = chunk_of(bi, gi)
            g = 0 if tt < 4 else 1
            (ps, Bp), = pl
            x, Bx = xt.next()
            S.op("sp", lambda e: e.dma_start(out=x[:], in_=xsrc[:, oc, tt * TW:(tt + 1) * TW]), writes=[Bx], dma=1)
            S.op("dve", lambda e: e.scalar_tensor_tensor(out=x[:], in0=ps[:], scalar=self.modcol(l, g, which, oc), in1=x[:],
                                                         op0=ALU.mult, op1=ALU.add), reads=[Bp, Bx], writes=[Bx])
            S.op("sp", lambda e: e.dma_start(out=xv[:, oc, tt * TW:(tt + 1) * TW], in_=x[:]), reads=[Bx], dma=1)
        return epi

    def stage_ffn(self, l):
        nc, S = self.nc, self.S
        self.stage_norm(l, 1)
        Wgu = self.W("ffn_w_gu", l)
        if S.stopped:
            return
        with ExitStack() as stk:
            sg = Rot([stk.enter_context(nc.sbuf_tensor(self.nm("sg"), [128, TW], F32)) for _ in range(3)], S.bufs_n(3))
            ao = Rot([stk.enter_context(nc.sbuf_tensor(self.nm("ao"), [128, TW], BF16)) for _ in range(4)], S.bufs_n(4))
            av = self.actT.rearrange("(c p) t -> p c t", p=128)
            blocks = []
            for jb in range(FC // 2):
                blocks.append(([(Wgu, jb * 256, 256), (Wgu, DFF + jb * 256, 256)], [(0, 256), (128, 384)]))

            def epi(bi, gi, tt, pl):
                j = bi * 2 + gi
                (pg, Bg), (pu, Bu) = pl
                s, Bs = sg.next()
                a, Ba = ao.next()
                S.op("act", lambda e: e.activation(out=s[:], in_=pg[:], func=AF.Sigmoid), reads=[Bg], writes=[Bs])
                S.op("dve", lambda e: e.tensor_tensor(out=s[:], in0=s[:], in1=pg[:], op=ALU.mult), reads=[Bs, Bg], writes=[Bs])
                S.op("dve", lambda e: e.tensor_tensor(out=a[:], in0=s[:], in1=pu[:], op=ALU.mult), reads=[Bs, Bu], writes=[Ba])
                S.op("sp", lambda e: e.dma_start(out=av[:, j, tt * TW:(tt + 1) * TW], in_=a[:]), reads=[Ba], dma=1)
            self.gemm(self.hT, DC, blocks, epi, stk)
            S.flush()
        Wd = self.W("ffn_w_down", l)
        if S.stopped:
            return
        with ExitStack() as stk:
            blocks = [([(Wd, b * 256, 256)], [(0,), (128,)]) for b in range(8)]
            epi = self.make_resid_epilogue(stk, l, 5, lambda bi, gi: bi * 2 + gi)
            self.gemm(self.actT, FC, blocks, epi, stk, supertiles=[[0, 1], [2, 3], [4, 5]], nw=3)
            S.flush()

    def stage_attn(self, l, kind, j):
        nc, S = self.nc, self.S
        self.stage_norm(l, 0)
        win = (kind == 0)
        Wqkv = self.W("win_wqkv" if win else "ax_wqkv", j)
        Wo = self.W("win_wo" if win else "ax_wo", j)
        so_k = (self.swk if win else self.sak)
        so_v = (self.swv if win else self.sav)
        if S.stopped:
            return
        with ExitStack() as stk:
            def sb(n, s, d):
                return stk.enter_context(nc.sbuf_tensor(self.nm(n), s, d))
            rope = sb("rope", [128, 2, TL], F32)
            ropeP = sb("ropeP", [128, 128], BF16)
            axg = sb("axg", [128, 2], F32)
            kgrep = sb("kgrep", [128, 128], F32)
            Bc = S.buf()
            S.op("sp", lambda e: e.dma_start(out=rope[:], in_=self.rope_in), writes=[Bc], dma=1)
            S.op("sp", lambda e: e.dma_start(out=ropeP[:], in_=self.ropeP_in), writes=[Bc], dma=1)
            S.op("sp", lambda e: e.dma_start(out=axg[:], in_=self.axg_in), writes=[Bc], dma=1)
            S.op("sp", lambda e: e.dma_start(out=kgrep[:], in_=self.axkg_rep), writes=[Bc], dma=1)
            qn = Rot([sb("qn", [128, TW], BF16) for _ in range(3)], S.bufs_n(3))
            sqh = Rot([sb("sqh", [128, TW], BF16) for _ in range(2)], S.bufs_n(2))
            rs = Rot([sb("rs", [128, TW], F32) for _ in range(2)], S.bufs_n(2))
            t1 = Rot([sb("t1", [128, TW], F32) for _ in range(2)], S.bufs_n(2))
            t2 = Rot([sb("t2", [128, TW], F32) for _ in range(2)], S.bufs_n(2))
            qo = Rot([sb("qo", [128, TW], BF16) for _ in range(3)], S.bufs_n(3))
            xps = Rot([stk.enter_context(nc.psum_tensor(self.nm("xps"), [128, TW], F32)) for _ in range(3)], S.bufs_n(3))
            qv = self.qT.rearrange("(c p) t -> p c t", p=128)
            kv = self.kT.rearrange("(c p) t -> p c t", p=128)
            blocks = [([(Wqkv, b * 512, 512)], [(0,), (128,), (256,), (384,)]) for b in range(5)]

            def epi(bi, gi, tt, pl):
                oc = bi * 4 + gi
                (ps, Bp), = pl
                isk = oc >= NH
                dst = (kv[:, oc - NH, tt * TW:(tt + 1) * TW] if isk else qv[:, oc, tt * TW:(tt + 1) * TW])
                lat = tt < 4
                q_, Bq = qn.next()
                if win:
                    S.op("act", lambda e: e.activation(out=q_[:], in_=ps[:], func=AF.Copy), reads=[Bp], writes=[Bq])
                else:
                    s_, Bs = sqh.next()
                    r_, Br = rs.next()
                    m_, Bm = xps.next()
                    S.op("act", lambda e: e.activation(out=s_[:], in_=ps[:], func=AF.Square), reads=[Bp], writes=[Bs])
                    S.op("pe", lambda e: e.matmul(m_[:], lhsT=self.ones_h[:], rhs=s_[:], start=True, stop=True), reads=[Bs], writes=[Bm])
                    S.op("act", lambda e: e.activation(out=r_[:], in_=m_[:], func=AF.Sqrt, bias=self.eps_c[:, 0:1]),
                         reads=[Bm], writes=[Br])
                    S.op("dve", lambda e: e.reciprocal(out=r_[:], in_=r_[:]), reads=[Br], writes=[Br])
                    gcol = axg[:, 1:2] if isk else axg[:, 0:1]
                    S.op("dve", lambda e: e.scalar_tensor_tensor(out=q_[:], in0=ps[:], scalar=gcol, in1=r_[:], op0=ALU.mult, op1=ALU.mult),
                         reads=[Bp, Br, Bc], writes=[Bq])
                if not lat or os.environ.get('KDBG_NOROPE'):
                    S.op("sp", lambda e: e.dma_start(out=dst, in_=q_[:]), reads=[Bq], dma=1)
                    return
                m_, Bm = xps.next()
                a_, Ba = t1.next()
                b_, Bb = t2.next()
                o_, Bo = qo.next()
                S.op("pe", lambda e: e.matmul(m_[:], lhsT=ropeP[:], rhs=q_[:], start=True, stop=True), reads=[Bq, Bc], writes=[Bm])
                S.op("dve", lambda e: e.tensor_tensor(out=a_[:], in0=q_[:], in1=rope[:, 0, tt * TW:(tt + 1) * TW], op=ALU.mult),
                     reads=[Bq, Bc], writes=[Ba])
                S.op("dve", lambda e: e.tensor_tensor(out=b_[:], in0=m_[:], in1=rope[:, 1, tt * TW:(tt + 1) * TW], op=ALU.mult),
                     reads=[Bm, Bc], writes=[Bb])
                S.op("pool", lambda e: e.tensor_tensor(out=o_[:], in0=a_[:], in1=b_[:], op=ALU.add), reads=[Ba, Bb], writes=[Bo])
                S.op("sp", lambda e: e.dma_start(out=dst, in_=o_[:]), reads=[Bo], dma=1)
            A_sb, BA, wkv, Bwkv = self.gemm(self.hT, DC, blocks, epi, stk, nw=2, extra=3)
            for i_, c0 in enumerate((2048, 2560)):
                S.op("pool", lambda e, i_=i_, c0=c0: e.dma_start(
                    out=wkv[i_][:], in_=Wqkv.rearrange("(c p) n -> p c n", p=128)[:, :, c0:c0 + 512]), writes=[Bwkv[i_]], dma=1)
            vb = Rot([sb("vb", [128, 512], BF16) for _ in range(3)], S.bufs_n(3))
            vf = Rot([sb("vf", [128, 512], F32) for _ in range(3)], S.bufs_n(3))
            ssq = Rot([sb("ssq", [128, 4], F32) for _ in range(2)], S.bufs_n(2))
            junk = Rot([sb("junk", [128, 128], F32) for _ in range(2)], S.bufs_n(2))
            for tc in range(0 if not os.environ.get('KDBG_NOVK') else 999, int(os.environ.get('KDBG_VKMAX', T // 128))):
                ctx = tc >= TL // 128
                for which in ((1, 0) if ctx else (1,)):
                    ps, Bp = xps.next()
                    for kc in range(DC):
                        S.op("pe", lambda e, ps=ps, kc=kc, tc=tc, which=which: e.matmul(
                            ps[:], lhsT=A_sb[:, kc, tc * 128:(tc + 1) * 128], rhs=wkv[which][:, kc, :],
                            start=(kc == 0), stop=(kc == DC - 1)), reads=[BA[tc // 4], Bwkv[which]], writes=[Bp])
                    if which == 1:
                        v_, Bv = vb.next()
                        S.op("act", lambda e, v_=v_, ps=ps: e.activation(out=v_[:], in_=ps[:], func=AF.Copy), reads=[Bp], writes=[Bv])
                        S.op("sp", lambda e, v_=v_, tc=tc: e.dma_start(out=self.vtm[tc * 128:(tc + 1) * 128, :], in_=v_[:]), reads=[Bv], dma=1)
                    if ctx:
                        sq_, half = divmod(tc - TL // 128, 2)
                        f_, Bf = vf.next()
                        if which == 1 or win:
                            S.op("act", lambda e, f_=f_, ps=ps: e.activation(out=f_[:], in_=ps[:], func=AF.Copy), reads=[Bp], writes=[Bf])
                        else:
                            s_, Bs = ssq.next()
                            jk, Bj = junk.next()
                            for h in range(NKV):
                                S.op("act", lambda e, h=h, jk=jk, ps=ps, s_=s_: e.activation(
                                    out=jk[:], in_=ps[:, h * 128:(h + 1) * 128], func=AF.Square, accum_out=s_[:, h:h + 1]),
                                    reads=[Bp], writes=[Bj, Bs])
                            S.op("dve", lambda e, s_=s_: e.tensor_scalar(out=s_[:], in0=s_[:], scalar1=1.0 / HD, scalar2=EPS,
                                                                        op0=ALU.mult, op1=ALU.add), reads=[Bs], writes=[Bs])
                            S.op("act", lambda e, s_=s_: e.activation(out=s_[:], in_=s_[:], func=AF.Sqrt), reads=[Bs], writes=[Bs])
                            S.op("dve", lambda e, s_=s_: e.reciprocal(out=s_[:], in_=s_[:]), reads=[Bs], writes=[Bs])
                            for h in range(NKV):
                                S.op("dve", lambda e, h=h, f_=f_, ps=ps, s_=s_: e.scalar_tensor_tensor(
                                    out=f_[:, h * 128:(h + 1) * 128], in0=ps[:, h * 128:(h + 1) * 128], scalar=s_[:, h:h + 1],
                                    in1=kgrep[:], op0=ALU.mult, op1=ALU.mult), reads=[Bp, Bs, Bc], writes=[Bf])
                        dst = (so_v if which == 1 else so_k)[sq_, j, half * 128:(half + 1) * 128, :]
                        if not os.environ.get('KDBG_NOSTATE'):
                            S.op("sp", lambda e, f_=f_, dst=dst: e.dma_start(out=dst, in_=f_[:]), reads=[Bf], dma=1)
            S.flush()
        self.attn_core(l, kind, j)
        if S.stopped:
            return
        with ExitStack() as stk:
            blocks = [([(Wo, b * 512, 512)], [(0,), (128,), (256,), (384,)]) for b in range(4)]
            epi = self.make_resid_epilogue(stk, l, 2, lambda bi, gi: bi * 4 + gi)
            self.gemm(self.oT, DC, blocks, epi, stk)
            S.flush()

    def attn_core(self, l, kind, j):
        nc, S = self.nc, self.S
        win = (kind == 0)
        ckT = (self.cwkT if win else self.cakT)[j]
        cv_ = (self.cwv if win else self.cav)[j]
        scale = HD ** -0.5
        if S.stopped:
            return
        S.DEFER = S.DEFER0
        with ExitStack() as stk:
            def sb(n, s, d):
                return stk.enter_context(nc.sbuf_tensor(self.nm(n), s, d))
            NKC = T // 128
            k_sb = sb("k_sb", [128, NKV, T + PAST], BF16)
            v_sb = sb("v_sb", [128, NKC + 4, 512], BF16)
            Bk, Bv, Bc = S.buf(), S.buf(), S.buf()
            S.op("sp", lambda e: e.dma_start(out=k_sb[:, :, 0:T], in_=self.kT.rearrange("(g p) t -> p g t", p=128)), writes=[Bk], dma=1)
            S.op("pool", lambda e: e.dma_start(out=k_sb[:, :, T:T + PAST], in_=ckT.rearrange("g p t -> p g t")), writes=[Bk], dma=1)
            S.op("sp", lambda e: e.dma_start(out=v_sb[:, 0:NKC, :], in_=self.vtm.rearrange("(c p) n -> p c n", p=128)), writes=[Bv], dma=1)
            S.op("pool", lambda e: e.dma_start(out=v_sb[:, NKC:NKC + 4, :], in_=cv_.rearrange("(c p) n -> p c n", p=128)), writes=[Bv], dma=1)
            wmask = sb("wmask", [128, 2, 512], BF16)
            S.op("sp", lambda e: e.dma_start(out=wmask[:], in_=self.wmask_in), writes=[Bc], dma=1)
            esk = None
            if win:
                esr = sb("esr", [1, NH * 128], F32)
                esk = sb("esk", [1, NH * 128], BF16)
                ones1 = sb("ones1", [1, 128], BF16)
                S.op("sp", lambda e: e.dma_start(out=esr[:], in_=self.esink_in[j]), writes=[Bc], dma=1)
                S.op("act", lambda e: e.activation(out=esr[:], in_=esr[:], func=AF.Exp), reads=[Bc], writes=[Bc])
                esl = sb("esl", [1, NH * 128], BF16)
                esf = sb("esf", [1, NH * 128], F32)
                S.op("act", lambda e: e.activation(out=esk[:], in_=esr[:], func=AF.Copy), reads=[Bc], writes=[Bc])
                S.op("act", lambda e: e.activation(out=esf[:], in_=esk[:], func=AF.Copy), reads=[Bc], writes=[Bc])
                S.op("dve", lambda e: e.tensor_tensor(out=esf[:], in0=esr[:], in1=esf[:], op=ALU.subtract), reads=[Bc], writes=[Bc])
                S.op("act", lambda e: e.activation(out=esl[:], in_=esf[:], func=AF.Copy), reads=[Bc], writes=[Bc])
                S.op("dve", lambda e: e.memset(ones1[:], 1.0), writes=[Bc])
            q_sb = [sb("q_sb", [128, 4, T], BF16) for _ in range(2)]
            Bq = S.bufs_n(2)
            o_sb = [sb("o_sb", [128, 4, T], BF16) for _ in range(2)]
            Bo = S.bufs_n(2)
            pT = Rot([sb("pT", [128, 512], BF16) for _ in range(4)], S.bufs_n(4))
            rc = Rot([sb("rc", [128, 512], F32) for _ in range(2)], S.bufs_n(2))
            sps = Rot([stk.enter_context(nc.psum_tensor(self.nm("sps"), [128, 512], F32)) for _ in range(3)], S.bufs_n(3))
            dps = Rot([stk.enter_context(nc.psum_tensor(self.nm("dps"), [128, 512], F32)) for _ in range(2)], S.bufs_n(2))
            ops_ = Rot([stk.enter_context(nc.psum_tensor(self.nm("ops"), [128, 512], F32)) for _ in range(2)], S.bufs_n(2))
            seqs = [(0, TL, True)] + [(TL + 256 * s, 256, False) for s in range(4)]
            qv = self.qT.rearrange("(h p) t -> p h t", p=128)
            ov = self.oT.rearrange("(h p) t -> p h t", p=128)
            tasks = []
            for g in range(NKV):
                for (t0, ln, has_cache) in seqs:
                    nqb = ln // 128
                    for qb in range(nqb):
                        chunks = []
                        if win and has_cache:
                            for d_, mk in ((-1, 0), (0, None), (1, 1)):
                                kb = qb + d_
                                if 0 <= kb < nqb:
                                    chunks.append((t0 + kb * 128, (t0 + kb * 128) // 128, mk))
                        else:
                            for kb in range(nqb):
                                chunks.append((t0 + kb * 128, (t0 + kb * 128) // 128, None))
                        if has_cache:
                            for c in range(4):
                                chunks.append((T + c * 128, NKC + c, None))
                        tasks.append((g, t0 + qb * 128, chunks))
            flat = [(ti, ci) for ti, tk in enumerate(tasks) for ci in range(len(tk[2]))]
            LOOK = 2
            loaded = set()
            sbank = {}
            tstate = {}

            def load_q(g):
                if g < NKV and g not in loaded:
                    loaded.add(g)
                    S.op("sp", lambda e, g=g: e.dma_start(out=q_sb[g % 2][:], in_=qv[:, 4 * g:4 * g + 4, :]), writes=[Bq[g % 2]], dma=1)

            def emit_s(idx):
                ti, ci = flat[idx]
                g, q0, chunks = tasks[ti]
                load_q(g)
                kc0 = chunks[ci][0]
                sp_, Bs = sps.next()
                sbank[idx] = (sp_, Bs)
                qs, Bqs = q_sb[g % 2], Bq[g % 2]
                S.op("pe", lambda e, sp_=sp_, kc0=kc0, g=g, qs=qs, q0=q0: e.matmul(
                    sp_[:].rearrange("p (h q) -> p h q", h=4), lhsT=k_sb[:, g, kc0:kc0 + 128], rhs=qs[:, :, q0:q0 + 128],
                    start=True, stop=True), reads=[Bk, Bqs], writes=[Bs])

            for idx in range(min(LOOK, len(flat))):
                emit_s(idx)
            for idx in range(len(flat)):
                if idx + LOOK < len(flat):
                    emit_s(idx + LOOK)
                ti, ci = flat[idx]
                g, q0, chunks = tasks[ti]
                n = len(chunks)
                kc0, vc, mk = chunks[ci]
                os_, Bos = o_sb[g % 2], Bo[g % 2]
                if ci == 0:
                    tstate[ti] = (dps.next(), ops_.next())
                (dp, Bd), (op_, Bop) = tstate[ti]
                sp_, Bs = sbank.pop(idx)
                p_, Bp = pT.next()
                S.op("act", lambda e, sp_=sp_, p_=p_: e.activation(out=p_[:], in_=sp_[:], func=AF.Exp, scale=scale),
                     reads=[Bs], writes=[Bp])
                if mk is not None:
                    S.op("pool", lambda e, p_=p_, mk=mk: e.tensor_tensor(out=p_[:], in0=p_[:], in1=wmask[:, mk, :], op=ALU.mult),
                         reads=[Bp, Bc], writes=[Bp])
                last = (ci == n - 1) and not win
                S.op("pe", lambda e, dp=dp, p_=p_, ci=ci, last=last: e.matmul(
                    dp[:], lhsT=self.ones_bf[:], rhs=p_[:], start=(ci == 0), stop=last), reads=[Bp], writes=[Bd])
                S.op("pe", lambda e, op_=op_, p_=p_, ci=ci, vc=vc, g=g, n=n: e.matmul(
                    op_[:], lhsT=v_sb[:, vc, g * 128:(g + 1) * 128], rhs=p_[:], start=(ci == 0), stop=(ci == n - 1)),
                    reads=[Bp, Bv], writes=[Bop])
                if ci == n - 1:
                    if win:
                        S.op("pe", lambda e, dp=dp, g=g: e.matmul(dp[:], lhsT=ones1[:], rhs=esk[:, g * 512:(g + 1) * 512],
                                                                  start=False, stop=False), reads=[Bc], writes=[Bd])
                        S.op("pe", lambda e, dp=dp, g=g: e.matmul(dp[:], lhsT=ones1[:], rhs=esl[:, g * 512:(g + 1) * 512],
                                                                  start=False, stop=True), reads=[Bc], writes=[Bd])
                    r_, Br = rc.next()
                    S.op("dve", lambda e, r_=r_, dp=dp: e.reciprocal(out=r_[:], in_=dp[:]), reads=[Bd], writes=[Br])
                    S.op("dve", lambda e, r_=r_, op_=op_, os_=os_, q0=q0: e.tensor_tensor(
                        out=os_[:, :, q0:q0 + 128], in0=op_[:].rearrange("p (h q) -> p h q", h=4),
                        in1=r_[:].rearrange("p (h q) -> p h q", h=4), op=ALU.mult), reads=[Bop, Br], writes=[Bos])
                    del tstate[ti]
                    if ti + 1 == len(tasks) or tasks[ti + 1][0] != g:
                        S.op("sp", lambda e, g=g, os_=os_: e.dma_start(out=ov[:, 4 * g:4 * g + 4, :], in_=os_[:]), reads=[Bos], dma=1)
            S.flush()

    def build(self):
        nc = self.nc
        self.declare()
        with ExitStack() as stk:
            self.S = Sched(nc, stk)
            self.S.stop = self.stop
            self.persistent(stk)
            try:
                self.stage_mod()
                for l in range(self.n_layers):
                    kind, j = l % 3, l // 3
                    if kind == 1:
                        self.stage_hyena(l, j)
                    else:
                        self.stage_attn(l, kind, j)
                    self.stage_ffn(l)
                self.stage_norm(0, 0, final=True)
            except StopBuild:
                pass
        return nc


    def hy_decl(self):
        if hasattr(self, "yin"):
            return
        self.yin = self.dscr("yin", [3 * D, T], F32)
        self.vfm = self.dscr("vfm", [D, T], F32)
        self.x1fm = self.dscr("x1fm", [D, T], F32)
        self.x2fm = self.dscr("x2fm", [D, T], F32)
        self.z1fm = self.dscr("z1fm", [D, T], F32)
        self.vtmh = self.dscr("vtmh", [T, D], BF16)
        self.ztmh = self.dscr("ztmh", [T, D], BF16)
        self.z2T = self.dscr("z2T", [D, T], BF16)
        self.Hs = {2048: self.dscr("Hs2048", [2, 2, 2048, D], F32), 256: self.dscr("Hs256", [2, 2, 256, D], F32)}

    def stage_hyena(self, l, j):
        nc, S = self.nc, self.S
        self.hy_decl()
        self.stage_norm(l, 0)
        if S.stopped:
            return
        Win = self.W("hy_w_in", j)
        with ExitStack() as stk:
            yo = Rot([stk.enter_context(nc.sbuf_tensor(self.nm("yo"), [128, TW], F32)) for _ in range(4)], S.bufs_n(4))
            yv = self.yin.rearrange("(c p) t -> p c t", p=128)
            blocks = [([(Win, b * 512, 512)], [(0,), (128,), (256,), (384,)]) for b in range(12)]
            cnt = [0]

            def epi(bi, gi, tt, pl):
                oc = bi * 4 + gi
                (ps, Bp), = pl
                y_, By = yo.next()
                cnt[0] += 1
                S.op("act", lambda e: e.activation(out=y_[:], in_=ps[:], func=AF.Copy), reads=[Bp], writes=[By])
                S.op("sp", lambda e: e.dma_start(out=yv[:, oc, tt * TW:(tt + 1) * TW], in_=y_[:]), reads=[By], dma=1)
            self.gemm(self.hT, DC, blocks, epi, stk)
            S.flush()
        self.hy_shortconv(j)
        self.hy_filter(j, 2048)
        self.hy_filter(j, 256)
        self.hy_conv(j, 0)
        self.hy_conv(j, 1)
        if S.stopped:
            return
        Wo = self.W("hy_wo", j)
        with ExitStack() as stk:
            blocks = [([(Wo, b * 512, 512)], [(0,), (128,), (256,), (384,)]) for b in range(4)]
            epi = self.make_resid_epilogue(stk, l, 2, lambda bi, gi: bi * 4 + gi)
            self.gemm(self.z2T, DC, blocks, epi, stk)
            S.flush()

    SEQS = [(0, TL)] + [(TL + 256 * s_, 256) for s_ in range(4)]

    def hy_shortconv(self, j):
        nc, S = self.nc, self.S
        if S.stopped:
            return
        S.DEFER = 0
        with ExitStack() as stk:
            def sb(n, s, d):
                return stk.enter_context(nc.sbuf_tensor(self.nm(n), s, d))
            cw = sb("cw", [128, 48, 4], F32)
            ident = sb("ident", [128, 128], BF16)
            Bc = S.buf()
            S.op("sp", lambda e: e.dma_start(out=cw[:], in_=self.hyconv_in), writes=[Bc], dma=1)
            S.op("sp", lambda e: e.dma_start(out=ident[:], in_=self.ident_in), writes=[Bc], dma=1)
            yi = Rot([sb("yi", [128, T], F32) for _ in range(2)], S.bufs_n(2))
            uo = Rot([sb("uo", [128, T], F32) for _ in range(2)], S.bufs_n(2))
            ub = Rot([sb("ub", [128, T], BF16) for _ in range(2)], S.bufs_n(2))
            vt = Rot([sb("vt", [128, 4, 128], BF16) for _ in range(3)], S.bufs_n(3))
            tps = Rot([stk.enter_context(nc.psum_tensor(self.nm("tps"), [128, 4, 128], BF16)) for _ in range(3)], S.bufs_n(3))
            yv = self.yin.rearrange("(c p) t -> p c t", p=128)
            dsts = [self.vfm.rearrange("(c p) t -> p c t", p=128), self.x1fm.rearrange("(c p) t -> p c t", p=128),
                    self.x2fm.rearrange("(c p) t -> p c t", p=128)]
            ynext = None
            for oc in range(48):
                if ynext is None:
                    y_, By = yi.next()
                    S.op("sp", lambda e, y_=y_, oc=oc: e.dma_start(out=y_[:], in_=yv[:, oc, :]), writes=[By], dma=1)
                else:
                    y_, By = ynext
                u_, Bu = uo.next()
                S.op("act", lambda e, y_=y_, u_=u_, oc=oc: e.activation(out=u_[:], in_=y_[:], func=AF.Identity,
                                                                       bias=cw[:, oc, 3:4], scale=cw[:, oc, 1:2]), reads=[By, Bc], writes=[Bu])
                for (a, ln) in self.SEQS:
                    b = a + ln
                    S.op("dve", lambda e, y_=y_, u_=u_, oc=oc, a=a, b=b: e.scalar_tensor_tensor(
                        out=u_[:, a + 1:b], in0=y_[:, a:b - 1], scalar=cw[:, oc, 0:1], in1=u_[:, a + 1:b], op0=ALU.mult, op1=ALU.add),
                        reads=[By, Bu, Bc], writes=[Bu])
                    S.op("dve", lambda e, y_=y_, u_=u_, oc=oc, a=a, b=b: e.scalar_tensor_tensor(
                        out=u_[:, a:b - 1], in0=y_[:, a + 1:b], scalar=cw[:, oc, 2:3], in1=u_[:, a:b - 1], op0=ALU.mult, op1=ALU.add),
                        reads=[By, Bu, Bc], writes=[Bu])
                if oc + 1 < 48:
                    ynext = yi.next()
                    S.op("sp", lambda e, y2=ynext[0], oc=oc: e.dma_start(out=y2[:], in_=yv[:, oc + 1, :]), writes=[ynext[1]], dma=1)
                S.op("sp", lambda e, u_=u_, oc=oc: e.dma_start(out=dsts[oc // 16][:, oc % 16, :], in_=u_[:]), reads=[Bu], dma=1)
                if oc < 16:
                    self.fm_to_tm(u_, Bu, oc, self.vtmh, ub, vt, tps, ident, Bc)
            S.flush()

    def fm_to_tm(self, u_, Bu, oc, dst_tm, ub, vt, tps, ident, Bc, c0=0, ncols=T):
        S = self.S
        b_, Bb = ub.next()
        S.op("act", lambda e: e.activation(out=b_[:, 0:ncols], in_=u_[:, 0:ncols], func=AF.Copy), reads=[Bu], writes=[Bb])
        dv = dst_tm.rearrange("(c p) d -> p c d", p=128)
        for q in range(ncols // 512):
            tp, Bt = tps.next()
            v_, Bv = vt.next()
            for i in range(4):
                S.op("pe", lambda e, tp=tp, i=i, q=q: e.transpose(tp[:, i, :], b_[:, q * 512 + i * 128: q * 512 + (i + 1) * 128], ident[:]),
                     reads=[Bb, Bc], writes=[Bt])
            S.op("act", lambda e, tp=tp, v_=v_: e.activation(out=v_[:], in_=tp[:], func=AF.Copy), reads=[Bt], writes=[Bv])
            tc0 = (c0 + q * 512) // 128
            S.op("sp", lambda e, v_=v_, tc0=tc0: e.dma_start(out=dv[:, tc0:tc0 + 4, oc * 128:(oc + 1) * 128], in_=v_[:]), reads=[Bv], dma=1)

    def hy_filter(self, j, L):
        nc, S = self.nc, self.S
        if S.stopped:
            return
        S.DEFER = 0
        nT = L // 128
        NW = min(512, L)
        TWO_PI = 2.0 * math.pi
        Hs = self.Hs[L]
        feats_in = self.feats2048_in if L == 2048 else self.feats256_in
        tnn_in = self.tnn2048_in if L == 2048 else self.tnn256_in
        tabF = self.tabF2048_in if L == 2048 else self.tabF256_in
        with ExitStack() as stk:
            def sb(n, s, d):
                return stk.enter_context(nc.sbuf_tensor(self.nm(n), s, d))
            feats = sb("feats", [33, L], F32)
            w1 = sb("fw1", [33, 64], F32)
            w2 = sb("fw2", [64, 64], F32)
            w3 = sb("fw3", [64, 4 * D], F32)
            fb = sb("ffb", [64, 4], F32)
            fraw = sb("fraw", [64, 4], F32)
            tnn = sb("tnn", [128, nT], F32)
            drep = sb("drep", [128, D], F32)
            a1 = sb("a1", [64, L], F32)
            a2 = sb("a2", [64, L], F32)
            Bc, Ba1, Ba2 = S.buf(), S.buf(), S.buf()
            S.op("sp", lambda e: e.dma_start(out=feats[:], in_=feats_in), writes=[Bc], dma=1)
            S.op("sp", lambda e: e.dma_start(out=w1[:], in_=self.hyw1_in), writes=[Bc], dma=1)
            S.op("sp", lambda e: e.dma_start(out=w2[:], in_=self.hyw2_in), writes=[Bc], dma=1)
            S.op("sp", lambda e: e.dma_start(out=w3[:], in_=self.hyw3_in), writes=[Bc], dma=1)
            S.op("sp", lambda e: e.dma_start(out=fraw[:], in_=self.hyfb_in), writes=[Bc], dma=1)
            S.op("sp", lambda e: e.dma_start(out=tnn[:], in_=tnn_in), writes=[Bc], dma=1)
            S.op("sp", lambda e: e.dma_start(out=drep[:], in_=self.drep_in), writes=[Bc], dma=1)
            S.op("act", lambda e: e.activation(out=fb[:, 0:1], in_=fraw[:, 0:1], func=AF.Copy), reads=[Bc], writes=[Bc])
            S.op("act", lambda e: e.activation(out=fb[:, 2:3], in_=fraw[:, 1:2], func=AF.Copy), reads=[Bc], writes=[Bc])
            S.op("dve", lambda e: e.tensor_tensor(out=fb[:, 1:2], in0=fraw[:, 2:3], in1=fraw[:, 0:1], op=ALU.mult), reads=[Bc], writes=[Bc])
            S.op("dve", lambda e: e.tensor_tensor(out=fb[:, 3:4], in0=fraw[:, 3:4], in1=fraw[:, 1:2], op=ALU.mult), reads=[Bc], writes=[Bc])
            fps = Rot([stk.enter_context(nc.psum_tensor(self.nm("fps"), [128, 512], F32)) for _ in range(4)], S.bufs_n(4))
            nps = Rot([stk.enter_context(nc.psum_tensor(self.nm("fnps"), [128, 512], F32)) for _ in range(1)], S.bufs_n(1))
            hps = Rot([stk.enter_context(nc.psum_tensor(self.nm("hps"), [128, 512], F32)) for _ in range(3)], S.bufs_n(3))
            arg = Rot([sb("arg", [64, NW], F32) for _ in range(2)], S.bufs_n(2))
            kf = Rot([sb("kf", [64, NW], F32) for _ in range(2)], S.bufs_n(2))
            ki = Rot([sb("ki", [64, NW], mybir.dt.int32) for _ in range(2)], S.bufs_n(2))

            def sin_layer(src, Bsrc, wt, kdim, col, dst, Bdst):
                for c in range(L // NW):
                    ps, Bp = fps.next()
                    a_, Ba = arg.next()
                    k_, Bk = kf.next()
                    i_, Bi = ki.next()
                    S.op("pe", lambda e, ps=ps, c=c: e.matmul(ps[0:64, 0:NW], lhsT=wt[0:kdim, :], rhs=src[0:kdim, c * NW:(c + 1) * NW],
                                                             start=True, stop=True), reads=[Bsrc, Bc], writes=[Bp])
                    S.op("act", lambda e, ps=ps, a_=a_: e.activation(out=a_[:], in_=ps[0:64, 0:NW], func=AF.Identity,
                                                                    bias=fb[:, col + 1:col + 2], scale=fb[:, col:col + 1]), reads=[Bp, Bc], writes=[Ba])
                    S.op("dve", lambda e, a_=a_, k_=k_: e.tensor_scalar(out=k_[:], in0=a_[:], scalar1=1.0 / TWO_PI, scalar2=12582912.0, op0=ALU.mult, op1=ALU.add),
                         reads=[Ba], writes=[Bk])
                    S.op("dve", lambda e, k_=k_: e.tensor_scalar(out=k_[:], in0=k_[:], scalar1=12582912.0, scalar2=None, op0=ALU.subtract),
                         reads=[Bk], writes=[Bk])
                    S.op("dve", lambda e, a_=a_, k_=k_: e.scalar_tensor_tensor(out=a_[:], in0=k_[:], scalar=-TWO_PI, in1=a_[:], op0=ALU.mult, op1=ALU.add),
                         reads=[Bk, Ba], writes=[Ba])
                    S.op("dve", lambda e, a_=a_, k_=k_: e.tensor_scalar(out=k_[:], in0=a_[:], scalar1=math.pi, scalar2=TWO_PI, op0=ALU.is_gt, op1=ALU.mult),
                         reads=[Ba], writes=[Bk])
                    S.op("dve", lambda e, a_=a_, k_=k_: e.tensor_tensor(out=a_[:], in0=a_[:], in1=k_[:], op=ALU.subtract), reads=[Ba, Bk], writes=[Ba])
                    S.op("dve", lambda e, a_=a_, k_=k_: e.tensor_scalar(out=k_[:], in0=a_[:], scalar1=-math.pi, scalar2=TWO_PI, op0=ALU.is_lt, op1=ALU.mult),
                         reads=[Ba], writes=[Bk])
                    S.op("dve", lambda e, a_=a_, k_=k_: e.tensor_tensor(out=a_[:], in0=a_[:], in1=k_[:], op=ALU.add), reads=[Ba, Bk], writes=[Ba])
                    S.op("act", lambda e, a_=a_, c=c: e.activation(out=dst[:, c * NW:(c + 1) * NW], in_=a_[:], func=AF.Sin), reads=[Ba], writes=[Bdst])
            sin_layer(feats, Bc, w1, 33, 0, a1, Ba1)
            sin_layer(a1, Ba1, w2, 64, 2, a2, Ba2)
            gp = [sb("gp", [128, nT, 512], BF16) for _ in range(2)]
            gm = [sb("gm", [128, nT, 512], BF16) for _ in range(2)]
            Bg = S.bufs_n(2)
            dec = Rot([sb("dec", [128, 512], F32) for _ in range(2)], S.bufs_n(2))
            fw = Rot([sb("fw", [128, 512], F32) for _ in range(2)], S.bufs_n(2))
            bw = Rot([sb("bw", [128, 512], F32) for _ in range(2)], S.bufs_n(2))
            ab = Rot([sb("ab", [128, 2, 512], BF16) for _ in range(2)], S.bufs_n(2))
            rn = Rot([sb("rn", [128, 512], F32) for _ in range(2)], S.bufs_n(2))
            tb = Rot([sb("ftb", [128, 2, nT, 128], BF16) for _ in range(3)], S.bufs_n(3))
            ho = Rot([sb("ho", [128, 2, 512], F32) for _ in range(2)], S.bufs_n(2))
            it = 0
            for o in range(2):
                for db in range(4):
                    k = it % 2
                    it += 1
                    npz, Bn = nps.next()
                    for tc in range(nT):
                        pf, Bpf = fps.next()
                        pb, Bpb = fps.next()
                        d_, Bd = dec.next()
                        f_, Bf = fw.next()
                        b_, Bb = bw.next()
                        a_, Bab = ab.next()
                        cf = o * D + db * 512
                        S.op("pe", lambda e, pf=pf, tc=tc, cf=cf: e.matmul(pf[:], lhsT=a2[:, tc * 128:(tc + 1) * 128], rhs=w3[:, cf:cf + 512],
                                                                          start=True, stop=True), reads=[Ba2, Bc], writes=[Bpf])
                        S.op("pe", lambda e, pb=pb, tc=tc, cf=cf: e.matmul(pb[:], lhsT=a2[:, tc * 128:(tc + 1) * 128], rhs=w3[:, 2 * D + cf:2 * D + cf + 512],
                                                                          start=True, stop=True), reads=[Ba2, Bc], writes=[Bpb])
                        S.op("act", lambda e, d_=d_, tc=tc, db=db: e.activation(out=d_[:], in_=drep[:, db * 512:(db + 1) * 512], func=AF.Exp,
                                                                               scale=tnn[:, tc:tc + 1]), reads=[Bc], writes=[Bd])
                        S.op("dve", lambda e, f_=f_, pf=pf, d_=d_: e.tensor_tensor(out=f_[:], in0=pf[:], in1=d_[:], op=ALU.mult), reads=[Bpf, Bd], writes=[Bf])
                        S.op("dve", lambda e, b_=b_, pb=pb, d_=d_: e.tensor_tensor(out=b_[:], in0=pb[:], in1=d_[:], op=ALU.mult), reads=[Bpb, Bd], writes=[Bb])
                        if tc == 0:
                            S.op("dve", lambda e, b_=b_: e.memset(b_[0:1, :], 0.0), reads=[Bb], writes=[Bb])
                        S.op("pool", lambda e, f_=f_, b_=b_, k=k, tc=tc: e.tensor_tensor(out=gp[k][:, tc, :], in0=f_[:], in1=b_[:], op=ALU.add),
                             reads=[Bf, Bb], writes=[Bg[k]])
                        S.op("pool", lambda e, f_=f_, b_=b_, k=k, tc=tc: e.tensor_tensor(out=gm[k][:, tc, :], in0=f_[:], in1=b_[:], op=ALU.subtract),
                             reads=[Bf, Bb], writes=[Bg[k]])
                        S.op("act", lambda e, f_=f_, a_=a_: e.activation(out=a_[:, 0, :], in_=f_[:], func=AF.Abs), reads=[Bf], writes=[Bab])
                        S.op("act", lambda e, b_=b_, a_=a_: e.activation(out=a_[:, 1, :], in_=b_[:], func=AF.Abs), reads=[Bb], writes=[Bab])
                        for h in range(2):
                            S.op("pe", lambda e, npz=npz, a_=a_, h=h, tc=tc: e.matmul(npz[:], lhsT=self.ones_bf[:], rhs=a_[:, h, :],
                                                                                     start=(tc == 0 and h == 0), stop=(tc == nT - 1 and h == 1)),
                                 reads=[Bab], writes=[Bn])
                    r_, Br = rn.next()
                    S.op("dve", lambda e, r_=r_, npz=npz: e.tensor_scalar(out=r_[:], in0=npz[:], scalar1=EPS, scalar2=None, op0=ALU.add),
                         reads=[Bn], writes=[Br])
                    S.op("dve", lambda e, r_=r_: e.reciprocal(out=r_[:], in_=r_[:]), reads=[Br], writes=[Br])
                    tnext = None
                    for fc in range(nT):
                        if tnext is None:
                            t_, Bt = tb.next()
                            S.op("sp", lambda e, t_=t_, fc=fc: e.dma_start(out=t_[:], in_=tabF[fc]), writes=[Bt], dma=1)
                        else:
                            t_, Bt = tnext
                        if fc + 1 < nT:
                            tnext = tb.next()
                            S.op("sp", lambda e, t2=tnext[0], fc=fc: e.dma_start(out=t2[:], in_=tabF[fc + 1]), writes=[tnext[1]], dma=1)
                        h_, Bh = ho.next()
                        for ri, gsrc in enumerate((gp, gm)):
                            hp, Bhp = hps.next()
                            for tc in range(nT):
                                S.op("pe", lambda e, hp=hp, t_=t_, ri=ri, tc=tc, gsrc=gsrc, k=k: e.matmul(
                                    hp[:], lhsT=t_[:, ri, tc, :], rhs=gsrc[k][:, tc, :], start=(tc == 0), stop=(tc == nT - 1)),
                                    reads=[Bt, Bg[k]], writes=[Bhp])
                            S.op("dve", lambda e, hp=hp, h_=h_, ri=ri, r_=r_: e.tensor_tensor(out=h_[:, ri, :], in0=hp[:], in1=r_[:], op=ALU.mult),
                                 reads=[Bhp, Br], writes=[Bh])
                        S.op("sp", lambda e, h_=h_, o=o, fc=fc, db=db: e.dma_start(
                            out=Hs[o, :, fc * 128:(fc + 1) * 128, db * 512:(db + 1) * 512].rearrange("r p d -> p r d"), in_=h_[:]),
                            reads=[Bh], dma=1)
            S.flush()

    def hy_conv(self, j, o):
        nc, S = self.nc, self.S
        if S.stopped:
            return
        S.DEFER = 0
        src_tm = self.vtmh if o == 0 else self.ztmh
        src_fm = self.vfm if o == 0 else self.z1fm
        gate_fm = self.x1fm if o == 0 else self.x2fm
        with ExitStack() as stk:
            def sb(n, s, d):
                return stk.enter_context(nc.sbuf_tensor(self.nm(n), s, d))
            skip = sb("skip", [128, 2, DC], F32)
            ident = sb("ident2", [128, 128], BF16)
            Bc = S.buf()
            S.op("sp", lambda e: e.dma_start(out=skip[:], in_=self.hyskip_in), writes=[Bc], dma=1)
            S.op("sp", lambda e: e.dma_start(out=ident[:], in_=self.ident_in), writes=[Bc], dma=1)
            vb = Rot([sb("cvb", [128, 16, 512], BF16) for _ in range(2)], S.bufs_n(2))
            yre = Rot([sb("yre", [128, 16, 512], BF16) for _ in range(1)], S.bufs_n(1))
            yim = Rot([sb("yim", [128, 16, 512], BF16) for _ in range(1)], S.bufs_n(1))
            tb = Rot([sb("ctb", [128, 2, 16, 128], BF16) for _ in range(3)], S.bufs_n(3))
            hh = Rot([sb("hh", [128, 2, 512], F32) for _ in range(2)], S.bufs_n(2))
            tt_ = [Rot([sb("cp%d" % i, [128, 512], F32) for _ in range(2)], S.bufs_n(2)) for i in range(4)]
            ti = Rot([sb("cti", [128, 2, 16, 512], BF16) for _ in range(2)], S.bufs_n(2))
            ui = Rot([sb("cui", [128, 512], F32) for _ in range(2)], S.bufs_n(2))
            gi_ = Rot([sb("cgi", [128, 512], F32) for _ in range(2)], S.bufs_n(2))
            zo = Rot([sb("czo", [128, 512], F32) for _ in range(2)], S.bufs_n(2))
            zb = Rot([sb("czb", [128, 512], BF16) for _ in range(2)], S.bufs_n(2))
            vt = Rot([sb("cvt", [128, 4, 128], BF16) for _ in range(2)], S.bufs_n(2))
            ups = Rot([stk.enter_context(nc.psum_tensor(self.nm("ups"), [128, 512], F32)) for _ in range(4)], S.bufs_n(4))
            yps = Rot([stk.enter_context(nc.psum_tensor(self.nm("yps"), [128, 512], F32)) for _ in range(2)], S.bufs_n(2))
            tps = Rot([stk.enter_context(nc.psum_tensor(self.nm("ctps"), [128, 4, 128], BF16)) for _ in range(2)], S.bufs_n(2))
            sfv = src_fm.rearrange("(c p) t -> p c t", p=128)
            gfv = gate_fm.rearrange("(c p) t -> p c t", p=128)
            z1v = self.z1fm.rearrange("(c p) t -> p c t", p=128)
            z2v = self.z2T.rearrange("(c p) t -> p c t", p=128)
            for (t0, L) in self.SEQS:
                nT = L // 128
                NW = min(512, L)
                Hs = self.Hs[L]
                tabF = self.tabF2048_in if L == 2048 else self.tabF256_in
                tabI = self.tabI2048_in if L == 2048 else self.tabI256_in
                for db in range(4):
                    v_, Bv = vb.next()
                    yr, Byr = yre.next()
                    yi, Byi = yim.next()
                    S.op("sp", lambda e, v_=v_, t0=t0, L=L, nT=nT, db=db: e.dma_start(
                        out=v_[:, 0:nT, :], in_=src_tm[t0:t0 + L, db * 512:(db + 1) * 512].rearrange("(c p) d -> p c d", p=128)),
                        writes=[Bv], dma=1)
                    for fc in range(nT):
                        t_, Bt = tb.next()
                        h_, Bh = hh.next()
                        S.op("sp", lambda e, t_=t_, fc=fc, nT=nT, tabF=tabF: e.dma_start(out=t_[:, :, 0:nT, :], in_=tabF[fc]), writes=[Bt], dma=1)
                        S.op("sp", lambda e, h_=h_, fc=fc, db=db, Hs=Hs: e.dma_start(
                            out=h_[:], in_=Hs[o, :, fc * 128:(fc + 1) * 128, db * 512:(db + 1) * 512].rearrange("r p d -> p r d")),
                            writes=[Bh], dma=1)
                        pu = []
                        for ri in range(2):
                            p_, Bp = ups.next()
                            for tc in range(nT):
                                S.op("pe", lambda e, p_=p_, t_=t_, ri=ri, tc=tc, v_=v_, nT=nT: e.matmul(
                                    p_[:], lhsT=t_[:, ri, tc, :], rhs=v_[:, tc, :], start=(tc == 0), stop=(tc == nT - 1)),
                                    reads=[Bt, Bv], writes=[Bp])
                            pu.append((p_, Bp))
                        (ur, Bur), (um, Bum) = pu
                        (a1_, Ba1), (a2_, Ba2), (a3_, Ba3), (a4_, Ba4) = [r.next() for r in tt_]
                        S.op("dve", lambda e, a1_=a1_, ur=ur, h_=h_: e.tensor_tensor(out=a1_[:], in0=ur[:], in1=h_[:, 0, :], op=ALU.mult), reads=[Bur, Bh], writes=[Ba1])
                        S.op("dve", lambda e, a2_=a2_, um=um, h_=h_: e.tensor_tensor(out=a2_[:], in0=um[:], in1=h_[:, 1, :], op=ALU.mult), reads=[Bum, Bh], writes=[Ba2])
                        S.op("dve", lambda e, a3_=a3_, um=um, h_=h_: e.tensor_tensor(out=a3_[:], in0=um[:], in1=h_[:, 0, :], op=ALU.mult), reads=[Bum, Bh], writes=[Ba3])
                        S.op("dve", lambda e, a4_=a4_, ur=ur, h_=h_: e.tensor_tensor(out=a4_[:], in0=ur[:], in1=h_[:, 1, :], op=ALU.mult), reads=[Bur, Bh], writes=[Ba4])
                        S.op("pool", lambda e, yr=yr, fc=fc, a1_=a1_, a2_=a2_: e.tensor_tensor(out=yr[:, fc, :], in0=a1_[:], in1=a2_[:], op=ALU.subtract),
                             reads=[Ba1, Ba2], writes=[Byr])
                        S.op("pool", lambda e, yi=yi, fc=fc, a3_=a3_, a4_=a4_: e.tensor_tensor(out=yi[:, fc, :], in0=a3_[:], in1=a4_[:], op=ALU.add),
                             reads=[Ba3, Ba4], writes=[Byi])
                    def ld_ug(tt, dc):
                        oc_ = db * 4 + dc
                        c0_ = t0 + tt * NW
                        u2, Bu2 = ui.next()
                        g2, Bg2 = gi_.next()
                        S.op("sp", lambda e, NW=NW: e.dma_start(out=u2[:, 0:NW], in_=sfv[:, oc_, c0_:c0_ + NW]), writes=[Bu2], dma=1)
                        S.op("sp", lambda e, NW=NW: e.dma_start(out=g2[:, 0:NW], in_=gfv[:, oc_, c0_:c0_ + NW]), writes=[Bg2], dma=1)
                        return (u2, Bu2, g2, Bg2)

                    def ld_tab(tt):
                        c2, Bc2 = ti.next()
                        S.op("sp", lambda e, NW=NW, nT=nT, tabI=tabI: [e.dma_start(
                            out=c2[:, r, 0:nT, 0:NW], in_=tabI[r, :, :, tt * NW:(tt + 1) * NW]) for r in range(2)],
                            writes=[Bc2], dma=2)
                        return (c2, Bc2)
                    its = [(tt, dc) for tt in range(L // NW) for dc in range(4)]
                    pre_ug = ld_ug(*its[0])
                    pre_tab = ld_tab(0)
                    for ii, (tt, dc) in enumerate(its):
                        if dc == 0:
                            c_, Bci = pre_tab
                            if tt + 1 < L // NW:
                                pre_tab = ld_tab(tt + 1)
                        if True:
                            oc = db * 4 + dc
                            col0 = t0 + tt * NW
                            y_, By = yps.next()
                            u_, Bu, g_, Bgt = pre_ug
                            if ii + 1 < len(its):
                                pre_ug = ld_ug(*its[ii + 1])
                            for fc in range(nT):
                                S.op("pe", lambda e, y_=y_, yr=yr, fc=fc, dc=dc, c_=c_, NW=NW: e.matmul(
                                    y_[:, 0:NW], lhsT=yr[:, fc, dc * 128:(dc + 1) * 128], rhs=c_[:, 0, fc, 0:NW], start=(fc == 0), stop=False),
                                    reads=[Byr, Bci], writes=[By])
                                S.op("pe", lambda e, y_=y_, yi=yi, fc=fc, dc=dc, c_=c_, NW=NW, nT=nT: e.matmul(
                                    y_[:, 0:NW], lhsT=yi[:, fc, dc * 128:(dc + 1) * 128], rhs=c_[:, 1, fc, 0:NW], start=False, stop=(fc == nT - 1)),
                                    reads=[Byi, Bci], writes=[By])
                            S.op("dve", lambda e, u_=u_, y_=y_, oc=oc, NW=NW: e.scalar_tensor_tensor(
                                out=u_[:, 0:NW], in0=u_[:, 0:NW], scalar=skip[:, o, oc:oc + 1], in1=y_[:, 0:NW], op0=ALU.mult, op1=ALU.add),
                                reads=[Bu, By, Bc], writes=[Bu])
                            if o == 0:
                                z_, Bz = zo.next()
                                S.op("pool", lambda e, z_=z_, u_=u_, g_=g_, NW=NW: e.tensor_tensor(out=z_[:, 0:NW], in0=u_[:, 0:NW], in1=g_[:, 0:NW], op=ALU.mult),
                                     reads=[Bu, Bgt], writes=[Bz])
                                S.op("sp", lambda e, z_=z_, oc=oc, col0=col0, NW=NW: e.dma_start(out=z1v[:, oc, col0:col0 + NW], in_=z_[:, 0:NW]), reads=[Bz], dma=1)
                                self.fm_to_tm(z_, Bz, oc, self.ztmh, zb, vt, tps, ident, Bc, c0=col0, ncols=NW) if NW == 512 else \
                                    self.fm_to_tm_small(z_, Bz, oc, zb, vt, tps, ident, Bc, col0)
                            else:
                                zb_, Bzb = zb.next()
                                S.op("pool", lambda e, zb_=zb_, u_=u_, g_=g_, NW=NW: e.tensor_tensor(out=zb_[:, 0:NW], in0=u_[:, 0:NW], in1=g_[:, 0:NW], op=ALU.mult),
                                     reads=[Bu, Bgt], writes=[Bzb])
                                S.op("sp", lambda e, zb_=zb_, oc=oc, col0=col0, NW=NW: e.dma_start(out=z2v[:, oc, col0:col0 + NW], in_=zb_[:, 0:NW]), reads=[Bzb], dma=1)
            S.flush()

    def fm_to_tm_small(self, u_, Bu, oc, ub, vt, tps, ident, Bc, c0):
        S = self.S
        b_, Bb = ub.next()
        S.op("act", lambda e: e.activation(out=b_[:, 0:256], in_=u_[:, 0:256], func=AF.Copy), reads=[Bu], writes=[Bb])
        dv = self.ztmh.rearrange("(c p) d -> p c d", p=128)
        tp, Bt = tps.next()
        v_, Bv = vt.next()
        for i in range(2):
            S.op("pe", lambda e, i=i: e.transpose(tp[:, i, :], b_[:, i * 128:(i + 1) * 128], ident[:]), reads=[Bb, Bc], writes=[Bt])
        S.op("act", lambda e: e.activation(out=v_[:, 0:2, :], in_=tp[:, 0:2, :], func=AF.Copy), reads=[Bt], writes=[Bv])
        tc0 = c0 // 128
        S.op("sp", lambda e: e.dma_start(out=dv[:, tc0:tc0 + 2, oc * 128:(oc + 1) * 128], in_=v_[:, 0:2, :]), reads=[Bv], dma=1)


def pp(v):
    v = np.asarray(v)
    n = v.shape[-1] // 128
    return np.ascontiguousarray(np.moveaxis(v.reshape(v.shape[:-1] + (n, 128)), -1, 0))


def rope_tables():
    half = HD // 2
    t = np.arange(TL)
    row = (t // 64).astype(np.float32)
    col = (t % 64).astype(np.float32)
    inv = (10000.0 ** (-np.arange(0, half, 2, dtype=np.float32) / half)).astype(np.float32)
    ang = np.zeros((128, TL), np.float32)
    ang[0:32] = inv[:, None] * row[None, :]
    ang[32:64] = inv[:, None] * row[None, :]
    ang[64:96] = inv[:, None] * col[None, :]
    ang[96:128] = inv[:, None] * col[None, :]
    tab = np.stack([np.cos(ang), np.sin(ang)], axis=1).astype(np.float32)
    P = np.zeros((128, 128), np.float32)
    for base in (0, 64):
        for m in range(32):
            P[base + m + 32, base + m] = -1.0
            P[base + m, base + m + 32] = 1.0
    return tab, P.astype(NPBF)


def window_masks():
    kj = np.arange(128)[:, None]
    qi = np.arange(128)[None, :]
    prev = (kj >= qi).astype(np.float32)
    nxt = (kj <= qi).astype(np.float32)
    m = np.stack([np.tile(prev, (1, 4)), np.tile(nxt, (1, 4))], axis=1)
    return m.astype(NPBF)


def make_in_maps(inp, kb):
    f = lambda a: np.ascontiguousarray(np.asarray(a, dtype=np.float32))
    rope, P = rope_tables()
    wm = window_masks()
    shared = {
        "normg_in": pp(np.stack([f(inp["norm_mix_g"]), f(inp["norm_ffn_g"])], 0)),
        "modb_in": pp(f(inp["mod_b"])),
        "finalg_in": pp(f(inp["final_g"])),
        "esink_in": np.ascontiguousarray(np.repeat(f(inp["win_sink"]), 128, axis=1)[:, None, :]),
        "axg_in": np.ascontiguousarray(np.stack([f(inp["ax_q_g"])[0], f(inp["ax_k_g"])[0]], axis=1)),
        "axkg_rep": np.ascontiguousarray(np.tile(f(inp["ax_k_g"])[0][None, :], (128, 1))),
        "rope_in": rope, "ropeP_in": P, "wmask_in": wm,
    }
    shared.update(hyena_shared(inp))
    for name in kb.inputs:
        if "__" in name:
            base, idx = name.split("__")
            shared[name] = np.ascontiguousarray(f(inp[base])[int(idx)])
    maps = []
    xs, xp = f(inp["x_sample"]), f(inp["x_prompt"])
    for i in range(8):
        m = dict(shared)
        xt = np.concatenate([xs[i], xp[4 * i:4 * i + 4].reshape(1024, D)], axis=0)
        m["xT_in"] = np.ascontiguousarray(xt.T)
        m["cvec"] = np.ascontiguousarray(np.stack([pp(f(inp["c"])[i]), pp(f(inp["c_ctx"]))], axis=-1))
        m["cwkT"] = np.ascontiguousarray(f(inp["cache_win_k"])[i].transpose(0, 2, 3, 1))
        m["cwv"] = np.ascontiguousarray(f(inp["cache_win_v"])[i].reshape(2, PAST, 512))
        m["cakT"] = np.ascontiguousarray(f(inp["cache_ax_k"])[i].transpose(0, 2, 3, 1))
        m["cav"] = np.ascontiguousarray(f(inp["cache_ax_v"])[i].reshape(1, PAST, 512))
        maps.append({k: v for k, v in m.items() if k in kb.inputs})
    return maps


def hyena_shared(inp):
    f = lambda a: np.ascontiguousarray(np.asarray(a, dtype=np.float32))
    out = {}
    cw = f(inp["hy_conv_w"])[0]
    cb = f(inp["hy_conv_b"])[0]
    out["hyconv_in"] = pp(np.stack([cw[0], cw[1], cw[2], cb], axis=0)).transpose(0, 2, 1).copy()
    out["ident_in"] = np.eye(128, dtype=np.float32).astype(NPBF)
    out["hyskip_in"] = pp(f(inp["hy_skip"])[0])
    out["hyw1_in"] = f(inp["hy_f_w1"])[0]
    out["hyw2_in"] = f(inp["hy_f_w2"])[0]
    out["hyw3_in"] = f(inp["hy_f_w3"])[0]
    fr = f(inp["hy_freq"])[0]
    out["hyfb_in"] = np.ascontiguousarray(np.stack([fr[0], fr[1], f(inp["hy_f_b1"])[0], f(inp["hy_f_b2"])[0]], axis=1))
    HY_MIN = math.log(1e-2) / 1.5
    HY_MAX = math.log(1e-2) / 0.3
    deltas = np.abs(np.linspace(HY_MIN, HY_MAX, D, dtype=np.float32))
    out["drep_in"] = np.ascontiguousarray(np.tile(deltas[None, :], (128, 1)).astype(np.float32))
    for L in (2048, 256):
        t = np.arange(L, dtype=np.float32)
        tn = (t / max(L - 1, 1)).astype(np.float32)
        bands = 16
        fbv = np.linspace(1e-4, bands - 1, bands, dtype=np.float32)
        w = (np.float32(2.0 * math.pi) * t / np.float32(L)).astype(np.float32)
        feats = np.concatenate([tn[:, None], np.cos(w[:, None] * fbv), -np.sin(w[:, None] * fbv)], axis=-1).astype(np.float32)
        out["feats%d_in" % L] = np.ascontiguousarray(feats.T)
        out["tnn%d_in" % L] = pp(-tn)
        nT = L // 128
        tt = np.arange(L, dtype=np.float64)
        ff = np.arange(L, dtype=np.float64)
        ang = np.pi * np.outer(tt, 2 * ff + 1) / (2 * L)
        C = np.cos(ang)
        S_ = np.sin(ang)
        CF = np.stack([C, S_], 0).reshape(2, nT, 128, nT, 128)
        out["tabF%d_in" % L] = np.ascontiguousarray(CF.transpose(3, 2, 0, 1, 4)).astype(NPBF)
        CI = np.stack([C.T, S_.T], 0) / L
        out["tabI%d_in" % L] = np.ascontiguousarray(CI.reshape(2, nT, 128, L).transpose(0, 2, 1, 3)).astype(NPBF)
    return out


_CACHE = {}


def get_kb():
    if "kb" not in _CACHE:
        kb = KB()
        kb.build()
        _CACHE["kb"] = kb
    return _CACHE["kb"]


def kernel(**inp):
    kb = get_kb()
    maps = make_in_maps(inp, kb)
    res = run_bass_kernel_spmd(kb.nc, maps, core_ids=list(range(8)))
    R = res.results
    y_s = np.stack([R[i]["yT"][:, :TL].T for i in range(8)], 0)
    y_p = np.concatenate([R[i]["yT"][:, TL:].T.reshape(4, 256, D) for i in range(8)], 0)
    outs = [np.ascontiguousarray(y_p), np.ascontiguousarray(y_s)]
    for nm_ in ("swk", "swv", "sak", "sav"):
        a = np.concatenate([R[i][nm_] for i in range(8)], 0)
        outs.append(np.ascontiguousarray(a.reshape(a.shape[0], a.shape[1], 256, NKV, HD)))
    return tuple(outs)
```

```python
import math
import os
from contextlib import ExitStack

import numpy as np
import ml_dtypes

import concourse.bass as bass
import concourse.mybir as mybir
from concourse.bass_utils import run_bass_kernel_spmd

F32 = mybir.dt.float32
BF16 = mybir.dt.bfloat16
AF = mybir.ActivationFunctionType
ALU = mybir.AluOpType
NPBF = ml_dtypes.bfloat16

D = 2048
DC = 16
T = 3072
TL = 2048
NTT = 6
TW = 512
DFF = 5632
FC = 44
NH = 16
NKV = 4
HD = 128
PAST = 512
EPS = 1e-6
DEPTH = 4
ENG_NAMES = ("pe", "act", "dve", "pool", "sp")


class Buf:
    __slots__ = ("name", "w", "r")

    def __init__(self, name=""):
        self.name = name
        self.w = None
        self.r = []


class Op:
    __slots__ = ("eng", "fn", "deps", "flag", "val", "sem", "dma")

    def __init__(self, eng, fn, dma):
        self.eng = eng
        self.fn = fn
        self.deps = []
        self.flag = False
        self.val = 0
        self.sem = None
        self.dma = dma


class Sched:
    def __init__(self, nc, stack, n_dma_sems=48):
        self.nc = nc
        self.sem = {e: stack.enter_context(nc.semaphore("s_" + e)) for e in ENG_NAMES}
        self.cnt = {e: 0 for e in ENG_NAMES}
        self.dma_sems = [stack.enter_context(nc.semaphore("s_dma%d" % i)) for i in range(n_dma_sems)]
        self.dma_val = [0] * n_dma_sems
        self.dma_last = [None] * n_dma_sems
        self.dma_rr = 0
        self.ops = {e: [] for e in ENG_NAMES}
        self.waited = {e: {} for e in ENG_NAMES}
        self.bufs = []
        self.n_ops = 0
        self.stopped = False
        self.sp_def = []
        self.DEFER = int(os.environ.get("K_DEFER", "3"))
        self.DEFER0 = self.DEFER

    def buf(self, name=""):
        b = Buf(name)
        self.bufs.append(b)
        return b

    def bufs_n(self, n, name=""):
        return [self.buf("%s%d" % (name, i)) for i in range(n)]

    def op(self, eng, fn, reads=(), writes=(), dma=0):
        o = Op(eng, fn, dma)
        deps = {}

        def add(d, war):
            if d is None or d is o:
                return
            if d.dma == 0 and d.eng == eng and dma == 0:
                if eng == "pe" or war:
                    return
            deps[id(d)] = d

        for b in reads:
            add(b.w, False)
        for b in writes:
            add(b.w, False)
            for r in b.r:
                add(r, True)
        if dma:
            k = self.dma_rr
            self.dma_rr = (k + 1) % len(self.dma_sems)
            prev = self.dma_last[k]
            if prev is not None:
                deps[id(prev)] = prev
            self.dma_val[k] += 16 * dma
            o.val = self.dma_val[k]
            o.sem = self.dma_sems[k]
            self.dma_last[k] = o
        for d in deps.values():
            d.flag = True
        o.deps = list(deps.values())
        for b in reads:
            if dma == 0:
                b.r = [r for r in b.r if not (r.dma == 0 and r.eng == eng)]
            b.r.append(o)
        for b in writes:
            b.w = o
            b.r = []
        self.n_ops += 1
        if eng == "sp":
            if reads and self.DEFER > 0:
                self.sp_def.append([o, 0])
                return o
            if self.sp_def:
                defd = set(id(x[0]) for x in self.sp_def)
                if any(id(d) in defd for d in o.deps):
                    for x in self.sp_def:
                        self.ops["sp"].append(x[0])
                    self.sp_def = []
            self.ops["sp"].append(o)
            keep = []
            for x in self.sp_def:
                x[1] += 1
                if x[1] >= self.DEFER:
                    self.ops["sp"].append(x[0])
                else:
                    keep.append(x)
            self.sp_def = keep
            return o
        self.ops[eng].append(o)
        return o

    def flush(self):
        nc = self.nc
        for x in self.sp_def:
            self.ops["sp"].append(x[0])
        self.sp_def = []
        lasts = {}
        for e in ENG_NAMES:
            for o in reversed(self.ops[e]):
                if o.dma == 0:
                    lasts[e] = o
                    o.flag = True
                    break
        for e in ENG_NAMES:
            for o in self.ops[e]:
                if o.dma == 0 and o.flag:
                    self.cnt[e] += 1
                    o.val = self.cnt[e]
                    o.sem = self.sem[e]
        dma_final = [(self.dma_sems[k], self.dma_val[k]) for k in range(len(self.dma_sems))
                     if self.dma_val[k] > 0]
        with nc.Block() as block:
            for e in ENG_NAMES:
                def body(eng, e=e):
                    waited = self.waited[e]

                    def wait(sem, val):
                        key = id(sem)
                        if waited.get(key, 0) < val:
                            eng.wait_ge(sem, val)
                            waited[key] = val

                    for o in self.ops[e]:
                        for d in o.deps:
                            wait(d.sem, d.val)
                        r = o.fn(eng)
                        if o.dma:
                            if not isinstance(r, (list, tuple)):
                                r = [r]
                            assert len(r) == o.dma, (len(r), o.dma)
                            for ins in r:
                                ins.then_inc(o.sem, 16)
                        elif o.flag:
                            r.then_inc(o.sem, 1)
                    for e2 in ENG_NAMES:
                        if e2 in lasts:
                            wait(lasts[e2].sem, lasts[e2].val)
                    for sem, val in dma_final:
                        wait(sem, val)
                name = {"pe": "tensor", "act": "scalar", "dve": "vector",
                        "pool": "gpsimd", "sp": "sync"}[e]
                getattr(block, name)(body)
        self.ops = {e: [] for e in ENG_NAMES}
        for b in self.bufs:
            b.w = None
            b.r = []
        self.bufs = []
        self.dma_last = [None] * len(self.dma_sems)
        self.nflush = getattr(self, "nflush", 0) + 1
        if self.nflush == getattr(self, "stop", -1):
            self.stopped = True


class StopBuild(Exception):
    pass


class Rot:
    def __init__(self, tiles, bufs):
        self.t = tiles
        self.b = bufs
        self.i = 0

    def next(self):
        k = self.i % len(self.t)
        self.i += 1
        return self.t[k], self.b[k]


class KB:
    def __init__(self, n_layers=DEPTH, dbg=False, stop=-1):
        self.nc = bass.Bass("TRN2", target_bir_lowering=False)
        self.stop = stop
        self.n_layers = n_layers
        self.dbg = dbg
        self.uid = 0
        self.inputs = {}

    def nm(self, s):
        self.uid += 1
        return "%s_%d" % (s, self.uid)

    def din(self, name, shape, dt=F32):
        ap = self.nc.dram_tensor(name, list(shape), dt, kind="ExternalInput").ap()
        self.inputs[name] = (tuple(shape), dt)
        return ap

    def dout(self, name, shape, dt=F32):
        return self.nc.dram_tensor(name, list(shape), dt, kind="ExternalOutput").ap()

    def dscr(self, name, shape, dt):
        return self.nc.dram_tensor(name, list(shape), dt).ap()

    IN_SPECS = {
        "xT_in": ([D, T], F32), "cvec": ([128, DC, 2], F32), "normg_in": ([128, 2, DEPTH, DC], F32),
        "modb_in": ([128, DEPTH, 96], F32), "finalg_in": ([128, DC], F32), "mod_w": ([DEPTH, D, 6 * D], F32),
        "win_wqkv": ([2, D, 3072], F32), "win_wo": ([2, D, D], F32), "esink_in": ([2, 1, NH * 128], F32),
        "ax_wqkv": ([1, D, 3072], F32), "ax_wo": ([1, D, D], F32), "axg_in": ([128, 2], F32),
        "axkg_rep": ([128, 128], F32), "ffn_w_gu": ([DEPTH, D, 2 * DFF], F32), "ffn_w_down": ([DEPTH, DFF, D], F32),
        "cwkT": ([2, NKV, 128, PAST], F32), "cwv": ([2, PAST, 512], F32), "cakT": ([1, NKV, 128, PAST], F32),
        "cav": ([1, PAST, 512], F32), "rope_in": ([128, 2, TL], F32), "ropeP_in": ([128, 128], BF16),
        "wmask_in": ([128, 2, 512], BF16),
        "hy_w_in": ([1, D, 3 * D], F32), "hy_wo": ([1, D, D], F32),
        "hyconv_in": ([128, 48, 4], F32), "ident_in": ([128, 128], BF16), "hyskip_in": ([128, 2, DC], F32),
        "hyw1_in": ([33, 64], F32), "hyw2_in": ([64, 64], F32), "hyw3_in": ([64, 4 * D], F32), "hyfb_in": ([64, 4], F32),
        "drep_in": ([128, D], F32),
        "feats2048_in": ([33, 2048], F32), "feats256_in": ([33, 256], F32),
        "tnn2048_in": ([128, 16], F32), "tnn256_in": ([128, 2], F32),
        "tabF2048_in": ([16, 128, 2, 16, 128], BF16), "tabF256_in": ([2, 128, 2, 2, 128], BF16),
        "tabI2048_in": ([2, 128, 16, 2048], BF16), "tabI256_in": ([2, 128, 2, 256], BF16),
    }

    LAYERED = ("mod_w", "win_wqkv", "win_wo", "ax_wqkv", "ax_wo", "ffn_w_gu", "ffn_w_down", "hy_w_in", "hy_wo")

    def W(self, base, idx):
        name = "%s__%d" % (base, idx)
        if name not in self.inputs:
            shape, dt = type(self).IN_SPECS[base]
            self._lay = getattr(self, "_lay", {})
            self._lay[name] = self.din(name, shape[1:], dt)
        return self._lay[name]

    def __getattr__(self, name):
        specs = type(self).IN_SPECS
        if name in specs and name not in type(self).LAYERED:
            shape, dt = specs[name]
            ap = self.din(name, shape, dt)
            setattr(self, name, ap)
            return ap
        raise AttributeError(name)

    def declare(self):
        nc = self.nc
        self.yT = self.dout("yT", [D, T])
        self.swk = self.dout("swk", [4, 2, 256, 512])
        self.swv = self.dout("swv", [4, 2, 256, 512])
        self.sak = self.dout("sak", [4, 1, 256, 512])
        self.sav = self.dout("sav", [4, 1, 256, 512])
        if self.dbg:
            self.xT = self.dout("xT_dbg", [D, T])
        else:
            self.xT = self.dscr("xT", [D, T], F32)
        self.hT = self.dscr("hT", [D, T], BF16)
        self.qT = self.dscr("qT", [D, T], BF16)
        self.kT = self.dscr("kT", [512, T], BF16)
        self.vtm = self.dscr("vtm", [T, 512], BF16)
        self.oT = self.dscr("oT", [D, T], BF16)
        self.actT = self.dscr("actT", [DFF, T], BF16)

    def persistent(self, stk):
        nc = self.nc
        self.mod = stk.enter_context(nc.sbuf_tensor("mod", [128, DEPTH, 2, 96], F32))
        self.modb = stk.enter_context(nc.sbuf_tensor("modb_sb", [128, DEPTH, 96], F32))
        self.normg = stk.enter_context(nc.sbuf_tensor("normg_sb", [128, 2, DEPTH, DC], F32))
        self.amod = stk.enter_context(nc.sbuf_tensor("amod", [128, DEPTH, 2, 2, DC], F32))
        self.finalg = stk.enter_context(nc.sbuf_tensor("finalg_sb", [128, DC], F32))
        self.ones_bf = stk.enter_context(nc.sbuf_tensor("ones_bf", [128, 128], BF16))
        self.ones_d = stk.enter_context(nc.sbuf_tensor("ones_d", [128, 128], BF16))
        self.ones_h = stk.enter_context(nc.sbuf_tensor("ones_h", [128, 128], BF16))
        self.zero_c = stk.enter_context(nc.sbuf_tensor("zero_c", [128, 1], F32))
        self.eps_c = stk.enter_context(nc.sbuf_tensor("eps_c", [128, 1], F32))

    def stage_mod(self):
        nc, S = self.nc, self.S
        if S.stopped:
            return
        with ExitStack() as stk:
            def sb(n, s, d):
                return stk.enter_context(nc.sbuf_tensor(self.nm(n), s, d))
            cv = sb("cv", [128, DC, 2], F32)
            sg = sb("sg", [128, DC, 2], F32)
            sT = sb("sT", [128, DC, 2], BF16)
            NW = 4
            wbuf = [sb("mw", [128, DC, 512], BF16) for _ in range(NW)]
            Bw = S.bufs_n(NW)
            pst = [stk.enter_context(nc.psum_tensor(self.nm("mps"), [128, 4, 2], F32)) for _ in range(4)]
            Bp = S.bufs_n(4)
            Bcv, BsT, Bmod, Bmb, Bng, Bc = S.buf(), S.buf(), S.buf(), S.buf(), S.buf(), S.buf()
            S.op("sp", lambda e: e.dma_start(out=cv[:], in_=self.cvec), writes=[Bcv], dma=1)
            S.op("sp", lambda e: e.dma_start(out=self.modb[:], in_=self.modb_in), writes=[Bmb], dma=1)
            S.op("sp", lambda e: e.dma_start(out=self.normg[:], in_=self.normg_in), writes=[Bng], dma=1)
            S.op("sp", lambda e: e.dma_start(out=self.finalg[:], in_=self.finalg_in), writes=[Bng], dma=1)
            S.op("dve", lambda e: e.memset(self.ones_bf[:], 1.0), writes=[Bc])
            S.op("dve", lambda e: e.memset(self.ones_d[:], 1.0 / D), writes=[Bc])
            S.op("dve", lambda e: e.memset(self.ones_h[:], 1.0 / HD), writes=[Bc])
            S.op("dve", lambda e: e.memset(self.zero_c[:], 0.0), writes=[Bc])
            S.op("dve", lambda e: e.memset(self.eps_c[:], EPS), writes=[Bc])
            S.op("act", lambda e: e.activation(out=sg[:], in_=cv[:], func=AF.Sigmoid), reads=[Bcv], writes=[BsT])
            S.op("dve", lambda e: e.tensor_tensor(out=sT[:], in0=sg[:], in1=cv[:], op=ALU.mult), reads=[Bcv, BsT], writes=[BsT])
            i = 0
            for l in range(self.n_layers):
                wv = self.W("mod_w", l).rearrange("(c p) n -> p c n", p=128)
                for nb in range(24):
                    s = i % NW
                    ps, bp = pst[i % 4], Bp[i % 4]
                    S.op("pool", lambda e, s=s, nb=nb, wv=wv: e.dma_start(out=wbuf[s][:], in_=wv[:, :, nb * 512:(nb + 1) * 512]),
                         writes=[Bw[s]], dma=1)
                    for oc in range(4):
                        for kc in range(DC):
                            S.op("pe", lambda e, s=s, oc=oc, kc=kc, ps=ps: e.matmul(
                                ps[:, oc, :], lhsT=wbuf[s][:, kc, oc * 128:(oc + 1) * 128], rhs=sT[:, kc, :],
                                start=(kc == 0), stop=(kc == DC - 1)), reads=[Bw[s], BsT], writes=[bp])
                    for g in range(2):
                        S.op("dve", lambda e, l=l, g=g, nb=nb, ps=ps: e.tensor_tensor(
                            out=self.mod[:, l, g, nb * 4:(nb + 1) * 4], in0=ps[:, :, g],
                            in1=self.modb[:, l, nb * 4:(nb + 1) * 4], op=ALU.add), reads=[bp, Bmb], writes=[Bmod])
                    i += 1
            for l in range(self.n_layers):
                for g in range(2):
                    for w in range(2):
                        sc = 16 + 48 * w
                        S.op("dve", lambda e, l=l, g=g, w=w, sc=sc: e.scalar_tensor_tensor(
                            out=self.amod[:, l, g, w, :], in0=self.mod[:, l, g, sc:sc + 16], scalar=1.0,
                            in1=self.normg[:, w, l, :], op0=ALU.add, op1=ALU.mult), reads=[Bmod, Bng], writes=[Bc])
            S.flush()

    def modcol(self, l, g, which, c):
        return self.mod[:, l, g, which * 16 + c: which * 16 + c + 1]

    def stage_copy_in(self):
        nc, S = self.nc, self.S
        if S.stopped:
            return
        with ExitStack() as stk:
            xs = [stk.enter_context(nc.sbuf_tensor(self.nm("cpx"), [128, DC, TW], F32)) for _ in range(3)]
            Bx = S.bufs_n(3)
            xi = self.xT_in.rearrange("(c p) t -> p c t", p=128)
            xo = self.xT.rearrange("(c p) t -> p c t", p=128)
            for tt in range(NTT):
                k = tt % 3
                S.op("sp", lambda e, k=k, tt=tt: e.dma_start(out=xs[k][:], in_=xi[:, :, tt * TW:(tt + 1) * TW]), writes=[Bx[k]], dma=1)
                S.op("sp", lambda e, k=k, tt=tt: e.dma_start(out=xo[:, :, tt * TW:(tt + 1) * TW], in_=xs[k][:]), reads=[Bx[k]], dma=1)
            S.flush()

    def stage_norm(self, l, w, final=False):
        nc, S = self.nc, self.S
        if S.stopped:
            return
        S.DEFER = S.DEFER0
        with ExitStack() as stk:
            def sb(n, s, d):
                return stk.enter_context(nc.sbuf_tensor(self.nm(n), s, d))
            NX = 2
            xin = [sb("xin", [128, DC, TW], F32) for _ in range(NX)]
            Bx = [S.bufs_n(4) for _ in range(NX)]
            sq = [sb("sq", [128, DC, TW], BF16) for _ in range(2)]
            Bsq = [S.bufs_n(4) for _ in range(2)]
            pss = [stk.enter_context(nc.psum_tensor(self.nm("nps"), [128, TW], F32)) for _ in range(2)]
            Bps = S.bufs_n(2)
            rstd = [sb("rstd", [128, TW], F32) for _ in range(2)]
            Br = S.bufs_n(2)
            tmp = Rot([sb("ntmp", [128, TW], F32) for _ in range(4)], S.bufs_n(4))
            odt = F32 if final else BF16
            hs = [sb("hs", [128, DC, TW], odt) for _ in range(2)]
            Bh = [S.bufs_n(4) for _ in range(2)]
            xv = (self.xT_in if (l == 0 and w == 0 and not final) else self.xT).rearrange("(c p) t -> p c t", p=128)
            ov = (self.yT if final else self.hT).rearrange("(c p) t -> p c t", p=128)

            def load(tt):
                k = tt % NX
                for q in range(4):
                    S.op("sp", lambda e, k=k, q=q, tt=tt: e.dma_start(
                        out=xin[k][:, 4 * q:4 * q + 4, :], in_=xv[:, 4 * q:4 * q + 4, tt * TW:(tt + 1) * TW]),
                        writes=[Bx[k][q]], dma=1)
            load(0)
            for tt in range(NTT):
                if tt + 1 < NTT:
                    load(tt + 1)
                g = 0 if tt < 4 else 1
                k = tt % NX
                k2 = tt % 2
                for q in range(4):
                    S.op("act", lambda e, k=k, k2=k2, q=q: e.activation(
                        out=sq[k2][:, 4 * q:4 * q + 4, :], in_=xin[k][:, 4 * q:4 * q + 4, :], func=AF.Square),
                        reads=[Bx[k][q]], writes=[Bsq[k2][q]])
                for c in range(DC):
                    S.op("pe", lambda e, k2=k2, c=c: e.matmul(pss[k2][:], lhsT=self.ones_d[:], rhs=sq[k2][:, c, :],
                                                             start=(c == 0), stop=(c == DC - 1)),
                         reads=[Bsq[k2][c // 4]], writes=[Bps[k2]])
                S.op("act", lambda e, k2=k2: e.activation(out=rstd[k2][:], in_=pss[k2][:], func=AF.Sqrt, bias=self.eps_c[:, 0:1]),
                     reads=[Bps[k2]], writes=[Br[k2]])
                S.op("dve", lambda e, k2=k2: e.reciprocal(out=rstd[k2][:], in_=rstd[k2][:]), reads=[Br[k2]], writes=[Br[k2]])
                for c in range(DC):
                    tm, Btm = tmp.next()
                    S.op("dve", lambda e, k=k, k2=k2, c=c, tm=tm: e.tensor_tensor(
                        out=tm[:], in0=xin[k][:, c, :], in1=rstd[k2][:], op=ALU.mult),
                        reads=[Bx[k][c // 4], Br[k2]], writes=[Btm])
                    if final:
                        sc_ap, bi_ap = self.finalg[:, c:c + 1], self.zero_c[:, 0:1]
                    else:
                        sc_ap, bi_ap = self.amod[:, l, g, w, c:c + 1], self.modcol(l, g, 3 * w, c)
                    if c % 3 == 2:
                        S.op("dve", lambda e, k2=k2, c=c, tm=tm, sc_ap=sc_ap, bi_ap=bi_ap: e.tensor_scalar(
                            out=hs[k2][:, c, :], in0=tm[:], scalar1=sc_ap, scalar2=bi_ap, op0=ALU.mult, op1=ALU.add),
                            reads=[Btm], writes=[Bh[k2][c // 4]])
                    else:
                        S.op("act", lambda e, k2=k2, c=c, tm=tm, sc_ap=sc_ap, bi_ap=bi_ap: e.activation(
                            out=hs[k2][:, c, :], in_=tm[:], func=AF.Identity, bias=bi_ap, scale=sc_ap),
                            reads=[Btm], writes=[Bh[k2][c // 4]])
                for q in range(4):
                    S.op("sp", lambda e, k2=k2, q=q, tt=tt: e.dma_start(
                        out=ov[:, 4 * q:4 * q + 4, tt * TW:(tt + 1) * TW], in_=hs[k2][:, 4 * q:4 * q + 4, :]),
                        reads=[Bh[k2][q]], dma=1)
            S.flush()

    def gemm(self, A_dram, KC, blocks, epilogue, stk, supertiles=None, nw=3, extra=None):
        self.S.DEFER = self.S.DEFER0
        nc, S = self.nc, self.S
        if supertiles is None:
            supertiles = [list(range(NTT))]
        maxt = max(len(s) for s in supertiles)
        bw = max(sum(n for _, _, n in segs) for segs, _ in blocks)
        A_sb = stk.enter_context(nc.sbuf_tensor(self.nm("A"), [128, KC, maxt * TW], BF16))
        BA = S.bufs_n(maxt)
        wb = [stk.enter_context(nc.sbuf_tensor(self.nm("W"), [128, KC, bw], BF16)) for _ in range(nw)]
        Bw = S.bufs_n(nw)
        npb = 8 if extra is None else 8 - extra
        banks = Rot([stk.enter_context(nc.psum_tensor(self.nm("gps"), [128, TW], F32)) for _ in range(npb)], S.bufs_n(npb))
        Av = A_dram.rearrange("(c p) t -> p c t", p=128)
        wi = 0
        for st in supertiles:
            for j, tt in enumerate(st):
                npc = 4 if KC > 16 else 2
                h = KC // npc
                S.op("sp", lambda e, j=j, tt=tt, h=h, npc=npc: [
                    e.dma_start(out=A_sb[:, i * h:(i + 1) * h, j * TW:(j + 1) * TW], in_=Av[:, i * h:(i + 1) * h, tt * TW:(tt + 1) * TW])
                    for i in range(npc)], writes=[BA[j]], dma=npc)
            for bi, (segs, groups) in enumerate(blocks):
                s = wi % nw
                wi += 1

                nsp = 4 if KC > 16 else 1
                hk = KC // nsp

                def wload(e, s=s, segs=segs, nsp=nsp, hk=hk):
                    r = []
                    off = 0
                    for W_ap, c0, n in segs:
                        Wv = W_ap.rearrange("(c p) n -> p c n", p=128)
                        for i in range(nsp):
                            r.append(e.dma_start(out=wb[s][:, i * hk:(i + 1) * hk, off:off + n], in_=Wv[:, i * hk:(i + 1) * hk, c0:c0 + n]))
                        off += n
                    return r
                S.op("pool", wload, writes=[Bw[s]], dma=len(segs) * nsp)
                for j, tt in enumerate(st):
                    for gi, grp in enumerate(groups):
                        pl = []
                        for coff in grp:
                            ps, Bp = banks.next()
                            for kc in range(KC):
                                S.op("pe", lambda e, ps=ps, s=s, kc=kc, coff=coff, j=j: e.matmul(
                                    ps[:], lhsT=wb[s][:, kc, coff:coff + 128], rhs=A_sb[:, kc, j * TW:(j + 1) * TW],
                                    start=(kc == 0), stop=(kc == KC - 1)), reads=[Bw[s], BA[j]], writes=[Bp])
                            pl.append((ps, Bp))
                        epilogue(bi, gi, tt, pl)
        return A_sb, BA, wb, Bw

    def make_resid_epilogue(self, stk, l, which, chunk_of):
        nc, S = self.nc, self.S
        xt = Rot([stk.enter_context(nc.sbuf_tensor(self.nm("xr"), [128, TW], F32)) for _ in range(6)], S.bufs_n(6))
        xv = self.xT.rearrange("(c p) t -> p c t", p=128)
        xsrc = (self.xT_in if (l == 0 and which == 2) else self.xT).rearrange("(c p) t -> p c t", p=128)

        def epi(bi, gi, tt, pl):
            oc = chunk_of(bi, gi)
            g = 0 if tt < 4 else 1
            (ps, Bp), = pl
            x, Bx = xt.next()
            S.op("sp", lambda e: e.dma_start(out=x[:], in_=xsrc[:, oc, tt * TW:(tt + 1) * TW]), writes=[Bx], dma=1)
            S.op("dve", lambda e: e.scalar_tensor_tensor(out=x[:], in0=ps[:], scalar=self.modcol(l, g, which, oc), in1=x[:],
                                                         op0=ALU.mult, op1=ALU.add), reads=[Bp, Bx], writes=[Bx])
            S.op("sp", lambda e: e.dma_start(out=xv[:, oc, tt * TW:(tt + 1) * TW], in_=x[:]), reads=[Bx], dma=1)
        return epi

    def stage_ffn(self, l):
        nc, S = self.nc, self.S
        self.stage_norm(l, 1)
        Wgu = self.W("ffn_w_gu", l)
        if S.stopped:
            return
        with ExitStack() as stk:
            sg = Rot([stk.enter_context(nc.sbuf_tensor(self.nm("sg"), [128, TW], F32)) for _ in range(3)], S.bufs_n(3))
            ao = Rot([stk.enter_context(nc.sbuf_tensor(self.nm("ao"), [128, TW], BF16)) for _ in range(4)], S.bufs_n(4))
            av = self.actT.rearrange("(c p) t -> p c t", p=128)
            blocks = []
            for jb in range(FC // 2):
                blocks.append(([(Wgu, jb * 256, 256), (Wgu, DFF + jb * 256, 256)], [(0, 256), (128, 384)]))

            def epi(bi, gi, tt, pl):
                j = bi * 2 + gi
                (pg, Bg), (pu, Bu) = pl
                s, Bs = sg.next()
                a, Ba = ao.next()
                S.op("act", lambda e: e.activation(out=s[:], in_=pg[:], func=AF.Sigmoid), reads=[Bg], writes=[Bs])
                S.op("dve", lambda e: e.tensor_tensor(out=s[:], in0=s[:], in1=pg[:], op=ALU.mult), reads=[Bs, Bg], writes=[Bs])
                S.op("dve", lambda e: e.tensor_tensor(out=a[:], in0=s[:], in1=pu[:], op=ALU.mult), reads=[Bs, Bu], writes=[Ba])
                S.op("sp", lambda e: e.dma_start(out=av[:, j, tt * TW:(tt + 1) * TW], in_=a[:]), reads=[Ba], dma=1)
            self.gemm(self.hT, DC, blocks, epi, stk)
            S.flush()
        Wd = self.W("ffn_w_down", l)
        if S.stopped:
            return
        with ExitStack() as stk:
            blocks = [([(Wd, b * 256, 256)], [(0,), (128,)]) for b in range(8)]
            epi = self.make_resid_epilogue(stk, l, 5, lambda bi, gi: bi * 2 + gi)
            self.gemm(self.actT, FC, blocks, epi, stk, supertiles=[[0, 1], [2, 3], [4, 5]], nw=3)
            S.flush()

    def stage_attn(self, l, kind, j):
        nc, S = self.nc, self.S
        self.stage_norm(l, 0)
        win = (kind == 0)
        Wqkv = self.W("win_wqkv" if win else "ax_wqkv", j)
        Wo = self.W("win_wo" if win else "ax_wo", j)
        so_k = (self.swk if win else self.sak)
        so_v = (self.swv if win else self.sav)
        if S.stopped:
            return
        with ExitStack() as stk:
            def sb(n, s, d):
                return stk.enter_context(nc.sbuf_tensor(self.nm(n), s, d))
            rope = sb("rope", [128, 2, TL], F32)
            ropeP = sb("ropeP", [128, 128], BF16)
            axg = sb("axg", [128, 2], F32)
            kgrep = sb("kgrep", [128, 128], F32)
            Bc = S.buf()
            S.op("sp", lambda e: e.dma_start(out=rope[:], in_=self.rope_in), writes=[Bc], dma=1)
            S.op("sp", lambda e: e.dma_start(out=ropeP[:], in_=self.ropeP_in), writes=[Bc], dma=1)
            S.op("sp", lambda e: e.dma_start(out=axg[:], in_=self.axg_in), writes=[Bc], dma=1)
            S.op("sp", lambda e: e.dma_start(out=kgrep[:], in_=self.axkg_rep), writes=[Bc], dma=1)
            qn = Rot([sb("qn", [128, TW], BF16) for _ in range(3)], S.bufs_n(3))
            sqh = Rot([sb("sqh", [128, TW], BF16) for _ in range(2)], S.bufs_n(2))
            rs = Rot([sb("rs", [128, TW], F32) for _ in range(2)], S.bufs_n(2))
            t1 = Rot([sb("t1", [128, TW], F32) for _ in range(2)], S.bufs_n(2))
            t2 = Rot([sb("t2", [128, TW], F32) for _ in range(2)], S.bufs_n(2))
            qo = Rot([sb("qo", [128, TW], BF16) for _ in range(3)], S.bufs_n(3))
            xps = Rot([stk.enter_context(nc.psum_tensor(self.nm("xps"), [128, TW], F32)) for _ in range(3)], S.bufs_n(3))
            qv = self.qT.rearrange("(c p) t -> p c t", p=128)
            kv = self.kT.rearrange("(c p) t -> p c t", p=128)
            blocks = [([(Wqkv, b * 512, 512)], [(0,), (128,), (256,), (384,)]) for b in range(5)]

            def epi(bi, gi, tt, pl):
                oc = bi * 4 + gi
                (ps, Bp), = pl
                isk = oc >= NH
                dst = (kv[:, oc - NH, tt * TW:(tt + 1) * TW] if isk else qv[:, oc, tt * TW:(tt + 1) * TW])
                lat = tt < 4
                q_, Bq = qn.next()
                if win:
                    S.op("act", lambda e: e.activation(out=q_[:], in_=ps[:], func=AF.Copy), reads=[Bp], writes=[Bq])
                else:
                    s_, Bs = sqh.next()
                    r_, Br = rs.next()
                    m_, Bm = xps.next()
                    S.op("act", lambda e: e.activation(out=s_[:], in_=ps[:], func=AF.Square), reads=[Bp], writes=[Bs])
                    S.op("pe", lambda e: e.matmul(m_[:], lhsT=self.ones_h[:], rhs=s_[:], start=True, stop=True), reads=[Bs], writes=[Bm])
                    S.op("act", lambda e: e.activation(out=r_[:], in_=m_[:], func=AF.Sqrt, bias=self.eps_c[:, 0:1]),
                         reads=[Bm], writes=[Br])
                    S.op("dve", lambda e: e.reciprocal(out=r_[:], in_=r_[:]), reads=[Br], writes=[Br])
                    gcol = axg[:, 1:2] if isk else axg[:, 0:1]
                    S.op("dve", lambda e: e.scalar_tensor_tensor(out=q_[:], in0=ps[:], scalar=gcol, in1=r_[:], op0=ALU.mult, op1=ALU.mult),
                         reads=[Bp, Br, Bc], writes=[Bq])
                if not lat or os.environ.get('KDBG_NOROPE'):
                    S.op("sp", lambda e: e.dma_start(out=dst, in_=q_[:]), reads=[Bq], dma=1)
                    return
                m_, Bm = xps.next()
                a_, Ba = t1.next()
                b_, Bb = t2.next()
                o_, Bo = qo.next()
                S.op("pe", lambda e: e.matmul(m_[:], lhsT=ropeP[:], rhs=q_[:], start=True, stop=True), reads=[Bq, Bc], writes=[Bm])
                S.op("dve", lambda e: e.tensor_tensor(out=a_[:], in0=q_[:], in1=rope[:, 0, tt * TW:(tt + 1) * TW], op=ALU.mult),
                     reads=[Bq, Bc], writes=[Ba])
                S.op("dve", lambda e: e.tensor_tensor(out=b_[:], in0=m_[:], in1=rope[:, 1, tt * TW:(tt + 1) * TW], op=ALU.mult),
                     reads=[Bm, Bc], writes=[Bb])
                S.op("pool", lambda e: e.tensor_tensor(out=o_[:], in0=a_[:], in1=b_[:], op=ALU.add), reads=[Ba, Bb], writes=[Bo])
                S.op("sp", lambda e: e.dma_start(out=dst, in_=o_[:]), reads=[Bo], dma=1)
            A_sb, BA, wkv, Bwkv = self.gemm(self.hT, DC, blocks, epi, stk, nw=2, extra=3)
            for i_, c0 in enumerate((2048, 2560)):
                S.op("pool", lambda e, i_=i_, c0=c0: e.dma_start(
                    out=wkv[i_][:], in_=Wqkv.rearrange("(c p) n -> p c n", p=128)[:, :, c0:c0 + 512]), writes=[Bwkv[i_]], dma=1)
            vb = Rot([sb("vb", [128, 512], BF16) for _ in range(3)], S.bufs_n(3))
            vf = Rot([sb("vf", [128, 512], F32) for _ in range(3)], S.bufs_n(3))
            ssq = Rot([sb("ssq", [128, 4], F32) for _ in range(2)], S.bufs_n(2))
            junk = Rot([sb("junk", [128, 128], F32) for _ in range(2)], S.bufs_n(2))
            for tc in range(0 if not os.environ.get('KDBG_NOVK') else 999, int(os.environ.get('KDBG_VKMAX', T // 128))):
                ctx = tc >= TL // 128
                for which in ((1, 0) if ctx else (1,)):
                    ps, Bp = xps.next()
                    for kc in range(DC):
                        S.op("pe", lambda e, ps=ps, kc=kc, tc=tc, which=which: e.matmul(
                            ps[:], lhsT=A_sb[:, kc, tc * 128:(tc + 1) * 128], rhs=wkv[which][:, kc, :],
                            start=(kc == 0), stop=(kc == DC - 1)), reads=[BA[tc // 4], Bwkv[which]], writes=[Bp])
                    if which == 1:
                        v_, Bv = vb.next()
                        S.op("act", lambda e, v_=v_, ps=ps: e.activation(out=v_[:], in_=ps[:], func=AF.Copy), reads=[Bp], writes=[Bv])
                        S.op("sp", lambda e, v_=v_, tc=tc: e.dma_start(out=self.vtm[tc * 128:(tc + 1) * 128, :], in_=v_[:]), reads=[Bv], dma=1)
                    if ctx:
                        sq_, half = divmod(tc - TL // 128, 2)
                        f_, Bf = vf.next()
                        if which == 1 or win:
                            S.op("act", lambda e, f_=f_, ps=ps: e.activation(out=f_[:], in_=ps[:], func=AF.Copy), reads=[Bp], writes=[Bf])
                        else:
                            s_, Bs = ssq.next()
                            jk, Bj = junk.next()
                            for h in range(NKV):
                                S.op("act", lambda e, h=h, jk=jk, ps=ps, s_=s_: e.activation(
                                    out=jk[:], in_=ps[:, h * 128:(h + 1) * 128], func=AF.Square, accum_out=s_[:, h:h + 1]),
                                    reads=[Bp], writes=[Bj, Bs])
                            S.op("dve", lambda e, s_=s_: e.tensor_scalar(out=s_[:], in0=s_[:], scalar1=1.0 / HD, scalar2=EPS,
                                                                        op0=ALU.mult, op1=ALU.add), reads=[Bs], writes=[Bs])
                            S.op("act", lambda e, s_=s_: e.activation(out=s_[:], in_=s_[:], func=AF.Sqrt), reads=[Bs], writes=[Bs])
                            S.op("dve", lambda e, s_=s_: e.reciprocal(out=s_[:], in_=s_[:]), reads=[Bs], writes=[Bs])
                            for h in range(NKV):
                                S.op("dve", lambda e, h=h, f_=f_, ps=ps, s_=s_: e.scalar_tensor_tensor(
                                    out=f_[:, h * 128:(h + 1) * 128], in0=ps[:, h * 128:(h + 1) * 128], scalar=s_[:, h:h + 1],
                                    in1=kgrep[:], op0=ALU.mult, op1=ALU.mult), reads=[Bp, Bs, Bc], writes=[Bf])
                        dst = (so_v if which == 1 else so_k)[sq_, j, half * 128:(half + 1) * 128, :]
                        if not os.environ.get('KDBG_NOSTATE'):
                            S.op("sp", lambda e, f_=f_, dst=dst: e.dma_start(out=dst, in_=f_[:]), reads=[Bf], dma=1)
            S.flush()
        self.attn_core(l, kind, j)
        if S.stopped:
            return
        with ExitStack() as stk:
            blocks = [([(Wo, b * 512, 512)], [(0,), (128,), (256,), (384,)]) for b in range(4)]
            epi = self.make_resid_epilogue(stk, l, 2, lambda bi, gi: bi * 4 + gi)
            self.gemm(self.oT, DC, blocks, epi, stk)
            S.flush()

    def attn_core(self, l, kind, j):
        nc, S = self.nc, self.S
        win = (kind == 0)
        ckT = (self.cwkT if win else self.cakT)[j]
        cv_ = (self.cwv if win else self.cav)[j]
        scale = HD ** -0.5
        if S.stopped:
            return
        S.DEFER = S.DEFER0
        with ExitStack() as stk:
            def sb(n, s, d):
                return stk.enter_context(nc.sbuf_tensor(self.nm(n), s, d))
            NKC = T // 128
            k_sb = sb("k_sb", [128, NKV, T + PAST], BF16)
            v_sb = sb("v_sb", [128, NKC + 4, 512], BF16)
            Bk, Bv, Bc = S.buf(), S.buf(), S.buf()
            S.op("sp", lambda e: e.dma_start(out=k_sb[:, :, 0:T], in_=self.kT.rearrange("(g p) t -> p g t", p=128)), writes=[Bk], dma=1)
            S.op("pool", lambda e: e.dma_start(out=k_sb[:, :, T:T + PAST], in_=ckT.rearrange("g p t -> p g t")), writes=[Bk], dma=1)
            S.op("sp", lambda e: e.dma_start(out=v_sb[:, 0:NKC, :], in_=self.vtm.rearrange("(c p) n -> p c n", p=128)), writes=[Bv], dma=1)
            S.op("pool", lambda e: e.dma_start(out=v_sb[:, NKC:NKC + 4, :], in_=cv_.rearrange("(c p) n -> p c n", p=128)), writes=[Bv], dma=1)
            wmask = sb("wmask", [128, 2, 512], BF16)
            S.op("sp", lambda e: e.dma_start(out=wmask[:], in_=self.wmask_in), writes=[Bc], dma=1)
            esk = None
            if win:
                esr = sb("esr", [1, NH * 128], F32)
                esk = sb("esk", [1, NH * 128], BF16)
                ones1 = sb("ones1", [1, 128], BF16)
                S.op("sp", lambda e: e.dma_start(out=esr[:], in_=self.esink_in[j]), writes=[Bc], dma=1)
                S.op("act", lambda e: e.activation(out=esr[:], in_=esr[:], func=AF.Exp), reads=[Bc], writes=[Bc])
                esl = sb("esl", [1, NH * 128], BF16)
                esf = sb("esf", [1, NH * 128], F32)
                S.op("act", lambda e: e.activation(out=esk[:], in_=esr[:], func=AF.Copy), reads=[Bc], writes=[Bc])
                S.op("act", lambda e: e.activation(out=esf[:], in_=esk[:], func=AF.Copy), reads=[Bc], writes=[Bc])
                S.op("dve", lambda e: e.tensor_tensor(out=esf[:], in0=esr[:], in1=esf[:], op=ALU.subtract), reads=[Bc], writes=[Bc])
                S.op("act", lambda e: e.activation(out=esl[:], in_=esf[:], func=AF.Copy), reads=[Bc], writes=[Bc])
                S.op("dve", lambda e: e.memset(ones1[:], 1.0), writes=[Bc])
            q_sb = [sb("q_sb", [128, 4, T], BF16) for _ in range(2)]
            Bq = S.bufs_n(2)
            o_sb = [sb("o_sb", [128, 4, T], BF16) for _ in range(2)]
            Bo = S.bufs_n(2)
            pT = Rot([sb("pT", [128, 512], BF16) for _ in range(5)], S.bufs_n(5))
            rc = Rot([sb("rc", [128, 512], F32) for _ in range(2)], S.bufs_n(2))
            sps = Rot([stk.enter_context(nc.psum_tensor(self.nm("sps"), [128, 512], F32)) for _ in range(4)], S.bufs_n(4))
            dps = Rot([stk.enter_context(nc.psum_tensor(self.nm("dps"), [128, 512], F32)) for _ in range(2)], S.bufs_n(2))
            ops_ = Rot([stk.enter_context(nc.psum_tensor(self.nm("ops"), [128, 512], F32)) for _ in range(2)], S.bufs_n(2))
            seqs = [(0, TL, True)] + [(TL + 256 * s, 256, False) for s in range(4)]
            qv = self.qT.rearrange("(h p) t -> p h t", p=128)
            ov = self.oT.rearrange("(h p) t -> p h t", p=128)
            tasks = []
            for g in range(NKV):
                for (t0, ln, has_cache) in seqs:
                    nqb = ln // 128
                    for qb in range(nqb):
                        chunks = []
                        if win and has_cache:
                            for d_, mk in ((-1, 0), (0, None), (1, 1)):
                                kb = qb + d_
                                if 0 <= kb < nqb:
                                    chunks.append((t0 + kb * 128, (t0 + kb * 128) // 128, mk))
                        else:
                            for kb in range(nqb):
                                chunks.append((t0 + kb * 128, (t0 + kb * 128) // 128, None))
                        if has_cache:
                            for c in range(4):
                                chunks.append((T + c * 128, NKC + c, None))
                        tasks.append((g, t0 + qb * 128, chunks))
            flat = [(ti, ci) for ti, tk in enumerate(tasks) for ci in range(len(tk[2]))]
            LOOK = 3
            loaded = set()
            sbank = {}
            tstate = {}

            def load_q(g):
                if g < NKV and g not in loaded:
                    loaded.add(g)
                    S.op("sp", lambda e, g=g: e.dma_start(out=q_sb[g % 2][:], in_=qv[:, 4 * g:4 * g + 4, :]), writes=[Bq[g % 2]], dma=1)

            def emit_s(idx):
                ti, ci = flat[idx]
                g, q0, chunks = tasks[ti]
                load_q(g)
                kc0 = chunks[ci][0]
                sp_, Bs = sps.next()
                sbank[idx] = (sp_, Bs)
                qs, Bqs = q_sb[g % 2], Bq[g % 2]
                S.op("pe", lambda e, sp_=sp_, kc0=kc0, g=g, qs=qs, q0=q0: e.matmul(
                    sp_[:].rearrange("p (h q) -> p h q", h=4), lhsT=k_sb[:, g, kc0:kc0 + 128], rhs=qs[:, :, q0:q0 + 128],
                    start=True, stop=True), reads=[Bk, Bqs], writes=[Bs])

            for idx in range(min(LOOK, len(flat))):
                emit_s(idx)
            for idx in range(len(flat)):
                if idx + LOOK < len(flat):
                    emit_s(idx + LOOK)
                ti, ci = flat[idx]
                g, q0, chunks = tasks[ti]
                n = len(chunks)
                kc0, vc, mk = chunks[ci]
                os_, Bos = o_sb[g % 2], Bo[g % 2]
                if ci == 0:
                    tstate[ti] = (dps.next(), ops_.next())
                (dp, Bd), (op_, Bop) = tstate[ti]
                sp_, Bs = sbank.pop(idx)
                p_, Bp = pT.next()
                S.op("act", lambda e, sp_=sp_, p_=p_: e.activation(out=p_[:], in_=sp_[:], func=AF.Exp, scale=scale),
                     reads=[Bs], writes=[Bp])
                if mk is not None:
                    S.op("pool", lambda e, p_=p_, mk=mk: e.tensor_tensor(out=p_[:], in0=p_[:], in1=wmask[:, mk, :], op=ALU.mult),
                         reads=[Bp, Bc], writes=[Bp])
                last = (ci == n - 1) and not win
                S.op("pe", lambda e, dp=dp, p_=p_, ci=ci, last=last: e.matmul(
                    dp[:], lhsT=self.ones_bf[:], rhs=p_[:], start=(ci == 0), stop=last), reads=[Bp], writes=[Bd])
                S.op("pe", lambda e, op_=op_, p_=p_, ci=ci, vc=vc, g=g, n=n: e.matmul(
                    op_[:], lhsT=v_sb[:, vc, g * 128:(g + 1) * 128], rhs=p_[:], start=(ci == 0), stop=(ci == n - 1)),
                    reads=[Bp, Bv], writes=[Bop])
                if ci == n - 1:
                    if win:
                        S.op("pe", lambda e, dp=dp, g=g: e.matmul(dp[:], lhsT=ones1[:], rhs=esk[:, g * 512:(g + 1) * 512],
                                                                  start=False, stop=False), reads=[Bc], writes=[Bd])
                        S.op("pe", lambda e, dp=dp, g=g: e.matmul(dp[:], lhsT=ones1[:], rhs=esl[:, g * 512:(g + 1) * 512],
                                                                  start=False, stop=True), reads=[Bc], writes=[Bd])
                    r_, Br = rc.next()
                    S.op("dve", lambda e, r_=r_, dp=dp: e.reciprocal(out=r_[:], in_=dp[:]), reads=[Bd], writes=[Br])
                    S.op("dve", lambda e, r_=r_, op_=op_, os_=os_, q0=q0: e.tensor_tensor(
                        out=os_[:, :, q0:q0 + 128], in0=op_[:].rearrange("p (h q) -> p h q", h=4),
                        in1=r_[:].rearrange("p (h q) -> p h q", h=4), op=ALU.mult), reads=[Bop, Br], writes=[Bos])
                    del tstate[ti]
                    if ti + 1 == len(tasks) or tasks[ti + 1][0] != g:
                        S.op("sp", lambda e, g=g, os_=os_: e.dma_start(out=ov[:, 4 * g:4 * g + 4, :], in_=os_[:]), reads=[Bos], dma=1)
            S.flush()

    def build(self):
        nc = self.nc
        self.declare()
        with ExitStack() as stk:
            self.S = Sched(nc, stk)
            self.S.stop = self.stop
            self.persistent(stk)
            try:
                self.stage_mod()
                for l in range(self.n_layers):
                    kind, j = l % 3, l // 3
                    if kind == 1:
                        self.stage_hyena(l, j)
                    else:
                        self.stage_attn(l, kind, j)
                    self.stage_ffn(l)
                self.stage_norm(0, 0, final=True)
            except StopBuild:
                pass
        return nc


    def hy_decl(self):
        if hasattr(self, "yin"):
            return
        self.yin = self.dscr("yin", [3 * D, T], F32)
        self.vfm = self.dscr("vfm", [D, T], F32)
        self.x1fm = self.dscr("x1fm", [D, T], F32)
        self.x2fm = self.dscr("x2fm", [D, T], F32)
        self.z1fm = self.dscr("z1fm", [D, T], F32)
        self.vtmh = self.dscr("vtmh", [T, D], BF16)
        self.ztmh = self.dscr("ztmh", [T, D], BF16)
        self.z2T = self.dscr("z2T", [D, T], BF16)
        self.Hs = {2048: self.dscr("Hs2048", [2, 2, 2048, D], F32), 256: self.dscr("Hs256", [2, 2, 256, D], F32)}

    def stage_hyena(self, l, j):
        nc, S = self.nc, self.S
        self.hy_decl()
        self.stage_norm(l, 0)
        if S.stopped:
            return
        Win = self.W("hy_w_in", j)
        with ExitStack() as stk:
            yo = Rot([stk.enter_context(nc.sbuf_tensor(self.nm("yo"), [128, TW], F32)) for _ in range(4)], S.bufs_n(4))
            yv = self.yin.rearrange("(c p) t -> p c t", p=128)
            blocks = [([(Win, b * 512, 512)], [(0,), (128,), (256,), (384,)]) for b in range(12)]
            cnt = [0]

            def epi(bi, gi, tt, pl):
                oc = bi * 4 + gi
                (ps, Bp), = pl
                y_, By = yo.next()
                cnt[0] += 1
                S.op("act", lambda e: e.activation(out=y_[:], in_=ps[:], func=AF.Copy), reads=[Bp], writes=[By])
                S.op("sp", lambda e: e.dma_start(out=yv[:, oc, tt * TW:(tt + 1) * TW], in_=y_[:]), reads=[By], dma=1)
            self.gemm(self.hT, DC, blocks, epi, stk)
            S.flush()
        self.hy_shortconv(j)
        self.hy_filter(j, 2048)
        self.hy_filter(j, 256)
        self.hy_conv(j, 0)
        self.hy_conv(j, 1)
        if S.stopped:
            return
        Wo = self.W("hy_wo", j)
        with ExitStack() as stk:
            blocks = [([(Wo, b * 512, 512)], [(0,), (128,), (256,), (384,)]) for b in range(4)]
            epi = self.make_resid_epilogue(stk, l, 2, lambda bi, gi: bi * 4 + gi)
            self.gemm(self.z2T, DC, blocks, epi, stk)
            S.flush()

    SEQS = [(0, TL)] + [(TL + 256 * s_, 256) for s_ in range(4)]

    def hy_shortconv(self, j):
        nc, S = self.nc, self.S
        if S.stopped:
            return
        S.DEFER = 0
        with ExitStack() as stk:
            def sb(n, s, d):
                return stk.enter_context(nc.sbuf_tensor(self.nm(n), s, d))
            cw = sb("cw", [128, 48, 4], F32)
            ident = sb("ident", [128, 128], BF16)
            Bc = S.buf()
            S.op("sp", lambda e: e.dma_start(out=cw[:], in_=self.hyconv_in), writes=[Bc], dma=1)
            S.op("sp", lambda e: e.dma_start(out=ident[:], in_=self.ident_in), writes=[Bc], dma=1)
            yi = Rot([sb("yi", [128, T], F32) for _ in range(2)], S.bufs_n(2))
            uo = Rot([sb("uo", [128, T], F32) for _ in range(2)], S.bufs_n(2))
            ub = Rot([sb("ub", [128, T], BF16) for _ in range(2)], S.bufs_n(2))
            vt = Rot([sb("vt", [128, 4, 128], BF16) for _ in range(3)], S.bufs_n(3))
            tps = Rot([stk.enter_context(nc.psum_tensor(self.nm("tps"), [128, 4, 128], BF16)) for _ in range(3)], S.bufs_n(3))
            yv = self.yin.rearrange("(c p) t -> p c t", p=128)
            dsts = [self.vfm.rearrange("(c p) t -> p c t", p=128), self.x1fm.rearrange("(c p) t -> p c t", p=128),
                    self.x2fm.rearrange("(c p) t -> p c t", p=128)]
            ynext = None
            for oc in range(48):
                if ynext is None:
                    y_, By = yi.next()
                    S.op("sp", lambda e, y_=y_, oc=oc: e.dma_start(out=y_[:], in_=yv[:, oc, :]), writes=[By], dma=1)
                else:
                    y_, By = ynext
                u_, Bu = uo.next()
                S.op("act", lambda e, y_=y_, u_=u_, oc=oc: e.activation(out=u_[:], in_=y_[:], func=AF.Identity,
                                                                       bias=cw[:, oc, 3:4], scale=cw[:, oc, 1:2]), reads=[By, Bc], writes=[Bu])
                for (a, ln) in self.SEQS:
                    b = a + ln
                    S.op("dve", lambda e, y_=y_, u_=u_, oc=oc, a=a, b=b: e.scalar_tensor_tensor(
                        out=u_[:, a + 1:b], in0=y_[:, a:b - 1], scalar=cw[:, oc, 0:1], in1=u_[:, a + 1:b], op0=ALU.mult, op1=ALU.add),
                        reads=[By, Bu, Bc], writes=[Bu])
                    S.op("dve", lambda e, y_=y_, u_=u_, oc=oc, a=a, b=b: e.scalar_tensor_tensor(
                        out=u_[:, a:b - 1], in0=y_[:, a + 1:b], scalar=cw[:, oc, 2:3], in1=u_[:, a:b - 1], op0=ALU.mult, op1=ALU.add),
                        reads=[By, Bu, Bc], writes=[Bu])
                if oc + 1 < 48:
                    ynext = yi.next()
                    S.op("sp", lambda e, y2=ynext[0], oc=oc: e.dma_start(out=y2[:], in_=yv[:, oc + 1, :]), writes=[ynext[1]], dma=1)
                S.op("sp", lambda e, u_=u_, oc=oc: e.dma_start(out=dsts[oc // 16][:, oc % 16, :], in_=u_[:]), reads=[Bu], dma=1)
                if oc < 16:
                    self.fm_to_tm(u_, Bu, oc, self.vtmh, ub, vt, tps, ident, Bc)
            S.flush()

    def fm_to_tm(self, u_, Bu, oc, dst_tm, ub, vt, tps, ident, Bc, c0=0, ncols=T):
        S = self.S
        b_, Bb = ub.next()
        S.op("act", lambda e: e.activation(out=b_[:, 0:ncols], in_=u_[:, 0:ncols], func=AF.Copy), reads=[Bu], writes=[Bb])
        dv = dst_tm.rearrange("(c p) d -> p c d", p=128)
        for q in range(ncols // 512):
            tp, Bt = tps.next()
            v_, Bv = vt.next()
            for i in range(4):
                S.op("pe", lambda e, tp=tp, i=i, q=q: e.transpose(tp[:, i, :], b_[:, q * 512 + i * 128: q * 512 + (i + 1) * 128], ident[:]),
                     reads=[Bb, Bc], writes=[Bt])
            S.op("act", lambda e, tp=tp, v_=v_: e.activation(out=v_[:], in_=tp[:], func=AF.Copy), reads=[Bt], writes=[Bv])
            tc0 = (c0 + q * 512) // 128
            S.op("sp", lambda e, v_=v_, tc0=tc0: e.dma_start(out=dv[:, tc0:tc0 + 4, oc * 128:(oc + 1) * 128], in_=v_[:]), reads=[Bv], dma=1)

    def hy_filter(self, j, L):
        nc, S = self.nc, self.S
        if S.stopped:
            return
        S.DEFER = 0
        nT = L // 128
        NW = min(512, L)
        TWO_PI = 2.0 * math.pi
        Hs = self.Hs[L]
        feats_in = self.feats2048_in if L == 2048 else self.feats256_in
        tnn_in = self.tnn2048_in if L == 2048 else self.tnn256_in
        tabF = self.tabF2048_in if L == 2048 else self.tabF256_in
        with ExitStack() as stk:
            def sb(n, s, d):
                return stk.enter_context(nc.sbuf_tensor(self.nm(n), s, d))
            feats = sb("feats", [33, L], F32)
            w1 = sb("fw1", [33, 64], F32)
            w2 = sb("fw2", [64, 64], F32)
            w3 = sb("fw3", [64, 4 * D], F32)
            fb = sb("ffb", [64, 4], F32)
            fraw = sb("fraw", [64, 4], F32)
            tnn = sb("tnn", [128, nT], F32)
            drep = sb("drep", [128, D], F32)
            a1 = sb("a1", [64, L], F32)
            a2 = sb("a2", [64, L], F32)
            Bc, Ba1, Ba2 = S.buf(), S.buf(), S.buf()
            S.op("sp", lambda e: e.dma_start(out=feats[:], in_=feats_in), writes=[Bc], dma=1)
            S.op("sp", lambda e: e.dma_start(out=w1[:], in_=self.hyw1_in), writes=[Bc], dma=1)
            S.op("sp", lambda e: e.dma_start(out=w2[:], in_=self.hyw2_in), writes=[Bc], dma=1)
            S.op("sp", lambda e: e.dma_start(out=w3[:], in_=self.hyw3_in), writes=[Bc], dma=1)
            S.op("sp", lambda e: e.dma_start(out=fraw[:], in_=self.hyfb_in), writes=[Bc], dma=1)
            S.op("sp", lambda e: e.dma_start(out=tnn[:], in_=tnn_in), writes=[Bc], dma=1)
            S.op("sp", lambda e: e.dma_start(out=drep[:], in_=self.drep_in), writes=[Bc], dma=1)
            S.op("act", lambda e: e.activation(out=fb[:, 0:1], in_=fraw[:, 0:1], func=AF.Copy), reads=[Bc], writes=[Bc])
            S.op("act", lambda e: e.activation(out=fb[:, 2:3], in_=fraw[:, 1:2], func=AF.Copy), reads=[Bc], writes=[Bc])
            S.op("dve", lambda e: e.tensor_tensor(out=fb[:, 1:2], in0=fraw[:, 2:3], in1=fraw[:, 0:1], op=ALU.mult), reads=[Bc], writes=[Bc])
            S.op("dve", lambda e: e.tensor_tensor(out=fb[:, 3:4], in0=fraw[:, 3:4], in1=fraw[:, 1:2], op=ALU.mult), reads=[Bc], writes=[Bc])
            fps = Rot([stk.enter_context(nc.psum_tensor(self.nm("fps"), [128, 512], F32)) for _ in range(4)], S.bufs_n(4))
            nps = Rot([stk.enter_context(nc.psum_tensor(self.nm("fnps"), [128, 512], F32)) for _ in range(1)], S.bufs_n(1))
            hps = Rot([stk.enter_context(nc.psum_tensor(self.nm("hps"), [128, 512], F32)) for _ in range(3)], S.bufs_n(3))
            arg = Rot([sb("arg", [64, NW], F32) for _ in range(2)], S.bufs_n(2))
            kf = Rot([sb("kf", [64, NW], F32) for _ in range(2)], S.bufs_n(2))
            ki = Rot([sb("ki", [64, NW], mybir.dt.int32) for _ in range(2)], S.bufs_n(2))

            def sin_layer(src, Bsrc, wt, kdim, col, dst, Bdst):
                for c in range(L // NW):
                    ps, Bp = fps.next()
                    a_, Ba = arg.next()
                    k_, Bk = kf.next()
                    i_, Bi = ki.next()
                    S.op("pe", lambda e, ps=ps, c=c: e.matmul(ps[0:64, 0:NW], lhsT=wt[0:kdim, :], rhs=src[0:kdim, c * NW:(c + 1) * NW],
                                                             start=True, stop=True), reads=[Bsrc, Bc], writes=[Bp])
                    S.op("act", lambda e, ps=ps, a_=a_: e.activation(out=a_[:], in_=ps[0:64, 0:NW], func=AF.Identity,
                                                                    bias=fb[:, col + 1:col + 2], scale=fb[:, col:col + 1]), reads=[Bp, Bc], writes=[Ba])
                    S.op("dve", lambda e, a_=a_, k_=k_: e.tensor_scalar(out=k_[:], in0=a_[:], scalar1=1.0 / TWO_PI, scalar2=12582912.0, op0=ALU.mult, op1=ALU.add),
                         reads=[Ba], writes=[Bk])
                    S.op("dve", lambda e, k_=k_: e.tensor_scalar(out=k_[:], in0=k_[:], scalar1=12582912.0, scalar2=None, op0=ALU.subtract),
                         reads=[Bk], writes=[Bk])
                    S.op("dve", lambda e, a_=a_, k_=k_: e.scalar_tensor_tensor(out=a_[:], in0=k_[:], scalar=-TWO_PI, in1=a_[:], op0=ALU.mult, op1=ALU.add),
                         reads=[Bk, Ba], writes=[Ba])
                    S.op("dve", lambda e, a_=a_, k_=k_: e.tensor_scalar(out=k_[:], in0=a_[:], scalar1=math.pi, scalar2=TWO_PI, op0=ALU.is_gt, op1=ALU.mult),
                         reads=[Ba], writes=[Bk])
                    S.op("dve", lambda e, a_=a_, k_=k_: e.tensor_tensor(out=a_[:], in0=a_[:], in1=k_[:], op=ALU.subtract), reads=[Ba, Bk], writes=[Ba])
                    S.op("dve", lambda e, a_=a_, k_=k_: e.tensor_scalar(out=k_[:], in0=a_[:], scalar1=-math.pi, scalar2=TWO_PI, op0=ALU.is_lt, op1=ALU.mult),
                         reads=[Ba], writes=[Bk])
                    S.op("dve", lambda e, a_=a_, k_=k_: e.tensor_tensor(out=a_[:], in0=a_[:], in1=k_[:], op=ALU.add), reads=[Ba, Bk], writes=[Ba])
                    S.op("act", lambda e, a_=a_, c=c: e.activation(out=dst[:, c * NW:(c + 1) * NW], in_=a_[:], func=AF.Sin), reads=[Ba], writes=[Bdst])
            sin_layer(feats, Bc, w1, 33, 0, a1, Ba1)
            sin_layer(a1, Ba1, w2, 64, 2, a2, Ba2)
            gp = [sb("gp", [128, nT, 512], BF16) for _ in range(2)]
            gm = [sb("gm", [128, nT, 512], BF16) for _ in range(2)]
            Bg = S.bufs_n(2)
            dec = Rot([sb("dec", [128, 512], F32) for _ in range(2)], S.bufs_n(2))
            fw = Rot([sb("fw", [128, 512], F32) for _ in range(2)], S.bufs_n(2))
            bw = Rot([sb("bw", [128, 512], F32) for _ in range(2)], S.bufs_n(2))
            ab = Rot([sb("ab", [128, 2, 512], BF16) for _ in range(2)], S.bufs_n(2))
            rn = Rot([sb("rn", [128, 512], F32) for _ in range(2)], S.bufs_n(2))
            tb = Rot([sb("ftb", [128, 2, nT, 128], BF16) for _ in range(3)], S.bufs_n(3))
            ho = Rot([sb("ho", [128, 2, 512], F32) for _ in range(2)], S.bufs_n(2))
            it = 0
            for o in range(2):
                for db in range(4):
                    k = it % 2
                    it += 1
                    npz, Bn = nps.next()
                    for tc in range(nT):
                        pf, Bpf = fps.next()
                        pb, Bpb = fps.next()
                        d_, Bd = dec.next()
                        f_, Bf = fw.next()
                        b_, Bb = bw.next()
                        a_, Bab = ab.next()
                        cf = o * D + db * 512
                        S.op("pe", lambda e, pf=pf, tc=tc, cf=cf: e.matmul(pf[:], lhsT=a2[:, tc * 128:(tc + 1) * 128], rhs=w3[:, cf:cf + 512],
                                                                          start=True, stop=True), reads=[Ba2, Bc], writes=[Bpf])
                        S.op("pe", lambda e, pb=pb, tc=tc, cf=cf: e.matmul(pb[:], lhsT=a2[:, tc * 128:(tc + 1) * 128], rhs=w3[:, 2 * D + cf:2 * D + cf + 512],
                                                                          start=True, stop=True), reads=[Ba2, Bc], writes=[Bpb])
                        S.op("act", lambda e, d_=d_, tc=tc, db=db: e.activation(out=d_[:], in_=drep[:, db * 512:(db + 1) * 512], func=AF.Exp,
                                                                               scale=tnn[:, tc:tc + 1]), reads=[Bc], writes=[Bd])
                        S.op("dve", lambda e, f_=f_, pf=pf, d_=d_: e.tensor_tensor(out=f_[:], in0=pf[:], in1=d_[:], op=ALU.mult), reads=[Bpf, Bd], writes=[Bf])
                        S.op("dve", lambda e, b_=b_, pb=pb, d_=d_: e.tensor_tensor(out=b_[:], in0=pb[:], in1=d_[:], op=ALU.mult), reads=[Bpb, Bd], writes=[Bb])
                        if tc == 0:
                            S.op("dve", lambda e, b_=b_: e.memset(b_[0:1, :], 0.0), reads=[Bb], writes=[Bb])
                        S.op("pool", lambda e, f_=f_, b_=b_, k=k, tc=tc: e.tensor_tensor(out=gp[k][:, tc, :], in0=f_[:], in1=b_[:], op=ALU.add),
                             reads=[Bf, Bb], writes=[Bg[k]])
                        S.op("pool", lambda e, f_=f_, b_=b_, k=k, tc=tc: e.tensor_tensor(out=gm[k][:, tc, :], in0=f_[:], in1=b_[:], op=ALU.subtract),
                             reads=[Bf, Bb], writes=[Bg[k]])
                        S.op("act", lambda e, f_=f_, a_=a_: e.activation(out=a_[:, 0, :], in_=f_[:], func=AF.Abs), reads=[Bf], writes=[Bab])
                        S.op("act", lambda e, b_=b_, a_=a_: e.activation(out=a_[:, 1, :], in_=b_[:], func=AF.Abs), reads=[Bb], writes=[Bab])
                        for h in range(2):
                            S.op("pe", lambda e, npz=npz, a_=a_, h=h, tc=tc: e.matmul(npz[:], lhsT=self.ones_bf[:], rhs=a_[:, h, :],
                                                                                     start=(tc == 0 and h == 0), stop=(tc == nT - 1 and h == 1)),
                                 reads=[Bab], writes=[Bn])
                    r_, Br = rn.next()
                    S.op("dve", lambda e, r_=r_, npz=npz: e.tensor_scalar(out=r_[:], in0=npz[:], scalar1=EPS, scalar2=None, op0=ALU.add),
                         reads=[Bn], writes=[Br])
                    S.op("dve", lambda e, r_=r_: e.reciprocal(out=r_[:], in_=r_[:]), reads=[Br], writes=[Br])
                    tnext = None
                    for fc in range(nT):
                        if tnext is None:
                            t_, Bt = tb.next()
                            S.op("sp", lambda e, t_=t_, fc=fc: e.dma_start(out=t_[:], in_=tabF[fc]), writes=[Bt], dma=1)
                        else:
                            t_, Bt = tnext
                        if fc + 1 < nT:
                            tnext = tb.next()
                            S.op("sp", lambda e, t2=tnext[0], fc=fc: e.dma_start(out=t2[:], in_=tabF[fc + 1]), writes=[tnext[1]], dma=1)
                        h_, Bh = ho.next()
                        for ri, gsrc in enumerate((gp, gm)):
                            hp, Bhp = hps.next()
                            for tc in range(nT):
                                S.op("pe", lambda e, hp=hp, t_=t_, ri=ri, tc=tc, gsrc=gsrc, k=k: e.matmul(
                                    hp[:], lhsT=t_[:, ri, tc, :], rhs=gsrc[k][:, tc, :], start=(tc == 0), stop=(tc == nT - 1)),
                                    reads=[Bt, Bg[k]], writes=[Bhp])
                            S.op("dve", lambda e, hp=hp, h_=h_, ri=ri, r_=r_: e.tensor_tensor(out=h_[:, ri, :], in0=hp[:], in1=r_[:], op=ALU.mult),
                                 reads=[Bhp, Br], writes=[Bh])
                        S.op("sp", lambda e, h_=h_, o=o, fc=fc, db=db: e.dma_start(
                            out=Hs[o, :, fc * 128:(fc + 1) * 128, db * 512:(db + 1) * 512].rearrange("r p d -> p r d"), in_=h_[:]),
                            reads=[Bh], dma=1)
            S.flush()

    def hy_conv(self, j, o):
        nc, S = self.nc, self.S
        if S.stopped:
            return
        S.DEFER = 0
        src_tm = self.vtmh if o == 0 else self.ztmh
        src_fm = self.vfm if o == 0 else self.z1fm
        gate_fm = self.x1fm if o == 0 else self.x2fm
        with ExitStack() as stk:
            def sb(n, s, d):
                return stk.enter_context(nc.sbuf_tensor(self.nm(n), s, d))
            skip = sb("skip", [128, 2, DC], F32)
            ident = sb("ident2", [128, 128], BF16)
            Bc = S.buf()
            S.op("sp", lambda e: e.dma_start(out=skip[:], in_=self.hyskip_in), writes=[Bc], dma=1)
            S.op("sp", lambda e: e.dma_start(out=ident[:], in_=self.ident_in), writes=[Bc], dma=1)
            vb = Rot([sb("cvb", [128, 16, 512], BF16) for _ in range(2)], S.bufs_n(2))
            yre = Rot([sb("yre", [128, 16, 512], BF16) for _ in range(1)], S.bufs_n(1))
            yim = Rot([sb("yim", [128, 16, 512], BF16) for _ in range(1)], S.bufs_n(1))
            tb = Rot([sb("ctb", [128, 2, 16, 128], BF16) for _ in range(3)], S.bufs_n(3))
            hh = Rot([sb("hh", [128, 2, 512], F32) for _ in range(2)], S.bufs_n(2))
            tt_ = [Rot([sb("cp%d" % i, [128, 512], F32) for _ in range(2)], S.bufs_n(2)) for i in range(4)]
            ti = Rot([sb("cti", [128, 2, 16, 512], BF16) for _ in range(2)], S.bufs_n(2))
            ui = Rot([sb("cui", [128, 512], F32) for _ in range(2)], S.bufs_n(2))
            gi_ = Rot([sb("cgi", [128, 512], F32) for _ in range(2)], S.bufs_n(2))
            zo = Rot([sb("czo", [128, 512], F32) for _ in range(3)], S.bufs_n(3))
            zb = Rot([sb("czb", [128, 512], BF16) for _ in range(2)], S.bufs_n(2))
            vt = Rot([sb("cvt", [128, 4, 128], BF16) for _ in range(2)], S.bufs_n(2))
            ups = Rot([stk.enter_context(nc.psum_tensor(self.nm("ups"), [128, 512], F32)) for _ in range(4)], S.bufs_n(4))
            yps = Rot([stk.enter_context(nc.psum_tensor(self.nm("yps"), [128, 512], F32)) for _ in range(2)], S.bufs_n(2))
            tps = Rot([stk.enter_context(nc.psum_tensor(self.nm("ctps"), [128, 4, 128], BF16)) for _ in range(2)], S.bufs_n(2))
            sfv = src_fm.rearrange("(c p) t -> p c t", p=128)
            gfv = gate_fm.rearrange("(c p) t -> p c t", p=128)
            z1v = self.z1fm.rearrange("(c p) t -> p c t", p=128)
            z2v = self.z2T.rearrange("(c p) t -> p c t", p=128)
            pending = []
            for (t0, L) in self.SEQS:
                nT = L // 128
                NW = min(512, L)
                Hs = self.Hs[L]
                tabF = self.tabF2048_in if L == 2048 else self.tabF256_in
                tabI = self.tabI2048_in if L == 2048 else self.tabI256_in
                for db in range(4):
                    v_, Bv = vb.next()
                    yr, Byr = yre.next()
                    yi, Byi = yim.next()
                    S.op("sp", lambda e, v_=v_, t0=t0, L=L, nT=nT, db=db: e.dma_start(
                        out=v_[:, 0:nT, :], in_=src_tm[t0:t0 + L, db * 512:(db + 1) * 512].rearrange("(c p) d -> p c d", p=128)),
                        writes=[Bv], dma=1)
                    for fc in range(nT):
                        t_, Bt = tb.next()
                        h_, Bh = hh.next()
                        S.op("sp", lambda e, t_=t_, fc=fc, nT=nT, tabF=tabF: e.dma_start(out=t_[:, :, 0:nT, :], in_=tabF[fc]), writes=[Bt], dma=1)
                        S.op("sp", lambda e, h_=h_, fc=fc, db=db, Hs=Hs: e.dma_start(
                            out=h_[:], in_=Hs[o, :, fc * 128:(fc + 1) * 128, db * 512:(db + 1) * 512].rearrange("r p d -> p r d")),
                            writes=[Bh], dma=1)
                        pu = []
                        for ri in range(2):
                            p_, Bp = ups.next()
                            for tc in range(nT):
                                S.op("pe", lambda e, p_=p_, t_=t_, ri=ri, tc=tc, v_=v_, nT=nT: e.matmul(
                                    p_[:], lhsT=t_[:, ri, tc, :], rhs=v_[:, tc, :], start=(tc == 0), stop=(tc == nT - 1)),
                                    reads=[Bt, Bv], writes=[Bp])
                            pu.append((p_, Bp))
                        (ur, Bur), (um, Bum) = pu
                        (a1_, Ba1), (a2_, Ba2), (a3_, Ba3), (a4_, Ba4) = [r.next() for r in tt_]
                        S.op("dve", lambda e, a1_=a1_, ur=ur, h_=h_: e.tensor_tensor(out=a1_[:], in0=ur[:], in1=h_[:, 0, :], op=ALU.mult), reads=[Bur, Bh], writes=[Ba1])
                        S.op("dve", lambda e, a2_=a2_, um=um, h_=h_: e.tensor_tensor(out=a2_[:], in0=um[:], in1=h_[:, 1, :], op=ALU.mult), reads=[Bum, Bh], writes=[Ba2])
                        S.op("dve", lambda e, a3_=a3_, um=um, h_=h_: e.tensor_tensor(out=a3_[:], in0=um[:], in1=h_[:, 0, :], op=ALU.mult), reads=[Bum, Bh], writes=[Ba3])
                        S.op("dve", lambda e, a4_=a4_, ur=ur, h_=h_: e.tensor_tensor(out=a4_[:], in0=ur[:], in1=h_[:, 1, :], op=ALU.mult), reads=[Bur, Bh], writes=[Ba4])
                        S.op("pool", lambda e, yr=yr, fc=fc, a1_=a1_, a2_=a2_: e.tensor_tensor(out=yr[:, fc, :], in0=a1_[:], in1=a2_[:], op=ALU.subtract),
                             reads=[Ba1, Ba2], writes=[Byr])
                        S.op("pool", lambda e, yi=yi, fc=fc, a3_=a3_, a4_=a4_: e.tensor_tensor(out=yi[:, fc, :], in0=a3_[:], in1=a4_[:], op=ALU.add),
                             reads=[Ba3, Ba4], writes=[Byi])
                    def ld_ug(tt, dc):
                        oc_ = db * 4 + dc
                        c0_ = t0 + tt * NW
                        u2, Bu2 = ui.next()
                        g2, Bg2 = gi_.next()
                        S.op("sp", lambda e, NW=NW: e.dma_start(out=u2[:, 0:NW], in_=sfv[:, oc_, c0_:c0_ + NW]), writes=[Bu2], dma=1)
                        S.op("sp", lambda e, NW=NW: e.dma_start(out=g2[:, 0:NW], in_=gfv[:, oc_, c0_:c0_ + NW]), writes=[Bg2], dma=1)
                        return (u2, Bu2, g2, Bg2)

                    def ld_tab(tt):
                        c2, Bc2 = ti.next()
                        S.op("sp", lambda e, NW=NW, nT=nT, tabI=tabI: [e.dma_start(
                            out=c2[:, r, 0:nT, 0:NW], in_=tabI[r, :, :, tt * NW:(tt + 1) * NW]) for r in range(2)],
                            writes=[Bc2], dma=2)
                        return (c2, Bc2)
                    its = [(tt, dc) for tt in range(L // NW) for dc in range(4)]
                    pre_ug = ld_ug(*its[0])
                    pre_tab = ld_tab(0)
                    for ii, (tt, dc) in enumerate(its):
                        if dc == 0:
                            c_, Bci = pre_tab
                            if tt + 1 < L // NW:
                                pre_tab = ld_tab(tt + 1)
                        if True:
                            oc = db * 4 + dc
                            col0 = t0 + tt * NW
                            y_, By = yps.next()
                            u_, Bu, g_, Bgt = pre_ug
                            if ii + 1 < len(its):
                                pre_ug = ld_ug(*its[ii + 1])
                            for fc in range(nT):
                                S.op("pe", lambda e, y_=y_, yr=yr, fc=fc, dc=dc, c_=c_, NW=NW: e.matmul(
                                    y_[:, 0:NW], lhsT=yr[:, fc, dc * 128:(dc + 1) * 128], rhs=c_[:, 0, fc, 0:NW], start=(fc == 0), stop=False),
                                    reads=[Byr, Bci], writes=[By])
                                S.op("pe", lambda e, y_=y_, yi=yi, fc=fc, dc=dc, c_=c_, NW=NW, nT=nT: e.matmul(
                                    y_[:, 0:NW], lhsT=yi[:, fc, dc * 128:(dc + 1) * 128], rhs=c_[:, 1, fc, 0:NW], start=False, stop=(fc == nT - 1)),
                                    reads=[Byi, Bci], writes=[By])
                            for fn_ in pending:
                                fn_()
                            pending.clear()
                            S.op("dve", lambda e, u_=u_, y_=y_, oc=oc, NW=NW: e.scalar_tensor_tensor(
                                out=u_[:, 0:NW], in0=u_[:, 0:NW], scalar=skip[:, o, oc:oc + 1], in1=y_[:, 0:NW], op0=ALU.mult, op1=ALU.add),
                                reads=[Bu, By, Bc], writes=[Bu])
                            if o == 0:
                                z_, Bz = zo.next()
                                S.op("pool", lambda e, z_=z_, u_=u_, g_=g_, NW=NW: e.tensor_tensor(out=z_[:, 0:NW], in0=u_[:, 0:NW], in1=g_[:, 0:NW], op=ALU.mult),
                                     reads=[Bu, Bgt], writes=[Bz])
                                S.op("sp", lambda e, z_=z_, oc=oc, col0=col0, NW=NW: e.dma_start(out=z1v[:, oc, col0:col0 + NW], in_=z_[:, 0:NW]), reads=[Bz], dma=1)
                                if NW == 512:
                                    pending.append(lambda z_=z_, Bz=Bz, oc=oc, col0=col0, NW=NW: self.fm_to_tm(
                                        z_, Bz, oc, self.ztmh, zb, vt, tps, ident, Bc, c0=col0, ncols=NW))
                                else:
                                    pending.append(lambda z_=z_, Bz=Bz, oc=oc, col0=col0: self.fm_to_tm_small(
                                        z_, Bz, oc, zb, vt, tps, ident, Bc, col0))
                            else:
                                zb_, Bzb = zb.next()
                                S.op("pool", lambda e, zb_=zb_, u_=u_, g_=g_, NW=NW: e.tensor_tensor(out=zb_[:, 0:NW], in0=u_[:, 0:NW], in1=g_[:, 0:NW], op=ALU.mult),
                                     reads=[Bu, Bgt], writes=[Bzb])
                                S.op("sp", lambda e, zb_=zb_, oc=oc, col0=col0, NW=NW: e.dma_start(out=z2v[:, oc, col0:col0 + NW], in_=zb_[:, 0:NW]), reads=[Bzb], dma=1)
            for fn_ in pending:
                fn_()
            pending.clear()
            S.flush()

    def fm_to_tm_small(self, u_, Bu, oc, ub, vt, tps, ident, Bc, c0):
        S = self.S
        b_, Bb = ub.next()
        S.op("act", lambda e: e.activation(out=b_[:, 0:256], in_=u_[:, 0:256], func=AF.Copy), reads=[Bu], writes=[Bb])
        dv = self.ztmh.rearrange("(c p) d -> p c d", p=128)
        tp, Bt = tps.next()
        v_, Bv = vt.next()
        for i in range(2):
            S.op("pe", lambda e, i=i: e.transpose(tp[:, i, :], b_[:, i * 128:(i + 1) * 128], ident[:]), reads=[Bb, Bc], writes=[Bt])
        S.op("act", lambda e: e.activation(out=v_[:, 0:2, :], in_=tp[:, 0:2, :], func=AF.Copy), reads=[Bt], writes=[Bv])
        tc0 = c0 // 128
        S.op("sp", lambda e: e.dma_start(out=dv[:, tc0:tc0 + 2, oc * 128:(oc + 1) * 128], in_=v_[:, 0:2, :]), reads=[Bv], dma=1)


def pp(v):
    v = np.asarray(v)
    n = v.shape[-1] // 128
    return np.ascontiguousarray(np.moveaxis(v.reshape(v.shape[:-1] + (n, 128)), -1, 0))


def rope_tables():
    half = HD // 2
    t = np.arange(TL)
    row = (t // 64).astype(np.float32)
    col = (t % 64).astype(np.float32)
    inv = (10000.0 ** (-np.arange(0, half, 2, dtype=np.float32) / half)).astype(np.float32)
    ang = np.zeros((128, TL), np.float32)
    ang[0:32] = inv[:, None] * row[None, :]
    ang[32:64] = inv[:, None] * row[None, :]
    ang[64:96] = inv[:, None] * col[None, :]
    ang[96:128] = inv[:, None] * col[None, :]
    tab = np.stack([np.cos(ang), np.sin(ang)], axis=1).astype(np.float32)
    P = np.zeros((128, 128), np.float32)
    for base in (0, 64):
        for m in range(32):
            P[base + m + 32, base + m] = -1.0
            P[base + m, base + m + 32] = 1.0
    return tab, P.astype(NPBF)


def window_masks():
    kj = np.arange(128)[:, None]
    qi = np.arange(128)[None, :]
    prev = (kj >= qi).astype(np.float32)
    nxt = (kj <= qi).astype(np.float32)
    m = np.stack([np.tile(prev, (1, 4)), np.tile(nxt, (1, 4))], axis=1)
    return m.astype(NPBF)


def make_in_maps(inp, kb):
    f = lambda a: np.ascontiguousarray(np.asarray(a, dtype=np.float32))
    rope, P = rope_tables()
    wm = window_masks()
    shared = {
        "normg_in": pp(np.stack([f(inp["norm_mix_g"]), f(inp["norm_ffn_g"])], 0)),
        "modb_in": pp(f(inp["mod_b"])),
        "finalg_in": pp(f(inp["final_g"])),
        "esink_in": np.ascontiguousarray(np.repeat(f(inp["win_sink"]), 128, axis=1)[:, None, :]),
        "axg_in": np.ascontiguousarray(np.stack([f(inp["ax_q_g"])[0], f(inp["ax_k_g"])[0]], axis=1)),
        "axkg_rep": np.ascontiguousarray(np.tile(f(inp["ax_k_g"])[0][None, :], (128, 1))),
        "rope_in": rope, "ropeP_in": P, "wmask_in": wm,
    }
    shared.update(hyena_shared(inp))
    for name in kb.inputs:
        if "__" in name:
            base, idx = name.split("__")
            shared[name] = np.ascontiguousarray(f(inp[base])[int(idx)])
    maps = []
    xs, xp = f(inp["x_sample"]), f(inp["x_prompt"])
    for i in range(8):
        m = dict(shared)
        xt = np.concatenate([xs[i], xp[4 * i:4 * i + 4].reshape(1024, D)], axis=0)
        m["xT_in"] = np.ascontiguousarray(xt.T)
        m["cvec"] = np.ascontiguousarray(np.stack([pp(f(inp["c"])[i]), pp(f(inp["c_ctx"]))], axis=-1))
        m["cwkT"] = np.ascontiguousarray(f(inp["cache_win_k"])[i].transpose(0, 2, 3, 1))
        m["cwv"] = np.ascontiguousarray(f(inp["cache_win_v"])[i].reshape(2, PAST, 512))
        m["cakT"] = np.ascontiguousarray(f(inp["cache_ax_k"])[i].transpose(0, 2, 3, 1))
        m["cav"] = np.ascontiguousarray(f(inp["cache_ax_v"])[i].reshape(1, PAST, 512))
        maps.append({k: v for k, v in m.items() if k in kb.inputs})
    return maps


def hyena_shared(inp):
    f = lambda a: np.ascontiguousarray(np.asarray(a, dtype=np.float32))
    out = {}
    cw = f(inp["hy_conv_w"])[0]
    cb = f(inp["hy_conv_b"])[0]
    out["hyconv_in"] = pp(np.stack([cw[0], cw[1], cw[2], cb], axis=0)).transpose(0, 2, 1).copy()
    out["ident_in"] = np.eye(128, dtype=np.float32).astype(NPBF)
    out["hyskip_in"] = pp(f(inp["hy_skip"])[0])
    out["hyw1_in"] = f(inp["hy_f_w1"])[0]
    out["hyw2_in"] = f(inp["hy_f_w2"])[0]
    out["hyw3_in"] = f(inp["hy_f_w3"])[0]
    fr = f(inp["hy_freq"])[0]
    out["hyfb_in"] = np.ascontiguousarray(np.stack([fr[0], fr[1], f(inp["hy_f_b1"])[0], f(inp["hy_f_b2"])[0]], axis=1))
    HY_MIN = math.log(1e-2) / 1.5
    HY_MAX = math.log(1e-2) / 0.3
    deltas = np.abs(np.linspace(HY_MIN, HY_MAX, D, dtype=np.float32))
    out["drep_in"] = np.ascontiguousarray(np.tile(deltas[None, :], (128, 1)).astype(np.float32))
    for L in (2048, 256):
        t = np.arange(L, dtype=np.float32)
        tn = (t / max(L - 1, 1)).astype(np.float32)
        bands = 16
        fbv = np.linspace(1e-4, bands - 1, bands, dtype=np.float32)
        w = (np.float32(2.0 * math.pi) * t / np.float32(L)).astype(np.float32)
        feats = np.concatenate([tn[:, None], np.cos(w[:, None] * fbv), -np.sin(w[:, None] * fbv)], axis=-1).astype(np.float32)
        out["feats%d_in" % L] = np.ascontiguousarray(feats.T)
        out["tnn%d_in" % L] = pp(-tn)
        nT = L // 128
        tt = np.arange(L, dtype=np.float64)
        ff = np.arange(L, dtype=np.float64)
        ang = np.pi * np.outer(tt, 2 * ff + 1) / (2 * L)
        C = np.cos(ang)
        S_ = np.sin(ang)
        CF = np.stack([C, S_], 0).reshape(2, nT, 128, nT, 128)
        out["tabF%d_in" % L] = np.ascontiguousarray(CF.transpose(3, 2, 0, 1, 4)).astype(NPBF)
        CI = np.stack([C.T, S_.T], 0) / L
        out["tabI%d_in" % L] = np.ascontiguousarray(CI.reshape(2, nT, 128, L).transpose(0, 2, 1, 3)).astype(NPBF)
    return out


_CACHE = {}


def get_kb():
    if "kb" not in _CACHE:
        kb = KB()
        kb.build()
        _CACHE["kb"] = kb
    return _CACHE["kb"]


def kernel(**inp):
    kb = get_kb()
    maps = make_in_maps(inp, kb)
    res = run_bass_kernel_spmd(kb.nc, maps, core_ids=list(range(8)))
    R = res.results
    y_s = np.stack([R[i]["yT"][:, :TL].T for i in range(8)], 0)
    y_p = np.concatenate([R[i]["yT"][:, TL:].T.reshape(4, 256, D) for i in range(8)], 0)
    outs = [np.ascontiguousarray(y_p), np.ascontiguousarray(y_s)]
    for nm_ in ("swk", "swv", "sak", "sav"):
        a = np.concatenate([R[i][nm_] for i in range(8)], 0)
        outs.append(np.ascontiguousarray(a.reshape(a.shape[0], a.shape[1], 256, NKV, HD)))
    return tuple(outs)
```

```python
import math
import os
from contextlib import ExitStack

import numpy as np
import ml_dtypes

import concourse.bass as bass
import concourse.mybir as mybir
from concourse.bass_utils import run_bass_kernel_spmd

F32 = mybir.dt.float32
BF16 = mybir.dt.bfloat16
AF = mybir.ActivationFunctionType
ALU = mybir.AluOpType
NPBF = ml_dtypes.bfloat16

D = 2048
DC = 16
T = 3072
TL = 2048
NTT = 6
TW = 512
DFF = 5632
FC = 44
NH = 16
NKV = 4
HD = 128
PAST = 512
EPS = 1e-6
DEPTH = 4
ENG_NAMES = ("pe", "act", "dve", "pool", "sp")


class Buf:
    __slots__ = ("name", "w", "r")

    def __init__(self, name=""):
        self.name = name
        self.w = None
        self.r = []


class Op:
    __slots__ = ("eng", "fn", "deps", "flag", "val", "sem", "dma")

    def __init__(self, eng, fn, dma):
        self.eng = eng
        self.fn = fn
        self.deps = []
        self.flag = False
        self.val = 0
        self.sem = None
        self.dma = dma


class Sched:
    def __init__(self, nc, stack, n_dma_sems=48):
        self.nc = nc
        self.sem = {e: stack.enter_context(nc.semaphore("s_" + e)) for e in ENG_NAMES}
        self.cnt = {e: 0 for e in ENG_NAMES}
        self.dma_sems = [stack.enter_context(nc.semaphore("s_dma%d" % i)) for i in range(n_dma_sems)]
        self.dma_val = [0] * n_dma_sems
        self.dma_last = [None] * n_dma_sems
        self.dma_rr = 0
        self.ops = {e: [] for e in ENG_NAMES}
        self.waited = {e: {} for e in ENG_NAMES}
        self.bufs = []
        self.n_ops = 0
        self.stopped = False
        self.sp_def = []
        self.DEFER = int(os.environ.get("K_DEFER", "3"))
        self.DEFER0 = self.DEFER

    def buf(self, name=""):
        b = Buf(name)
        self.bufs.append(b)
        return b

    def bufs_n(self, n, name=""):
        return [self.buf("%s%d" % (name, i)) for i in range(n)]

    def op(self, eng, fn, reads=(), writes=(), dma=0):
        o = Op(eng, fn, dma)
        deps = {}

        def add(d, war):
            if d is None or d is o:
                return
            if d.dma == 0 and d.eng == eng and dma == 0:
                if eng == "pe" or war:
                    return
            deps[id(d)] = d

        for b in reads:
            add(b.w, False)
        for b in writes:
            add(b.w, False)
            for r in b.r:
                add(r, True)
        if dma:
            k = self.dma_rr
            self.dma_rr = (k + 1) % len(self.dma_sems)
            prev = self.dma_last[k]
            if prev is not None:
                deps[id(prev)] = prev
            self.dma_val[k] += 16 * dma
            o.val = self.dma_val[k]
            o.sem = self.dma_sems[k]
            self.dma_last[k] = o
        for d in deps.values():
            d.flag = True
        o.deps = list(deps.values())
        for b in reads:
            if dma == 0:
                b.r = [r for r in b.r if not (r.dma == 0 and r.eng == eng)]
            b.r.append(o)
        for b in writes:
            b.w = o
            b.r = []
        self.n_ops += 1
        if eng == "sp":
            if reads and self.DEFER > 0:
                self.sp_def.append([o, 0])
                return o
            if self.sp_def:
                defd = set(id(x[0]) for x in self.sp_def)
                if any(id(d) in defd for d in o.deps):
                    for x in self.sp_def:
                        self.ops["sp"].append(x[0])
                    self.sp_def = []
            self.ops["sp"].append(o)
            keep = []
            for x in self.sp_def:
                x[1] += 1
                if x[1] >= self.DEFER:
                    self.ops["sp"].append(x[0])
                else:
                    keep.append(x)
            self.sp_def = keep
            return o
        self.ops[eng].append(o)
        return o

    def flush(self):
        nc = self.nc
        for x in self.sp_def:
            self.ops["sp"].append(x[0])
        self.sp_def = []
        lasts = {}
        for e in ENG_NAMES:
            for o in reversed(self.ops[e]):
                if o.dma == 0:
                    lasts[e] = o
                    o.flag = True
                    break
        for e in ENG_NAMES:
            for o in self.ops[e]:
                if o.dma == 0 and o.flag:
                    self.cnt[e] += 1
                    o.val = self.cnt[e]
                    o.sem = self.sem[e]
        dma_final = [(self.dma_sems[k], self.dma_val[k]) for k in range(len(self.dma_sems))
                     if self.dma_val[k] > 0]
        with nc.Block() as block:
            for e in ENG_NAMES:
                def body(eng, e=e):
                    waited = self.waited[e]

                    def wait(sem, val):
                        key = id(sem)
                        if waited.get(key, 0) < val:
                            eng.wait_ge(sem, val)
                            waited[key] = val

                    for o in self.ops[e]:
                        for d in o.deps:
                            wait(d.sem, d.val)
                        r = o.fn(eng)
                        if o.dma:
                            if not isinstance(r, (list, tuple)):
                                r = [r]
                            assert len(r) == o.dma, (len(r), o.dma)
                            for ins in r:
                                ins.then_inc(o.sem, 16)
                        elif o.flag:
                            r.then_inc(o.sem, 1)
                    for e2 in ENG_NAMES:
                        if e2 in lasts:
                            wait(lasts[e2].sem, lasts[e2].val)
                    for sem, val in dma_final:
                        wait(sem, val)
                name = {"pe": "tensor", "act": "scalar", "dve": "vector",
                        "pool": "gpsimd", "sp": "sync"}[e]
                getattr(block, name)(body)
        self.ops = {e: [] for e in ENG_NAMES}
        for b in self.bufs:
            b.w = None
            b.r = []
        self.bufs = []
        self.dma_last = [None] * len(self.dma_sems)
        self.nflush = getattr(self, "nflush", 0) + 1
        if self.nflush == getattr(self, "stop", -1):
            self.stopped = True


class StopBuild(Exception):
    pass


class Rot:
    def __init__(self, tiles, bufs):
        self.t = tiles
        self.b = bufs
        self.i = 0

    def next(self):
        k = self.i % len(self.t)
        self.i += 1
        return self.t[k], self.b[k]


class KB:
    def __init__(self, n_layers=DEPTH, dbg=False, stop=-1):
        self.nc = bass.Bass("TRN2", target_bir_lowering=False)
        self.stop = stop
        self.n_layers = n_layers
        self.dbg = dbg
        self.uid = 0
        self.inputs = {}

    def nm(self, s):
        self.uid += 1
        return "%s_%d" % (s, self.uid)

    def din(self, name, shape, dt=F32):
        ap = self.nc.dram_tensor(name, list(shape), dt, kind="ExternalInput").ap()
        self.inputs[name] = (tuple(shape), dt)
        return ap

    def dout(self, name, shape, dt=F32):
        return self.nc.dram_tensor(name, list(shape), dt, kind="ExternalOutput").ap()

    def dscr(self, name, shape, dt):
        return self.nc.dram_tensor(name, list(shape), dt).ap()

    IN_SPECS = {
        "xT_in": ([D, T], F32), "cvec": ([128, DC, 2], F32), "normg_in": ([128, 2, DEPTH, DC], F32),
        "modb_in": ([128, DEPTH, 96], F32), "finalg_in": ([128, DC], F32), "mod_w": ([DEPTH, D, 6 * D], F32),
        "win_wqkv": ([2, D, 3072], F32), "win_wo": ([2, D, D], F32), "esink_in": ([2, 1, NH * 128], F32),
        "ax_wqkv": ([1, D, 3072], F32), "ax_wo": ([1, D, D], F32), "axg_in": ([128, 2], F32),
        "axkg_rep": ([128, 128], F32), "ffn_w_gu": ([DEPTH, D, 2 * DFF], F32), "ffn_w_down": ([DEPTH, DFF, D], F32),
        "cwkT": ([2, NKV, 128, PAST], F32), "cwv": ([2, PAST, 512], F32), "cakT": ([1, NKV, 128, PAST], F32),
        "cav": ([1, PAST, 512], F32), "rope_in": ([128, 2, TL], F32), "ropeP_in": ([128, 128], BF16),
        "wmask_in": ([128, 2, 512], BF16),
        "hy_w_in": ([1, D, 3 * D], F32), "hy_wo": ([1, D, D], F32),
        "hyconv_in": ([128, 48, 4], F32), "ident_in": ([128, 128], BF16), "hyskip_in": ([128, 2, DC], F32),
        "hyw1_in": ([33, 64], F32), "hyw2_in": ([64, 64], F32), "hyw3_in": ([64, 4 * D], F32), "hyfb_in": ([64, 4], F32),
        "drep_in": ([128, D], F32),
        "feats2048_in": ([33, 2048], F32), "feats256_in": ([33, 256], F32),
        "tnn2048_in": ([128, 16], F32), "tnn256_in": ([128, 2], F32),
        "tabF2048_in": ([16, 128, 2, 16, 128], BF16), "tabF256_in": ([2, 128, 2, 2, 128], BF16),
        "tabI2048_in": ([2, 128, 16, 2048], BF16), "tabI256_in": ([2, 128, 2, 256], BF16),
    }

    LAYERED = ("mod_w", "win_wqkv", "win_wo", "ax_wqkv", "ax_wo", "ffn_w_gu", "ffn_w_down", "hy_w_in", "hy_wo")

    def W(self, base, idx):
        name = "%s__%d" % (base, idx)
        if name not in self.inputs:
            shape, dt = type(self).IN_SPECS[base]
            self._lay = getattr(self, "_lay", {})
            self._lay[name] = self.din(name, shape[1:], dt)
        return self._lay[name]

    def __getattr__(self, name):
        specs = type(self).IN_SPECS
        if name in specs and name not in type(self).LAYERED:
            shape, dt = specs[name]
            ap = self.din(name, shape, dt)
            setattr(self, name, ap)
            return ap
        raise AttributeError(name)

    def declare(self):
        nc = self.nc
        self.yT = self.dout("yT", [D, T])
        self.swk = self.dout("swk", [4, 2, 256, 512])
        self.swv = self.dout("swv", [4, 2, 256, 512])
        self.sak = self.dout("sak", [4, 1, 256, 512])
        self.sav = self.dout("sav", [4, 1, 256, 512])
        if self.dbg:
            self.xT = self.dout("xT_dbg", [D, T])
        else:
            self.xT = self.dscr("xT", [D, T], F32)
        self.hT = self.dscr("hT", [D, T], BF16)
        self.qT = self.dscr("qT", [D, T], BF16)
        self.kT = self.dscr("kT", [512, T], BF16)
        self.vtm = self.dscr("vtm", [T, 512], BF16)
        self.oT = self.dscr("oT", [D, T], BF16)
        self.actT = self.dscr("actT", [DFF, T], BF16)

    def persistent(self, stk):
        nc = self.nc
        self.mod = stk.enter_context(nc.sbuf_tensor("mod", [128, DEPTH, 2, 96], F32))
        self.modb = stk.enter_context(nc.sbuf_tensor("modb_sb", [128, DEPTH, 96], F32))
        self.normg = stk.enter_context(nc.sbuf_tensor("normg_sb", [128, 2, DEPTH, DC], F32))
        self.amod = stk.enter_context(nc.sbuf_tensor("amod", [128, DEPTH, 2, 2, DC], F32))
        self.finalg = stk.enter_context(nc.sbuf_tensor("finalg_sb", [128, DC], F32))
        self.ones_bf = stk.enter_context(nc.sbuf_tensor("ones_bf", [128, 128], BF16))
        self.ones_d = stk.enter_context(nc.sbuf_tensor("ones_d", [128, 128], BF16))
        self.ones_h = stk.enter_context(nc.sbuf_tensor("ones_h", [128, 128], BF16))
        self.zero_c = stk.enter_context(nc.sbuf_tensor("zero_c", [128, 1], F32))
        self.eps_c = stk.enter_context(nc.sbuf_tensor("eps_c", [128, 1], F32))

    def stage_mod(self):
        nc, S = self.nc, self.S
        if S.stopped:
            return
        with ExitStack() as stk:
            def sb(n, s, d):
                return stk.enter_context(nc.sbuf_tensor(self.nm(n), s, d))
            cv = sb("cv", [128, DC, 2], F32)
            sg = sb("sg", [128, DC, 2], F32)
            sT = sb("sT", [128, DC, 2], BF16)
            NW = 4
            wbuf = [sb("mw", [128, DC, 512], BF16) for _ in range(NW)]
            Bw = S.bufs_n(NW)
            pst = [stk.enter_context(nc.psum_tensor(self.nm("mps"), [128, 4, 2], F32)) for _ in range(4)]
            Bp = S.bufs_n(4)
            Bcv, BsT, Bmod, Bmb, Bng, Bc = S.buf(), S.buf(), S.buf(), S.buf(), S.buf(), S.buf()
            S.op("sp", lambda e: e.dma_start(out=cv[:], in_=self.cvec), writes=[Bcv], dma=1)
            S.op("sp", lambda e: e.dma_start(out=self.modb[:], in_=self.modb_in), writes=[Bmb], dma=1)
            S.op("sp", lambda e: e.dma_start(out=self.normg[:], in_=self.normg_in), writes=[Bng], dma=1)
            S.op("sp", lambda e: e.dma_start(out=self.finalg[:], in_=self.finalg_in), writes=[Bng], dma=1)
            S.op("dve", lambda e: e.memset(self.ones_bf[:], 1.0), writes=[Bc])
            S.op("dve", lambda e: e.memset(self.ones_d[:], 1.0 / D), writes=[Bc])
            S.op("dve", lambda e: e.memset(self.ones_h[:], 1.0 / HD), writes=[Bc])
            S.op("dve", lambda e: e.memset(self.zero_c[:], 0.0), writes=[Bc])
            S.op("dve", lambda e: e.memset(self.eps_c[:], EPS), writes=[Bc])
            S.op("act", lambda e: e.activation(out=sg[:], in_=cv[:], func=AF.Sigmoid), reads=[Bcv], writes=[BsT])
            S.op("dve", lambda e: e.tensor_tensor(out=sT[:], in0=sg[:], in1=cv[:], op=ALU.mult), reads=[Bcv, BsT], writes=[BsT])
            i = 0
            for l in range(self.n_layers):
                wv = self.W("mod_w", l).rearrange("(c p) n -> p c n", p=128)
                for nb in range(24):
                    s = i % NW
                    ps, bp = pst[i % 4], Bp[i % 4]
                    S.op("pool", lambda e, s=s, nb=nb, wv=wv: e.dma_start(out=wbuf[s][:], in_=wv[:, :, nb * 512:(nb + 1) * 512]),
                         writes=[Bw[s]], dma=1)
                    for oc in range(4):
                        for kc in range(DC):
                            S.op("pe", lambda e, s=s, oc=oc, kc=kc, ps=ps: e.matmul(
                                ps[:, oc, :], lhsT=wbuf[s][:, kc, oc * 128:(oc + 1) * 128], rhs=sT[:, kc, :],
                                start=(kc == 0), stop=(kc == DC - 1)), reads=[Bw[s], BsT], writes=[bp])
                    for g in range(2):
                        S.op("dve", lambda e, l=l, g=g, nb=nb, ps=ps: e.tensor_tensor(
                            out=self.mod[:, l, g, nb * 4:(nb + 1) * 4], in0=ps[:, :, g],
                            in1=self.modb[:, l, nb * 4:(nb + 1) * 4], op=ALU.add), reads=[bp, Bmb], writes=[Bmod])
                    i += 1
            for l in range(self.n_layers):
                for g in range(2):
                    for w in range(2):
                        sc = 16 + 48 * w
                        S.op("dve", lambda e, l=l, g=g, w=w, sc=sc: e.scalar_tensor_tensor(
                            out=self.amod[:, l, g, w, :], in0=self.mod[:, l, g, sc:sc + 16], scalar=1.0,
                            in1=self.normg[:, w, l, :], op0=ALU.add, op1=ALU.mult), reads=[Bmod, Bng], writes=[Bc])
            S.flush()

    def modcol(self, l, g, which, c):
        return self.mod[:, l, g, which * 16 + c: which * 16 + c + 1]

    def stage_copy_in(self):
        nc, S = self.nc, self.S
        if S.stopped:
            return
        with ExitStack() as stk:
            xs = [stk.enter_context(nc.sbuf_tensor(self.nm("cpx"), [128, DC, TW], F32)) for _ in range(3)]
            Bx = S.bufs_n(3)
            xi = self.xT_in.rearrange("(c p) t -> p c t", p=128)
            xo = self.xT.rearrange("(c p) t -> p c t", p=128)
            for tt in range(NTT):
                k = tt % 3
                S.op("sp", lambda e, k=k, tt=tt: e.dma_start(out=xs[k][:], in_=xi[:, :, tt * TW:(tt + 1) * TW]), writes=[Bx[k]], dma=1)
                S.op("sp", lambda e, k=k, tt=tt: e.dma_start(out=xo[:, :, tt * TW:(tt + 1) * TW], in_=xs[k][:]), reads=[Bx[k]], dma=1)
            S.flush()

    def stage_norm(self, l, w, final=False):
        nc, S = self.nc, self.S
        if S.stopped:
            return
        S.DEFER = S.DEFER0
        with ExitStack() as stk:
            def sb(n, s, d):
                return stk.enter_context(nc.sbuf_tensor(self.nm(n), s, d))
            NX = 2
            xin = [sb("xin", [128, DC, TW], F32) for _ in range(NX)]
            Bx = [S.bufs_n(4) for _ in range(NX)]
            sq = [sb("sq", [128, DC, TW], BF16) for _ in range(2)]
            Bsq = [S.bufs_n(4) for _ in range(2)]
            pss = [stk.enter_context(nc.psum_tensor(self.nm("nps"), [128, TW], F32)) for _ in range(2)]
            Bps = S.bufs_n(2)
            rstd = [sb("rstd", [128, TW], F32) for _ in range(2)]
            Br = S.bufs_n(2)
            tmp = Rot([sb("ntmp", [128, TW], F32) for _ in range(4)], S.bufs_n(4))
            odt = F32 if final else BF16
            hs = [sb("hs", [128, DC, TW], odt) for _ in range(2)]
            Bh = [S.bufs_n(4) for _ in range(2)]
            xv = (self.xT_in if (l == 0 and w == 0 and not final) else self.xT).rearrange("(c p) t -> p c t", p=128)
            ov = (self.yT if final else self.hT).rearrange("(c p) t -> p c t", p=128)

            def load(tt):
                k = tt % NX
                for q in range(4):
                    S.op("sp", lambda e, k=k, q=q, tt=tt: e.dma_start(
                        out=xin[k][:, 4 * q:4 * q + 4, :], in_=xv[:, 4 * q:4 * q + 4, tt * TW:(tt + 1) * TW]),
                        writes=[Bx[k][q]], dma=1)
            load(0)
            for tt in range(NTT):
                if tt + 1 < NTT:
                    load(tt + 1)
                g = 0 if tt < 4 else 1
                k = tt % NX
                k2 = tt % 2
                for q in range(4):
                    S.op("act", lambda e, k=k, k2=k2, q=q: e.activation(
                        out=sq[k2][:, 4 * q:4 * q + 4, :], in_=xin[k][:, 4 * q:4 * q + 4, :], func=AF.Square),
                        reads=[Bx[k][q]], writes=[Bsq[k2][q]])
                for c in range(DC):
                    S.op("pe", lambda e, k2=k2, c=c: e.matmul(pss[k2][:], lhsT=self.ones_d[:], rhs=sq[k2][:, c, :],
                                                             start=(c == 0), stop=(c == DC - 1)),
                         reads=[Bsq[k2][c // 4]], writes=[Bps[k2]])
                S.op("act", lambda e, k2=k2: e.activation(out=rstd[k2][:], in_=pss[k2][:], func=AF.Sqrt, bias=self.eps_c[:, 0:1]),
                     reads=[Bps[k2]], writes=[Br[k2]])
                S.op("dve", lambda e, k2=k2: e.reciprocal(out=rstd[k2][:], in_=rstd[k2][:]), reads=[Br[k2]], writes=[Br[k2]])
                for c in range(DC):
                    tm, Btm = tmp.next()
                    S.op("dve", lambda e, k=k, k2=k2, c=c, tm=tm: e.tensor_tensor(
                        out=tm[:], in0=xin[k][:, c, :], in1=rstd[k2][:], op=ALU.mult),
                        reads=[Bx[k][c // 4], Br[k2]], writes=[Btm])
                    if final:
                        sc_ap, bi_ap = self.finalg[:, c:c + 1], self.zero_c[:, 0:1]
                    else:
                        sc_ap, bi_ap = self.amod[:, l, g, w, c:c + 1], self.modcol(l, g, 3 * w, c)
                    if c % 3 == 2:
                        S.op("dve", lambda e, k2=k2, c=c, tm=tm, sc_ap=sc_ap, bi_ap=bi_ap: e.tensor_scalar(
                            out=hs[k2][:, c, :], in0=tm[:], scalar1=sc_ap, scalar2=bi_ap, op0=ALU.mult, op1=ALU.add),
                            reads=[Btm], writes=[Bh[k2][c // 4]])
                    else:
                        S.op("act", lambda e, k2=k2, c=c, tm=tm, sc_ap=sc_ap, bi_ap=bi_ap: e.activation(
                            out=hs[k2][:, c, :], in_=tm[:], func=AF.Identity, bias=bi_ap, scale=sc_ap),
                            reads=[Btm], writes=[Bh[k2][c // 4]])
                for q in range(4):
                    S.op("sp", lambda e, k2=k2, q=q, tt=tt: e.dma_start(
                        out=ov[:, 4 * q:4 * q + 4, tt * TW:(tt + 1) * TW], in_=hs[k2][:, 4 * q:4 * q + 4, :]),
                        reads=[Bh[k2][q]], dma=1)
            S.flush()

    def gemm(self, A_dram, KC, blocks, epilogue, stk, supertiles=None, nw=3, extra=None):
        self.S.DEFER = self.S.DEFER0
        nc, S = self.nc, self.S
        if supertiles is None:
            supertiles = [list(range(NTT))]
        pend = []
        maxt = max(len(s) for s in supertiles)
        bw = max(sum(n for _, _, n in segs) for segs, _ in blocks)
        A_sb = stk.enter_context(nc.sbuf_tensor(self.nm("A"), [128, KC, maxt * TW], BF16))
        BA = S.bufs_n(maxt)
        wb = [stk.enter_context(nc.sbuf_tensor(self.nm("W"), [128, KC, bw], BF16)) for _ in range(nw)]
        Bw = S.bufs_n(nw)
        npb = 8 if extra is None else 8 - extra
        banks = Rot([stk.enter_context(nc.psum_tensor(self.nm("gps"), [128, TW], F32)) for _ in range(npb)], S.bufs_n(npb))
        Av = A_dram.rearrange("(c p) t -> p c t", p=128)
        wi = 0
        for st in supertiles:
            for j, tt in enumerate(st):
                npc = 4 if KC > 16 else 2
                h = KC // npc
                S.op("sp", lambda e, j=j, tt=tt, h=h, npc=npc: [
                    e.dma_start(out=A_sb[:, i * h:(i + 1) * h, j * TW:(j + 1) * TW], in_=Av[:, i * h:(i + 1) * h, tt * TW:(tt + 1) * TW])
                    for i in range(npc)], writes=[BA[j]], dma=npc)
            for bi, (segs, groups) in enumerate(blocks):
                s = wi % nw
                wi += 1

                nsp = 4 if KC > 16 else 1
                hk = KC // nsp

                def wload(e, s=s, segs=segs, nsp=nsp, hk=hk):
                    r = []
                    off = 0
                    for W_ap, c0, n in segs:
                        Wv = W_ap.rearrange("(c p) n -> p c n", p=128)
                        for i in range(nsp):
                            r.append(e.dma_start(out=wb[s][:, i * hk:(i + 1) * hk, off:off + n], in_=Wv[:, i * hk:(i + 1) * hk, c0:c0 + n]))
                        off += n
                    return r
                S.op("pool", wload, writes=[Bw[s]], dma=len(segs) * nsp)
                for j, tt in enumerate(st):
                    for gi, grp in enumerate(groups):
                        pl = []
                        for coff in grp:
                            ps, Bp = banks.next()
                            for kc in range(KC):
                                S.op("pe", lambda e, ps=ps, s=s, kc=kc, coff=coff, j=j: e.matmul(
                                    ps[:], lhsT=wb[s][:, kc, coff:coff + 128], rhs=A_sb[:, kc, j * TW:(j + 1) * TW],
                                    start=(kc == 0), stop=(kc == KC - 1)), reads=[Bw[s], BA[j]], writes=[Bp])
                            pl.append((ps, Bp))
                        if pend:
                            epilogue(*pend.pop())
                        pend.append((bi, gi, tt, pl))
        if pend:
            epilogue(*pend.pop())
        return A_sb, BA, wb, Bw

    def make_resid_epilogue(self, stk, l, which, chunk_of):
        nc, S = self.nc, self.S
        xt = Rot([stk.enter_context(nc.sbuf_tensor(self.nm("xr"), [128, TW], F32)) for _ in range(6)], S.bufs_n(6))
        xv = self.xT.rearrange("(c p) t -> p c t", p=128)
        xsrc = (self.xT_in if (l == 0 and which == 2) else self.xT).rearrange("(c p) t -> p c t", p=128)

        def epi(bi, gi, tt, pl):
            oc = chunk_of(bi, gi)
            g = 0 if tt < 4 else 1
            (ps, Bp), = pl
            x, Bx = xt.next()
            S.op("sp", lambda e: e.dma_start(out=x[:], in_=xsrc[:, oc, tt * TW:(tt + 1) * TW]), writes=[Bx], dma=1)
            S.op("dve", lambda e: e.scalar_tensor_tensor(out=x[:], in0=ps[:], scalar=self.modcol(l, g, which, oc), in1=x[:],
                                                         op0=ALU.mult, op1=ALU.add), reads=[Bp, Bx], writes=[Bx])
            S.op("sp", lambda e: e.dma_start(out=xv[:, oc, tt * TW:(tt + 1) * TW], in_=x[:]), reads=[Bx], dma=1)
        return epi

    def stage_ffn(self, l):
        nc, S = self.nc, self.S
        self.stage_norm(l, 1)
        Wgu = self.W("ffn_w_gu", l)
        if S.stopped:
            return
        with ExitStack() as stk:
            sg = Rot([stk.enter_context(nc.sbuf_tensor(self.nm("sg"), [128, TW], F32)) for _ in range(3)], S.bufs_n(3))
            ao = Rot([stk.enter_context(nc.sbuf_tensor(self.nm("ao"), [128, TW], BF16)) for _ in range(4)], S.bufs_n(4))
            av = self.actT.rearrange("(c p) t -> p c t", p=128)
            blocks = []
            for jb in range(FC // 2):
                blocks.append(([(Wgu, jb * 256, 256), (Wgu, DFF + jb * 256, 256)], [(0, 256), (128, 384)]))

            def epi(bi, gi, tt, pl):
                j = bi * 2 + gi
                (pg, Bg), (pu, Bu) = pl
                s, Bs = sg.next()
                a, Ba = ao.next()
                S.op("act", lambda e: e.activation(out=s[:], in_=pg[:], func=AF.Sigmoid), reads=[Bg], writes=[Bs])
                S.op("dve", lambda e: e.tensor_tensor(out=s[:], in0=s[:], in1=pg[:], op=ALU.mult), reads=[Bs, Bg], writes=[Bs])
                S.op("dve", lambda e: e.tensor_tensor(out=a[:], in0=s[:], in1=pu[:], op=ALU.mult), reads=[Bs, Bu], writes=[Ba])
                S.op("sp", lambda e: e.dma_start(out=av[:, j, tt * TW:(tt + 1) * TW], in_=a[:]), reads=[Ba], dma=1)
            self.gemm(self.hT, DC, blocks, epi, stk)
            S.flush()
        Wd = self.W("ffn_w_down", l)
        if S.stopped:
            return
        with ExitStack() as stk:
            blocks = [([(Wd, b * 256, 256)], [(0,), (128,)]) for b in range(8)]
            epi = self.make_resid_epilogue(stk, l, 5, lambda bi, gi: bi * 2 + gi)
            self.gemm(self.actT, FC, blocks, epi, stk, supertiles=[[0, 1], [2, 3], [4, 5]], nw=3)
            S.flush()

    def stage_attn(self, l, kind, j):
        nc, S = self.nc, self.S
        self.stage_norm(l, 0)
        win = (kind == 0)
        Wqkv = self.W("win_wqkv" if win else "ax_wqkv", j)
        Wo = self.W("win_wo" if win else "ax_wo", j)
        so_k = (self.swk if win else self.sak)
        so_v = (self.swv if win else self.sav)
        if S.stopped:
            return
        with ExitStack() as stk:
            def sb(n, s, d):
                return stk.enter_context(nc.sbuf_tensor(self.nm(n), s, d))
            rope = sb("rope", [128, 2, TL], F32)
            ropeP = sb("ropeP", [128, 128], BF16)
            axg = sb("axg", [128, 2], F32)
            kgrep = sb("kgrep", [128, 128], F32)
            Bc = S.buf()
            S.op("sp", lambda e: e.dma_start(out=rope[:], in_=self.rope_in), writes=[Bc], dma=1)
            S.op("sp", lambda e: e.dma_start(out=ropeP[:], in_=self.ropeP_in), writes=[Bc], dma=1)
            S.op("sp", lambda e: e.dma_start(out=axg[:], in_=self.axg_in), writes=[Bc], dma=1)
            S.op("sp", lambda e: e.dma_start(out=kgrep[:], in_=self.axkg_rep), writes=[Bc], dma=1)
            qn = Rot([sb("qn", [128, TW], BF16) for _ in range(3)], S.bufs_n(3))
            sqh = Rot([sb("sqh", [128, TW], BF16) for _ in range(2)], S.bufs_n(2))
            rs = Rot([sb("rs", [128, TW], F32) for _ in range(2)], S.bufs_n(2))
            t1 = Rot([sb("t1", [128, TW], F32) for _ in range(2)], S.bufs_n(2))
            t2 = Rot([sb("t2", [128, TW], F32) for _ in range(2)], S.bufs_n(2))
            qo = Rot([sb("qo", [128, TW], BF16) for _ in range(3)], S.bufs_n(3))
            xps = Rot([stk.enter_context(nc.psum_tensor(self.nm("xps"), [128, TW], F32)) for _ in range(3)], S.bufs_n(3))
            qv = self.qT.rearrange("(c p) t -> p c t", p=128)
            kv = self.kT.rearrange("(c p) t -> p c t", p=128)
            blocks = [([(Wqkv, b * 512, 512)], [(0,), (128,), (256,), (384,)]) for b in range(5)]

            def epi(bi, gi, tt, pl):
                oc = bi * 4 + gi
                (ps, Bp), = pl
                isk = oc >= NH
                dst = (kv[:, oc - NH, tt * TW:(tt + 1) * TW] if isk else qv[:, oc, tt * TW:(tt + 1) * TW])
                lat = tt < 4
                q_, Bq = qn.next()
                if win:
                    S.op("act", lambda e: e.activation(out=q_[:], in_=ps[:], func=AF.Copy), reads=[Bp], writes=[Bq])
                else:
                    s_, Bs = sqh.next()
                    r_, Br = rs.next()
                    m_, Bm = xps.next()
                    S.op("act", lambda e: e.activation(out=s_[:], in_=ps[:], func=AF.Square), reads=[Bp], writes=[Bs])
                    S.op("pe", lambda e: e.matmul(m_[:], lhsT=self.ones_h[:], rhs=s_[:], start=True, stop=True), reads=[Bs], writes=[Bm])
                    S.op("act", lambda e: e.activation(out=r_[:], in_=m_[:], func=AF.Sqrt, bias=self.eps_c[:, 0:1]),
                         reads=[Bm], writes=[Br])
                    S.op("dve", lambda e: e.reciprocal(out=r_[:], in_=r_[:]), reads=[Br], writes=[Br])
                    gcol = axg[:, 1:2] if isk else axg[:, 0:1]
                    S.op("dve", lambda e: e.scalar_tensor_tensor(out=q_[:], in0=ps[:], scalar=gcol, in1=r_[:], op0=ALU.mult, op1=ALU.mult),
                         reads=[Bp, Br, Bc], writes=[Bq])
                if not lat or os.environ.get('KDBG_NOROPE'):
                    S.op("sp", lambda e: e.dma_start(out=dst, in_=q_[:]), reads=[Bq], dma=1)
                    return
                m_, Bm = xps.next()
                a_, Ba = t1.next()
                b_, Bb = t2.next()
                o_, Bo = qo.next()
                S.op("pe", lambda e: e.matmul(m_[:], lhsT=ropeP[:], rhs=q_[:], start=True, stop=True), reads=[Bq, Bc], writes=[Bm])
                S.op("dve", lambda e: e.tensor_tensor(out=a_[:], in0=q_[:], in1=rope[:, 0, tt * TW:(tt + 1) * TW], op=ALU.mult),
                     reads=[Bq, Bc], writes=[Ba])
                S.op("dve", lambda e: e.tensor_tensor(out=b_[:], in0=m_[:], in1=rope[:, 1, tt * TW:(tt + 1) * TW], op=ALU.mult),
                     reads=[Bm, Bc], writes=[Bb])
                S.op("pool", lambda e: e.tensor_tensor(out=o_[:], in0=a_[:], in1=b_[:], op=ALU.add), reads=[Ba, Bb], writes=[Bo])
                S.op("sp", lambda e: e.dma_start(out=dst, in_=o_[:]), reads=[Bo], dma=1)
            A_sb, BA, wkv, Bwkv = self.gemm(self.hT, DC, blocks, epi, stk, nw=2, extra=3)
            for i_, c0 in enumerate((2048, 2560)):
                S.op("pool", lambda e, i_=i_, c0=c0: e.dma_start(
                    out=wkv[i_][:], in_=Wqkv.rearrange("(c p) n -> p c n", p=128)[:, :, c0:c0 + 512]), writes=[Bwkv[i_]], dma=1)
            vb = Rot([sb("vb", [128, 512], BF16) for _ in range(3)], S.bufs_n(3))
            vf = Rot([sb("vf", [128, 512], F32) for _ in range(3)], S.bufs_n(3))
            ssq = Rot([sb("ssq", [128, 4], F32) for _ in range(2)], S.bufs_n(2))
            junk = Rot([sb("junk", [128, 128], F32) for _ in range(2)], S.bufs_n(2))
            for tc in range(0 if not os.environ.get('KDBG_NOVK') else 999, int(os.environ.get('KDBG_VKMAX', T // 128))):
                ctx = tc >= TL // 128
                for which in ((1, 0) if ctx else (1,)):
                    ps, Bp = xps.next()
                    for kc in range(DC):
                        S.op("pe", lambda e, ps=ps, kc=kc, tc=tc, which=which: e.matmul(
                            ps[:], lhsT=A_sb[:, kc, tc * 128:(tc + 1) * 128], rhs=wkv[which][:, kc, :],
                            start=(kc == 0), stop=(kc == DC - 1)), reads=[BA[tc // 4], Bwkv[which]], writes=[Bp])
                    if which == 1:
                        v_, Bv = vb.next()
                        S.op("act", lambda e, v_=v_, ps=ps: e.activation(out=v_[:], in_=ps[:], func=AF.Copy), reads=[Bp], writes=[Bv])
                        S.op("sp", lambda e, v_=v_, tc=tc: e.dma_start(out=self.vtm[tc * 128:(tc + 1) * 128, :], in_=v_[:]), reads=[Bv], dma=1)
                    if ctx:
                        sq_, half = divmod(tc - TL // 128, 2)
                        f_, Bf = vf.next()
                        if which == 1 or win:
                            S.op("act", lambda e, f_=f_, ps=ps: e.activation(out=f_[:], in_=ps[:], func=AF.Copy), reads=[Bp], writes=[Bf])
                        else:
                            s_, Bs = ssq.next()
                            jk, Bj = junk.next()
                            for h in range(NKV):
                                S.op("act", lambda e, h=h, jk=jk, ps=ps, s_=s_: e.activation(
                                    out=jk[:], in_=ps[:, h * 128:(h + 1) * 128], func=AF.Square, accum_out=s_[:, h:h + 1]),
                                    reads=[Bp], writes=[Bj, Bs])
                            S.op("dve", lambda e, s_=s_: e.tensor_scalar(out=s_[:], in0=s_[:], scalar1=1.0 / HD, scalar2=EPS,
                                                                        op0=ALU.mult, op1=ALU.add), reads=[Bs], writes=[Bs])
                            S.op("act", lambda e, s_=s_: e.activation(out=s_[:], in_=s_[:], func=AF.Sqrt), reads=[Bs], writes=[Bs])
                            S.op("dve", lambda e, s_=s_: e.reciprocal(out=s_[:], in_=s_[:]), reads=[Bs], writes=[Bs])
                            for h in range(NKV):
                                S.op("dve", lambda e, h=h, f_=f_, ps=ps, s_=s_: e.scalar_tensor_tensor(
                                    out=f_[:, h * 128:(h + 1) * 128], in0=ps[:, h * 128:(h + 1) * 128], scalar=s_[:, h:h + 1],
                                    in1=kgrep[:], op0=ALU.mult, op1=ALU.mult), reads=[Bp, Bs, Bc], writes=[Bf])
                        dst = (so_v if which == 1 else so_k)[sq_, j, half * 128:(half + 1) * 128, :]
                        if not os.environ.get('KDBG_NOSTATE'):
                            S.op("sp", lambda e, f_=f_, dst=dst: e.dma_start(out=dst, in_=f_[:]), reads=[Bf], dma=1)
            S.flush()
        self.attn_core(l, kind, j)
        if S.stopped:
            return
        with ExitStack() as stk:
            blocks = [([(Wo, b * 512, 512)], [(0,), (128,), (256,), (384,)]) for b in range(4)]
            epi = self.make_resid_epilogue(stk, l, 2, lambda bi, gi: bi * 4 + gi)
            self.gemm(self.oT, DC, blocks, epi, stk)
            S.flush()

    def attn_core(self, l, kind, j):
        nc, S = self.nc, self.S
        win = (kind == 0)
        ckT = (self.cwkT if win else self.cakT)[j]
        cv_ = (self.cwv if win else self.cav)[j]
        scale = HD ** -0.5
        if S.stopped:
            return
        S.DEFER = S.DEFER0
        with ExitStack() as stk:
            def sb(n, s, d):
                return stk.enter_context(nc.sbuf_tensor(self.nm(n), s, d))
            NKC = T // 128
            k_sb = sb("k_sb", [128, NKV, T + PAST], BF16)
            v_sb = sb("v_sb", [128, NKC + 4, 512], BF16)
            Bk, Bv, Bc = S.buf(), S.buf(), S.buf()
            S.op("sp", lambda e: e.dma_start(out=k_sb[:, :, 0:T], in_=self.kT.rearrange("(g p) t -> p g t", p=128)), writes=[Bk], dma=1)
            S.op("pool", lambda e: e.dma_start(out=k_sb[:, :, T:T + PAST], in_=ckT.rearrange("g p t -> p g t")), writes=[Bk], dma=1)
            S.op("sp", lambda e: e.dma_start(out=v_sb[:, 0:NKC, :], in_=self.vtm.rearrange("(c p) n -> p c n", p=128)), writes=[Bv], dma=1)
            S.op("pool", lambda e: e.dma_start(out=v_sb[:, NKC:NKC + 4, :], in_=cv_.rearrange("(c p) n -> p c n", p=128)), writes=[Bv], dma=1)
            wmask = sb("wmask", [128, 2, 512], BF16)
            S.op("sp", lambda e: e.dma_start(out=wmask[:], in_=self.wmask_in), writes=[Bc], dma=1)
            esk = None
            if win:
                esr = sb("esr", [1, NH * 128], F32)
                esk = sb("esk", [1, NH * 128], BF16)
                ones1 = sb("ones1", [1, 128], BF16)
                S.op("sp", lambda e: e.dma_start(out=esr[:], in_=self.esink_in[j]), writes=[Bc], dma=1)
                S.op("act", lambda e: e.activation(out=esr[:], in_=esr[:], func=AF.Exp), reads=[Bc], writes=[Bc])
                esl = sb("esl", [1, NH * 128], BF16)
                esf = sb("esf", [1, NH * 128], F32)
                S.op("act", lambda e: e.activation(out=esk[:], in_=esr[:], func=AF.Copy), reads=[Bc], writes=[Bc])
                S.op("act", lambda e: e.activation(out=esf[:], in_=esk[:], func=AF.Copy), reads=[Bc], writes=[Bc])
                S.op("dve", lambda e: e.tensor_tensor(out=esf[:], in0=esr[:], in1=esf[:], op=ALU.subtract), reads=[Bc], writes=[Bc])
                S.op("act", lambda e: e.activation(out=esl[:], in_=esf[:], func=AF.Copy), reads=[Bc], writes=[Bc])
                S.op("dve", lambda e: e.memset(ones1[:], 1.0), writes=[Bc])
            q_sb = [sb("q_sb", [128, 4, T], BF16) for _ in range(2)]
            Bq = S.bufs_n(2)
            o_sb = [sb("o_sb", [128, 4, T], BF16) for _ in range(2)]
            Bo = S.bufs_n(2)
            pT = Rot([sb("pT", [128, 512], BF16) for _ in range(5)], S.bufs_n(5))
            rc = Rot([sb("rc", [128, 512], F32) for _ in range(2)], S.bufs_n(2))
            sps = Rot([stk.enter_context(nc.psum_tensor(self.nm("sps"), [128, 512], F32)) for _ in range(4)], S.bufs_n(4))
            dps = Rot([stk.enter_context(nc.psum_tensor(self.nm("dps"), [128, 512], F32)) for _ in range(2)], S.bufs_n(2))
            ops_ = Rot([stk.enter_context(nc.psum_tensor(self.nm("ops"), [128, 512], F32)) for _ in range(2)], S.bufs_n(2))
            seqs = [(0, TL, True)] + [(TL + 256 * s, 256, False) for s in range(4)]
            qv = self.qT.rearrange("(h p) t -> p h t", p=128)
            ov = self.oT.rearrange("(h p) t -> p h t", p=128)
            tasks = []
            for g in range(NKV):
                for (t0, ln, has_cache) in seqs:
                    nqb = ln // 128
                    for qb in range(nqb):
                        chunks = []
                        if win and has_cache:
                            for d_, mk in ((-1, 0), (0, None), (1, 1)):
                                kb = qb + d_
                                if 0 <= kb < nqb:
                                    chunks.append((t0 + kb * 128, (t0 + kb * 128) // 128, mk))
                        else:
                            for kb in range(nqb):
                                chunks.append((t0 + kb * 128, (t0 + kb * 128) // 128, None))
                        if has_cache:
                            for c in range(4):
                                chunks.append((T + c * 128, NKC + c, None))
                        tasks.append((g, t0 + qb * 128, chunks))
            flat = [(ti, ci) for ti, tk in enumerate(tasks) for ci in range(len(tk[2]))]
            LOOK = 3
            loaded = set()
            sbank = {}
            tstate = {}

            def load_q(g):
                if g < NKV and g not in loaded:
                    loaded.add(g)
                    S.op("sp", lambda e, g=g: e.dma_start(out=q_sb[g % 2][:], in_=qv[:, 4 * g:4 * g + 4, :]), writes=[Bq[g % 2]], dma=1)

            def emit_s(idx):
                ti, ci = flat[idx]
                g, q0, chunks = tasks[ti]
                load_q(g)
                kc0 = chunks[ci][0]
                sp_, Bs = sps.next()
                sbank[idx] = (sp_, Bs)
                qs, Bqs = q_sb[g % 2], Bq[g % 2]
                S.op("pe", lambda e, sp_=sp_, kc0=kc0, g=g, qs=qs, q0=q0: e.matmul(
                    sp_[:].rearrange("p (h q) -> p h q", h=4), lhsT=k_sb[:, g, kc0:kc0 + 128], rhs=qs[:, :, q0:q0 + 128],
                    start=True, stop=True), reads=[Bk, Bqs], writes=[Bs])

            for idx in range(min(LOOK, len(flat))):
                emit_s(idx)
            for idx in range(len(flat)):
                if idx + LOOK < len(flat):
                    emit_s(idx + LOOK)
                ti, ci = flat[idx]
                g, q0, chunks = tasks[ti]
                n = len(chunks)
                kc0, vc, mk = chunks[ci]
                os_, Bos = o_sb[g % 2], Bo[g % 2]
                if ci == 0:
                    tstate[ti] = (dps.next(), ops_.next())
                (dp, Bd), (op_, Bop) = tstate[ti]
                sp_, Bs = sbank.pop(idx)
                p_, Bp = pT.next()
                S.op("act", lambda e, sp_=sp_, p_=p_: e.activation(out=p_[:], in_=sp_[:], func=AF.Exp, scale=scale),
                     reads=[Bs], writes=[Bp])
                if mk is not None:
                    S.op("pool", lambda e, p_=p_, mk=mk: e.tensor_tensor(out=p_[:], in0=p_[:], in1=wmask[:, mk, :], op=ALU.mult),
                         reads=[Bp, Bc], writes=[Bp])
                last = (ci == n - 1) and not win
                S.op("pe", lambda e, dp=dp, p_=p_, ci=ci, last=last: e.matmul(
                    dp[:], lhsT=self.ones_bf[:], rhs=p_[:], start=(ci == 0), stop=last), reads=[Bp], writes=[Bd])
                S.op("pe", lambda e, op_=op_, p_=p_, ci=ci, vc=vc, g=g, n=n: e.matmul(
                    op_[:], lhsT=v_sb[:, vc, g * 128:(g + 1) * 128], rhs=p_[:], start=(ci == 0), stop=(ci == n - 1)),
                    reads=[Bp, Bv], writes=[Bop])
                if ci == n - 1:
                    if win:
                        S.op("pe", lambda e, dp=dp, g=g: e.matmul(dp[:], lhsT=ones1[:], rhs=esk[:, g * 512:(g + 1) * 512],
                                                                  start=False, stop=False), reads=[Bc], writes=[Bd])
                        S.op("pe", lambda e, dp=dp, g=g: e.matmul(dp[:], lhsT=ones1[:], rhs=esl[:, g * 512:(g + 1) * 512],
                                                                  start=False, stop=True), reads=[Bc], writes=[Bd])
                    r_, Br = rc.next()
                    S.op("dve", lambda e, r_=r_, dp=dp: e.reciprocal(out=r_[:], in_=dp[:]), reads=[Bd], writes=[Br])
                    S.op("dve", lambda e, r_=r_, op_=op_, os_=os_, q0=q0: e.tensor_tensor(
                        out=os_[:, :, q0:q0 + 128], in0=op_[:].rearrange("p (h q) -> p h q", h=4),
                        in1=r_[:].rearrange("p (h q) -> p h q", h=4), op=ALU.mult), reads=[Bop, Br], writes=[Bos])
                    del tstate[ti]
                    if ti + 1 == len(tasks) or tasks[ti + 1][0] != g:
                        S.op("sp", lambda e, g=g, os_=os_: e.dma_start(out=ov[:, 4 * g:4 * g + 4, :], in_=os_[:]), reads=[Bos], dma=1)
            S.flush()

    def build(self):
        nc = self.nc
        self.declare()
        with ExitStack() as stk:
            self.S = Sched(nc, stk)
            self.S.stop = self.stop
            self.persistent(stk)
            try:
                self.stage_mod()
                for l in range(self.n_layers):
                    kind, j = l % 3, l // 3
                    if kind == 1:
                        self.stage_hyena(l, j)
                    else:
                        self.stage_attn(l, kind, j)
                    self.stage_ffn(l)
                self.stage_norm(0, 0, final=True)
            except StopBuild:
                pass
        return nc


    def hy_decl(self):
        if hasattr(self, "yin"):
            return
        self.yin = self.dscr("yin", [3 * D, T], F32)
        self.vfm = self.dscr("vfm", [D, T], F32)
        self.x1fm = self.dscr("x1fm", [D, T], F32)
        self.x2fm = self.dscr("x2fm", [D, T], F32)
        self.z1fm = self.dscr("z1fm", [D, T], F32)
        self.vtmh = self.dscr("vtmh", [T, D], BF16)
        self.ztmh = self.dscr("ztmh", [T, D], BF16)
        self.z2T = self.dscr("z2T", [D, T], BF16)
        self.Hs = {2048: self.dscr("Hs2048", [2, 2, 2048, D], F32), 256: self.dscr("Hs256", [2, 2, 256, D], F32)}

    def stage_hyena(self, l, j):
        nc, S = self.nc, self.S
        self.hy_decl()
        self.stage_norm(l, 0)
        if S.stopped:
            return
        Win = self.W("hy_w_in", j)
        with ExitStack() as stk:
            yo = Rot([stk.enter_context(nc.sbuf_tensor(self.nm("yo"), [128, TW], F32)) for _ in range(4)], S.bufs_n(4))
            yv = self.yin.rearrange("(c p) t -> p c t", p=128)
            blocks = [([(Win, b * 512, 512)], [(0,), (128,), (256,), (384,)]) for b in range(12)]
            cnt = [0]

            def epi(bi, gi, tt, pl):
                oc = bi * 4 + gi
                (ps, Bp), = pl
                y_, By = yo.next()
                cnt[0] += 1
                S.op("act", lambda e: e.activation(out=y_[:], in_=ps[:], func=AF.Copy), reads=[Bp], writes=[By])
                S.op("sp", lambda e: e.dma_start(out=yv[:, oc, tt * TW:(tt + 1) * TW], in_=y_[:]), reads=[By], dma=1)
            self.gemm(self.hT, DC, blocks, epi, stk)
            S.flush()
        self.hy_shortconv(j)
        self.hy_filter(j, 2048)
        self.hy_filter(j, 256)
        self.hy_conv(j, 0)
        self.hy_conv(j, 1)
        if S.stopped:
            return
        Wo = self.W("hy_wo", j)
        with ExitStack() as stk:
            blocks = [([(Wo, b * 512, 512)], [(0,), (128,), (256,), (384,)]) for b in range(4)]
            epi = self.make_resid_epilogue(stk, l, 2, lambda bi, gi: bi * 4 + gi)
            self.gemm(self.z2T, DC, blocks, epi, stk)
            S.flush()

    SEQS = [(0, TL)] + [(TL + 256 * s_, 256) for s_ in range(4)]

    def hy_shortconv(self, j):
        nc, S = self.nc, self.S
        if S.stopped:
            return
        S.DEFER = 0
        with ExitStack() as stk:
            def sb(n, s, d):
                return stk.enter_context(nc.sbuf_tensor(self.nm(n), s, d))
            cw = sb("cw", [128, 48, 4], F32)
            ident = sb("ident", [128, 128], BF16)
            Bc = S.buf()
            S.op("sp", lambda e: e.dma_start(out=cw[:], in_=self.hyconv_in), writes=[Bc], dma=1)
            S.op("sp", lambda e: e.dma_start(out=ident[:], in_=self.ident_in), writes=[Bc], dma=1)
            yi = Rot([sb("yi", [128, T], F32) for _ in range(2)], S.bufs_n(2))
            uo = Rot([sb("uo", [128, T], F32) for _ in range(2)], S.bufs_n(2))
            ub = Rot([sb("ub", [128, T], BF16) for _ in range(2)], S.bufs_n(2))
            vt = Rot([sb("vt", [128, 4, 128], BF16) for _ in range(3)], S.bufs_n(3))
            tps = Rot([stk.enter_context(nc.psum_tensor(self.nm("tps"), [128, 4, 128], BF16)) for _ in range(3)], S.bufs_n(3))
            yv = self.yin.rearrange("(c p) t -> p c t", p=128)
            dsts = [self.vfm.rearrange("(c p) t -> p c t", p=128), self.x1fm.rearrange("(c p) t -> p c t", p=128),
                    self.x2fm.rearrange("(c p) t -> p c t", p=128)]
            ynext = None
            for oc in range(48):
                if ynext is None:
                    y_, By = yi.next()
                    S.op("sp", lambda e, y_=y_, oc=oc: e.dma_start(out=y_[:], in_=yv[:, oc, :]), writes=[By], dma=1)
                else:
                    y_, By = ynext
                u_, Bu = uo.next()
                S.op("act", lambda e, y_=y_, u_=u_, oc=oc: e.activation(out=u_[:], in_=y_[:], func=AF.Identity,
                                                                       bias=cw[:, oc, 3:4], scale=cw[:, oc, 1:2]), reads=[By, Bc], writes=[Bu])
                for (a, ln) in self.SEQS:
                    b = a + ln
                    S.op("dve", lambda e, y_=y_, u_=u_, oc=oc, a=a, b=b: e.scalar_tensor_tensor(
                        out=u_[:, a + 1:b], in0=y_[:, a:b - 1], scalar=cw[:, oc, 0:1], in1=u_[:, a + 1:b], op0=ALU.mult, op1=ALU.add),
                        reads=[By, Bu, Bc], writes=[Bu])
                    S.op("dve", lambda e, y_=y_, u_=u_, oc=oc, a=a, b=b: e.scalar_tensor_tensor(
                        out=u_[:, a:b - 1], in0=y_[:, a + 1:b], scalar=cw[:, oc, 2:3], in1=u_[:, a:b - 1], op0=ALU.mult, op1=ALU.add),
                        reads=[By, Bu, Bc], writes=[Bu])
                if oc + 1 < 48:
                    ynext = yi.next()
                    S.op("sp", lambda e, y2=ynext[0], oc=oc: e.dma_start(out=y2[:], in_=yv[:, oc + 1, :]), writes=[ynext[1]], dma=1)
                S.op("sp", lambda e, u_=u_, oc=oc: e.dma_start(out=dsts[oc // 16][:, oc % 16, :], in_=u_[:]), reads=[Bu], dma=1)
                if oc < 16:
                    self.fm_to_tm(u_, Bu, oc, self.vtmh, ub, vt, tps, ident, Bc)
            S.flush()

    def fm_to_tm(self, u_, Bu, oc, dst_tm, ub, vt, tps, ident, Bc, c0=0, ncols=T):
        S = self.S
        b_, Bb = ub.next()
        S.op("act", lambda e: e.activation(out=b_[:, 0:ncols], in_=u_[:, 0:ncols], func=AF.Copy), reads=[Bu], writes=[Bb])
        dv = dst_tm.rearrange("(c p) d -> p c d", p=128)
        for q in range(ncols // 512):
            tp, Bt = tps.next()
            v_, Bv = vt.next()
            for i in range(4):
                S.op("pe", lambda e, tp=tp, i=i, q=q: e.transpose(tp[:, i, :], b_[:, q * 512 + i * 128: q * 512 + (i + 1) * 128], ident[:]),
                     reads=[Bb, Bc], writes=[Bt])
            S.op("act", lambda e, tp=tp, v_=v_: e.activation(out=v_[:], in_=tp[:], func=AF.Copy), reads=[Bt], writes=[Bv])
            tc0 = (c0 + q * 512) // 128
            S.op("sp", lambda e, v_=v_, tc0=tc0: e.dma_start(out=dv[:, tc0:tc0 + 4, oc * 128:(oc + 1) * 128], in_=v_[:]), reads=[Bv], dma=1)

    def hy_filter(self, j, L):
        nc, S = self.nc, self.S
        if S.stopped:
            return
        S.DEFER = 0
        nT = L // 128
        NW = min(512, L)
        TWO_PI = 2.0 * math.pi
        Hs = self.Hs[L]
        feats_in = self.feats2048_in if L == 2048 else self.feats256_in
        tnn_in = self.tnn2048_in if L == 2048 else self.tnn256_in
        tabF = self.tabF2048_in if L == 2048 else self.tabF256_in
        with ExitStack() as stk:
            def sb(n, s, d):
                return stk.enter_context(nc.sbuf_tensor(self.nm(n), s, d))
            feats = sb("feats", [33, L], F32)
            w1 = sb("fw1", [33, 64], F32)
            w2 = sb("fw2", [64, 64], F32)
            w3 = sb("fw3", [64, 4 * D], F32)
            fb = sb("ffb", [64, 4], F32)
            fraw = sb("fraw", [64, 4], F32)
            tnn = sb("tnn", [128, nT], F32)
            drep = sb("drep", [128, D], F32)
            a1 = sb("a1", [64, L], F32)
            a2 = sb("a2", [64, L], F32)
            Bc, Ba1, Ba2 = S.buf(), S.buf(), S.buf()
            S.op("sp", lambda e: e.dma_start(out=feats[:], in_=feats_in), writes=[Bc], dma=1)
            S.op("sp", lambda e: e.dma_start(out=w1[:], in_=self.hyw1_in), writes=[Bc], dma=1)
            S.op("sp", lambda e: e.dma_start(out=w2[:], in_=self.hyw2_in), writes=[Bc], dma=1)
            S.op("sp", lambda e: e.dma_start(out=w3[:], in_=self.hyw3_in), writes=[Bc], dma=1)
            S.op("sp", lambda e: e.dma_start(out=fraw[:], in_=self.hyfb_in), writes=[Bc], dma=1)
            S.op("sp", lambda e: e.dma_start(out=tnn[:], in_=tnn_in), writes=[Bc], dma=1)
            S.op("sp", lambda e: e.dma_start(out=drep[:], in_=self.drep_in), writes=[Bc], dma=1)
            S.op("act", lambda e: e.activation(out=fb[:, 0:1], in_=fraw[:, 0:1], func=AF.Copy), reads=[Bc], writes=[Bc])
            S.op("act", lambda e: e.activation(out=fb[:, 2:3], in_=fraw[:, 1:2], func=AF.Copy), reads=[Bc], writes=[Bc])
            S.op("dve", lambda e: e.tensor_tensor(out=fb[:, 1:2], in0=fraw[:, 2:3], in1=fraw[:, 0:1], op=ALU.mult), reads=[Bc], writes=[Bc])
            S.op("dve", lambda e: e.tensor_tensor(out=fb[:, 3:4], in0=fraw[:, 3:4], in1=fraw[:, 1:2], op=ALU.mult), reads=[Bc], writes=[Bc])
            fps = Rot([stk.enter_context(nc.psum_tensor(self.nm("fps"), [128, 512], F32)) for _ in range(4)], S.bufs_n(4))
            nps = Rot([stk.enter_context(nc.psum_tensor(self.nm("fnps"), [128, 512], F32)) for _ in range(1)], S.bufs_n(1))
            hps = Rot([stk.enter_context(nc.psum_tensor(self.nm("hps"), [128, 512], F32)) for _ in range(3)], S.bufs_n(3))
            arg = Rot([sb("arg", [64, NW], F32) for _ in range(2)], S.bufs_n(2))
            kf = Rot([sb("kf", [64, NW], F32) for _ in range(2)], S.bufs_n(2))
            ki = Rot([sb("ki", [64, NW], mybir.dt.int32) for _ in range(2)], S.bufs_n(2))

            def sin_layer(src, Bsrc, wt, kdim, col, dst, Bdst):
                for c in range(L // NW):
                    ps, Bp = fps.next()
                    a_, Ba = arg.next()
                    k_, Bk = kf.next()
                    i_, Bi = ki.next()
                    S.op("pe", lambda e, ps=ps, c=c: e.matmul(ps[0:64, 0:NW], lhsT=wt[0:kdim, :], rhs=src[0:kdim, c * NW:(c + 1) * NW],
                                                             start=True, stop=True), reads=[Bsrc, Bc], writes=[Bp])
                    S.op("act", lambda e, ps=ps, a_=a_: e.activation(out=a_[:], in_=ps[0:64, 0:NW], func=AF.Identity,
                                                                    bias=fb[:, col + 1:col + 2], scale=fb[:, col:col + 1]), reads=[Bp, Bc], writes=[Ba])
                    S.op("dve", lambda e, a_=a_, k_=k_: e.tensor_scalar(out=k_[:], in0=a_[:], scalar1=1.0 / TWO_PI, scalar2=12582912.0, op0=ALU.mult, op1=ALU.add),
                         reads=[Ba], writes=[Bk])
                    S.op("dve", lambda e, k_=k_: e.tensor_scalar(out=k_[:], in0=k_[:], scalar1=12582912.0, scalar2=None, op0=ALU.subtract),
                         reads=[Bk], writes=[Bk])
                    S.op("dve", lambda e, a_=a_, k_=k_: e.scalar_tensor_tensor(out=a_[:], in0=k_[:], scalar=-TWO_PI, in1=a_[:], op0=ALU.mult, op1=ALU.add),
                         reads=[Bk, Ba], writes=[Ba])
                    S.op("dve", lambda e, a_=a_, k_=k_: e.tensor_scalar(out=k_[:], in0=a_[:], scalar1=math.pi, scalar2=TWO_PI, op0=ALU.is_gt, op1=ALU.mult),
                         reads=[Ba], writes=[Bk])
                    S.op("dve", lambda e, a_=a_, k_=k_: e.tensor_tensor(out=a_[:], in0=a_[:], in1=k_[:], op=ALU.subtract), reads=[Ba, Bk], writes=[Ba])
                    S.op("dve", lambda e, a_=a_, k_=k_: e.tensor_scalar(out=k_[:], in0=a_[:], scalar1=-math.pi, scalar2=TWO_PI, op0=ALU.is_lt, op1=ALU.mult),
                         reads=[Ba], writes=[Bk])
                    S.op("dve", lambda e, a_=a_, k_=k_: e.tensor_tensor(out=a_[:], in0=a_[:], in1=k_[:], op=ALU.add), reads=[Ba, Bk], writes=[Ba])
                    S.op("act", lambda e, a_=a_, c=c: e.activation(out=dst[:, c * NW:(c + 1) * NW], in_=a_[:], func=AF.Sin), reads=[Ba], writes=[Bdst])
            sin_layer(feats, Bc, w1, 33, 0, a1, Ba1)
            sin_layer(a1, Ba1, w2, 64, 2, a2, Ba2)
            gp = [sb("gp", [128, nT, 512], BF16) for _ in range(2)]
            gm = [sb("gm", [128, nT, 512], BF16) for _ in range(2)]
            Bg = S.bufs_n(2)
            dec = Rot([sb("dec", [128, 512], F32) for _ in range(2)], S.bufs_n(2))
            fw = Rot([sb("fw", [128, 512], F32) for _ in range(2)], S.bufs_n(2))
            bw = Rot([sb("bw", [128, 512], F32) for _ in range(2)], S.bufs_n(2))
            ab = Rot([sb("ab", [128, 2, 512], BF16) for _ in range(3)], S.bufs_n(3))
            rn = Rot([sb("rn", [128, 512], F32) for _ in range(2)], S.bufs_n(2))
            tb = Rot([sb("ftb", [128, 2, nT, 128], BF16) for _ in range(3)], S.bufs_n(3))
            ho = Rot([sb("ho", [128, 2, 512], F32) for _ in range(2)], S.bufs_n(2))
            it = 0
            for o in range(2):
                for db in range(4):
                    k = it % 2
                    it += 1
                    npz, Bn = nps.next()
                    fpend = []
                    for tc in range(nT):
                        pf, Bpf = fps.next()
                        pb, Bpb = fps.next()
                        d_, Bd = dec.next()
                        f_, Bf = fw.next()
                        b_, Bb = bw.next()
                        a_, Bab = ab.next()
                        cf = o * D + db * 512
                        S.op("pe", lambda e, pf=pf, tc=tc, cf=cf: e.matmul(pf[:], lhsT=a2[:, tc * 128:(tc + 1) * 128], rhs=w3[:, cf:cf + 512],
                                                                          start=True, stop=True), reads=[Ba2, Bc], writes=[Bpf])
                        S.op("pe", lambda e, pb=pb, tc=tc, cf=cf: e.matmul(pb[:], lhsT=a2[:, tc * 128:(tc + 1) * 128], rhs=w3[:, 2 * D + cf:2 * D + cf + 512],
                                                                          start=True, stop=True), reads=[Ba2, Bc], writes=[Bpb])
                        for fn_ in fpend:
                            fn_()
                        fpend.clear()
                        S.op("act", lambda e, d_=d_, tc=tc, db=db: e.activation(out=d_[:], in_=drep[:, db * 512:(db + 1) * 512], func=AF.Exp,
                                                                               scale=tnn[:, tc:tc + 1]), reads=[Bc], writes=[Bd])
                        S.op("dve", lambda e, f_=f_, pf=pf, d_=d_: e.tensor_tensor(out=f_[:], in0=pf[:], in1=d_[:], op=ALU.mult), reads=[Bpf, Bd], writes=[Bf])
                        S.op("dve", lambda e, b_=b_, pb=pb, d_=d_: e.tensor_tensor(out=b_[:], in0=pb[:], in1=d_[:], op=ALU.mult), reads=[Bpb, Bd], writes=[Bb])
                        if tc == 0:
                            S.op("dve", lambda e, b_=b_: e.memset(b_[0:1, :], 0.0), reads=[Bb], writes=[Bb])
                        S.op("pool", lambda e, f_=f_, b_=b_, k=k, tc=tc: e.tensor_tensor(out=gp[k][:, tc, :], in0=f_[:], in1=b_[:], op=ALU.add),
                             reads=[Bf, Bb], writes=[Bg[k]])
                        S.op("pool", lambda e, f_=f_, b_=b_, k=k, tc=tc: e.tensor_tensor(out=gm[k][:, tc, :], in0=f_[:], in1=b_[:], op=ALU.subtract),
                             reads=[Bf, Bb], writes=[Bg[k]])
                        S.op("act", lambda e, f_=f_, a_=a_: e.activation(out=a_[:, 0, :], in_=f_[:], func=AF.Abs), reads=[Bf], writes=[Bab])
                        S.op("act", lambda e, b_=b_, a_=a_: e.activation(out=a_[:, 1, :], in_=b_[:], func=AF.Abs), reads=[Bb], writes=[Bab])
                        def ones_mm(npz=npz, a_=a_, tc=tc, Bab=Bab, Bn=Bn):
                            for h in range(2):
                                S.op("pe", lambda e, h=h: e.matmul(npz[:], lhsT=self.ones_bf[:], rhs=a_[:, h, :],
                                                                   start=(tc == 0 and h == 0), stop=(tc == nT - 1 and h == 1)),
                                     reads=[Bab], writes=[Bn])
                        fpend.append(ones_mm)
                    for fn_ in fpend:
                        fn_()
                    fpend.clear()
                    r_, Br = rn.next()
                    S.op("dve", lambda e, r_=r_, npz=npz: e.tensor_scalar(out=r_[:], in0=npz[:], scalar1=EPS, scalar2=None, op0=ALU.add),
                         reads=[Bn], writes=[Br])
                    S.op("dve", lambda e, r_=r_: e.reciprocal(out=r_[:], in_=r_[:]), reads=[Br], writes=[Br])
                    tnext = None
                    for fc in range(nT):
                        if tnext is None:
                            t_, Bt = tb.next()
                            S.op("sp", lambda e, t_=t_, fc=fc: e.dma_start(out=t_[:], in_=tabF[fc]), writes=[Bt], dma=1)
                        else:
                            t_, Bt = tnext
                        if fc + 1 < nT:
                            tnext = tb.next()
                            S.op("sp", lambda e, t2=tnext[0], fc=fc: e.dma_start(out=t2[:], in_=tabF[fc + 1]), writes=[tnext[1]], dma=1)
                        h_, Bh = ho.next()
                        for ri, gsrc in enumerate((gp, gm)):
                            hp, Bhp = hps.next()
                            for tc in range(nT):
                                S.op("pe", lambda e, hp=hp, t_=t_, ri=ri, tc=tc, gsrc=gsrc, k=k: e.matmul(
                                    hp[:], lhsT=t_[:, ri, tc, :], rhs=gsrc[k][:, tc, :], start=(tc == 0), stop=(tc == nT - 1)),
                                    reads=[Bt, Bg[k]], writes=[Bhp])
                            S.op("dve", lambda e, hp=hp, h_=h_, ri=ri, r_=r_: e.tensor_tensor(out=h_[:, ri, :], in0=hp[:], in1=r_[:], op=ALU.mult),
                                 reads=[Bhp, Br], writes=[Bh])
                        S.op("sp", lambda e, h_=h_, o=o, fc=fc, db=db: e.dma_start(
                            out=Hs[o, :, fc * 128:(fc + 1) * 128, db * 512:(db + 1) * 512].rearrange("r p d -> p r d"), in_=h_[:]),
                            reads=[Bh], dma=1)
            S.flush()

    def hy_conv(self, j, o):
        nc, S = self.nc, self.S
        if S.stopped:
            return
        S.DEFER = 0
        src_tm = self.vtmh if o == 0 else self.ztmh
        src_fm = self.vfm if o == 0 else self.z1fm
        gate_fm = self.x1fm if o == 0 else self.x2fm
        with ExitStack() as stk:
            def sb(n, s, d):
                return stk.enter_context(nc.sbuf_tensor(self.nm(n), s, d))
            skip = sb("skip", [128, 2, DC], F32)
            ident = sb("ident2", [128, 128], BF16)
            Bc = S.buf()
            S.op("sp", lambda e: e.dma_start(out=skip[:], in_=self.hyskip_in), writes=[Bc], dma=1)
            S.op("sp", lambda e: e.dma_start(out=ident[:], in_=self.ident_in), writes=[Bc], dma=1)
            vb = Rot([sb("cvb", [128, 16, 512], BF16) for _ in range(2)], S.bufs_n(2))
            yre = Rot([sb("yre", [128, 16, 512], BF16) for _ in range(1)], S.bufs_n(1))
            yim = Rot([sb("yim", [128, 16, 512], BF16) for _ in range(1)], S.bufs_n(1))
            tb = Rot([sb("ctb", [128, 2, 16, 128], BF16) for _ in range(3)], S.bufs_n(3))
            hh = Rot([sb("hh", [128, 2, 512], F32) for _ in range(2)], S.bufs_n(2))
            tt_ = [Rot([sb("cp%d" % i, [128, 512], F32) for _ in range(2)], S.bufs_n(2)) for i in range(4)]
            ti = Rot([sb("cti", [128, 2, 16, 512], BF16) for _ in range(2)], S.bufs_n(2))
            ui = Rot([sb("cui", [128, 512], F32) for _ in range(2)], S.bufs_n(2))
            gi_ = Rot([sb("cgi", [128, 512], F32) for _ in range(2)], S.bufs_n(2))
            zo = Rot([sb("czo", [128, 512], F32) for _ in range(3)], S.bufs_n(3))
            zb = Rot([sb("czb", [128, 512], BF16) for _ in range(2)], S.bufs_n(2))
            vt = Rot([sb("cvt", [128, 4, 128], BF16) for _ in range(2)], S.bufs_n(2))
            ups = Rot([stk.enter_context(nc.psum_tensor(self.nm("ups"), [128, 512], F32)) for _ in range(4)], S.bufs_n(4))
            yps = Rot([stk.enter_context(nc.psum_tensor(self.nm("yps"), [128, 512], F32)) for _ in range(2)], S.bufs_n(2))
            tps = Rot([stk.enter_context(nc.psum_tensor(self.nm("ctps"), [128, 4, 128], BF16)) for _ in range(2)], S.bufs_n(2))
            sfv = src_fm.rearrange("(c p) t -> p c t", p=128)
            gfv = gate_fm.rearrange("(c p) t -> p c t", p=128)
            z1v = self.z1fm.rearrange("(c p) t -> p c t", p=128)
            z2v = self.z2T.rearrange("(c p) t -> p c t", p=128)
            pending = []
            for (t0, L) in self.SEQS:
                nT = L // 128
                NW = min(512, L)
                Hs = self.Hs[L]
                tabF = self.tabF2048_in if L == 2048 else self.tabF256_in
                tabI = self.tabI2048_in if L == 2048 else self.tabI256_in
                for db in range(4):
                    v_, Bv = vb.next()
                    yr, Byr = yre.next()
                    yi, Byi = yim.next()
                    S.op("sp", lambda e, v_=v_, t0=t0, L=L, nT=nT, db=db: e.dma_start(
                        out=v_[:, 0:nT, :], in_=src_tm[t0:t0 + L, db * 512:(db + 1) * 512].rearrange("(c p) d -> p c d", p=128)),
                        writes=[Bv], dma=1)
                    for fc in range(nT):
                        t_, Bt = tb.next()
                        h_, Bh = hh.next()
                        S.op("sp", lambda e, t_=t_, fc=fc, nT=nT, tabF=tabF: e.dma_start(out=t_[:, :, 0:nT, :], in_=tabF[fc]), writes=[Bt], dma=1)
                        S.op("sp", lambda e, h_=h_, fc=fc, db=db, Hs=Hs: e.dma_start(
                            out=h_[:], in_=Hs[o, :, fc * 128:(fc + 1) * 128, db * 512:(db + 1) * 512].rearrange("r p d -> p r d")),
                            writes=[Bh], dma=1)
                        pu = []
                        for ri in range(2):
                            p_, Bp = ups.next()
                            for tc in range(nT):
                                S.op("pe", lambda e, p_=p_, t_=t_, ri=ri, tc=tc, v_=v_, nT=nT: e.matmul(
                                    p_[:], lhsT=t_[:, ri, tc, :], rhs=v_[:, tc, :], start=(tc == 0), stop=(tc == nT - 1)),
                                    reads=[Bt, Bv], writes=[Bp])
                            pu.append((p_, Bp))
                        (ur, Bur), (um, Bum) = pu
                        (a1_, Ba1), (a2_, Ba2), (a3_, Ba3), (a4_, Ba4) = [r.next() for r in tt_]
                        S.op("dve", lambda e, a1_=a1_, ur=ur, h_=h_: e.tensor_tensor(out=a1_[:], in0=ur[:], in1=h_[:, 0, :], op=ALU.mult), reads=[Bur, Bh], writes=[Ba1])
                        S.op("dve", lambda e, a2_=a2_, um=um, h_=h_: e.tensor_tensor(out=a2_[:], in0=um[:], in1=h_[:, 1, :], op=ALU.mult), reads=[Bum, Bh], writes=[Ba2])
                        S.op("dve", lambda e, a3_=a3_, um=um, h_=h_: e.tensor_tensor(out=a3_[:], in0=um[:], in1=h_[:, 0, :], op=ALU.mult), reads=[Bum, Bh], writes=[Ba3])
                        S.op("dve", lambda e, a4_=a4_, ur=ur, h_=h_: e.tensor_tensor(out=a4_[:], in0=ur[:], in1=h_[:, 1, :], op=ALU.mult), reads=[Bur, Bh], writes=[Ba4])
                        S.op("pool", lambda e, yr=yr, fc=fc, a1_=a1_, a2_=a2_: e.tensor_tensor(out=yr[:, fc, :], in0=a1_[:], in1=a2_[:], op=ALU.subtract),
                             reads=[Ba1, Ba2], writes=[Byr])
                        S.op("pool", lambda e, yi=yi, fc=fc, a3_=a3_, a4_=a4_: e.tensor_tensor(out=yi[:, fc, :], in0=a3_[:], in1=a4_[:], op=ALU.add),
                             reads=[Ba3, Ba4], writes=[Byi])
                    def ld_ug(tt, dc):
                        oc_ = db * 4 + dc
                        c0_ = t0 + tt * NW
                        u2, Bu2 = ui.next()
                        g2, Bg2 = gi_.next()
                        S.op("sp", lambda e, NW=NW: e.dma_start(out=u2[:, 0:NW], in_=sfv[:, oc_, c0_:c0_ + NW]), writes=[Bu2], dma=1)
                        S.op("sp", lambda e, NW=NW: e.dma_start(out=g2[:, 0:NW], in_=gfv[:, oc_, c0_:c0_ + NW]), writes=[Bg2], dma=1)
                        return (u2, Bu2, g2, Bg2)

                    def ld_tab(tt):
                        c2, Bc2 = ti.next()
                        S.op("sp", lambda e, NW=NW, nT=nT, tabI=tabI: [e.dma_start(
                            out=c2[:, r, 0:nT, 0:NW], in_=tabI[r, :, :, tt * NW:(tt + 1) * NW]) for r in range(2)],
                            writes=[Bc2], dma=2)
                        return (c2, Bc2)
                    its = [(tt, dc) for tt in range(L // NW) for dc in range(4)]
                    pre_ug = ld_ug(*its[0])
                    pre_tab = ld_tab(0)
                    for ii, (tt, dc) in enumerate(its):
                        if dc == 0:
                            c_, Bci = pre_tab
                            if tt + 1 < L // NW:
                                pre_tab = ld_tab(tt + 1)
                        if True:
                            oc = db * 4 + dc
                            col0 = t0 + tt * NW
                            y_, By = yps.next()
                            u_, Bu, g_, Bgt = pre_ug
                            if ii + 1 < len(its):
                                pre_ug = ld_ug(*its[ii + 1])
                            for fc in range(nT):
                                S.op("pe", lambda e, y_=y_, yr=yr, fc=fc, dc=dc, c_=c_, NW=NW: e.matmul(
                                    y_[:, 0:NW], lhsT=yr[:, fc, dc * 128:(dc + 1) * 128], rhs=c_[:, 0, fc, 0:NW], start=(fc == 0), stop=False),
                                    reads=[Byr, Bci], writes=[By])
                                S.op("pe", lambda e, y_=y_, yi=yi, fc=fc, dc=dc, c_=c_, NW=NW, nT=nT: e.matmul(
                                    y_[:, 0:NW], lhsT=yi[:, fc, dc * 128:(dc + 1) * 128], rhs=c_[:, 1, fc, 0:NW], start=False, stop=(fc == nT - 1)),
                                    reads=[Byi, Bci], writes=[By])
                            for fn_ in pending:
                                fn_()
                            pending.clear()
                            S.op("dve", lambda e, u_=u_, y_=y_, oc=oc, NW=NW: e.scalar_tensor_tensor(
                                out=u_[:, 0:NW], in0=u_[:, 0:NW], scalar=skip[:, o, oc:oc + 1], in1=y_[:, 0:NW], op0=ALU.mult, op1=ALU.add),
                                reads=[Bu, By, Bc], writes=[Bu])
                            if o == 0:
                                z_, Bz = zo.next()
                                S.op("pool", lambda e, z_=z_, u_=u_, g_=g_, NW=NW: e.tensor_tensor(out=z_[:, 0:NW], in0=u_[:, 0:NW], in1=g_[:, 0:NW], op=ALU.mult),
                                     reads=[Bu, Bgt], writes=[Bz])
                                S.op("sp", lambda e, z_=z_, oc=oc, col0=col0, NW=NW: e.dma_start(out=z1v[:, oc, col0:col0 + NW], in_=z_[:, 0:NW]), reads=[Bz], dma=1)
                                if NW == 512:
                                    pending.append(lambda z_=z_, Bz=Bz, oc=oc, col0=col0, NW=NW: self.fm_to_tm(
                                        z_, Bz, oc, self.ztmh, zb, vt, tps, ident, Bc, c0=col0, ncols=NW))
                                else:
                                    pending.append(lambda z_=z_, Bz=Bz, oc=oc, col0=col0: self.fm_to_tm_small(
                                        z_, Bz, oc, zb, vt, tps, ident, Bc, col0))
                            else:
                                zb_, Bzb = zb.next()
                                S.op("pool", lambda e, zb_=zb_, u_=u_, g_=g_, NW=NW: e.tensor_tensor(out=zb_[:, 0:NW], in0=u_[:, 0:NW], in1=g_[:, 0:NW], op=ALU.mult),
                                     reads=[Bu, Bgt], writes=[Bzb])
                                S.op("sp", lambda e, zb_=zb_, oc=oc, col0=col0, NW=NW: e.dma_start(out=z2v[:, oc, col0:col0 + NW], in_=zb_[:, 0:NW]), reads=[Bzb], dma=1)
            for fn_ in pending:
                fn_()
            pending.clear()
            S.flush()

    def fm_to_tm_small(self, u_, Bu, oc, ub, vt, tps, ident, Bc, c0):
        S = self.S
        b_, Bb = ub.next()
        S.op("act", lambda e: e.activation(out=b_[:, 0:256], in_=u_[:, 0:256], func=AF.Copy), reads=[Bu], writes=[Bb])
        dv = self.ztmh.rearrange("(c p) d -> p c d", p=128)
        tp, Bt = tps.next()
        v_, Bv = vt.next()
        for i in range(2):
            S.op("pe", lambda e, i=i: e.transpose(tp[:, i, :], b_[:, i * 128:(i + 1) * 128], ident[:]), reads=[Bb, Bc], writes=[Bt])
        S.op("act", lambda e: e.activation(out=v_[:, 0:2, :], in_=tp[:, 0:2, :], func=AF.Copy), reads=[Bt], writes=[Bv])
        tc0 = c0 // 128
        S.op("sp", lambda e: e.dma_start(out=dv[:, tc0:tc0 + 2, oc * 128:(oc + 1) * 128], in_=v_[:, 0:2, :]), reads=[Bv], dma=1)


def pp(v):
    v = np.asarray(v)
    n = v.shape[-1] // 128
    return np.ascontiguousarray(np.moveaxis(v.reshape(v.shape[:-1] + (n, 128)), -1, 0))


def rope_tables():
    half = HD // 2
    t = np.arange(TL)
    row = (t // 64).astype(np.float32)
    col = (t % 64).astype(np.float32)
    inv = (10000.0 ** (-np.arange(0, half, 2, dtype=np.float32) / half)).astype(np.float32)
    ang = np.zeros((128, TL), np.float32)
    ang[0:32] = inv[:, None] * row[None, :]
    ang[32:64] = inv[:, None] * row[None, :]
    ang[64:96] = inv[:, None] * col[None, :]
    ang[96:128] = inv[:, None] * col[None, :]
    tab = np.stack([np.cos(ang), np.sin(ang)], axis=1).astype(np.float32)
    P = np.zeros((128, 128), np.float32)
    for base in (0, 64):
        for m in range(32):
            P[base + m + 32, base + m] = -1.0
            P[base + m, base + m + 32] = 1.0
    return tab, P.astype(NPBF)


def window_masks():
    kj = np.arange(128)[:, None]
    qi = np.arange(128)[None, :]
    prev = (kj >= qi).astype(np.float32)
    nxt = (kj <= qi).astype(np.float32)
    m = np.stack([np.tile(prev, (1, 4)), np.tile(nxt, (1, 4))], axis=1)
    return m.astype(NPBF)


def make_in_maps(inp, kb):
    f = lambda a: np.ascontiguousarray(np.asarray(a, dtype=np.float32))
    rope, P = rope_tables()
    wm = window_masks()
    shared = {
        "normg_in": pp(np.stack([f(inp["norm_mix_g"]), f(inp["norm_ffn_g"])], 0)),
        "modb_in": pp(f(inp["mod_b"])),
        "finalg_in": pp(f(inp["final_g"])),
        "esink_in": np.ascontiguousarray(np.repeat(f(inp["win_sink"]), 128, axis=1)[:, None, :]),
        "axg_in": np.ascontiguousarray(np.stack([f(inp["ax_q_g"])[0], f(inp["ax_k_g"])[0]], axis=1)),
        "axkg_rep": np.ascontiguousarray(np.tile(f(inp["ax_k_g"])[0][None, :], (128, 1))),
        "rope_in": rope, "ropeP_in": P, "wmask_in": wm,
    }
    shared.update(hyena_shared(inp))
    for name in kb.inputs:
        if "__" in name:
            base, idx = name.split("__")
            shared[name] = np.ascontiguousarray(f(inp[base])[int(idx)])
    maps = []
    xs, xp = f(inp["x_sample"]), f(inp["x_prompt"])
    for i in range(8):
        m = dict(shared)
        xt = np.concatenate([xs[i], xp[4 * i:4 * i + 4].reshape(1024, D)], axis=0)
        m["xT_in"] = np.ascontiguousarray(xt.T)
        m["cvec"] = np.ascontiguousarray(np.stack([pp(f(inp["c"])[i]), pp(f(inp["c_ctx"]))], axis=-1))
        m["cwkT"] = np.ascontiguousarray(f(inp["cache_win_k"])[i].transpose(0, 2, 3, 1))
        m["cwv"] = np.ascontiguousarray(f(inp["cache_win_v"])[i].reshape(2, PAST, 512))
        m["cakT"] = np.ascontiguousarray(f(inp["cache_ax_k"])[i].transpose(0, 2, 3, 1))
        m["cav"] = np.ascontiguousarray(f(inp["cache_ax_v"])[i].reshape(1, PAST, 512))
        maps.append({k: v for k, v in m.items() if k in kb.inputs})
    return maps


def hyena_shared(inp):
    f = lambda a: np.ascontiguousarray(np.asarray(a, dtype=np.float32))
    out = {}
    cw = f(inp["hy_conv_w"])[0]
    cb = f(inp["hy_conv_b"])[0]
    out["hyconv_in"] = pp(np.stack([cw[0], cw[1], cw[2], cb], axis=0)).transpose(0, 2, 1).copy()
    out["ident_in"] = np.eye(128, dtype=np.float32).astype(NPBF)
    out["hyskip_in"] = pp(f(inp["hy_skip"])[0])
    out["hyw1_in"] = f(inp["hy_f_w1"])[0]
    out["hyw2_in"] = f(inp["hy_f_w2"])[0]
    out["hyw3_in"] = f(inp["hy_f_w3"])[0]
    fr = f(inp["hy_freq"])[0]
    out["hyfb_in"] = np.ascontiguousarray(np.stack([fr[0], fr[1], f(inp["hy_f_b1"])[0], f(inp["hy_f_b2"])[0]], axis=1))
    HY_MIN = math.log(1e-2) / 1.5
    HY_MAX = math.log(1e-2) / 0.3
    deltas = np.abs(np.linspace(HY_MIN, HY_MAX, D, dtype=np.float32))
    out["drep_in"] = np.ascontiguousarray(np.tile(deltas[None, :], (128, 1)).astype(np.float32))
    for L in (2048, 256):
        t = np.arange(L, dtype=np.float32)
        tn = (t / max(L - 1, 1)).astype(np.float32)
        bands = 16
        fbv = np.linspace(1e-4, bands - 1, bands, dtype=np.float32)
        w = (np.float32(2.0 * math.pi) * t / np.float32(L)).astype(np.float32)
        feats = np.concatenate([tn[:, None], np.cos(w[:, None] * fbv), -np.sin(w[:, None] * fbv)], axis=-1).astype(np.float32)
        out["feats%d_in" % L] = np.ascontiguousarray(feats.T)
        out["tnn%d_in" % L] = pp(-tn)
        nT = L // 128
        tt = np.arange(L, dtype=np.float64)
        ff = np.arange(L, dtype=np.float64)
        ang = np.pi * np.outer(tt, 2 * ff + 1) / (2 * L)
        C = np.cos(ang)
        S_ = np.sin(ang)
        CF = np.stack([C, S_], 0).reshape(2, nT, 128, nT, 128)
        out["tabF%d_in" % L] = np.ascontiguousarray(CF.transpose(3, 2, 0, 1, 4)).astype(NPBF)
        CI = np.stack([C.T, S_.T], 0) / L
        out["tabI%d_in" % L] = np.ascontiguousarray(CI.reshape(2, nT, 128, L).transpose(0, 2, 1, 3)).astype(NPBF)
    return out


_CACHE = {}


def get_kb():
    if "kb" not in _CACHE:
        kb = KB()
        kb.build()
        _CACHE["kb"] = kb
    return _CACHE["kb"]


def kernel(**inp):
    kb = get_kb()
    maps = make_in_maps(inp, kb)
    res = run_bass_kernel_spmd(kb.nc, maps, core_ids=list(range(8)))
    R = res.results
    y_s = np.stack([R[i]["yT"][:, :TL].T for i in range(8)], 0)
    y_p = np.concatenate([R[i]["yT"][:, TL:].T.reshape(4, 256, D) for i in range(8)], 0)
    outs = [np.ascontiguousarray(y_p), np.ascontiguousarray(y_s)]
    for nm_ in ("swk", "swv", "sak", "sav"):
        a = np.concatenate([R[i][nm_] for i in range(8)], 0)
        outs.append(np.ascontiguousarray(a.reshape(a.shape[0], a.shape[1], 256, NKV, HD)))
    return tuple(outs)
```
